# Optimizing a Trainium2 kernel written in Bass

```python
import math
import jax, jax.numpy as jnp
from jax import lax
import numpy as np

D_MODEL = 2048
BATCH = 32
SEQ = 256
DEPTH = 2
DEC_BATCH = 8
DEC_SEQ = 1024
PAST_LEN = 256

GRID_W = 64
N_BRANCH = 4
BRANCH_WIDTH = D_MODEL // N_BRANCH
HEAD_DIM = 64
A_QK = 64
A_V = 2 * A_QK
A_HEADS = BRANCH_WIDTH // A_V
B_HEADS = BRANCH_WIDTH // HEAD_DIM
NA_ROWS = 8
NA_COLS = 16
NA_SPAN = 2 * NA_COLS
C_HEADS = BRANCH_WIDTH // HEAD_DIM
C_KV_HEADS = C_HEADS // 4
C_GROUP = C_HEADS // C_KV_HEADS
SWA_WINDOW = 128
BLOCK = 128
HY_WIDTH = BRANCH_WIDTH
HY_ORDER = 2
HY_BANDS = 16
HY_EMB = 1 + 2 * HY_BANDS
HY_FFN = 64
HY_DECAY_MIN = 3.0701
HY_DECAY_MAX = 15.3506
D_FF = 4 * D_MODEL
ROPE_BASE = 10000.0
EPS = 1e-6
NEG_INF = -1e30
A_QK_COLS = 2 * A_HEADS * A_QK
A_V_COLS = A_HEADS * A_V
B_COLS = B_HEADS * HEAD_DIM
C_Q_COLS = C_HEADS * HEAD_DIM
C_KV_COLS = C_KV_HEADS * HEAD_DIM
HY_COLS = 3 * HY_WIDTH
GATE_COLS = N_BRANCH * D_MODEL
IN_SPLITS = (A_QK_COLS, A_QK_COLS, A_V_COLS, B_COLS, B_COLS, B_COLS, C_Q_COLS, C_KV_COLS, C_KV_COLS, HY_COLS, GATE_COLS)
IN_COLS = sum(IN_SPLITS)

kernel_name = 'hybrid_diffusion_prefix_trunk_step'

f32 = jnp.float32


def rmsnorm(x, g):
    xf = x.astype(f32)
    y = xf * lax.rsqrt(jnp.mean(xf * xf, axis=-1, keepdims=True) + EPS)
    return (y * g.astype(f32)).astype(x.dtype)


def adaln(cvec, w_ada, b_ada):
    m = jax.nn.silu(cvec) @ w_ada + b_ada
    return jnp.split(m, 6, axis=-1)


def in_projection(x, shift, scale, g, w_in):
    h = rmsnorm(x, g) * (1 + scale) + shift
    offsets = np.cumsum(IN_SPLITS)[:-1].tolist()
    return jnp.split(h @ w_in, offsets, axis=-1)


def rope_2d(x):
    L, d = x.shape[1], x.shape[-1]
    half = d // 2
    nf = half // 2
    inv = ROPE_BASE ** (-jnp.arange(nf, dtype=f32) / nf)
    t = jnp.arange(L)
    shape = (1, L) + (1,) * (x.ndim - 3) + (nf,)

    def rot(xh, pos):
        ang = (pos.astype(f32)[:, None] * inv[None, :]).reshape(shape)
        cos, sin = jnp.cos(ang), jnp.sin(ang)
        x1, x2 = xh[..., :nf], xh[..., nf:]
        return jnp.concatenate([x1 * cos - x2 * sin, x1 * sin + x2 * cos], axis=-1)

    xf = x.astype(f32)
    out = jnp.concatenate([rot(xf[..., :half], t // GRID_W), rot(xf[..., half:], t % GRID_W)], axis=-1)
    return out.astype(x.dtype)


def over_query_blocks(fn, q):
    B, L = q.shape[:2]
    nb = L // BLOCK
    qb = jnp.moveaxis(q.reshape((B, nb, BLOCK) + q.shape[2:]), 1, 0)
    o = jnp.moveaxis(lax.map(fn, qb), 0, 1)
    return o.reshape((B, L) + o.shape[3:])


def softmax_with_sink(s, sink_gr):
    col = jnp.broadcast_to(sink_gr.astype(f32)[:, :, None, None], s.shape[:-1] + (1,))
    return jax.nn.softmax(jnp.concatenate([s, col], axis=-1), axis=-1)[..., :-1]


def dense_attention(q, k, v, sink):
    G, R, d = q.shape[2:]
    scale = d ** -0.5
    kf, vf = k.astype(f32), v.astype(f32)

    def block(qb):
        s = jnp.einsum('bqgrd,bkgd->bgrqk', qb.astype(f32) * scale, kf)
        pr = jax.nn.softmax(s, axis=-1) if sink is None else softmax_with_sink(s, sink.reshape(G, R))
        return jnp.einsum('bgrqk,bkgd->bqgrd', pr, vf)

    return over_query_blocks(block, q).astype(q.dtype)


def diff_lambda_value(lam_p, li):
    lam_init = 0.8 - 0.6 * math.exp(-0.3 * li)
    lf = lam_p.astype(f32)
    lam = jnp.exp(jnp.sum(lf[0] * lf[1])) - jnp.exp(jnp.sum(lf[2] * lf[3])) + lam_init
    return lam, lam_init


def diff_attention(q, k, v, lam, lam_init, g):
    scale = q.shape[-1] ** -0.5
    kf, vf = k.astype(f32), v.astype(f32)

    def block(qb):
        s = jnp.einsum('bqchd,bkchd->bchqk', qb.astype(f32) * scale, kf)
        pr = jax.nn.softmax(s, axis=-1)
        w = pr[:, 0] - lam * pr[:, 1]
        return jnp.einsum('bhqk,bkhd->bqhd', w, vf)

    o = over_query_blocks(block, q)
    return (rmsnorm(o, g) * (1.0 - lam_init)).astype(q.dtype)


def na_latent(q, k, v, kc, vc, rel_bias):
    B, L, H, d = q.shape
    rows = L // GRID_W
    wr = min(NA_ROWS, rows)
    ncb = GRID_W // NA_COLS
    nk = wr * NA_SPAN
    r = jnp.arange(rows)
    row_idx = jnp.clip(r - wr // 2, 0, rows - wr)[:, None] + jnp.arange(wr)[None, :]
    qcol = jnp.arange(ncb)[:, None] * NA_COLS + jnp.arange(NA_COLS)[None, :]
    span0 = jnp.clip(jnp.arange(ncb) * NA_COLS - NA_COLS // 2, 0, GRID_W - NA_SPAN)
    col_idx = span0[:, None] + jnp.arange(NA_SPAN)[None, :]
    col0 = jnp.clip(qcol - NA_COLS // 2, 0, GRID_W - NA_COLS)
    kcol = col_idx[:, None, :]
    valid = (kcol >= col0[..., None]) & (kcol < col0[..., None] + NA_COLS)
    dr = row_idx - r[:, None] + NA_ROWS - 1
    dc = jnp.clip(kcol - qcol[..., None], -(NA_COLS - 1), NA_COLS - 1) + NA_COLS - 1
    bias = rel_bias.astype(f32)[:, dr[:, None, None, :, None], dc[None, :, :, None, :]]
    bias = bias.reshape(H, rows, ncb, NA_COLS, nk)
    mask = jnp.broadcast_to(valid[:, :, None, :], (ncb, NA_COLS, wr, NA_SPAN)).reshape(ncb, NA_COLS, nk)

    def gather(t):
        t = t.reshape(B, rows, GRID_W, H, d)[:, row_idx]
        t = t[:, :, :, col_idx]
        return jnp.moveaxis(t, 3, 2).reshape(B, rows, ncb, nk, H, d).astype(f32)

    kg, vg = gather(k), gather(v)
    qg = q.reshape(B, rows, ncb, NA_COLS, H, d).astype(f32) * d ** -0.5
    s_loc = jnp.einsum('brnqhd,brnkhd->bhrnqk', qg, kg) + bias[None]
    s_loc = jnp.where(mask[None, None, None], s_loc, NEG_INF)
    s_ctx = jnp.einsum('brnqhd,bkhd->bhrnqk', qg, kc.astype(f32))
    pr = jax.nn.softmax(jnp.concatenate([s_loc, s_ctx], axis=-1), axis=-1)
    o = (jnp.einsum('bhrnqk,brnkhd->brnqhd', pr[..., :nk], vg)
         + jnp.einsum('bhrnqk,bkhd->brnqhd', pr[..., nk:], vc.astype(f32)))
    return o.reshape(B, L, H, d).astype(q.dtype)


def swa_latent(q, k, v, kc, vc, sink):
    B, L, G, R, d = q.shape
    nb = L // BLOCK
    nk = 3 * BLOCK

    def band(t):
        tp = jnp.pad(t, ((0, 0), (BLOCK, BLOCK), (0, 0), (0, 0))).reshape(B, nb + 2, BLOCK, G, t.shape[-1])
        return jnp.concatenate([tp[:, :-2], tp[:, 1:-1], tp[:, 2:]], axis=2).astype(f32)

    kb, vb = band(k), band(v)
    qb = q.reshape(B, nb, BLOCK, G, R, d).astype(f32) * d ** -0.5
    qpos = jnp.arange(nb)[:, None] * BLOCK + jnp.arange(BLOCK)[None, :]
    kpos = jnp.arange(nb)[:, None] * BLOCK - BLOCK + jnp.arange(nk)[None, :]
    valid = ((kpos[:, None, :] >= 0) & (kpos[:, None, :] < L)
             & (jnp.abs(qpos[:, :, None] - kpos[:, None, :]) <= SWA_WINDOW))
    s_loc = jnp.einsum('bnqgrd,bnkgd->bngrqk', qb, kb)
    s_loc = jnp.where(valid[None, :, None, None], s_loc, NEG_INF)
    s_ctx = jnp.einsum('bnqgrd,bkgd->bngrqk', qb, kc.astype(f32))
    pr = softmax_with_sink(jnp.concatenate([s_loc, s_ctx], axis=-1), sink.reshape(G, R))
    o = (jnp.einsum('bngrqk,bnkgd->bnqgrd', pr[..., :nk], vb)
         + jnp.einsum('bngrqk,bkgd->bnqgrd', pr[..., nk:], vc.astype(f32)))
    return o.reshape(B, L, G, R, vc.shape[-1]).astype(q.dtype)


def short_conv3(u, w):
    up = jnp.pad(u, ((0, 0), (1, 1), (0, 0)))
    return up[:, :-2] * w[0] + up[:, 1:-1] * w[1] + up[:, 2:] * w[2]


def hyena_filters(L, p):
    n = jnp.arange(L, dtype=f32)[:, None]
    t = n / max(L - 1, 1)
    w = 2.0 * math.pi * n / L
    bands = jnp.linspace(1e-4, HY_BANDS - 1, HY_BANDS, dtype=f32)[None, :]
    z = jnp.concatenate([t, jnp.cos(bands * w), -jnp.sin(bands * w)], axis=-1)
    freq = p['hy_freq'].astype(f32)
    h = jnp.sin(freq * (z @ p['hy_w1'].astype(f32) + p['hy_b1'].astype(f32)))
    h = jnp.sin(freq * (h @ p['hy_w2'].astype(f32) + p['hy_b2'].astype(f32)))
    h = (h @ p['hy_w3'].astype(f32)) * jnp.exp(-t * jnp.abs(p['hy_decay'].astype(f32)))
    return h.reshape(L, HY_ORDER, 2, HY_WIDTH)


def long_conv_bidir(u, hf, hb, bias):
    L = u.shape[1]
    n = 2 * L
    uf = u.astype(f32)
    Hf = jnp.fft.rfft(hf, n=n, axis=0)[None]
    Hb = jnp.fft.rfft(hb, n=n, axis=0)[None]
    fwd = jnp.fft.irfft(jnp.fft.rfft(uf, n=n, axis=1) * Hf, n=n, axis=1)[:, :L]
    bwd = jnp.fft.irfft(jnp.fft.rfft(uf[:, ::-1], n=n, axis=1) * Hb, n=n, axis=1)[:, :L][:, ::-1]
    return (fwd + bwd + uf * bias.astype(f32)).astype(u.dtype)


def hyena(u, p):
    L = u.shape[1]
    uc = short_conv3(u, p['hy_short'])
    v, x1, x2 = jnp.split(uc, 3, axis=-1)
    h = hyena_filters(L, p)
    z = x1 * long_conv_bidir(v, h[:, 0, 0], h[:, 0, 1], p['hy_bias'][0])
    return x2 * long_conv_bidir(z, h[:, 1, 0], h[:, 1, 1], p['hy_bias'][1])


def context_mixers(parts, p, li):
    aq, ak, av, bq, bk, bv, cq, ck, cv, hy, _ = parts
    B, L = aq.shape[:2]
    q_a = aq.reshape(B, L, 2, A_HEADS, A_QK)
    k_a = ak.reshape(B, L, 2, A_HEADS, A_QK)
    v_a = av.reshape(B, L, A_HEADS, A_V)
    lam, lam_init = diff_lambda_value(p['diff_lambda'], li)
    o_a = diff_attention(q_a, k_a, v_a, lam, lam_init, p['diff_norm_g'])
    k_b = bk.reshape(B, L, B_HEADS, HEAD_DIM)
    v_b = bv.reshape(B, L, B_HEADS, HEAD_DIM)
    o_b = dense_attention(bq.reshape(B, L, B_HEADS, 1, HEAD_DIM), k_b, v_b, None)
    k_c = ck.reshape(B, L, C_KV_HEADS, HEAD_DIM)
    v_c = cv.reshape(B, L, C_KV_HEADS, HEAD_DIM)
    o_c = dense_attention(cq.reshape(B, L, C_KV_HEADS, C_GROUP, HEAD_DIM), k_c, v_c, p['swa_sink'])
    o_d = hyena(hy, p)
    branches = (o_a.reshape(B, L, BRANCH_WIDTH), o_b.reshape(B, L, BRANCH_WIDTH),
                o_c.reshape(B, L, BRANCH_WIDTH), o_d)
    return branches, (k_a, v_a, k_b, v_b, k_c, v_c)


def latent_mixers(parts, cache, p, li):
    aq, ak, av, bq, bk, bv, cq, ck, cv, hy, _ = parts
    ka_c, va_c, kb_c, vb_c, kc_c, vc_c = cache
    B, L = aq.shape[:2]
    q_a = rope_2d(aq.reshape(B, L, 2, A_HEADS, A_QK))
    k_a = rope_2d(ak.reshape(B, L, 2, A_HEADS, A_QK))
    v_a = av.reshape(B, L, A_HEADS, A_V)
    lam, lam_init = diff_lambda_value(p['diff_lambda'], li)
    o_a = diff_attention(q_a, jnp.concatenate([k_a, ka_c.astype(k_a.dtype)], axis=1),
                         jnp.concatenate([v_a, va_c.astype(v_a.dtype)], axis=1), lam, lam_init, p['diff_norm_g'])
    o_b = na_latent(bq.reshape(B, L, B_HEADS, HEAD_DIM), bk.reshape(B, L, B_HEADS, HEAD_DIM),
                    bv.reshape(B, L, B_HEADS, HEAD_DIM), kb_c, vb_c, p['na_bias'])
    q_c = rope_2d(cq.reshape(B, L, C_KV_HEADS, C_GROUP, HEAD_DIM))
    k_c = rope_2d(ck.reshape(B, L, C_KV_HEADS, HEAD_DIM))
    o_c = swa_latent(q_c, k_c, cv.reshape(B, L, C_KV_HEADS, HEAD_DIM), kc_c, vc_c, p['swa_sink'])
    o_d = hyena(hy, p)
    return (o_a.reshape(B, L, BRANCH_WIDTH), o_b.reshape(B, L, BRANCH_WIDTH),
            o_c.reshape(B, L, BRANCH_WIDTH), o_d)


def merge_branches(branches, gate_logits, w_branch, w_out):
    o = jnp.stack(branches, axis=2)
    g = jax.nn.sigmoid(gate_logits).reshape(o.shape[:3] + (D_MODEL,))
    proj = jnp.einsum('blnw,nwd->blnd', o, w_branch)
    return jnp.sum(g * proj, axis=2) @ w_out


def channel_mixer(x, shift, scale, gate, g, w_up, w_down):
    h = rmsnorm(x, g) * (1 + scale) + shift
    a = jax.nn.relu(h @ w_up)
    return x + gate * ((a * a) @ w_down)


def setup_inputs(seed: int = 0) -> dict:
    key = jax.random.key(seed)
    ks = jax.random.split(key, 33)

    def nrm(i, shape, s):
        return jax.random.normal(ks[i], shape, jnp.float32) * s

    D = D_MODEL
    decay_base = jnp.broadcast_to(jnp.linspace(HY_DECAY_MIN, HY_DECAY_MAX, 2 * HY_ORDER * HY_WIDTH, dtype=jnp.float32),
                                  (DEPTH, 2 * HY_ORDER * HY_WIDTH))
    return {
        'x_prompt': nrm(0, (BATCH, SEQ, D), 1.0),
        'x_sample': nrm(1, (DEC_BATCH, DEC_SEQ, D), 1.0),
        'cache_a_k': nrm(2, (DEC_BATCH, DEPTH, PAST_LEN, 2, A_HEADS, A_QK), 1.0),
        'cache_a_v': nrm(3, (DEC_BATCH, DEPTH, PAST_LEN, A_HEADS, A_V), 1.0),
        'cache_b_k': nrm(4, (DEC_BATCH, DEPTH, PAST_LEN, B_HEADS, HEAD_DIM), 1.0),
        'cache_b_v': nrm(5, (DEC_BATCH, DEPTH, PAST_LEN, B_HEADS, HEAD_DIM), 1.0),
        'cache_c_k': nrm(6, (DEC_BATCH, DEPTH, PAST_LEN, C_KV_HEADS, HEAD_DIM), 1.0),
        'cache_c_v': nrm(7, (DEC_BATCH, DEPTH, PAST_LEN, C_KV_HEADS, HEAD_DIM), 1.0),
        'c': nrm(8, (DEC_BATCH, D), 1.0),
        'c_ctx': nrm(9, (D,), 1.0),
        'w_ada': nrm(10, (DEPTH, D, 6 * D), 0.5 * D ** -0.5),
        'b_ada': nrm(11, (DEPTH, 6 * D), 0.02),
        'g_mix': 1.0 + nrm(12, (DEPTH, D), 0.05),
        'w_in': nrm(13, (DEPTH, D, IN_COLS), D ** -0.5),
        'diff_lambda': nrm(14, (DEPTH, 4, A_QK), 0.1),
        'diff_norm_g': 1.0 + nrm(15, (DEPTH, A_V), 0.05),
        'na_bias': nrm(16, (DEPTH, B_HEADS, 2 * NA_ROWS - 1, 2 * NA_COLS - 1), 0.1),
        'swa_sink': nrm(17, (DEPTH, C_HEADS), 0.5),
        'hy_short': nrm(18, (DEPTH, 3, HY_COLS), 3 ** -0.5),
        'hy_w1': nrm(19, (DEPTH, HY_EMB, HY_FFN), HY_EMB ** -0.5),
        'hy_b1': nrm(20, (DEPTH, HY_FFN), 0.02),
        'hy_w2': nrm(21, (DEPTH, HY_FFN, HY_FFN), HY_FFN ** -0.5),
        'hy_b2': nrm(22, (DEPTH, HY_FFN), 0.02),
        'hy_w3': nrm(23, (DEPTH, HY_FFN, 2 * HY_ORDER * HY_WIDTH), 0.05 * HY_FFN ** -0.5),
        'hy_freq': 1.0 + nrm(24, (DEPTH, HY_FFN), 0.05),
        'hy_decay': decay_base * (1.0 + nrm(25, (DEPTH, 2 * HY_ORDER * HY_WIDTH), 0.05)),
        'hy_bias': nrm(26, (DEPTH, HY_ORDER, HY_WIDTH), 0.1),
        'w_branch': nrm(27, (DEPTH, N_BRANCH, BRANCH_WIDTH, D), BRANCH_WIDTH ** -0.5),
        'w_out': nrm(28, (DEPTH, D, D), D ** -0.5),
        'g_mlp': 1.0 + nrm(29, (DEPTH, D), 0.05),
        'w_up': nrm(30, (DEPTH, D, D_FF), D ** -0.5),
        'w_down': nrm(31, (DEPTH, D_FF, D), D_FF ** -0.5),
        'g_final': 1.0 + nrm(32, (D,), 0.05),
    }


def reference(x_prompt, x_sample, cache_a_k, cache_a_v, cache_b_k, cache_b_v, cache_c_k, cache_c_v,
              c, c_ctx, w_ada, b_ada, g_mix, w_in, diff_lambda, diff_norm_g, na_bias, swa_sink,
              hy_short, hy_w1, hy_b1, hy_w2, hy_b2, hy_w3, hy_freq, hy_decay, hy_bias,
              w_branch, w_out, g_mlp, w_up, w_down, g_final):
    x_ctx, x_lat = x_prompt, x_sample
    st_ak, st_av, st_bk, st_bv, st_ck, st_cv = [], [], [], [], [], []
    for li in range(DEPTH):
        p = {'diff_lambda': diff_lambda[li], 'diff_norm_g': diff_norm_g[li], 'na_bias': na_bias[li],
             'swa_sink': swa_sink[li], 'hy_short': hy_short[li], 'hy_w1': hy_w1[li], 'hy_b1': hy_b1[li],
             'hy_w2': hy_w2[li], 'hy_b2': hy_b2[li], 'hy_w3': hy_w3[li], 'hy_freq': hy_freq[li],
             'hy_decay': hy_decay[li], 'hy_bias': hy_bias[li]}
        sh1, sc1, gt1, sh2, sc2, gt2 = adaln(c_ctx[None, None, :], w_ada[li], b_ada[li])
        parts = in_projection(x_ctx, sh1, sc1, g_mix[li], w_in[li])
        branches, ctx_kv = context_mixers(parts, p, li)
        x_ctx = x_ctx + gt1 * merge_branches(branches, parts[-1], w_branch[li], w_out[li])
        x_ctx = channel_mixer(x_ctx, sh2, sc2, gt2, g_mlp[li], w_up[li], w_down[li])
        k_a, v_a, k_b, v_b, k_c, v_c = ctx_kv
        st_ak.append(k_a)
        st_av.append(v_a)
        st_bk.append(k_b)
        st_bv.append(v_b)
        st_ck.append(k_c)
        st_cv.append(v_c)
        sh1, sc1, gt1, sh2, sc2, gt2 = adaln(c[:, None, :], w_ada[li], b_ada[li])
        parts = in_projection(x_lat, sh1, sc1, g_mix[li], w_in[li])
        cache = (cache_a_k[:, li], cache_a_v[:, li], cache_b_k[:, li], cache_b_v[:, li],
                 cache_c_k[:, li], cache_c_v[:, li])
        branches = latent_mixers(parts, cache, p, li)
        x_lat = x_lat + gt1 * merge_branches(branches, parts[-1], w_branch[li], w_out[li])
        x_lat = channel_mixer(x_lat, sh2, sc2, gt2, g_mlp[li], w_up[li], w_down[li])
    y_prompt = rmsnorm(x_ctx, g_final)
    y_sample = rmsnorm(x_lat, g_final)
    new_a_k = jnp.stack(st_ak, axis=1)
    new_a_v = jnp.stack(st_av, axis=1)
    new_b_k = jnp.stack(st_bk, axis=1)
    new_b_v = jnp.stack(st_bv, axis=1)
    new_c_k = jnp.stack(st_ck, axis=1)
    new_c_v = jnp.stack(st_cv, axis=1)
    return (y_prompt, y_sample, new_a_k, new_a_v, new_b_k, new_b_v, new_c_k, new_c_v)
```

```python
import math
from contextlib import ExitStack, contextmanager
import numpy as np
import ml_dtypes
import concourse.bass as bass
import concourse.mybir as mybir
from concourse.bass_utils import run_bass_kernel_spmd

F32 = mybir.dt.float32
BF16 = mybir.dt.bfloat16
AF = mybir.ActivationFunctionType
ALU = mybir.AluOpType

SAME_ENG_SYNC = False


class Prog:
    ENGS = ("pe", "act", "dve", "pool", "sp")
    KROT = 8

    def __init__(self, nc):
        self.nc = nc
        self.ops = {e: [] for e in self.ENGS}
        self.res = {}
        self.ndma = {e: 0 for e in self.ENGS}
        self.dma_refs = {e: [] for e in self.ENGS}
        self._ps_i = 0
        self.uid = 0

    def name(self, base):
        self.uid += 1
        return f"{base}_{self.uid}"

    def _record(self, eng, fn, reads, writes, is_dma, hard=False):
        idx = len(self.ops[eng])
        ref = (eng, idx)
        deps = set()
        for k in reads:
            r = self.res.get(k)
            if r is not None and r[0] is not None:
                deps.add(r[0])
        for k in writes:
            r = self.res.get(k)
            if r is not None:
                if r[0] is not None:
                    deps.add(r[0])
                deps.update(r[1].values())
                deps.update(r[2])
        if not is_dma:
            keep = set()
            for d in deps:
                if d[0] == eng and not self.ops[eng][d[1]]["dma"]:
                    if (SAME_ENG_SYNC or hard) and eng != "pe":
                        keep.add(d)
                else:
                    keep.add(d)
            deps = keep
        op = dict(fn=fn, deps=deps, dma=is_dma, sig=False)
        if is_dma:
            op["dj"] = self.ndma[eng]
            self.ndma[eng] += 1
            self.dma_refs[eng].append(ref)
        self.ops[eng].append(op)
        for k in reads:
            r = self.res.setdefault(k, [None, {}, []])
            if is_dma:
                r[2].append(ref)
            else:
                r[1][eng] = ref
        for k in writes:
            self.res[k] = [ref, {}, []]
        return ref

    def op(self, eng, fn, reads=(), writes=(), hard=False):
        return self._record(eng, fn, tuple(reads), tuple(writes), False, hard)

    def dma(self, out, in_, reads=(), writes=(), q="sp", **kw):
        return self._record(q, lambda e: e.dma_start(out=out, in_=in_, **kw), tuple(reads), tuple(writes), True)

    def barrier(self):
        lasts = set()
        for e in self.ENGS:
            for i in range(len(self.ops[e]) - 1, -1, -1):
                o = self.ops[e][i]
                if o["fn"] is not None and not o["dma"]:
                    lasts.add((e, i))
                    break
            for ref in self.dma_refs[e][-self.KROT:]:
                lasts.add(ref)
        for e in self.ENGS:
            deps = set(d for d in lasts if not (d[0] == e and not self.ops[e][d[1]]["dma"]))
            self.ops[e].append(dict(fn=None, deps=deps, dma=False, sig=False))
        self.res = {}

    def emit(self):
        nc = self.nc
        with ExitStack() as es:
            esem = {e: es.enter_context(nc.semaphore(f"sem_{e}")) for e in self.ENGS}
            rsem = {e: [es.enter_context(nc.semaphore(f"rot_{e}_{j}")) for j in range(self.KROT)]
                    for e in self.ENGS if self.ndma[e] > 0}
            for e in self.ENGS:
                for o in self.ops[e]:
                    for d in o["deps"]:
                        self.ops[d[0]][d[1]]["sig"] = True
            for e in self.ENGS:
                c = 0
                for o in self.ops[e]:
                    if o["dma"]:
                        j = o["dj"]
                        o["done"] = ((e, "r", j % self.KROT), 16 * (j // self.KROT + 1))
                    elif o["sig"]:
                        c += 1
                        o["done"] = ((e, "c"), c)

            def semh(key):
                return esem[key[0]] if key[1] == "c" else rsem[key[0]][key[2]]

            def emit_eng(engobj, e):
                obs = {}
                for o in self.ops[e]:
                    waits = {}
                    for d in o["deps"]:
                        k, v = self.ops[d[0]][d[1]]["done"]
                        if obs.get(k, 0) < v:
                            waits[k] = max(waits.get(k, 0), v)
                    if o["dma"] and o["dj"] >= self.KROT:
                        j = o["dj"]
                        k, v = (e, "r", j % self.KROT), 16 * (j // self.KROT)
                        if obs.get(k, 0) < v:
                            waits[k] = max(waits.get(k, 0), v)
                    for k, v in waits.items():
                        engobj.wait_ge(semh(k), v)
                        obs[k] = v
                    if o["fn"] is None:
                        continue
                    ins = o["fn"](engobj)
                    if o["dma"]:
                        ins.then_inc(semh(o["done"][0]), 16)
                    elif o["sig"]:
                        ins.then_inc(esem[e], 1)

            with nc.Block() as block:
                block.tensor(lambda x: emit_eng(x, "pe"))
                block.scalar(lambda x: emit_eng(x, "act"))
                block.vector(lambda x: emit_eng(x, "dve"))
                block.gpsimd(lambda x: emit_eng(x, "pool"))
                block.sync(lambda x: emit_eng(x, "sp"))


D = 2048
NT = 2048
NTX = NT + 256
KC = 16
DEPTH = 2
IN_COLS = 13568
D_FF = 8192
OFF = dict(aq=0, ak=512, av=1024, bq=1536, bk=2048, bv=2560, cq=3072, ck=3584, cv=3712, hy=3840, gate=5376)
QK_ROW = dict(aq=0, ak=512, bq=1024, bk=1536, cq=2048, ck=2560)
NQK = 2688
V_COL = dict(av=0, bv=512, cv=1024)
NV = 1152
EPS = 1e-6


class Phase:
    def __init__(self, bld):
        self.b = bld
        self.es = ExitStack()

    def sb(self, name, shape, dt):
        return self.es.enter_context(self.b.nc.sbuf_tensor(self.b.P.name(name), list(shape), dt))

    def __enter__(self):
        return self

    def __exit__(self, *a):
        self.b.P.barrier()
        self.es.close()
        return False


class WStream:
    def __init__(self, bld, ph, units, nk, tag, nslots=2, nstage=3, kstage=4):
        self.b, self.units, self.nk, self.tag = bld, units, nk, tag
        self.nslots, self.nstage, self.kstage = nslots, nstage, kstage
        self.w = [ph.sb(f"w{tag}{s}", [128, nk, 512], BF16) for s in range(nslots)]
        self.st = [ph.sb(f"ws{tag}{s}", [128, kstage, 512], F32) for s in range(nstage)]
        self.sti = 0
        self.pieces = {}

    def fetch(self, i, npieces=None):
        if i >= len(self.units):
            return
        P = self.b.P
        src, ncols = self.units[i]
        slot = i % self.nslots
        total = self.nk // self.kstage
        done = self.pieces.get(i, 0)
        todo = total - done if npieces is None else min(npieces, total - done)
        for pc in range(done, done + todo):
            k0 = pc * self.kstage
            s = self.sti % self.nstage
            self.sti += 1
            st, w = self.st[s], self.w[slot]
            P.dma(st[:, :, :ncols], src(k0, self.kstage), writes=[("wst", self.tag, s)])
            P.op("pool", lambda e, st=st, w=w, k0=k0, ncols=ncols: e.tensor_copy(
                out=w[:, k0:k0 + self.kstage, :ncols], in_=st[:, :, :ncols]),
                reads=[("wst", self.tag, s)], writes=[("wbf", self.tag, slot)])
        self.pieces[i] = done + todo

    def get(self, i, prefetch=True):
        self.fetch(i)
        if prefetch:
            self.fetch(i + 1)
        return self.w[i % self.nslots], ("wbf", self.tag, i % self.nslots)


def wsrc(ap2d, c0, ncols):
    def f(k0, nk):
        return ap2d[k0 * 128:(k0 + nk) * 128, c0:c0 + ncols].rearrange("(k p) c -> p k c", p=128)
    return f


class Builder:
    def __init__(self, dbg=None):
        self.nc = nc = bass.Bass("TRN2", target_bir_lowering=False)
        self.P = Prog(nc)
        self.dbg = dbg or {}
        self.es = ExitStack()
        self.din = {}
        self.dout = {}
        self.psi = 0

    def inp(self, name, shape, dt=F32):
        self.din[name] = self.nc.dram_tensor(name, list(shape), dt, kind="ExternalInput").ap()
        return self.din[name]

    def outp(self, name, shape, dt=F32):
        self.dout[name] = self.nc.dram_tensor(name, list(shape), dt, kind="ExternalOutput").ap()
        return self.dout[name]

    def scr(self, name, shape, dt):
        if name in self.dbg:
            return self.outp(name, shape, dt)
        return self.nc.dram_tensor(name, list(shape), dt).ap()

    def gsb(self, name, shape, dt):
        return self.es.enter_context(self.nc.sbuf_tensor(name, list(shape), dt))

    def bank(self):
        i = self.psi % 8
        self.psi += 1
        return self.ps[i], ("ps", i)

    def phase(self):
        return Phase(self)

    def declare(self):
        inp, outp, scr = self.inp, self.outp, self.scr
        inp("xc", [1024, D]); inp("xl", [1024, D])
        inp("cak", [2, 256, 512]); inp("cav", [2, 256, 512]); inp("cbk", [2, 256, 512]); inp("cbv", [2, 256, 512])
        inp("cck", [2, 256, 128]); inp("ccv", [2, 256, 128])
        inp("cvec", [2, D])
        inp("w_ada", [2, D, 6 * D]); inp("b_ada", [2, 6 * D]); inp("g_mix", [2, D]); inp("w_in", [2, D, IN_COLS])
        inp("diff_lambda", [2, 4, 64]); inp("diff_norm_g", [2, 128]); inp("na_bias", [2, 8, 15, 31]); inp("swa_sink", [2, 8])
        inp("hy_short", [2, 3, 1536]); inp("hy_w1", [2, 33, 64]); inp("hy_b1", [2, 64]); inp("hy_w2", [2, 64, 64])
        inp("hy_b2", [2, 64]); inp("hy_w3", [2, 64, 2048]); inp("hy_freq", [2, 64]); inp("hy_decay", [2, 2048])
        inp("hy_bias", [2, 2, 512]); inp("w_branch", [2, 4, 512, D]); inp("w_out", [2, D, D]); inp("g_mlp", [2, D])
        inp("w_up", [2, D, D_FF]); inp("w_down", [2, D_FF, D]); inp("g_final", [D])
        inp("c_ident", [128, 128]); inp("c_identb", [128, 128], BF16); inp("c_onesb", [128, 128], BF16)
        inp("c_rperm", [128, 128]); inp("c_rcos", [128, 1024]); inp("c_rsin", [128, 1024])
        inp("c_m01", [128, 64]); inp("c_mng", [128, 64]); inp("c_cmask", [128, 2, 128], BF16)
        for L in (256, 1024):
            inp(f"c_z{L}", [33, L]); inp(f"c_nt{L}", [128, L // 128]); inp(f"c_F{L}", [L, 2 * L], BF16); inp(f"c_FT{L}", [2 * L, L], BF16)
        outp("y_c", [1024, D]); outp("y_l", [1024, D])
        outp("nak", [4, 2, 256, 512]); outp("nav", [4, 2, 256, 512]); outp("nbk", [4, 2, 256, 512]); outp("nbv", [4, 2, 256, 512])
        outp("nck", [4, 2, 256, 128]); outp("ncv", [4, 2, 256, 128])
        self.xT = scr("xT", [D, NT], F32)
        self.qkT = [scr(f"qkT{l}", [NQK, NTX], BF16) for l in range(2)]
        self.vtok = [scr(f"vtok{l}", [NTX, NV], BF16) for l in range(2)]
        self.hyT = scr("hyT", [1536, NT], F32)
        self.sgT = scr("sgT", [4 * D, NT], BF16)
        self.oT = scr("oT", [D, NT], BF16)
        self.mT = scr("mT", [D, NT], BF16)
        self.aT = scr("aT", [D_FF, NT], BF16)
        self.nbpad = [scr(f"nbpad{l}", [1, 64 + 3720 + 64], F32) for l in range(2)]

    def globals_(self):
        nc, P, g = self.nc, self.P, self.gsb
        self.ps = [self.es.enter_context(nc.psum_tensor(f"psb{i}", [128, 512], F32)) for i in range(8)]
        self.ident = g("ident", [128, 128], F32); self.identb = g("identb", [128, 128], BF16)
        self.onesb = g("onesb", [128, 128], BF16); self.rperm = g("rperm", [128, 128], F32)
        self.epst = g("epst", [128, 1], F32)
        self.mod = g("mod", [128, 2, 96, 2], F32)
        self.gm = g("gm", [128, 2, 2, 16, 2], F32)
        self.gfin = g("gfin", [128, 16], F32)
        self.zero16 = g("zero16", [128, 16], F32)
        d = self.din
        P.dma(self.ident[:], d["c_ident"], writes=["ident"])
        P.dma(self.identb[:], d["c_identb"], writes=["identb"])
        P.dma(self.onesb[:], d["c_onesb"], writes=["onesb"])
        P.dma(self.rperm[:], d["c_rperm"], writes=["rperm"])
        P.op("dve", lambda e: e.memset(self.epst[:], EPS), writes=["epst"])
        P.op("dve", lambda e: e.memset(self.zero16[:], 0.0), writes=["zero16"])

    def load_T(self, ph, dst, dst_keys, src2d, n):
        P = self.P
        tmp = ph.sb("ltT", [128, 128], F32)
        k = ("ltT", P.uid)
        P.dma(tmp[:n, :], src2d, writes=[k])
        pb, pk = self.bank()
        P.op("pe", lambda e: e.transpose(pb[:, :n], tmp[:n, :], self.ident[:n, :n]), reads=[k, "ident"], writes=[pk])
        P.op("dve", lambda e: e.tensor_copy(out=dst, in_=pb[:, :n]), reads=[pk], writes=dst_keys)

    def phase_xT(self):
        P, d = self.P, self.din
        with self.phase() as ph:
            xin = [ph.sb(f"xin{i}", [128, 4, D], F32) for i in range(2)]
            stg = [ph.sb(f"xst{i}", [128, 512], F32) for i in range(4)]
            si = 0
            for g4 in range(4):
                src = d["xc"] if g4 < 2 else d["xl"]
                r0 = (g4 % 2) * 512
                xt = xin[g4 % 2]
                xk = ("xin", g4 % 2)
                P.dma(xt[:], src[r0:r0 + 512, :].rearrange("(t p) f -> p t f", p=128), writes=[xk])
                for fc in range(KC):
                    pb, pk = self.bank()
                    for t in range(4):
                        P.op("pe", lambda e, pb=pb, xt=xt, t=t, fc=fc: e.transpose(
                            pb[:, t * 128:(t + 1) * 128], xt[:, t, fc * 128:(fc + 1) * 128], self.ident[:]),
                            reads=[xk, "ident"], writes=[pk])
                    s = stg[si % 4]; sk = ("xst", si % 4); si += 1
                    eng = "act" if fc % 2 == 0 else "dve"
                    if eng == "act":
                        P.op("act", lambda e, s=s, pb=pb: e.copy(out=s[:], in_=pb[:]), reads=[pk], writes=[sk])
                    else:
                        P.op("dve", lambda e, s=s, pb=pb: e.tensor_copy(out=s[:], in_=pb[:]), reads=[pk], writes=[sk])
                    P.dma(self.xT[fc * 128:(fc + 1) * 128, g4 * 512:(g4 + 1) * 512], s[:], reads=[sk], writes=[("xT", fc, g4)])

    def phase_adaln(self):
        P, d = self.P, self.din
        with self.phase() as ph:
            cT = ph.sb("cT", [128, 2, 16], F32)
            sT = ph.sb("sT", [128, 16, 2], F32)
            bada = ph.sb("bada", [128, 2, 96], F32)
            gT = ph.sb("gT", [128, 2, 2, 16], F32)
            for v in range(2):
                self.load_T(ph, cT[:, v, :], ["cT"], d["cvec"][v].rearrange("(k p) -> k p", p=128), 16)
            for v in range(2):
                P.op("act", lambda e, v=v: e.activation(out=sT[:, :, v], in_=cT[:, v, :], func=AF.Silu), reads=["cT"], writes=["sT"])
            for l in range(2):
                self.load_T(ph, bada[:, l, :], ["bada"], d["b_ada"][l].rearrange("(k p) -> k p", p=128), 96)
                self.load_T(ph, gT[:, l, 0, :], ["gT"], d["g_mix"][l].rearrange("(k p) -> k p", p=128), 16)
                self.load_T(ph, gT[:, l, 1, :], ["gT"], d["g_mlp"][l].rearrange("(k p) -> k p", p=128), 16)
            self.load_T(ph, self.gfin[:], ["gfin"], d["g_final"].rearrange("(k p) -> k p", p=128), 16)
            st = [ph.sb(f"adst{i}", [128, 4, 512], F32) for i in range(3)]
            m2 = [ph.sb(f"adm{i}", [2, 512], F32) for i in range(2)]
            si = 0
            for l in range(2):
                for cg in range(24):
                    pb, pk = self.bank()
                    for ks in range(4):
                        s = st[si % 3]; sk = ("adst", si % 3); si += 1
                        P.dma(s[:], d["w_ada"][l, ks * 512:(ks + 1) * 512, cg * 512:(cg + 1) * 512].rearrange("(k p) c -> p k c", p=128), writes=[sk])
                        for kk in range(4):
                            kc = ks * 4 + kk
                            P.op("pe", lambda e, pb=pb, s=s, kk=kk, kc=kc: e.matmul(
                                pb[0:2, :], lhsT=sT[:, kc, :], rhs=s[:, kk, :], start=(kc == 0), stop=(kc == 15)), reads=[sk, "sT"], writes=[pk])
                    m = m2[cg % 2]; mk = ("adm", cg % 2)
                    P.op("act", lambda e, m=m, pb=pb: e.copy(out=m[:], in_=pb[0:2, :]), reads=[pk], writes=[mk])
                    pb2, pk2 = self.bank()
                    for j in range(4):
                        P.op("pe", lambda e, pb2=pb2, m=m, j=j: e.transpose(pb2[:, 2 * j:2 * j + 2], m[0:2, j * 128:(j + 1) * 128], self.ident[0:2, 0:2]),
                             reads=[mk, "ident"], writes=[pk2])
                    for v in range(2):
                        P.op("dve", lambda e, pb2=pb2, l=l, cg=cg, v=v: e.tensor_tensor(
                            out=self.mod[:, l, cg * 4:(cg + 1) * 4, v], in0=pb2[:, v:8:2], in1=bada[:, l, cg * 4:(cg + 1) * 4], op=ALU.add),
                            reads=[pk2, "bada"], writes=["mod"])
            for l in range(2):
                for w in range(2):
                    sc0 = 16 if w == 0 else 64
                    for v in range(2):
                        P.op("dve", lambda e, l=l, w=w, v=v, sc0=sc0: e.scalar_tensor_tensor(
                            out=self.gm[:, l, w, :, v], in0=self.mod[:, l, sc0:sc0 + 16, v], scalar=1.0, in1=gT[:, l, w, :],
                            op0=ALU.add, op1=ALU.mult), reads=["mod", "gT"], writes=["gm"], hard=True)

    def modv(self, l, which, ft, grp):
        return self.mod[:, l, which * 16 + ft, grp:grp + 1]

    def norm_tiles(self, ph, scale_ap, shift_ap, emit_out):
        P = self.P
        xs = [ph.sb(f"nx{i}", [128, 16, 512], F32) for i in range(2)]
        sq = [ph.sb(f"nsq{i}", [128, 512], BF16) for i in range(4)]
        rs = [ph.sb(f"nrs{i}", [128, 512], F32) for i in range(2)]
        tmp = [ph.sb(f"ntmp{i}", [128, 512], F32) for i in range(4)]
        xTv = self.xT.rearrange("(k p) t -> p k t", p=128)
        ti = 0
        for tt in range(4):
            grp = 0 if tt < 2 else 1
            x = xs[tt % 2]; xk = ("nx", tt % 2)
            for h in range(2):
                P.dma(x[:, h * 8:(h + 1) * 8, :], xTv[:, h * 8:(h + 1) * 8, tt * 512:(tt + 1) * 512],
                      reads=[("xT", fc, tt) for fc in range(h * 8, h * 8 + 8)], writes=[(xk, h)])
            pb, pk = self.bank()
            for fc in range(KC):
                s = sq[fc % 4]; sk = ("nsq", fc % 4)
                P.op("act", lambda e, s=s, x=x, fc=fc: e.activation(out=s[:], in_=x[:, fc, :], func=AF.Square), reads=[(xk, fc // 8)], writes=[sk])
                P.op("pe", lambda e, pb=pb, s=s, fc=fc: e.matmul(pb[:], lhsT=self.onesb[:], rhs=s[:], start=(fc == 0), stop=(fc == 15)),
                     reads=[sk, "onesb"], writes=[pk])
            r = rs[tt % 2]; rk = ("nrs", tt % 2)
            self.act_pow(r[:], pb[:], -0.5, [pk], [rk], scale=1.0 / D, bias=self.epst[:, 0:1])
            for fc in range(KC):
                t = tmp[ti % 4]; tk = ("ntmp", ti % 4); ti += 1
                P.op("dve", lambda e, t=t, x=x, fc=fc, r=r, grp=grp: e.scalar_tensor_tensor(
                    out=t[:], in0=x[:, fc, :], scalar=scale_ap(fc, grp), in1=r[:], op0=ALU.mult, op1=ALU.mult),
                    reads=[(xk, fc // 8), rk, "gm", "gfin"], writes=[tk])
                emit_out(tt, fc, grp, t, tk)

    def phase_norm_h(self, ph, l, which):
        P = self.P
        hT = self.hT

        def out(tt, fc, grp, t, tk):
            P.op("act", lambda e: e.activation(out=hT[:, fc, tt * 512:(tt + 1) * 512], in_=t[:], func=AF.Identity,
                                                bias=self.modv(l, 0 if which == 0 else 3, fc, grp), scale=1.0),
                 reads=[tk, "mod"], writes=[("hT", fc, tt)])
        with self.phase() as p2:
            self.norm_tiles(p2, lambda fc, grp: self.gm[:, l, which, fc, grp:grp + 1], None, out)

    def mm_group(self, pb, pk, lhs_fn, rhs_fn, nk, reads):
        P = self.P
        for kc in range(nk):
            la, ra = lhs_fn(kc), rhs_fn(kc)
            P.op("pe", lambda e, kc=kc, la=la, ra=ra, pb=pb: e.matmul(pb[:, 0:ra.shape[-1]], lhsT=la, rhs=ra, start=(kc == 0), stop=(kc == nk - 1)),
                 reads=reads(kc), writes=[pk])

    def inproj_units(self):
        units = []
        for nm in ("aq", "ak", "av", "bq", "bk", "bv", "cq"):
            units.append((OFF[nm], 512, nm))
        units.append((OFF["ck"], 256, "ckv"))
        for i in range(3):
            units.append((OFF["hy"] + i * 512, 512, ("hy", i)))
        for i in range(16):
            units.append((OFF["gate"] + i * 512, 512, ("gate", i)))
        return units

    def inproj_ws(self, ph, l):
        win = self.din["w_in"][l]
        ws = WStream(self, ph, [(wsrc(win, c0, nc_), nc_) for (c0, nc_, _) in self.inproj_units()], 16, "in")
        ws.fetch(0)
        return ws

    def up_ws(self, ph, l):
        ws = WStream(self, ph, [(wsrc(self.din["w_up"][l], cg * 512, 512), 512) for cg in range(16)], 16, "wu")
        ws.fetch(0)
        return ws

    def phase_inproj(self, l, ws):
        P, d = self.P, self.din
        hT = self.hT
        with self.phase() as ph:
            units = self.inproj_units()
            stb = [ph.sb(f"ipb{i}", [128, NT], BF16) for i in range(3)]
            stf = [ph.sb(f"ipf{i}", [128, NT], F32) for i in range(2)]
            sta = [ph.sb(f"ipa{i}", [128, 512], F32) for i in range(3)]
            stab = [ph.sb(f"ipab{i}", [128, 512], BF16) for i in range(3)]
            r32 = [ph.sb(f"ipr{i}", [128, 512], F32) for i in range(2)]
            t1 = [ph.sb(f"ipt{i}", [128, 512], F32) for i in range(2)]
            t2 = [ph.sb(f"ipu{i}", [128, 512], F32) for i in range(2)]
            rcos = ph.sb("rcos", [128, 1024], F32); rsin = ph.sb("rsin", [128, 1024], F32)
            P.dma(rcos[:], d["c_rcos"], writes=["rcos"]); P.dma(rsin[:], d["c_rsin"], writes=["rsin"])
            cnt = dict(b=0, f=0, a=0, r=0)
            ev = [0]
            for ui, (c0, ncols, kind) in enumerate(units):
                w, wk = ws.get(ui)
                kname = kind if isinstance(kind, str) else kind[0]
                fm_tiles = []
                if kname in ("aq", "ak", "bq", "bk", "cq"):
                    fm_tiles = [(j, "qk", QK_ROW[kname] + j * 128) for j in range(4)]
                elif kname == "ckv":
                    fm_tiles = [(0, "qk", QK_ROW["ck"])]
                elif kname == "hy":
                    fm_tiles = [(j, "hy", kind[1] * 512 + j * 128) for j in range(4)]
                elif kname == "gate":
                    fm_tiles = [(j, "gate", kind[1] * 512 + j * 128) for j in range(4)]
                roped = kname in ("aq", "ak", "cq", "ckv")
                for (j, okind, row0) in fm_tiles:
                    if okind == "hy":
                        st = stf[cnt["f"] % 2]; stk = ("ipf", cnt["f"] % 2); cnt["f"] += 1
                    else:
                        st = stb[cnt["b"] % 3]; stk = ("ipb", cnt["b"] % 3); cnt["b"] += 1
                    for tt in range(4):
                        pb, pk = self.bank()
                        self.mm_group(pb, pk, lambda kc: w[:, kc, j * 128:(j + 1) * 128], lambda kc: hT[:, kc, tt * 512:(tt + 1) * 512], 16,
                                      lambda kc: [wk, ("hT", kc, tt)])
                        o = st[:, tt * 512:(tt + 1) * 512]
                        if okind == "gate":
                            P.op("act", lambda e, o=o, pb=pb: e.activation(out=o, in_=pb[:], func=AF.Sigmoid), reads=[pk], writes=[stk])
                        elif okind == "qk" and roped and tt >= 2:
                            i = cnt["r"] % 2; cnt["r"] += 1
                            r, rk = r32[i], ("ipr", i)
                            a1, a1k = t1[i], ("ipt", i)
                            a2, a2k = t2[i], ("ipu", i)
                            tp = (tt - 2) * 512
                            P.op("act", lambda e, r=r, pb=pb: e.copy(out=r[:], in_=pb[:]), reads=[pk], writes=[rk])
                            pb2, pk2 = self.bank()
                            P.op("pe", lambda e, pb2=pb2, r=r: e.matmul(pb2[:], lhsT=self.rperm[:], rhs=r[:], start=True, stop=True),
                                 reads=[rk, "rperm"], writes=[pk2])
                            P.op("dve", lambda e, a1=a1, r=r, tp=tp: e.tensor_tensor(out=a1[:], in0=r[:], in1=rcos[:, tp:tp + 512], op=ALU.mult),
                                 reads=[rk, "rcos"], writes=[a1k])
                            P.op("dve", lambda e, a2=a2, pb2=pb2, tp=tp: e.tensor_tensor(out=a2[:], in0=pb2[:], in1=rsin[:, tp:tp + 512], op=ALU.mult),
                                 reads=[pk2, "rsin"], writes=[a2k])
                            P.op("pool", lambda e, o=o, a1=a1, a2=a2: e.tensor_tensor(out=o, in0=a1[:], in1=a2[:], op=ALU.add),
                                 reads=[a1k, a2k], writes=[stk])
                        else:
                            ev[0] += 1
                            if ev[0] % 2 == 0:
                                P.op("act", lambda e, o=o, pb=pb: e.copy(out=o, in_=pb[:]), reads=[pk], writes=[stk])
                            else:
                                P.op("dve", lambda e, o=o, pb=pb: e.tensor_copy(out=o, in_=pb[:]), reads=[pk], writes=[stk])
                    if okind == "qk":
                        dst = self.qkT[l][row0:row0 + 128, 0:NT]; dk = ("qkT", row0 // 128)
                    elif okind == "hy":
                        dst = self.hyT[row0:row0 + 128, :]; dk = ("hyT", row0 // 128)
                    else:
                        dst = self.sgT[row0:row0 + 128, :]; dk = ("sgT", row0 // 128)
                    P.dma(dst, st[:], reads=[stk], writes=[dk])
                if kname in ("ak", "av", "bk", "bv", "ckv"):
                    isv = kname in ("av", "bv", "ckv")
                    ntile = 16 if isv else 8
                    for t128 in range(ntile):
                        pb, pk = self.bank()
                        self.mm_group(pb, pk, lambda kc: hT[:, kc, t128 * 128:(t128 + 1) * 128], lambda kc: w[:, kc, 0:ncols], 16,
                                      lambda kc: [wk, ("hT", kc, t128 // 4)])
                        pbv = pb[:, 0:ncols]
                        if t128 < 8:
                            i = cnt["a"] % 3; cnt["a"] += 1
                            s, sk = sta[i], ("ipa", i)
                            P.op("act", lambda e, s=s, pbv=pbv, ncols=ncols: e.copy(out=s[:, 0:ncols], in_=pbv), reads=[pk], writes=[sk])
                            sq_, pos0 = t128 // 2, (t128 % 2) * 128
                            if kname == "ckv":
                                P.dma(self.dout["nck"][sq_, l, pos0:pos0 + 128, :], s[:, 0:128], reads=[sk])
                                P.dma(self.dout["ncv"][sq_, l, pos0:pos0 + 128, :], s[:, 128:256], reads=[sk])
                            else:
                                P.dma(self.dout["n" + kname][sq_, l, pos0:pos0 + 128, :], s[:, 0:512], reads=[sk])
                        if isv:
                            i = cnt["a"] % 3; cnt["a"] += 1
                            s2, s2k = stab[i], ("ipab", i)
                            if kname == "ckv":
                                P.op("act", lambda e, s2=s2, pb=pb: e.copy(out=s2[:, 0:128], in_=pb[:, 128:256]), reads=[pk], writes=[s2k])
                                P.dma(self.vtok[l][t128 * 128:(t128 + 1) * 128, V_COL["cv"]:V_COL["cv"] + 128], s2[:, 0:128], reads=[s2k],
                                      writes=[("vtok", "cv", t128)])
                            else:
                                P.op("act", lambda e, s2=s2, pb=pb: e.copy(out=s2[:], in_=pb[:]), reads=[pk], writes=[s2k])
                                P.dma(self.vtok[l][t128 * 128:(t128 + 1) * 128, V_COL[kname]:V_COL[kname] + 512], s2[:], reads=[s2k],
                                      writes=[("vtok", kname, t128)])

    def phase_merge(self, l):
        P, d = self.P, self.din
        with self.phase() as ph:
            oT = ph.sb("oTr", [128, 16, NT], BF16)
            oTv = self.oT.rearrange("(k p) t -> p k t", p=128)
            for n in range(4):
                P.dma(oT[:, n * 4:(n + 1) * 4, :], oTv[:, n * 4:(n + 1) * 4, :], reads=[("oT", n)], writes=[("oTr", n)])
            acc = ph.sb("macc", [128, 4, NT], F32)
            units = []
            for fg in range(4):
                for n in range(4):
                    units.append((wsrc(d["w_branch"][l, n], fg * 512, 512), 512))
            ws = WStream(self, ph, units, 4, "wb")
            sg = [ph.sb(f"msg{i}", [128, NT], BF16) for i in range(3)]
            tmp = [ph.sb(f"mtmp{i}", [128, 512], F32) for i in range(4)]
            mst = [ph.sb(f"mst{i}", [128, NT], BF16) for i in range(2)]
            sgi = 0; ti = 0; mi = 0
            for fg in range(4):
                for n in range(4):
                    w, wk = ws.get(fg * 4 + n)
                    for j in range(4):
                        ft = fg * 4 + j
                        s = sg[sgi % 3]; sk = ("msg", sgi % 3); sgi += 1
                        P.dma(s[:], self.sgT[n * D + ft * 128:n * D + (ft + 1) * 128, :], reads=[("sgT", (n * D + ft * 128) // 128)], writes=[sk])
                        if n == 3:
                            ms = mst[mi % 2]; msk = ("mst", mi % 2); mi += 1
                        for tt in range(4):
                            pb, pk = self.bank()
                            self.mm_group(pb, pk, lambda kc: w[:, kc, j * 128:(j + 1) * 128], lambda kc: oT[:, n * 4 + kc, tt * 512:(tt + 1) * 512], 4,
                                          lambda kc: [wk, ("oTr", n)])
                            a = acc[:, j, tt * 512:(tt + 1) * 512]; ak = ("macc", j, tt)
                            ss = s[:, tt * 512:(tt + 1) * 512]
                            if n == 0:
                                P.op("dve", lambda e, a=a, pb=pb, ss=ss: e.tensor_tensor(out=a, in0=pb[:], in1=ss, op=ALU.mult), reads=[pk, sk], writes=[ak])
                            else:
                                t = tmp[ti % 4]; tk = ("mtmp", ti % 4); ti += 1
                                P.op("dve", lambda e, t=t, pb=pb, ss=ss: e.tensor_tensor(out=t[:], in0=pb[:], in1=ss, op=ALU.mult), reads=[pk, sk], writes=[tk])
                                if n < 3:
                                    P.op("pool", lambda e, a=a, t=t: e.tensor_tensor(out=a, in0=a, in1=t[:], op=ALU.add), reads=[ak, tk], writes=[ak])
                                else:
                                    mo = ms[:, tt * 512:(tt + 1) * 512]
                                    P.op("pool", lambda e, mo=mo, a=a, t=t: e.tensor_tensor(out=mo, in0=a, in1=t[:], op=ALU.add), reads=[ak, tk], writes=[msk])
                        if n == 3:
                            P.dma(self.mT[ft * 128:(ft + 1) * 128, :], ms[:], reads=[msk], writes=[("mT", ft)])

    def resid_update(self, l, gwhich, ft, xo, xok, banks, tts):
        P = self.P
        for (pb, pk), tt in zip(banks, tts):
            grp = 0 if tt < 2 else 1
            xs = xo[:, tt * 512:(tt + 1) * 512]
            P.op("dve", lambda e, xs=xs, pb=pb, grp=grp: e.scalar_tensor_tensor(
                out=xs, in0=pb[:], scalar=self.modv(l, gwhich, ft, grp), in1=xs, op0=ALU.mult, op1=ALU.add),
                reads=[pk, (xok, tt), "mod"], writes=[(xok, tt)])

    def phase_wout(self, l):
        P, d = self.P, self.din
        with self.phase() as ph:
            mT = ph.sb("mTr", [128, 16, NT], BF16)
            mTv = self.mT.rearrange("(k p) t -> p k t", p=128)
            for q in range(4):
                P.dma(mT[:, q * 4:(q + 1) * 4, :], mTv[:, q * 4:(q + 1) * 4, :], reads=[("mT", f) for f in range(q * 4, q * 4 + 4)], writes=[("mTr", q)])
            ws = WStream(self, ph, [(wsrc(d["w_out"][l], cg * 512, 512), 512) for cg in range(4)], 16, "wo")
            xo_ = [ph.sb(f"xo{i}", [128, NT], F32) for i in range(3)]
            xi = 0
            for cg in range(4):
                w, wk = ws.get(cg)
                for j in range(4):
                    ft = cg * 4 + j
                    xo = xo_[xi % 3]; xok = ("xo", xi % 3); xi += 1
                    P.dma(xo[:], self.xT[ft * 128:(ft + 1) * 128, :], reads=[("xT", ft, t) for t in range(4)], writes=[(xok, t) for t in range(4)])
                    banks = []
                    for tt in range(4):
                        pb, pk = self.bank()
                        self.mm_group(pb, pk, lambda kc: w[:, kc, j * 128:(j + 1) * 128], lambda kc: mT[:, kc, tt * 512:(tt + 1) * 512], 16,
                                      lambda kc: [wk, ("mTr", kc // 4)])
                        banks.append((pb, pk))
                    self.resid_update(l, 2, ft, xo, xok, banks, range(4))
                    P.dma(self.xT[ft * 128:(ft + 1) * 128, :], xo[:], reads=[(xok, t) for t in range(4)], writes=[("xT", ft, t) for t in range(4)])

    def phase_up(self, l, ws):
        P, d = self.P, self.din
        hT = self.hT
        with self.phase() as ph:
            ast = [ph.sb(f"ast{i}", [128, NT], BF16) for i in range(3)]
            rl = [ph.sb(f"url{i}", [128, 512], F32) for i in range(4)]
            ai = 0; ri = 0
            for cg in range(16):
                w, wk = ws.get(cg)
                for j in range(4):
                    ft = cg * 4 + j
                    a = ast[ai % 3]; ak = ("ast", ai % 3); ai += 1
                    for tt in range(4):
                        pb, pk = self.bank()
                        self.mm_group(pb, pk, lambda kc: w[:, kc, j * 128:(j + 1) * 128], lambda kc: hT[:, kc, tt * 512:(tt + 1) * 512], 16,
                                      lambda kc: [wk, ("hT", kc, tt)])
                        r = rl[ri % 4]; rk = ("url", ri % 4); ri += 1
                        P.op("dve", lambda e, r=r, pb=pb: e.tensor_scalar(out=r[:], in0=pb[:], scalar1=0.0, scalar2=None, op0=ALU.max), reads=[pk], writes=[rk])
                        P.op("act", lambda e, a=a, r=r, tt=tt: e.activation(out=a[:, tt * 512:(tt + 1) * 512], in_=r[:], func=AF.Square), reads=[rk], writes=[ak])
                    P.dma(self.aT[ft * 128:(ft + 1) * 128, :], a[:], reads=[ak], writes=[("aT", ft)])

    def phase_down(self, l):
        P, d = self.P, self.din
        with self.phase() as ph:
            ws = WStream(self, ph, [(wsrc(d["w_down"][l], g * 512, 512), 512) for g in range(4)], 64, "wd")
            at_ = [ph.sb(f"dat{i}", [128, 1024], BF16) for i in range(6)]
            xo_ = [ph.sb(f"dxo{i}", [128, 1024], F32) for i in range(4)]
            ati = 0; xi = 0
            for g in range(4):
                w, wk = ws.get(g, prefetch=False)
                for pair in range(2):
                    banks = [[self.bank() for t2 in range(2)] for j in range(4)]
                    for kc in range(64):
                        if kc % 8 == 4:
                            ws.fetch(g + 1, 1)
                        at = at_[ati % 6]; atk = ("dat", ati % 6); ati += 1
                        P.dma(at[:], self.aT[kc * 128:(kc + 1) * 128, pair * 1024:(pair + 1) * 1024], reads=[("aT", kc)], writes=[atk])
                        for j in range(4):
                            for t2 in range(2):
                                pb, pk = banks[j][t2]
                                P.op("pe", lambda e, pb=pb, w=w, kc=kc, j=j, at=at, t2=t2: e.matmul(
                                    pb[:], lhsT=w[:, kc, j * 128:(j + 1) * 128], rhs=at[:, t2 * 512:(t2 + 1) * 512], start=(kc == 0), stop=(kc == 63)),
                                    reads=[wk, atk], writes=[pk])
                    for j in range(4):
                        ft = g * 4 + j
                        xo = xo_[xi % 4]; xok = ("dxo", xi % 4); xi += 1
                        tts = [pair * 2, pair * 2 + 1]
                        P.dma(xo[:], self.xT[ft * 128:(ft + 1) * 128, pair * 1024:(pair + 1) * 1024], reads=[("xT", ft, t) for t in tts],
                              writes=[(xok, t) for t in tts])
                        for t2 in range(2):
                            pb, pk = banks[j][t2]
                            tt = tts[t2]
                            grp = 0 if tt < 2 else 1
                            xs = xo[:, t2 * 512:(t2 + 1) * 512]
                            P.op("dve", lambda e, xs=xs, pb=pb, grp=grp, ft=ft: e.scalar_tensor_tensor(
                                out=xs, in0=pb[:], scalar=self.modv(l, 5, ft, grp), in1=xs, op0=ALU.mult, op1=ALU.add),
                                reads=[pk, (xok, tt), "mod"], writes=[(xok, tt)])
                        P.dma(self.xT[ft * 128:(ft + 1) * 128, pair * 1024:(pair + 1) * 1024], xo[:], reads=[(xok, t) for t in tts],
                              writes=[("xT", ft, t) for t in tts])

    def phase_final(self):
        P = self.P
        with self.phase() as ph:
            yst = [ph.sb(f"yst{i}", [128, D], F32) for i in range(2)]
            yT = [ph.sb(f"yT{i}", [128, 16, 512], F32) for i in range(2)]
            cnt = [0]

            def out(tt, fc, grp, t, tk):
                y = yT[tt % 2]
                P.op("act", lambda e: e.copy(out=y[:, fc, :], in_=t[:]), reads=[tk], writes=[("yT", tt % 2, fc)])
                if fc == 15:
                    for t128 in range(4):
                        ys = yst[cnt[0] % 2]; ysk = ("yst", cnt[0] % 2); cnt[0] += 1
                        for q in range(4):
                            pb, pk = self.bank()
                            for f4 in range(4):
                                f = q * 4 + f4
                                P.op("pe", lambda e, pb=pb, y=y, f=f, f4=f4, t128=t128: e.transpose(
                                    pb[:, f4 * 128:(f4 + 1) * 128], y[:, f, t128 * 128:(t128 + 1) * 128], self.ident[:]),
                                    reads=[("yT", tt % 2, f), "ident"], writes=[pk])
                            if q % 2 == 0:
                                P.op("act", lambda e, ys=ys, pb=pb, q=q: e.copy(out=ys[:, q * 512:(q + 1) * 512], in_=pb[:]), reads=[pk], writes=[ysk])
                            else:
                                P.op("dve", lambda e, ys=ys, pb=pb, q=q: e.tensor_copy(out=ys[:, q * 512:(q + 1) * 512], in_=pb[:]), reads=[pk], writes=[ysk])
                        tok = tt * 512 + t128 * 128
                        dst = self.dout["y_c"][tok:tok + 128, :] if tok < 1024 else self.dout["y_l"][tok - 1024:tok - 1024 + 128, :]
                        P.dma(dst, ys[:], reads=[ysk])
            self.norm_tiles(ph, lambda fc, grp: self.gfin[:, fc:fc + 1], None, out)

    def build(self, stages=None):
        def on(name):
            return stages is None or name in stages
        self.declare()
        self.globals_()
        self.P.barrier()
        if on("xT"):
            self.phase_xT()
        if on("adaln"):
            self.phase_adaln()
        if on("mixers"):
            self.phase_cache()
            self.phase_lam()
        for l in range(DEPTH):
            if stages is not None and f"L{l}" not in stages:
                continue
            with self.phase() as ph:
                self.hT = ph.sb("hT", [128, 16, NT], BF16)
                wsi = self.inproj_ws(ph, l)
                if on("norm1"):
                    self.phase_norm_h(ph, l, 0)
                if "hT" in self.dbg and l == 0:
                    self.P.dma(self.outp("hT", [128, 16, NT], BF16), self.hT[:], reads=[("hT", fc, tt) for fc in range(16) for tt in range(4)])
                if on("inproj"):
                    self.phase_inproj(l, wsi)
            if on("mixers"):
                self.phase_mixers(l)
            if on("merge"):
                self.phase_merge(l)
            if on("wout"):
                self.phase_wout(l)
            with self.phase() as ph:
                self.hT = ph.sb("hT", [128, 16, NT], BF16)
                wsu = self.up_ws(ph, l)
                if on("norm2"):
                    self.phase_norm_h(ph, l, 1)
                if on("up"):
                    self.phase_up(l, wsu)
            if on("down"):
                self.phase_down(l)
        if on("final"):
            self.phase_final()
        if "mod" in self.dbg:
            self.P.dma(self.outp("modo", [128, 2 * 96 * 2], F32), self.mod[:].rearrange("p a b c -> p (a b c)"), reads=["mod"])
        self.P.barrier()
        self.P.emit()
        self.es.close()
        return self.nc

    def bank_i(self, i):
        return self.ps[i], ("ps", i)

    def phase_cache(self):
        P, d = self.P, self.din
        with self.phase() as ph:
            ci = 0
            for l in range(2):
                for (kn, vn, w, krow, vcol) in (("cak", "cav", 512, QK_ROW["ak"], V_COL["av"]), ("cbk", "cbv", 512, QK_ROW["bk"], V_COL["bv"]),
                                               ("cck", "ccv", 128, QK_ROW["ck"], V_COL["cv"])):
                    kf = ph.sb("ckf", [128, 2, 512], F32); kb = ph.sb("ckb", [128, 2, 512], BF16)
                    vf = ph.sb("cvf", [128, 2, 512], F32); vb = ph.sb("cvb", [128, 2, 512], BF16)
                    kT = ph.sb("ckT", [128, 4, 256], BF16)
                    u = P.uid; P.uid += 1
                    P.dma(kf[:, :, :w], d[kn][l].rearrange("(c p) f -> p c f", p=128), writes=[("ckf", u)])
                    P.dma(vf[:, :, :w], d[vn][l].rearrange("(c p) f -> p c f", p=128), writes=[("cvf", u)])
                    P.op("dve", lambda e, kb=kb, kf=kf, w=w: e.tensor_copy(out=kb[:, :, :w], in_=kf[:, :, :w]), reads=[("ckf", u)], writes=[("ckb", u)])
                    P.op("pool", lambda e, vb=vb, vf=vf, w=w: e.tensor_copy(out=vb[:, :, :w], in_=vf[:, :, :w]), reads=[("cvf", u)], writes=[("cvb", u)])
                    P.dma(self.vtok[l][NT:NTX, vcol:vcol + w].rearrange("(c p) f -> p c f", p=128), vb[:, :, :w], reads=[("cvb", u)], writes=[("vtokc", l, vn)])
                    for fb in range(w // 128):
                        pb, pk = self.bank()
                        pbb = pb.bitcast(BF16)
                        for c in range(2):
                            P.op("pe", lambda e, pbb=pbb, kb=kb, c=c, fb=fb: e.transpose(pbb[:, c * 128:(c + 1) * 128], kb[:, c, fb * 128:(fb + 1) * 128], self.identb[:]),
                                 reads=[("ckb", u), "identb"], writes=[pk])
                        P.op("act", lambda e, kT=kT, pbb=pbb, fb=fb: e.copy(out=kT[:, fb, :], in_=pbb[:, 0:256]), reads=[pk], writes=[("ckT", u, fb)])
                        P.dma(self.qkT[l][krow + fb * 128:krow + (fb + 1) * 128, NT:NTX], kT[:, fb, :], reads=[("ckT", u, fb)], writes=[("qkTc", l, kn, fb)])

    def phase_lam(self):
        P, d = self.P, self.din
        self.lam = self.gsb("lam", [128, 2], F32)
        self.gsc = self.gsb("gsc", [128, 2], F32)
        self.esink = self.gsb("esink", [128, 2, 8], F32)
        with self.phase() as ph:
            dl = ph.sb("dl", [128, 2, 4, 64], F32)
            pr = ph.sb("dlp", [128, 2, 2, 64], F32)
            sm = ph.sb("dls", [128, 2, 2], F32)
            ex = ph.sb("dle", [128, 2, 2], F32)
            gn = ph.sb("dgn", [128, 2], F32)
            sk = ph.sb("ssk", [128, 2, 8], F32)
            for l in range(2):
                P.dma(dl[:, l], d["diff_lambda"][l].partition_broadcast(128), writes=["dl"])
                P.dma(gn[:, l:l + 1], d["diff_norm_g"][l].rearrange("(p o) -> p o", o=1), writes=["dgn"])
                P.dma(sk[:, l], d["swa_sink"][l].partition_broadcast(128), writes=["ssk"])
            for l in range(2):
                P.op("dve", lambda e, l=l: e.tensor_tensor(out=pr[:, l], in0=dl[:, l, 0:4:2, :], in1=dl[:, l, 1:4:2, :], op=ALU.mult), reads=["dl"], writes=["dlp"])
                P.op("dve", lambda e, l=l: e.tensor_reduce(out=sm[:, l], in_=pr[:, l], axis=mybir.AxisListType.X, op=ALU.add), reads=["dlp"], writes=["dls"], hard=True)
            P.op("act", lambda e: e.activation(out=ex[:], in_=sm[:], func=AF.Exp), reads=["dls"], writes=["dle"])
            P.op("act", lambda e: e.activation(out=self.esink[:], in_=sk[:], func=AF.Exp), reads=["ssk"], writes=["esink"])
            for l in range(2):
                lam_init = 0.8 - 0.6 * math.exp(-0.3 * l)
                P.op("dve", lambda e, l=l, lam_init=lam_init: e.scalar_tensor_tensor(
                    out=self.lam[:, l:l + 1], in0=ex[:, l, 0:1], scalar=lam_init, in1=ex[:, l, 1:2], op0=ALU.add, op1=ALU.subtract),
                    reads=["dle"], writes=["lam"])
                P.op("dve", lambda e, l=l, lam_init=lam_init: e.tensor_scalar(
                    out=self.gsc[:, l:l + 1], in0=gn[:, l:l + 1], scalar1=1.0 - lam_init, scalar2=None, op0=ALU.mult), reads=["dgn"], writes=["gsc"])

    def softmax_accum(self, bufs, acc, ncols, dvp, chunks, look=2):
        P = self.P
        (bo, bok), (bd, bdk) = acc
        nch = len(chunks)

        def stage_s(ci):
            ch = chunks[ci]
            bs, bsk = self.bank_i(4 + self.sbank % 4); self.sbank += 1
            ns = len(ch["s"])
            for mi, (c0, n, la, ra) in enumerate(ch["s"]):
                P.op("pe", lambda e, bs=bs, c0=c0, n=n, la=la, ra=ra, mi=mi, ns=ns: e.matmul(bs[:, c0:c0 + n], lhsT=la, rhs=ra, start=(mi == 0), stop=(mi == ns - 1),
                                                                                 skip_group_check=True), reads=ch["sreads"], writes=[bsk])
            i = bufs["pti"] % len(bufs["pt"]); bufs["pti"] += 1
            pt, ptk = bufs["pt"][i], ("pt", i)
            if ch.get("bias") is not None:
                j = bufs["tmi"] % len(bufs["tm"]); bufs["tmi"] += 1
                tm, tmk = bufs["tm"][j], ("ptm", j)
                P.op("dve", lambda e, tm=tm, bs=bs, b=ch["bias"]: e.scalar_tensor_tensor(out=tm[:, :ncols], in0=bs[:, :ncols], scalar=0.125, in1=b, op0=ALU.mult, op1=ALU.add),
                     reads=[bsk] + ch["breads"], writes=[tmk])
                P.op("act", lambda e, pt=pt, tm=tm: e.activation(out=pt[:, :ncols], in_=tm[:, :ncols], func=AF.Exp), reads=[tmk], writes=[ptk])
            else:
                P.op("act", lambda e, pt=pt, bs=bs: e.activation(out=pt[:, :ncols], in_=bs[:, :ncols], func=AF.Exp, scale=0.125), reads=[bsk], writes=[ptk])
            if ch.get("mask") is not None:
                mk = ch["mask"]
                P.op("pool", lambda e, pt=pt, mk=mk: e.tensor_tensor(out=pt[:, :ncols].rearrange("p (r q) -> p r q", q=128), in0=pt[:, :ncols].rearrange("p (r q) -> p r q", q=128),
                                                                   in1=mk, op=ALU.mult), reads=[ptk, "cmask"], writes=[ptk])
            return pt, ptk

        def stage_pv(ci, pt, ptk):
            ch = chunks[ci]
            for mi, (c0, n, va) in enumerate(ch["pv"]):
                P.op("pe", lambda e, c0=c0, n=n, va=va, pt=pt, ci=ci, mi=mi: e.matmul(
                    bo[0:dvp, c0:c0 + n], lhsT=va, rhs=pt[:, c0:c0 + n], start=(ci == 0 and mi == 0), stop=(ci == nch - 1), skip_group_check=True),
                    reads=[ptk] + ch["vreads"], writes=[bok])
            P.op("pe", lambda e, pt=pt, ci=ci: e.matmul(bd[0:dvp, 0:ncols], lhsT=self.onesb[:, 0:dvp], rhs=pt[:, :ncols], start=(ci == 0), stop=(ci == nch - 1)),
                 reads=[ptk, "onesb"], writes=[bdk])

        pend = {}
        for ci in range(nch):
            pend[ci] = stage_s(ci)
            if ci >= look:
                stage_pv(ci - look, *pend.pop(ci - look))
        for ci in sorted(pend):
            stage_pv(ci, *pend[ci])

    def act_pow(self, out, in_, power, reads, writes, scale=1.0, bias=None, w2=None):
        P = self.P
        if bias is None:
            P.op("act", lambda e: e.activation(out=out, in_=in_, func=AF.Ln, scale=scale), reads=reads, writes=writes)
        else:
            P.op("act", lambda e: e.activation(out=out, in_=in_, func=AF.Ln, scale=scale, bias=bias), reads=reads + ["epst"], writes=writes)
        P.op("act", lambda e: e.activation(out=out, in_=out, func=AF.Exp, scale=power), reads=writes, writes=writes)

    def attn_bufs(self, ph):
        return dict(pt=[ph.sb(f"pt{i}", [128, 512], BF16) for i in range(4)], pti=0,
                    tm=[ph.sb(f"ptm{i}", [128, 512], F32) for i in range(3)], tmi=0)

    def load_qkv(self, ph, l, qrow, nqh, krow, nkh, vcol, vw, tag):
        P = self.P
        Q = ph.sb("Q" + tag, [64, nqh, NT], BF16); Kt = ph.sb("K" + tag, [64, nkh, NTX], BF16); V = ph.sb("V" + tag, [128, 18, vw], BF16)
        qr = [("qkT", (qrow // 128) + i) for i in range((nqh * 64 + 127) // 128)]
        kr = [("qkT", (krow // 128) + i) for i in range((nkh * 64 + 127) // 128)]
        nm = {0: "av", 512: "bv", 1024: "cv"}[vcol]
        for h0 in range(0, nqh, 4):
            P.dma(Q[:, h0:h0 + 4, :], self.qkT[l][qrow + h0 * 64:qrow + (h0 + 4) * 64, 0:NT].rearrange("(h d) t -> d h t", d=64), reads=qr, writes=["Q" + tag])
        for h0 in range(0, nkh, 4):
            h1 = min(nkh, h0 + 4)
            P.dma(Kt[:, h0:h1, :], self.qkT[l][krow + h0 * 64:krow + h1 * 64, :].rearrange("(h d) t -> d h t", d=64), reads=kr, writes=["K" + tag])
        for c0 in range(0, 18, 6):
            P.dma(V[:, c0:c0 + 6, :], self.vtok[l][c0 * 128:(c0 + 6) * 128, vcol:vcol + vw].rearrange("(c p) f -> p c f", p=128),
                  reads=[("vtok", nm, t) for t in range(16)], writes=["V" + tag])
        return Q, Kt, V

    def phase_attn_a(self, l):
        P = self.P
        with self.phase() as ph:
            Q, Kt, V = self.load_qkv(ph, l, QK_ROW["aq"], 8, QK_ROW["ak"], 8, V_COL["av"], 512, "a")
            bufs = self.attn_bufs(ph)
            NB = 2
            rdA = [[ph.sb(f"ard{i}_{b}", [128, 512], F32) for i in range(2)] for b in range(NB)]
            t0A = [ph.sb(f"at0_{b}", [128, 512], F32) for b in range(NB)]; t1A = [ph.sb(f"at1_{b}", [128, 512], F32) for b in range(NB)]
            osqA = [ph.sb(f"aosq{b}", [128, 512], BF16) for b in range(NB)]; rrA = [ph.sb(f"arr{b}", [128, 512], F32) for b in range(NB)]
            ost = [ph.sb(f"aost{i}", [128, 512], BF16) for i in range(2)]
            oi = 0
            groups = [(s * 256, 256, [s * 256, s * 256 + 128]) for s in range(4)]
            groups += [(1024 + i * 512, 512, [1024 + k * 128 for k in range(10)]) for i in range(2)]
            ui = 0
            for (q0, nq, kst) in groups:
                for h in range(4):
                    accs = []
                    for c in range(2):
                        acc = (self.bank_i(2 * c), self.bank_i(2 * c + 1))
                        ch_ = c * 4 + h
                        chunks = [dict(s=[(0, nq, Kt[:, ch_, k0:k0 + 128], Q[:, ch_, q0:q0 + nq])], sreads=["Qa", "Ka"],
                                       pv=[(0, nq, V[:, k0 // 128, h * 128:(h + 1) * 128])], vreads=["Va"]) for k0 in kst]
                        self.softmax_accum(bufs, acc, nq, 128, chunks)
                        accs.append(acc)
                    (bo0, bok0), (bd0, bdk0) = accs[0]
                    (bo1, bok1), (bd1, bdk1) = accs[1]
                    b = ui % NB; ui += 1
                    rd, t0, t1, osq, rr = rdA[b], t0A[b], t1A[b], osqA[b], rrA[b]
                    k = lambda nm: (nm, b)
                    self.act_pow(rd[0][:, :nq], bd0[:, :nq], -1.0, [bdk0], [k("ard0")])
                    self.act_pow(rd[1][:, :nq], bd1[:, :nq], -1.0, [bdk1], [k("ard1")])
                    P.op("dve", lambda e, nq=nq, t0=t0, rd=rd: e.tensor_tensor(out=t0[:, :nq], in0=bo0[:, :nq], in1=rd[0][:, :nq], op=ALU.mult), reads=[bok0, k("ard0")], writes=[k("at0")])
                    P.op("dve", lambda e, nq=nq, t1=t1, rd=rd: e.scalar_tensor_tensor(out=t1[:, :nq], in0=bo1[:, :nq], scalar=self.lam[:, l:l + 1], in1=rd[1][:, :nq], op0=ALU.mult, op1=ALU.mult),
                         reads=[bok1, k("ard1"), "lam"], writes=[k("at1")])
                    P.op("pool", lambda e, nq=nq, t0=t0, t1=t1: e.tensor_tensor(out=t0[:, :nq], in0=t0[:, :nq], in1=t1[:, :nq], op=ALU.subtract), reads=[k("at0"), k("at1")], writes=[k("at0")])
                    P.op("act", lambda e, nq=nq, osq=osq, t0=t0: e.activation(out=osq[:, :nq], in_=t0[:, :nq], func=AF.Square), reads=[k("at0")], writes=[k("aosq")])
                    bs, bsk = self.bank_i(4 + self.sbank % 4); self.sbank += 1
                    P.op("pe", lambda e, bs=bs, nq=nq, osq=osq: e.matmul(bs[:, :nq], lhsT=self.onesb[:], rhs=osq[:, :nq], start=True, stop=True), reads=[k("aosq"), "onesb"], writes=[bsk])
                    self.act_pow(rr[:, :nq], bs[:, :nq], -0.5, [bsk], [k("arr")], scale=1.0 / 128, bias=self.epst[:, 0:1])
                    o = ost[oi % 2]; ok_ = ("aost", oi % 2); oi += 1
                    P.op("dve", lambda e, o=o, nq=nq, t0=t0, rr=rr: e.scalar_tensor_tensor(out=o[:, :nq], in0=t0[:, :nq], scalar=self.gsc[:, l:l + 1], in1=rr[:, :nq], op0=ALU.mult, op1=ALU.mult),
                         reads=[k("at0"), k("arr"), "gsc"], writes=[ok_])
                    P.dma(self.oT[h * 128:(h + 1) * 128, q0:q0 + nq], o[:, :nq], reads=[ok_], writes=[("oT", 0)])

    def phase_attn_bc_ctx(self, l, which, Q, Kt, V, bufs, ph):
        P = self.P
        rd = ph.sb("brd", [64, 512], F32)
        ost = [ph.sb(f"bost{i}", [64, 512], BF16) for i in range(2)]
        oi = 0
        tag = "b" if which == "b" else "c"
        row0 = 512 if which == "b" else 1024
        for s in range(4):
            q0 = s * 256
            for h in range(8):
                kh = h if which == "b" else h // 4
                acc = (self.bank_i((oi % 2) * 2), self.bank_i((oi % 2) * 2 + 1))
                chunks = [dict(s=[(0, 256, Kt[:, kh, k0:k0 + 128], Q[:, h, q0:q0 + 256])], sreads=["Q" + tag, "K" + tag],
                               pv=[(0, 256, V[:, k0 // 128, kh * 64:(kh + 1) * 64])], vreads=["V" + tag]) for k0 in (q0, q0 + 128)]
                self.softmax_accum(bufs, acc, 256, 64, chunks)
                (bo, bok), (bd, bdk) = acc
                if which == "c":
                    P.op("dve", lambda e, bd=bd, h=h: e.tensor_scalar(out=rd[:, :256], in0=bd[0:64, :256], scalar1=self.esink[0:64, l, h:h + 1], scalar2=None, op0=ALU.add),
                         reads=[bdk, "esink"], writes=["brd"])
                    self.act_pow(rd[:, :256], rd[:, :256], -1.0, ["brd"], ["brd"])
                else:
                    self.act_pow(rd[:, :256], bd[0:64, :256], -1.0, [bdk], ["brd"])
                o = ost[oi % 2]; ok_ = ("bost", oi % 2); oi += 1
                P.op("dve", lambda e, o=o, bo=bo: e.tensor_tensor(out=o[:, :256], in0=bo[0:64, :256], in1=rd[:, :256], op=ALU.mult), reads=[bok, "brd"], writes=[ok_])
                P.dma(self.oT[row0 + h * 64:row0 + (h + 1) * 64, q0:q0 + 256], o[:, :256], reads=[ok_], writes=[("oT", 1 if which == "b" else 2)])

    def phase_attn_b(self, l):
        P, d = self.P, self.din
        with self.phase() as ph:
            Q, Kt, V = self.load_qkv(ph, l, QK_ROW["bq"], 8, QK_ROW["bk"], 8, V_COL["bv"], 512, "b")
            bufs = self.attn_bufs(ph)
            self.phase_attn_bc_ctx(l, "b", Q, Kt, V, bufs, ph)
            braw = ph.sb("braw", [128, 14, 8, 64], F32); BT = ph.sb("BT", [128, 14, 8, 64], F32)
            m01 = ph.sb("m01", [128, 64], F32); mng = ph.sb("mng", [128, 64], F32); zt = ph.sb("bzt", [1, 64], F32)
            P.dma(m01[:], d["c_m01"], writes=["m01"]); P.dma(mng[:], d["c_mng"], writes=["mng"])
            P.op("dve", lambda e: e.memset(zt[:], 0.0), writes=["bzt"])
            nb = self.nbpad[l]
            P.dma(nb[0:1, 0:64], zt[:], reads=["bzt"], writes=["nbp0"])
            P.dma(nb[0:1, 64 + 3720:64 + 3720 + 64], zt[:], reads=["bzt"], writes=["nbp1"])
            P.dma(nb[0:1, 64:64 + 3720], d["na_bias"][l].rearrange("(o h) a b -> o (h a b)", o=1), writes=["nbp2"])
            for jj in range(2):
                for h in range(8):
                    src = bass.AP(nb.tensor, 64 + jj * 31 - 48 + h * 465, [[1, 64], [31, 14], [1, 64]])
                    P.dma(braw[jj * 64:(jj + 1) * 64, :, h, :], src, reads=["nbp0", "nbp1", "nbp2"], writes=[("braw", jj, h)])
            bt3 = BT[:].rearrange("p a h q -> p (a h) q")
            P.op("dve", lambda e: e.tensor_tensor(out=bt3, in0=braw[:].rearrange("p a h q -> p (a h) q")[:, :, ::-1],
                                                  in1=m01[:, None, :].broadcast_to([128, 112, 64]), op=ALU.mult), reads=[("braw", jj, h) for jj in range(2) for h in range(8)] + ["m01"], writes=["BT"])
            P.op("dve", lambda e: e.tensor_tensor(out=bt3, in0=bt3, in1=mng[:, None, :].broadcast_to([128, 112, 64]), op=ALU.add), reads=["BT", "mng"], writes=["BT"])
            OB = ph.sb("OB", [64, 8, 1024], BF16)
            rd = ph.sb("blrd", [64, 512], F32)
            V2 = ph.sb("Vb2", [128, 7, 512], BF16)
            P.dma(V2[:], self.vtok[l][1088:1088 + 7 * 128, 512:1024].rearrange("(c p) f -> p c f", p=128),
                  reads=[("vtok", "bv", t) for t in range(16)], writes=["Vb2"])
            for r in range(16):
                r0 = min(max(r - 4, 0), 8)
                q0 = 1024 + r * 64
                acc = (self.bank_i((r % 2) * 2), self.bank_i((r % 2) * 2 + 1))
                chunks = []
                for m in range(6):
                    if m < 4:
                        k0 = 1024 + (r0 + 2 * m) * 64
                        af = r0 + 2 * m - r + 7
                        bias = BT[:, af].rearrange("p h q -> p (h q)")
                    else:
                        k0 = NT + (m - 4) * 128
                        bias = None
                    Vc_ = V[:, k0 // 128] if k0 % 128 == 0 else V2[:, (k0 - 1088) // 128]
                    chunks.append(dict(s=[(h * 64, 64, Kt[:, h, k0:k0 + 128], Q[:, h, q0:q0 + 64]) for h in range(8)], sreads=["Qb", "Kb"],
                                       pv=[(h * 64, 64, Vc_[:, h * 64:(h + 1) * 64]) for h in range(8)], vreads=["Vb", "Vb2"], bias=bias, breads=["BT"]))
                self.softmax_accum(bufs, acc, 512, 64, chunks)
                (bo, bok), (bd, bdk) = acc
                self.act_pow(rd[:], bd[0:64, :], -1.0, [bdk], ["blrd"])
                P.op("dve", lambda e, bo=bo, r=r: e.tensor_tensor(out=OB[:, :, r * 64:(r + 1) * 64], in0=bo[0:64, :].rearrange("p (h q) -> p h q", q=64),
                                                                  in1=rd[:].rearrange("p (h q) -> p h q", q=64), op=ALU.mult), reads=[bok, "blrd"], writes=["OB"])
            P.dma(self.oT[512:1024, 1024:2048].rearrange("(h d) t -> d h t", d=64), OB[:], reads=["OB"], writes=[("oT", 1)])

    def phase_attn_c(self, l):
        P, d = self.P, self.din
        with self.phase() as ph:
            Q, Kt, V = self.load_qkv(ph, l, QK_ROW["cq"], 8, QK_ROW["ck"], 2, V_COL["cv"], 128, "c")
            bufs = self.attn_bufs(ph)
            self.phase_attn_bc_ctx(l, "c", Q, Kt, V, bufs, ph)
            cm = ph.sb("cmask", [128, 2, 128], BF16)
            P.dma(cm[:], d["c_cmask"], writes=["cmask"])
            OC = ph.sb("OC", [64, 8, 1024], BF16)
            rd = ph.sb("clrd", [64, 512], F32)
            it = 0
            for nbk in range(8):
                q0 = 1024 + nbk * 128
                for g in range(2):
                    acc = (self.bank_i((it % 2) * 2), self.bank_i((it % 2) * 2 + 1)); it += 1
                    chunks = []
                    for (kb, mi) in ((nbk - 1, 0), (nbk, None), (nbk + 1, 1), (8, None), (9, None)):
                        if kb < 0 or (kb > 7 and mi is not None):
                            continue
                        k0 = 1024 + kb * 128
                        mask = None if mi is None else cm[:, mi:mi + 1, :].broadcast_to([128, 4, 128])
                        chunks.append(dict(s=[(rq * 128, 128, Kt[:, g, k0:k0 + 128], Q[:, g * 4 + rq, q0:q0 + 128]) for rq in range(4)], sreads=["Qc", "Kc"],
                                           pv=[(0, 512, V[:, k0 // 128, g * 64:(g + 1) * 64])], vreads=["Vc"], mask=mask))
                    self.softmax_accum(bufs, acc, 512, 64, chunks)
                    (bo, bok), (bd, bdk) = acc
                    P.op("dve", lambda e, bd=bd, g=g: e.tensor_tensor(out=rd[:].rearrange("p (r q) -> p r q", q=128), in0=bd[0:64, :].rearrange("p (r q) -> p r q", q=128),
                                                                 in1=self.esink[0:64, l, g * 4:(g + 1) * 4].unsqueeze(2).broadcast_to([64, 4, 128]), op=ALU.add),
                         reads=[bdk, "esink"], writes=["clrd"])
                    self.act_pow(rd[:], rd[:], -1.0, ["clrd"], ["clrd"])
                    P.op("dve", lambda e, bo=bo, g=g, nbk=nbk: e.tensor_tensor(out=OC[:, g * 4:(g + 1) * 4, nbk * 128:(nbk + 1) * 128], in0=bo[0:64, :].rearrange("p (r q) -> p r q", q=128),
                                                                       in1=rd[:].rearrange("p (r q) -> p r q", q=128), op=ALU.mult), reads=[bok, "clrd"], writes=["OC"])
            P.dma(self.oT[1024:1536, 1024:2048].rearrange("(h d) t -> d h t", d=64), OC[:], reads=["OC"], writes=[("oT", 2)])

    def hy_ffn(self, ph, l, L, h2T, cst):
        P, d = self.P, self.din
        zT = ph.sb("hzT", [33, 1024], F32); h1T = ph.sb("hh1", [64, 1024], F32)
        arg = ph.sb("harg", [64, 512], F32); wr = ph.sb("hwr", [64, 512], F32)
        u = P.uid; P.uid += 1
        P.dma(zT[:, :L], d[f"c_z{L}"], writes=[("hzT", u)])
        for stage in range(2):
            src = zT if stage == 0 else h1T
            dst = h1T if stage == 0 else h2T
            K = 33 if stage == 0 else 64
            wt = cst["w1"] if stage == 0 else cst["w2"]
            fb = cst["fb"][:, stage:stage + 1]
            for n0 in range(0, L, 512):
                n = min(512, L - n0)
                pb, pk = self.bank()
                P.op("pe", lambda e, pb=pb, wt=wt, K=K, src=src, n0=n0, n=n: e.matmul(pb[0:64, :n], lhsT=wt[0:K, :], rhs=src[0:K, n0:n0 + n], start=True, stop=True),
                     reads=["hyc", ("hzT", u), ("hh", u, 0)], writes=[pk])
                P.op("dve", lambda e, pb=pb, n=n, fb=fb: e.tensor_scalar(out=arg[:, :n], in0=pb[0:64, :n], scalar1=cst["freq"][:, 0:1], scalar2=fb, op0=ALU.mult, op1=ALU.add),
                     reads=[pk, "hyc"], writes=["harg"], hard=True)
                P.op("dve", lambda e, n=n: e.tensor_scalar(out=wr[:, :n], in0=arg[:, :n], scalar1=math.pi, scalar2=-2 * math.pi, op0=ALU.is_gt, op1=ALU.mult), reads=["harg"], writes=["hwr"])
                P.op("dve", lambda e, n=n: e.tensor_tensor(out=wr[:, :n], in0=wr[:, :n], in1=arg[:, :n], op=ALU.add), reads=["hwr", "harg"], writes=["hwr"])
                P.op("dve", lambda e, n=n: e.tensor_scalar(out=arg[:, :n], in0=arg[:, :n], scalar1=-math.pi, scalar2=2 * math.pi, op0=ALU.is_lt, op1=ALU.mult), reads=["harg", "hwr"], writes=["harg"])
                P.op("dve", lambda e, n=n: e.tensor_tensor(out=wr[:, :n], in0=wr[:, :n], in1=arg[:, :n], op=ALU.add), reads=["hwr", "harg"], writes=["hwr"])
                P.op("act", lambda e, dst=dst, n0=n0, n=n: e.activation(out=dst[:, n0:n0 + n], in_=wr[:, :n], func=AF.Sin), reads=["hwr"], writes=[("hh", u, stage)])
        return ("hh", u, 1)

    def hy_G(self, ph, l, L, o, h2T, h2k, cst, F, Fk, hs, hd, G, Gk, tmp, hsk="hYre", hdk="hYs"):
        P = self.P
        nch = L // 128
        win, ta, tb = tmp
        ntc = cst["nt"][L]
        for nc_ in range(nch):
            banks = []
            for dr in range(2):
                pb, pk = self.bank()
                col = (o * 2 + dr) * 512
                P.op("pe", lambda e, pb=pb, nc_=nc_, col=col: e.matmul(pb[:, :], lhsT=h2T[0:64, nc_ * 128:(nc_ + 1) * 128], rhs=cst["w3"][0:64, col:col + 512], start=True, stop=True),
                     reads=[h2k, "hyc"], writes=[pk])
                banks.append((pb, pk))
            P.op("act", lambda e, nc_=nc_: e.activation(out=win[:], in_=cst["dabs"][:, o * 1024:(o + 1) * 1024], func=AF.Exp, scale=ntc[:, nc_:nc_ + 1]),
                 reads=["hyc"], writes=["hwin"])
            P.op("dve", lambda e, pb=banks[0][0]: e.tensor_tensor(out=ta[:], in0=pb[:], in1=win[:, 0:512], op=ALU.mult), reads=[banks[0][1], "hwin"], writes=["hta"])
            P.op("dve", lambda e, pb=banks[1][0]: e.tensor_tensor(out=tb[:], in0=pb[:], in1=win[:, 512:1024], op=ALU.mult), reads=[banks[1][1], "hwin"], writes=["htb"])
            P.op("pool", lambda e, nc_=nc_: e.tensor_tensor(out=hs[:, nc_, :], in0=ta[:], in1=tb[:], op=ALU.add), reads=["hta", "htb"], writes=[hsk])
            P.op("pool", lambda e, nc_=nc_: e.tensor_tensor(out=hd[:, nc_, :], in0=tb[:], in1=ta[:], op=ALU.subtract), reads=["hta", "htb"], writes=[hdk])
        for fc in range(nch):
            pr, prk = self.bank(); pi_, pik = self.bank()
            for nc_ in range(nch):
                P.op("pe", lambda e, pr=pr, nc_=nc_, fc=fc: e.matmul(pr[:], lhsT=F[:, nc_, fc * 128:(fc + 1) * 128], rhs=hs[:, nc_, :], start=(nc_ == 0), stop=(nc_ == nch - 1)),
                     reads=[Fk, hsk], writes=[prk])
            for nc_ in range(nch):
                P.op("pe", lambda e, pi_=pi_, nc_=nc_, fc=fc: e.matmul(pi_[:], lhsT=F[:, nc_, L + fc * 128:L + (fc + 1) * 128], rhs=hd[:, nc_, :], start=(nc_ == 0), stop=(nc_ == nch - 1)),
                     reads=[Fk, hdk], writes=[pik])
            P.op("dve", lambda e, pr=pr, fc=fc: e.tensor_tensor(out=G[:, fc, 0, :], in0=pr[:], in1=cst["biasb"][:, o, :], op=ALU.add), reads=[prk, "hyc"], writes=[Gk])
            P.op("act", lambda e, pi_=pi_, fc=fc: e.copy(out=G[:, fc, 1, :], in_=pi_[:]), reads=[pik], writes=[Gk])

    def hy_dft_mult(self, L, F, Fk, xin, xink, G, Gk, Yre, Ys, mt):
        P = self.P
        nch = L // 128
        for fc in range(nch):
            pr, prk = self.bank(); pi_, pik = self.bank()
            for tc in range(nch):
                P.op("pe", lambda e, pr=pr, tc=tc, fc=fc: e.matmul(pr[:], lhsT=F[:, tc, fc * 128:(fc + 1) * 128], rhs=xin[:, tc, :], start=(tc == 0), stop=(tc == nch - 1)),
                     reads=[Fk, xink], writes=[prk])
            for tc in range(nch):
                P.op("pe", lambda e, pi_=pi_, tc=tc, fc=fc: e.matmul(pi_[:], lhsT=F[:, tc, L + fc * 128:L + (fc + 1) * 128], rhs=xin[:, tc, :], start=(tc == 0), stop=(tc == nch - 1)),
                     reads=[Fk, xink], writes=[pik])
            m1, m2, m3, m4 = mt
            P.op("dve", lambda e, pr=pr, fc=fc: e.tensor_tensor(out=m1[:], in0=pr[:], in1=G[:, fc, 0, :], op=ALU.mult), reads=[prk, Gk], writes=["hm1"])
            P.op("dve", lambda e, pi_=pi_, fc=fc: e.tensor_tensor(out=m2[:], in0=pi_[:], in1=G[:, fc, 1, :], op=ALU.mult), reads=[pik, Gk], writes=["hm2"])
            P.op("dve", lambda e, pi_=pi_, fc=fc: e.tensor_tensor(out=m3[:], in0=pi_[:], in1=G[:, fc, 0, :], op=ALU.mult), reads=[pik, Gk], writes=["hm3"])
            P.op("dve", lambda e, pr=pr, fc=fc: e.tensor_tensor(out=m4[:], in0=pr[:], in1=G[:, fc, 1, :], op=ALU.mult), reads=[prk, Gk], writes=["hm4"])
            P.op("pool", lambda e, fc=fc: e.tensor_tensor(out=Yre[:, fc, :], in0=m1[:], in1=m2[:], op=ALU.add), reads=["hm1", "hm2"], writes=["hYre"])
            P.op("pool", lambda e, fc=fc: e.tensor_tensor(out=Ys[:, fc, :], in0=m3[:], in1=m4[:], op=ALU.subtract), reads=["hm3", "hm4"], writes=["hYs"])

    def hy_seq(self, ph, l, L, tok0, cst, F, Fk, FT, FTk, G0, G1, bufs, gen_G=None):
        P = self.P
        nch = L // 128
        uring, ucf, ucb, x2b, vtm, x1tm, Yre, Ys, mt, ost = bufs
        sw = cst["sw"]
        ui = 0
        sq = P.uid; P.uid += 1
        for part in range(3):
            for ct in range(4):
                tile = part * 4 + ct
                u = uring[ui % 2]; uk = ("hu", ui % 2)
                cf = ucf[ui % 2]; cfk = ("hcf", ui % 2); ui += 1
                P.dma(u[:, :L], self.hyT[tile * 128:(tile + 1) * 128, tok0:tok0 + L], reads=[("hyT", tile)], writes=[uk])
                P.op("dve", lambda e, cf=cf, u=u, tile=tile: e.tensor_scalar(out=cf[:, :L], in0=u[:, :L], scalar1=sw[:, 12 + tile:13 + tile], scalar2=None, op0=ALU.mult),
                     reads=[uk, "hyc"], writes=[cfk])
                P.op("dve", lambda e, cf=cf, u=u, tile=tile: e.scalar_tensor_tensor(out=cf[:, 1:L], in0=u[:, 0:L - 1], scalar=sw[:, tile:tile + 1], in1=cf[:, 1:L], op0=ALU.mult, op1=ALU.add),
                     reads=[uk, cfk, "hyc"], writes=[cfk])
                P.op("dve", lambda e, cf=cf, u=u, tile=tile: e.scalar_tensor_tensor(out=cf[:, 0:L - 1], in0=u[:, 1:L], scalar=sw[:, 24 + tile:25 + tile], in1=cf[:, 0:L - 1], op0=ALU.mult, op1=ALU.add),
                     reads=[uk, cfk, "hyc"], writes=[cfk])
                if part < 2:
                    P.op("act", lambda e, cf=cf, ct=ct: e.copy(out=ucb[:, ct, :L], in_=cf[:, :L]), reads=[cfk], writes=[("hucb", ct)])
                else:
                    P.op("act", lambda e, cf=cf, ct=ct: e.copy(out=x2b[:, ct, :L], in_=cf[:, :L]), reads=[cfk], writes=[("hx2", ct)])
            if part < 2:
                dst = vtm if part == 0 else x1tm
                dk = "hvtm" if part == 0 else "hx1tm"
                for tc in range(nch):
                    pb, pk = self.bank()
                    pbb = pb.bitcast(BF16)
                    for ct in range(4):
                        P.op("pe", lambda e, pbb=pbb, ct=ct, tc=tc: e.transpose(pbb[:, ct * 128:(ct + 1) * 128], ucb[:, ct, tc * 128:(tc + 1) * 128], self.identb[:]),
                             reads=[("hucb", ct), "identb"], writes=[pk])
                    if tc % 2 == 0:
                        P.op("act", lambda e, pbb=pbb, dst=dst, tc=tc: e.copy(out=dst[:, tc, :], in_=pbb[:, 0:512]), reads=[pk], writes=[dk])
                    else:
                        P.op("dve", lambda e, pbb=pbb, dst=dst, tc=tc: e.tensor_copy(out=dst[:, tc, :], in_=pbb[:, 0:512]), reads=[pk], writes=[dk])
        if gen_G is not None:
            gen_G(0)
        self.hy_dft_mult(L, F, Fk, vtm, "hvtm", G0, "hG0", Yre, Ys, mt)
        for tc in range(nch):
            pb, pk = self.bank()
            for fc in range(nch):
                P.op("pe", lambda e, pb=pb, fc=fc, tc=tc: e.matmul(pb[:], lhsT=FT[:, fc, tc * 128:(tc + 1) * 128], rhs=Yre[:, fc, :], start=(fc == 0), stop=False),
                     reads=[FTk, "hYre"], writes=[pk])
            for fc in range(nch):
                P.op("pe", lambda e, pb=pb, fc=fc, tc=tc: e.matmul(pb[:], lhsT=FT[:, nch + fc, tc * 128:(tc + 1) * 128], rhs=Ys[:, fc, :], start=False, stop=(fc == nch - 1)),
                     reads=[FTk, "hYs"], writes=[pk])
            P.op("dve", lambda e, pb=pb, tc=tc: e.scalar_tensor_tensor(out=vtm[:, tc, :], in0=pb[:], scalar=1.0 / L, in1=x1tm[:, tc, :], op0=ALU.mult, op1=ALU.mult),
                 reads=[pk, "hx1tm"], writes=["hvtm"])
        if gen_G is not None:
            gen_G(1)
        self.hy_dft_mult(L, F, Fk, vtm, "hvtm", G1, "hG1", Yre, Ys, mt)
        oi = 0
        for ct in range(4):
            for t0 in range(0, L, 512):
                n = min(512, L - t0)
                pb, pk = self.bank()
                for fc in range(nch):
                    P.op("pe", lambda e, pb=pb, fc=fc, ct=ct, t0=t0, n=n: e.matmul(pb[:, :n], lhsT=Yre[:, fc, ct * 128:(ct + 1) * 128], rhs=FT[:, fc, t0:t0 + n], start=(fc == 0), stop=False),
                         reads=[FTk, "hYre"], writes=[pk])
                for fc in range(nch):
                    P.op("pe", lambda e, pb=pb, fc=fc, ct=ct, t0=t0, n=n: e.matmul(pb[:, :n], lhsT=Ys[:, fc, ct * 128:(ct + 1) * 128], rhs=FT[:, nch + fc, t0:t0 + n], start=False, stop=(fc == nch - 1)),
                         reads=[FTk, "hYs"], writes=[pk])
                o = ost[oi % 2]; ok_ = ("host", oi % 2); oi += 1
                P.op("dve", lambda e, pb=pb, o=o, ct=ct, t0=t0, n=n: e.scalar_tensor_tensor(out=o[:, :n], in0=pb[:, :n], scalar=1.0 / L, in1=x2b[:, ct, t0:t0 + n], op0=ALU.mult, op1=ALU.mult),
                     reads=[pk, ("hx2", ct)], writes=[ok_])
                P.dma(self.oT[1536 + ct * 128:1536 + (ct + 1) * 128, tok0 + t0:tok0 + t0 + n], o[:, :n], reads=[ok_], writes=[("oT", 3)])

    def phase_hyena(self, l):
        P, d = self.P, self.din
        with self.phase() as ph:
            cst = dict(w1=ph.sb("hw1", [33, 64], F32), w2=ph.sb("hw2", [64, 64], F32), w3=ph.sb("hw3", [64, 2048], F32),
                       freq=ph.sb("hfreq", [64, 1], F32), fb=ph.sb("hfb", [64, 2], F32), dabs=ph.sb("hdabs", [128, 2048], F32),
                       biasb=ph.sb("hbiasb", [128, 2, 512], F32), sw=ph.sb("hsw", [128, 36], F32),
                       nt={256: ph.sb("hnt256", [128, 2], F32), 1024: ph.sb("hnt1024", [128, 8], F32)})
            P.dma(cst["w1"][:], d["hy_w1"][l], writes=["hyc"]); P.dma(cst["w2"][:], d["hy_w2"][l], writes=["hyc2"]); P.dma(cst["w3"][:], d["hy_w3"][l], writes=["hyc3"])
            P.dma(cst["freq"][:], d["hy_freq"][l].rearrange("(p o) -> p o", o=1), writes=["hyc4"])
            P.dma(cst["fb"][:, 0:1], d["hy_b1"][l].rearrange("(p o) -> p o", o=1), writes=["hyc5"])
            P.dma(cst["fb"][:, 1:2], d["hy_b2"][l].rearrange("(p o) -> p o", o=1), writes=["hyc6"])
            P.dma(cst["dabs"][:], d["hy_decay"][l].partition_broadcast(128), writes=["hyc7"])
            P.dma(cst["biasb"][:], d["hy_bias"][l].partition_broadcast(128), writes=["hyc8"])
            P.dma(cst["nt"][256][:], d["c_nt256"], writes=["hyc9"]); P.dma(cst["nt"][1024][:], d["c_nt1024"], writes=["hyc10"])
            self.load_T(ph, cst["sw"][:], ["hycsw"], d["hy_short"][l].rearrange("k (t p) -> (k t) p", p=128), 36)
            allk = ["hyc", "hyc2", "hyc3", "hyc4", "hyc5", "hyc6", "hyc7", "hyc8", "hyc9", "hyc10", "hycsw"]
            P.op("dve", lambda e: e.tensor_scalar(out=cst["fb"][:], in0=cst["fb"][:], scalar1=cst["freq"][:, 0:1], scalar2=None, op0=ALU.mult), reads=allk, writes=["hycA"], hard=True)
            P.op("act", lambda e: e.activation(out=cst["dabs"][:], in_=cst["dabs"][:], func=AF.Abs), reads=["hycA"], writes=["hycB"])
            P.op("dve", lambda e: e.tensor_copy(out=cst["nt"][256][:], in_=cst["nt"][256][:]), reads=["hycB"] + allk, writes=["hyc"])
            win = ph.sb("hwin", [128, 1024], F32); ta = ph.sb("hta", [128, 512], F32); tb = ph.sb("htb", [128, 512], F32)
            mt = [ph.sb(f"hm{i}", [128, 512], F32) for i in range(4)]
            ost = [ph.sb(f"host{i}", [128, 512], BF16) for i in range(2)]
            uring = [ph.sb(f"hur{i}", [128, 1024], F32) for i in range(2)]
            ucf = [ph.sb(f"hcf{i}", [128, 1024], F32) for i in range(2)]
            ucb = ph.sb("hucb", [128, 4, 1024], BF16); x2b = ph.sb("hx2b", [128, 4, 1024], BF16)
            vtm = ph.sb("hvtm", [128, 8, 512], BF16); x1tm = ph.sb("hx1tm", [128, 8, 512], BF16)
            Yre = ph.sb("hYre", [128, 8, 512], BF16); Ys = ph.sb("hYs", [128, 8, 512], BF16)
            hs, hd = Yre, Ys
            h2T = ph.sb("hh2", [64, 1024], F32)
            Ga = ph.sb("hGa", [128, 8, 2, 512], F32)
            bufs = (uring, ucf, ucb, x2b, vtm, x1tm, Yre, Ys, mt, ost)
            with self.phase() as p3:
                h2k = self.hy_ffn(p3, l, 256, h2T, cst)
            with self.phase() as p2:
                F = p2.sb("hF256", [128, 2, 512], BF16); FT = p2.sb("hFT256", [128, 4, 256], BF16)
                P.dma(F[:], d["c_F256"].rearrange("(c p) f -> p c f", p=128), writes=["hF"])
                P.dma(FT[:], d["c_FT256"].rearrange("(c p) f -> p c f", p=128), writes=["hFT"])
                G0 = Ga[:, 0:2]; G1 = Ga[:, 2:4]
                self.hy_G(p2, l, 256, 0, h2T, h2k, cst, F, "hF", hs, hd, G0, "hG0", (win, ta, tb))
                self.hy_G(p2, l, 256, 1, h2T, h2k, cst, F, "hF", hs, hd, G1, "hG1", (win, ta, tb))
                for s in range(4):
                    self.hy_seq(p2, l, 256, s * 256, cst, F, "hF", FT, "hFT", G0, G1, bufs)
            with self.phase() as p3:
                h2k = self.hy_ffn(p3, l, 1024, h2T, cst)
            with self.phase() as p2:
                F = p2.sb("hF1024", [128, 8, 2048], BF16); FT = p2.sb("hFT1024", [128, 16, 1024], BF16)
                for q in range(4):
                    P.dma(F[:, q * 2:(q + 1) * 2, :], d["c_F1024"][q * 256:(q + 1) * 256, :].rearrange("(c p) f -> p c f", p=128), writes=[("hF", q)])
                    P.dma(FT[:, q * 4:(q + 1) * 4, :], d["c_FT1024"][q * 512:(q + 1) * 512, :].rearrange("(c p) f -> p c f", p=128), writes=[("hFT", q)])
                P.op("pool", lambda e: e.tensor_copy(out=F[:, 0, 0:2], in_=F[:, 0, 0:2]), reads=[("hF", q) for q in range(4)], writes=["hF"])
                P.op("pool", lambda e: e.tensor_copy(out=FT[:, 0, 0:2], in_=FT[:, 0, 0:2]), reads=[("hFT", q) for q in range(4)], writes=["hFT"])

                def gen_G(o):
                    self.hy_G(p2, l, 1024, o, h2T, h2k, cst, F, "hF", hs, hd, Ga, "hG%d" % o, (win, ta, tb))
                self.hy_seq(p2, l, 1024, 1024, cst, F, "hF", FT, "hFT", Ga, Ga, bufs, gen_G=gen_G)

    def phase_mixers(self, l):
        self.sbank = 0
        sel = self.dbg.get("mix", "abcd") if isinstance(self.dbg, dict) else "abcd"
        if "a" in sel:
            self.phase_attn_a(l)
        if "b" in sel:
            self.phase_attn_b(l)
        if "c" in sel:
            self.phase_attn_c(l)
        if "d" in sel:
            self.phase_hyena(l)

def _consts():
    c = {}
    c["c_ident"] = np.eye(128, dtype=np.float32)
    c["c_identb"] = np.eye(128).astype(ml_dtypes.bfloat16)
    c["c_onesb"] = np.ones((128, 128)).astype(ml_dtypes.bfloat16)
    f = np.arange(128)
    partner = (f // 32) * 32 + ((f % 32) + 16) % 32
    pm = np.zeros((128, 128), np.float32)
    pm[partner, f] = 1.0
    c["c_rperm"] = pm
    t = np.arange(1024)
    fh = f % 64
    q = fh // 16
    j = fh % 16
    inv = (10000.0 ** (-(j.astype(np.float32)) / 16.0)).astype(np.float32)
    pos = np.where((q < 2)[:, None], (t // 64)[None, :], (t % 64)[None, :]).astype(np.float32)
    ang = (pos * inv[:, None]).astype(np.float32)
    c["c_rcos"] = np.cos(ang).astype(np.float32)
    sgn = np.where(q % 2 == 0, -1.0, 1.0).astype(np.float32)
    c["c_rsin"] = (np.sin(ang) * sgn[:, None]).astype(np.float32)
    kc = (np.arange(128) % 64)[:, None]
    qc = np.arange(64)[None, :]
    col0 = np.clip(qc - 8, 0, 48)
    valid = (kc >= col0) & (kc < col0 + 16)
    c["c_m01"] = valid.astype(np.float32)
    c["c_mng"] = np.where(valid, 0.0, -30000.0).astype(np.float32)
    for L in (256, 1024):
        n = np.arange(L, dtype=np.float32)[:, None]
        t = n / np.float32(max(L - 1, 1))
        w = (np.float32(2.0 * math.pi) * n / np.float32(L)).astype(np.float32)
        bands = np.linspace(1e-4, 15, 16, dtype=np.float32)[None, :]
        z = np.concatenate([t, np.cos(bands * w), -np.sin(bands * w)], axis=-1).astype(np.float32)
        c[f"c_z{L}"] = np.ascontiguousarray(z.T)
        c[f"c_nt{L}"] = np.ascontiguousarray((-t[:, 0]).reshape(L // 128, 128).T.astype(np.float32))
        nn = np.arange(L, dtype=np.float64)[:, None]
        om = math.pi * (2 * np.arange(L, dtype=np.float64)[None, :] + 1) / (2 * L)
        Fm = np.concatenate([np.cos(nn * om), np.sin(nn * om)], axis=1)
        c[f"c_F{L}"] = Fm.astype(np.float32).astype(ml_dtypes.bfloat16)
        c[f"c_FT{L}"] = np.ascontiguousarray(Fm.T).astype(np.float32).astype(ml_dtypes.bfloat16)
    kk = np.arange(128)[:, None]; qq = np.arange(128)[None, :]
    c["c_cmask"] = np.stack([(kk >= qq), (kk <= qq)], axis=1).astype(np.float32).astype(ml_dtypes.bfloat16)
    return c


_CACHE = {}


def kernel(**inputs):
    inputs = {k: np.asarray(v) for k, v in inputs.items()}
    if "nc" not in _CACHE:
        _CACHE["nc"] = Builder().build()
    nc = _CACHE["nc"]
    consts = _consts()
    shared = {k: np.ascontiguousarray(inputs[k]) for k in (
        "w_ada", "b_ada", "g_mix", "w_in", "diff_lambda", "diff_norm_g", "na_bias", "swa_sink", "hy_short", "hy_w1", "hy_b1",
        "hy_w2", "hy_b2", "hy_w3", "hy_freq", "hy_decay", "hy_bias", "w_branch", "w_out", "g_mlp", "w_up", "w_down", "g_final")}
    in_maps = []
    for i in range(8):
        m = dict(shared)
        m.update(consts)
        m["xc"] = np.ascontiguousarray(inputs["x_prompt"][4 * i:4 * i + 4].reshape(1024, D))
        m["xl"] = np.ascontiguousarray(inputs["x_sample"][i])
        m["cak"] = np.ascontiguousarray(inputs["cache_a_k"][i].reshape(2, 256, 512))
        m["cav"] = np.ascontiguousarray(inputs["cache_a_v"][i].reshape(2, 256, 512))
        m["cbk"] = np.ascontiguousarray(inputs["cache_b_k"][i].reshape(2, 256, 512))
        m["cbv"] = np.ascontiguousarray(inputs["cache_b_v"][i].reshape(2, 256, 512))
        m["cck"] = np.ascontiguousarray(inputs["cache_c_k"][i].reshape(2, 256, 128))
        m["ccv"] = np.ascontiguousarray(inputs["cache_c_v"][i].reshape(2, 256, 128))
        m["cvec"] = np.ascontiguousarray(np.stack([inputs["c_ctx"], inputs["c"][i]], axis=0))
        in_maps.append(m)
    res = run_bass_kernel_spmd(nc, in_maps, core_ids=list(range(8)))
    r = res.results
    y_prompt = np.concatenate([r[i]["y_c"].reshape(4, 256, D) for i in range(8)], axis=0)
    y_sample = np.stack([r[i]["y_l"] for i in range(8)], axis=0)
    def cat(name, shp):
        return np.concatenate([r[i][name] for i in range(8)], axis=0).reshape(shp)
    return (y_prompt.astype(np.float32), y_sample.astype(np.float32),
            cat("nak", (32, 2, 256, 2, 4, 64)), cat("nav", (32, 2, 256, 4, 128)),
            cat("nbk", (32, 2, 256, 8, 64)), cat("nbv", (32, 2, 256, 8, 64)),
            cat("nck", (32, 2, 256, 2, 64)), cat("ncv", (32, 2, 256, 2, 64)))
```

```python
import math
from contextlib import ExitStack, contextmanager
import numpy as np
import ml_dtypes
import concourse.bass as bass
import concourse.mybir as mybir
from concourse.bass_utils import run_bass_kernel_spmd

F32 = mybir.dt.float32
BF16 = mybir.dt.bfloat16
AF = mybir.ActivationFunctionType
ALU = mybir.AluOpType

SAME_ENG_SYNC = False


class Prog:
    ENGS = ("pe", "act", "dve", "pool", "sp")
    KROT = 8

    def __init__(self, nc):
        self.nc = nc
        self.ops = {e: [] for e in self.ENGS}
        self.res = {}
        self.ndma = {e: 0 for e in self.ENGS}
        self.dma_refs = {e: [] for e in self.ENGS}
        self._ps_i = 0
        self.uid = 0

    def name(self, base):
        self.uid += 1
        return f"{base}_{self.uid}"

    def _record(self, eng, fn, reads, writes, is_dma, hard=False):
        idx = len(self.ops[eng])
        ref = (eng, idx)
        deps = set()
        for k in reads:
            r = self.res.get(k)
            if r is not None and r[0] is not None:
                deps.add(r[0])
        for k in writes:
            r = self.res.get(k)
            if r is not None:
                if r[0] is not None:
                    deps.add(r[0])
                deps.update(r[1].values())
                deps.update(r[2])
        if not is_dma:
            keep = set()
            for d in deps:
                if d[0] == eng and not self.ops[eng][d[1]]["dma"]:
                    if (SAME_ENG_SYNC or hard) and eng != "pe":
                        keep.add(d)
                else:
                    keep.add(d)
            deps = keep
        op = dict(fn=fn, deps=deps, dma=is_dma, sig=False)
        if is_dma:
            op["dj"] = self.ndma[eng]
            self.ndma[eng] += 1
            self.dma_refs[eng].append(ref)
        self.ops[eng].append(op)
        for k in reads:
            r = self.res.setdefault(k, [None, {}, []])
            if is_dma:
                r[2].append(ref)
            else:
                r[1][eng] = ref
        for k in writes:
            self.res[k] = [ref, {}, []]
        return ref

    def op(self, eng, fn, reads=(), writes=(), hard=False):
        return self._record(eng, fn, tuple(reads), tuple(writes), False, hard)

    def dma(self, out, in_, reads=(), writes=(), q="sp", **kw):
        return self._record(q, lambda e: e.dma_start(out=out, in_=in_, **kw), tuple(reads), tuple(writes), True)

    def barrier(self):
        lasts = set()
        for e in self.ENGS:
            for i in range(len(self.ops[e]) - 1, -1, -1):
                o = self.ops[e][i]
                if o["fn"] is not None and not o["dma"]:
                    lasts.add((e, i))
                    break
            for ref in self.dma_refs[e][-self.KROT:]:
                lasts.add(ref)
        for e in self.ENGS:
            deps = set(d for d in lasts if not (d[0] == e and not self.ops[e][d[1]]["dma"]))
            self.ops[e].append(dict(fn=None, deps=deps, dma=False, sig=False))
        self.res = {}

    def emit(self):
        nc = self.nc
        with ExitStack() as es:
            esem = {e: es.enter_context(nc.semaphore(f"sem_{e}")) for e in self.ENGS}
            rsem = {e: [es.enter_context(nc.semaphore(f"rot_{e}_{j}")) for j in range(self.KROT)]
                    for e in self.ENGS if self.ndma[e] > 0}
            for e in self.ENGS:
                for o in self.ops[e]:
                    for d in o["deps"]:
                        self.ops[d[0]][d[1]]["sig"] = True
            for e in self.ENGS:
                c = 0
                for o in self.ops[e]:
                    if o["dma"]:
                        j = o["dj"]
                        o["done"] = ((e, "r", j % self.KROT), 16 * (j // self.KROT + 1))
                    elif o["sig"]:
                        c += 1
                        o["done"] = ((e, "c"), c)

            def semh(key):
                return esem[key[0]] if key[1] == "c" else rsem[key[0]][key[2]]

            def emit_eng(engobj, e):
                obs = {}
                for o in self.ops[e]:
                    waits = {}
                    for d in o["deps"]:
                        k, v = self.ops[d[0]][d[1]]["done"]
                        if obs.get(k, 0) < v:
                            waits[k] = max(waits.get(k, 0), v)
                    if o["dma"] and o["dj"] >= self.KROT:
                        j = o["dj"]
                        k, v = (e, "r", j % self.KROT), 16 * (j // self.KROT)
                        if obs.get(k, 0) < v:
                            waits[k] = max(waits.get(k, 0), v)
                    for k, v in waits.items():
                        engobj.wait_ge(semh(k), v)
                        obs[k] = v
                    if o["fn"] is None:
                        continue
                    ins = o["fn"](engobj)
                    if o["dma"]:
                        ins.then_inc(semh(o["done"][0]), 16)
                    elif o["sig"]:
                        ins.then_inc(esem[e], 1)

            with nc.Block() as block:
                block.tensor(lambda x: emit_eng(x, "pe"))
                block.scalar(lambda x: emit_eng(x, "act"))
                block.vector(lambda x: emit_eng(x, "dve"))
                block.gpsimd(lambda x: emit_eng(x, "pool"))
                block.sync(lambda x: emit_eng(x, "sp"))


D = 2048
NT = 2048
NTX = NT + 256
KC = 16
DEPTH = 2
IN_COLS = 13568
D_FF = 8192
OFF = dict(aq=0, ak=512, av=1024, bq=1536, bk=2048, bv=2560, cq=3072, ck=3584, cv=3712, hy=3840, gate=5376)
QK_ROW = dict(aq=0, ak=512, bq=1024, bk=1536, cq=2048, ck=2560)
NQK = 2688
V_COL = dict(av=0, bv=512, cv=1024)
NV = 1152
EPS = 1e-6


class Phase:
    def __init__(self, bld):
        self.b = bld
        self.es = ExitStack()

    def sb(self, name, shape, dt):
        return self.es.enter_context(self.b.nc.sbuf_tensor(self.b.P.name(name), list(shape), dt))

    def __enter__(self):
        return self

    def __exit__(self, *a):
        self.b.P.barrier()
        self.es.close()
        return False


class WStream:
    def __init__(self, bld, ph, units, nk, tag, nslots=2, nstage=3, kstage=4):
        self.b, self.units, self.nk, self.tag = bld, units, nk, tag
        self.nslots, self.nstage, self.kstage = nslots, nstage, kstage
        self.w = [ph.sb(f"w{tag}{s}", [128, nk, 512], BF16) for s in range(nslots)]
        self.st = [ph.sb(f"ws{tag}{s}", [128, kstage, 512], F32) for s in range(nstage)]
        self.sti = 0
        self.pieces = {}

    def fetch(self, i, npieces=None):
        if i >= len(self.units):
            return
        P = self.b.P
        src, ncols = self.units[i]
        slot = i % self.nslots
        total = self.nk // self.kstage
        done = self.pieces.get(i, 0)
        todo = total - done if npieces is None else min(npieces, total - done)
        for pc in range(done, done + todo):
            k0 = pc * self.kstage
            s = self.sti % self.nstage
            self.sti += 1
            st, w = self.st[s], self.w[slot]
            P.dma(st[:, :, :ncols], src(k0, self.kstage), writes=[("wst", self.tag, s)])
            P.op("pool", lambda e, st=st, w=w, k0=k0, ncols=ncols: e.tensor_copy(
                out=w[:, k0:k0 + self.kstage, :ncols], in_=st[:, :, :ncols]),
                reads=[("wst", self.tag, s)], writes=[("wbf", self.tag, slot)])
        self.pieces[i] = done + todo

    def get(self, i, prefetch=True):
        self.fetch(i)
        if prefetch:
            self.fetch(i + 1)
        return self.w[i % self.nslots], ("wbf", self.tag, i % self.nslots)


def wsrc(ap2d, c0, ncols):
    def f(k0, nk):
        return ap2d[k0 * 128:(k0 + nk) * 128, c0:c0 + ncols].rearrange("(k p) c -> p k c", p=128)
    return f


class Builder:
    def __init__(self, dbg=None):
        self.nc = nc = bass.Bass("TRN2", target_bir_lowering=False)
        self.P = Prog(nc)
        self.dbg = dbg or {}
        self.es = ExitStack()
        self.din = {}
        self.dout = {}
        self.psi = 0

    def inp(self, name, shape, dt=F32):
        self.din[name] = self.nc.dram_tensor(name, list(shape), dt, kind="ExternalInput").ap()
        return self.din[name]

    def outp(self, name, shape, dt=F32):
        self.dout[name] = self.nc.dram_tensor(name, list(shape), dt, kind="ExternalOutput").ap()
        return self.dout[name]

    def scr(self, name, shape, dt):
        if name in self.dbg:
            return self.outp(name, shape, dt)
        return self.nc.dram_tensor(name, list(shape), dt).ap()

    def gsb(self, name, shape, dt):
        return self.es.enter_context(self.nc.sbuf_tensor(name, list(shape), dt))

    def bank(self):
        i = self.psi % 8
        self.psi += 1
        return self.ps[i], ("ps", i)

    def phase(self):
        return Phase(self)

    def declare(self):
        inp, outp, scr = self.inp, self.outp, self.scr
        inp("xc", [1024, D]); inp("xl", [1024, D])
        inp("cak", [2, 256, 512]); inp("cav", [2, 256, 512]); inp("cbk", [2, 256, 512]); inp("cbv", [2, 256, 512])
        inp("cck", [2, 256, 128]); inp("ccv", [2, 256, 128])
        inp("cvec", [2, D])
        inp("w_ada", [2, D, 6 * D]); inp("b_ada", [2, 6 * D]); inp("g_mix", [2, D]); inp("w_in", [2, D, IN_COLS])
        inp("diff_lambda", [2, 4, 64]); inp("diff_norm_g", [2, 128]); inp("na_bias", [2, 8, 15, 31]); inp("swa_sink", [2, 8])
        inp("hy_short", [2, 3, 1536]); inp("hy_w1", [2, 33, 64]); inp("hy_b1", [2, 64]); inp("hy_w2", [2, 64, 64])
        inp("hy_b2", [2, 64]); inp("hy_w3", [2, 64, 2048]); inp("hy_freq", [2, 64]); inp("hy_decay", [2, 2048])
        inp("hy_bias", [2, 2, 512]); inp("w_branch", [2, 4, 512, D]); inp("w_out", [2, D, D]); inp("g_mlp", [2, D])
        inp("w_up", [2, D, D_FF]); inp("w_down", [2, D_FF, D]); inp("g_final", [D])
        inp("c_ident", [128, 128]); inp("c_identb", [128, 128], BF16); inp("c_onesb", [128, 128], BF16)
        inp("c_rperm", [128, 128]); inp("c_rcos", [128, 1024]); inp("c_rsin", [128, 1024])
        inp("c_m01", [128, 64]); inp("c_mng", [128, 64]); inp("c_cmask", [128, 2, 128], BF16)
        for L in (256, 1024):
            inp(f"c_z{L}", [33, L]); inp(f"c_nt{L}", [128, L // 128]); inp(f"c_F{L}", [L, 2 * L], BF16); inp(f"c_FT{L}", [2 * L, L], BF16)
        outp("y_c", [1024, D]); outp("y_l", [1024, D])
        outp("nak", [4, 2, 256, 512]); outp("nav", [4, 2, 256, 512]); outp("nbk", [4, 2, 256, 512]); outp("nbv", [4, 2, 256, 512])
        outp("nck", [4, 2, 256, 128]); outp("ncv", [4, 2, 256, 128])
        self.xT = scr("xT", [D, NT], F32)
        self.qkT = [scr(f"qkT{l}", [NQK, NTX], BF16) for l in range(2)]
        self.vtok = [scr(f"vtok{l}", [NTX, NV], BF16) for l in range(2)]
        self.hyT = scr("hyT", [1536, NT], F32)
        self.sgT = scr("sgT", [4 * D, NT], BF16)
        self.oT = scr("oT", [D, NT], BF16)
        self.mT = scr("mT", [D, NT], BF16)
        self.aT = scr("aT", [D_FF, NT], BF16)
        self.nbpad = [scr(f"nbpad{l}", [1, 64 + 3720 + 64], F32) for l in range(2)]

    def globals_(self):
        nc, P, g = self.nc, self.P, self.gsb
        self.ps = [self.es.enter_context(nc.psum_tensor(f"psb{i}", [128, 512], F32)) for i in range(8)]
        self.ident = g("ident", [128, 128], F32); self.identb = g("identb", [128, 128], BF16)
        self.onesb = g("onesb", [128, 128], BF16); self.rperm = g("rperm", [128, 128], F32)
        self.epst = g("epst", [128, 1], F32)
        self.mod = g("mod", [128, 2, 96, 2], F32)
        self.gm = g("gm", [128, 2, 2, 16, 2], F32)
        self.gfin = g("gfin", [128, 16], F32)
        self.zero16 = g("zero16", [128, 16], F32)
        d = self.din
        P.dma(self.ident[:], d["c_ident"], writes=["ident"])
        P.dma(self.identb[:], d["c_identb"], writes=["identb"])
        P.dma(self.onesb[:], d["c_onesb"], writes=["onesb"])
        P.dma(self.rperm[:], d["c_rperm"], writes=["rperm"])
        P.op("dve", lambda e: e.memset(self.epst[:], EPS), writes=["epst"])
        P.op("dve", lambda e: e.memset(self.zero16[:], 0.0), writes=["zero16"])

    def load_T(self, ph, dst, dst_keys, src2d, n):
        P = self.P
        tmp = ph.sb("ltT", [128, 128], F32)
        k = ("ltT", P.uid)
        P.dma(tmp[:n, :], src2d, writes=[k])
        pb, pk = self.bank()
        P.op("pe", lambda e: e.transpose(pb[:, :n], tmp[:n, :], self.ident[:n, :n]), reads=[k, "ident"], writes=[pk])
        P.op("dve", lambda e: e.tensor_copy(out=dst, in_=pb[:, :n]), reads=[pk], writes=dst_keys)

    def phase_xT(self):
        P, d = self.P, self.din
        with self.phase() as ph:
            xin = [ph.sb(f"xin{i}", [128, 4, D], F32) for i in range(2)]
            stg = [ph.sb(f"xst{i}", [128, 512], F32) for i in range(4)]
            si = 0
            for g4 in range(4):
                src = d["xc"] if g4 < 2 else d["xl"]
                r0 = (g4 % 2) * 512
                xt = xin[g4 % 2]
                xk = ("xin", g4 % 2)
                P.dma(xt[:], src[r0:r0 + 512, :].rearrange("(t p) f -> p t f", p=128), writes=[xk])
                for fc in range(KC):
                    pb, pk = self.bank()
                    for t in range(4):
                        P.op("pe", lambda e, pb=pb, xt=xt, t=t, fc=fc: e.transpose(
                            pb[:, t * 128:(t + 1) * 128], xt[:, t, fc * 128:(fc + 1) * 128], self.ident[:]),
                            reads=[xk, "ident"], writes=[pk])
                    s = stg[si % 4]; sk = ("xst", si % 4); si += 1
                    eng = "act" if fc % 2 == 0 else "dve"
                    if eng == "act":
                        P.op("act", lambda e, s=s, pb=pb: e.copy(out=s[:], in_=pb[:]), reads=[pk], writes=[sk])
                    else:
                        P.op("dve", lambda e, s=s, pb=pb: e.tensor_copy(out=s[:], in_=pb[:]), reads=[pk], writes=[sk])
                    P.dma(self.xT[fc * 128:(fc + 1) * 128, g4 * 512:(g4 + 1) * 512], s[:], reads=[sk], writes=[("xT", fc, g4)])

    def phase_adaln(self):
        P, d = self.P, self.din
        with self.phase() as ph:
            cT = ph.sb("cT", [128, 2, 16], F32)
            sT = ph.sb("sT", [128, 16, 2], F32)
            bada = ph.sb("bada", [128, 2, 96], F32)
            gT = ph.sb("gT", [128, 2, 2, 16], F32)
            for v in range(2):
                self.load_T(ph, cT[:, v, :], ["cT"], d["cvec"][v].rearrange("(k p) -> k p", p=128), 16)
            for v in range(2):
                P.op("act", lambda e, v=v: e.activation(out=sT[:, :, v], in_=cT[:, v, :], func=AF.Silu), reads=["cT"], writes=["sT"])
            for l in range(2):
                self.load_T(ph, bada[:, l, :], ["bada"], d["b_ada"][l].rearrange("(k p) -> k p", p=128), 96)
                self.load_T(ph, gT[:, l, 0, :], ["gT"], d["g_mix"][l].rearrange("(k p) -> k p", p=128), 16)
                self.load_T(ph, gT[:, l, 1, :], ["gT"], d["g_mlp"][l].rearrange("(k p) -> k p", p=128), 16)
            self.load_T(ph, self.gfin[:], ["gfin"], d["g_final"].rearrange("(k p) -> k p", p=128), 16)
            st = [ph.sb(f"adst{i}", [128, 4, 512], F32) for i in range(3)]
            m2 = [ph.sb(f"adm{i}", [2, 512], F32) for i in range(2)]
            si = 0
            for l in range(2):
                for cg in range(24):
                    pb, pk = self.bank()
                    for ks in range(4):
                        s = st[si % 3]; sk = ("adst", si % 3); si += 1
                        P.dma(s[:], d["w_ada"][l, ks * 512:(ks + 1) * 512, cg * 512:(cg + 1) * 512].rearrange("(k p) c -> p k c", p=128), writes=[sk])
                        for kk in range(4):
                            kc = ks * 4 + kk
                            P.op("pe", lambda e, pb=pb, s=s, kk=kk, kc=kc: e.matmul(
                                pb[0:2, :], lhsT=sT[:, kc, :], rhs=s[:, kk, :], start=(kc == 0), stop=(kc == 15)), reads=[sk, "sT"], writes=[pk])
                    m = m2[cg % 2]; mk = ("adm", cg % 2)
                    P.op("act", lambda e, m=m, pb=pb: e.copy(out=m[:], in_=pb[0:2, :]), reads=[pk], writes=[mk])
                    pb2, pk2 = self.bank()
                    for j in range(4):
                        P.op("pe", lambda e, pb2=pb2, m=m, j=j: e.transpose(pb2[:, 2 * j:2 * j + 2], m[0:2, j * 128:(j + 1) * 128], self.ident[0:2, 0:2]),
                             reads=[mk, "ident"], writes=[pk2])
                    for v in range(2):
                        P.op("dve", lambda e, pb2=pb2, l=l, cg=cg, v=v: e.tensor_tensor(
                            out=self.mod[:, l, cg * 4:(cg + 1) * 4, v], in0=pb2[:, v:8:2], in1=bada[:, l, cg * 4:(cg + 1) * 4], op=ALU.add),
                            reads=[pk2, "bada"], writes=["mod"])
            for l in range(2):
                for w in range(2):
                    sc0 = 16 if w == 0 else 64
                    for v in range(2):
                        P.op("dve", lambda e, l=l, w=w, v=v, sc0=sc0: e.scalar_tensor_tensor(
                            out=self.gm[:, l, w, :, v], in0=self.mod[:, l, sc0:sc0 + 16, v], scalar=1.0, in1=gT[:, l, w, :],
                            op0=ALU.add, op1=ALU.mult), reads=["mod", "gT"], writes=["gm"], hard=True)

    def modv(self, l, which, ft, grp):
        return self.mod[:, l, which * 16 + ft, grp:grp + 1]

    def norm_tiles(self, ph, scale_ap, shift_ap, emit_out):
        P = self.P
        xs = [ph.sb(f"nx{i}", [128, 16, 512], F32) for i in range(2)]
        sq = [ph.sb(f"nsq{i}", [128, 512], BF16) for i in range(4)]
        rs = [ph.sb(f"nrs{i}", [128, 512], F32) for i in range(2)]
        tmp = [ph.sb(f"ntmp{i}", [128, 512], F32) for i in range(4)]
        xTv = self.xT.rearrange("(k p) t -> p k t", p=128)
        ti = 0
        for tt in range(4):
            grp = 0 if tt < 2 else 1
            x = xs[tt % 2]; xk = ("nx", tt % 2)
            for h in range(2):
                P.dma(x[:, h * 8:(h + 1) * 8, :], xTv[:, h * 8:(h + 1) * 8, tt * 512:(tt + 1) * 512],
                      reads=[("xT", fc, tt) for fc in range(h * 8, h * 8 + 8)], writes=[(xk, h)])
            pb, pk = self.bank()
            for fc in range(KC):
                s = sq[fc % 4]; sk = ("nsq", fc % 4)
                P.op("act", lambda e, s=s, x=x, fc=fc: e.activation(out=s[:], in_=x[:, fc, :], func=AF.Square), reads=[(xk, fc // 8)], writes=[sk])
                P.op("pe", lambda e, pb=pb, s=s, fc=fc: e.matmul(pb[:], lhsT=self.onesb[:], rhs=s[:], start=(fc == 0), stop=(fc == 15)),
                     reads=[sk, "onesb"], writes=[pk])
            r = rs[tt % 2]; rk = ("nrs", tt % 2)
            self.act_pow(r[:], pb[:], -0.5, [pk], [rk], scale=1.0 / D, bias=self.epst[:, 0:1])
            for fc in range(KC):
                t = tmp[ti % 4]; tk = ("ntmp", ti % 4); ti += 1
                P.op("dve", lambda e, t=t, x=x, fc=fc, r=r, grp=grp: e.scalar_tensor_tensor(
                    out=t[:], in0=x[:, fc, :], scalar=scale_ap(fc, grp), in1=r[:], op0=ALU.mult, op1=ALU.mult),
                    reads=[(xk, fc // 8), rk, "gm", "gfin"], writes=[tk])
                emit_out(tt, fc, grp, t, tk)

    def phase_norm_h(self, ph, l, which):
        P = self.P
        hT = self.hT

        def out(tt, fc, grp, t, tk):
            P.op("act", lambda e: e.activation(out=hT[:, fc, tt * 512:(tt + 1) * 512], in_=t[:], func=AF.Identity,
                                                bias=self.modv(l, 0 if which == 0 else 3, fc, grp), scale=1.0),
                 reads=[tk, "mod"], writes=[("hT", fc, tt)])
        with self.phase() as p2:
            self.norm_tiles(p2, lambda fc, grp: self.gm[:, l, which, fc, grp:grp + 1], None, out)

    def mm_group(self, pb, pk, lhs_fn, rhs_fn, nk, reads):
        P = self.P
        for kc in range(nk):
            la, ra = lhs_fn(kc), rhs_fn(kc)
            P.op("pe", lambda e, kc=kc, la=la, ra=ra, pb=pb: e.matmul(pb[:, 0:ra.shape[-1]], lhsT=la, rhs=ra, start=(kc == 0), stop=(kc == nk - 1)),
                 reads=reads(kc), writes=[pk])

    def inproj_units(self):
        units = []
        for nm in ("aq", "ak", "av", "bq", "bk", "bv", "cq"):
            units.append((OFF[nm], 512, nm))
        units.append((OFF["ck"], 256, "ckv"))
        for i in range(3):
            units.append((OFF["hy"] + i * 512, 512, ("hy", i)))
        for i in range(16):
            units.append((OFF["gate"] + i * 512, 512, ("gate", i)))
        return units

    def inproj_ws(self, ph, l):
        win = self.din["w_in"][l]
        ws = WStream(self, ph, [(wsrc(win, c0, nc_), nc_) for (c0, nc_, _) in self.inproj_units()], 16, "in")
        ws.fetch(0)
        return ws

    def up_ws(self, ph, l):
        ws = WStream(self, ph, [(wsrc(self.din["w_up"][l], cg * 512, 512), 512) for cg in range(16)], 16, "wu")
        ws.fetch(0)
        return ws

    def phase_inproj(self, l, ws):
        P, d = self.P, self.din
        hT = self.hT
        with self.phase() as ph:
            units = self.inproj_units()
            stb = [ph.sb(f"ipb{i}", [128, NT], BF16) for i in range(3)]
            stf = [ph.sb(f"ipf{i}", [128, NT], F32) for i in range(2)]
            sta = [ph.sb(f"ipa{i}", [128, 512], F32) for i in range(3)]
            stab = [ph.sb(f"ipab{i}", [128, 512], BF16) for i in range(3)]
            r32 = [ph.sb(f"ipr{i}", [128, 512], F32) for i in range(2)]
            t1 = [ph.sb(f"ipt{i}", [128, 512], F32) for i in range(2)]
            t2 = [ph.sb(f"ipu{i}", [128, 512], F32) for i in range(2)]
            rcos = ph.sb("rcos", [128, 1024], F32); rsin = ph.sb("rsin", [128, 1024], F32)
            P.dma(rcos[:], d["c_rcos"], writes=["rcos"]); P.dma(rsin[:], d["c_rsin"], writes=["rsin"])
            cnt = dict(b=0, f=0, a=0, r=0)
            ev = [0]
            for ui, (c0, ncols, kind) in enumerate(units):
                w, wk = ws.get(ui)
                kname = kind if isinstance(kind, str) else kind[0]
                fm_tiles = []
                if kname in ("aq", "ak", "bq", "bk", "cq"):
                    fm_tiles = [(j, "qk", QK_ROW[kname] + j * 128) for j in range(4)]
                elif kname == "ckv":
                    fm_tiles = [(0, "qk", QK_ROW["ck"])]
                elif kname == "hy":
                    fm_tiles = [(j, "hy", kind[1] * 512 + j * 128) for j in range(4)]
                elif kname == "gate":
                    fm_tiles = [(j, "gate", kind[1] * 512 + j * 128) for j in range(4)]
                roped = kname in ("aq", "ak", "cq", "ckv")
                for (j, okind, row0) in fm_tiles:
                    if okind == "hy":
                        st = stf[cnt["f"] % 2]; stk = ("ipf", cnt["f"] % 2); cnt["f"] += 1
                    else:
                        st = stb[cnt["b"] % 3]; stk = ("ipb", cnt["b"] % 3); cnt["b"] += 1
                    for tt in range(4):
                        pb, pk = self.bank()
                        self.mm_group(pb, pk, lambda kc: w[:, kc, j * 128:(j + 1) * 128], lambda kc: hT[:, kc, tt * 512:(tt + 1) * 512], 16,
                                      lambda kc: [wk, ("hT", kc, tt)])
                        o = st[:, tt * 512:(tt + 1) * 512]
                        if okind == "gate":
                            P.op("act", lambda e, o=o, pb=pb: e.activation(out=o, in_=pb[:], func=AF.Sigmoid), reads=[pk], writes=[stk])
                        elif okind == "qk" and roped and tt >= 2:
                            i = cnt["r"] % 2; cnt["r"] += 1
                            r, rk = r32[i], ("ipr", i)
                            a1, a1k = t1[i], ("ipt", i)
                            a2, a2k = t2[i], ("ipu", i)
                            tp = (tt - 2) * 512
                            P.op("act", lambda e, r=r, pb=pb: e.copy(out=r[:], in_=pb[:]), reads=[pk], writes=[rk])
                            pb2, pk2 = self.bank()
                            P.op("pe", lambda e, pb2=pb2, r=r: e.matmul(pb2[:], lhsT=self.rperm[:], rhs=r[:], start=True, stop=True),
                                 reads=[rk, "rperm"], writes=[pk2])
                            P.op("dve", lambda e, a1=a1, r=r, tp=tp: e.tensor_tensor(out=a1[:], in0=r[:], in1=rcos[:, tp:tp + 512], op=ALU.mult),
                                 reads=[rk, "rcos"], writes=[a1k])
                            P.op("dve", lambda e, a2=a2, pb2=pb2, tp=tp: e.tensor_tensor(out=a2[:], in0=pb2[:], in1=rsin[:, tp:tp + 512], op=ALU.mult),
                                 reads=[pk2, "rsin"], writes=[a2k])
                            P.op("pool", lambda e, o=o, a1=a1, a2=a2: e.tensor_tensor(out=o, in0=a1[:], in1=a2[:], op=ALU.add),
                                 reads=[a1k, a2k], writes=[stk])
                        else:
                            ev[0] += 1
                            if ev[0] % 2 == 0:
                                P.op("act", lambda e, o=o, pb=pb: e.copy(out=o, in_=pb[:]), reads=[pk], writes=[stk])
                            else:
                                P.op("dve", lambda e, o=o, pb=pb: e.tensor_copy(out=o, in_=pb[:]), reads=[pk], writes=[stk])
                    if okind == "qk":
                        dst = self.qkT[l][row0:row0 + 128, 0:NT]; dk = ("qkT", row0 // 128)
                    elif okind == "hy":
                        dst = self.hyT[row0:row0 + 128, :]; dk = ("hyT", row0 // 128)
                    else:
                        dst = self.sgT[row0:row0 + 128, :]; dk = ("sgT", row0 // 128)
                    P.dma(dst, st[:], reads=[stk], writes=[dk])
                if kname in ("ak", "av", "bk", "bv", "ckv"):
                    isv = kname in ("av", "bv", "ckv")
                    ntile = 16 if isv else 8
                    for t128 in range(ntile):
                        pb, pk = self.bank()
                        self.mm_group(pb, pk, lambda kc: hT[:, kc, t128 * 128:(t128 + 1) * 128], lambda kc: w[:, kc, 0:ncols], 16,
                                      lambda kc: [wk, ("hT", kc, t128 // 4)])
                        pbv = pb[:, 0:ncols]
                        if t128 < 8:
                            i = cnt["a"] % 3; cnt["a"] += 1
                            s, sk = sta[i], ("ipa", i)
                            P.op("act", lambda e, s=s, pbv=pbv, ncols=ncols: e.copy(out=s[:, 0:ncols], in_=pbv), reads=[pk], writes=[sk])
                            sq_, pos0 = t128 // 2, (t128 % 2) * 128
                            if kname == "ckv":
                                P.dma(self.dout["nck"][sq_, l, pos0:pos0 + 128, :], s[:, 0:128], reads=[sk])
                                P.dma(self.dout["ncv"][sq_, l, pos0:pos0 + 128, :], s[:, 128:256], reads=[sk])
                            else:
                                P.dma(self.dout["n" + kname][sq_, l, pos0:pos0 + 128, :], s[:, 0:512], reads=[sk])
                        if isv:
                            i = cnt["a"] % 3; cnt["a"] += 1
                            s2, s2k = stab[i], ("ipab", i)
                            if kname == "ckv":
                                P.op("act", lambda e, s2=s2, pb=pb: e.copy(out=s2[:, 0:128], in_=pb[:, 128:256]), reads=[pk], writes=[s2k])
                                P.dma(self.vtok[l][t128 * 128:(t128 + 1) * 128, V_COL["cv"]:V_COL["cv"] + 128], s2[:, 0:128], reads=[s2k],
                                      writes=[("vtok", "cv", t128)])
                            else:
                                P.op("act", lambda e, s2=s2, pb=pb: e.copy(out=s2[:], in_=pb[:]), reads=[pk], writes=[s2k])
                                P.dma(self.vtok[l][t128 * 128:(t128 + 1) * 128, V_COL[kname]:V_COL[kname] + 512], s2[:], reads=[s2k],
                                      writes=[("vtok", kname, t128)])

    def phase_merge(self, l):
        P, d = self.P, self.din
        with self.phase() as ph:
            oT = ph.sb("oTr", [128, 16, NT], BF16)
            oTv = self.oT.rearrange("(k p) t -> p k t", p=128)
            for n in range(4):
                P.dma(oT[:, n * 4:(n + 1) * 4, :], oTv[:, n * 4:(n + 1) * 4, :], reads=[("oT", n)], writes=[("oTr", n)])
            acc = ph.sb("macc", [128, 4, NT], F32)
            units = []
            for fg in range(4):
                for n in range(4):
                    units.append((wsrc(d["w_branch"][l, n], fg * 512, 512), 512))
            ws = WStream(self, ph, units, 4, "wb")
            sg = [ph.sb(f"msg{i}", [128, NT], BF16) for i in range(3)]
            tmp = [ph.sb(f"mtmp{i}", [128, 512], F32) for i in range(4)]
            mst = [ph.sb(f"mst{i}", [128, NT], BF16) for i in range(2)]
            sgi = 0; ti = 0; mi = 0
            for fg in range(4):
                for n in range(4):
                    w, wk = ws.get(fg * 4 + n)
                    for j in range(4):
                        ft = fg * 4 + j
                        s = sg[sgi % 3]; sk = ("msg", sgi % 3); sgi += 1
                        P.dma(s[:], self.sgT[n * D + ft * 128:n * D + (ft + 1) * 128, :], reads=[("sgT", (n * D + ft * 128) // 128)], writes=[sk])
                        if n == 3:
                            ms = mst[mi % 2]; msk = ("mst", mi % 2); mi += 1
                        for tt in range(4):
                            pb, pk = self.bank()
                            self.mm_group(pb, pk, lambda kc: w[:, kc, j * 128:(j + 1) * 128], lambda kc: oT[:, n * 4 + kc, tt * 512:(tt + 1) * 512], 4,
                                          lambda kc: [wk, ("oTr", n)])
                            a = acc[:, j, tt * 512:(tt + 1) * 512]; ak = ("macc", j, tt)
                            ss = s[:, tt * 512:(tt + 1) * 512]
                            if n == 0:
                                P.op("dve", lambda e, a=a, pb=pb, ss=ss: e.tensor_tensor(out=a, in0=pb[:], in1=ss, op=ALU.mult), reads=[pk, sk], writes=[ak])
                            else:
                                t = tmp[ti % 4]; tk = ("mtmp", ti % 4); ti += 1
                                P.op("dve", lambda e, t=t, pb=pb, ss=ss: e.tensor_tensor(out=t[:], in0=pb[:], in1=ss, op=ALU.mult), reads=[pk, sk], writes=[tk])
                                if n < 3:
                                    P.op("pool", lambda e, a=a, t=t: e.tensor_tensor(out=a, in0=a, in1=t[:], op=ALU.add), reads=[ak, tk], writes=[ak])
                                else:
                                    mo = ms[:, tt * 512:(tt + 1) * 512]
                                    P.op("pool", lambda e, mo=mo, a=a, t=t: e.tensor_tensor(out=mo, in0=a, in1=t[:], op=ALU.add), reads=[ak, tk], writes=[msk])
                        if n == 3:
                            P.dma(self.mT[ft * 128:(ft + 1) * 128, :], ms[:], reads=[msk], writes=[("mT", ft)])

    def resid_update(self, l, gwhich, ft, xo, xok, banks, tts):
        P = self.P
        for (pb, pk), tt in zip(banks, tts):
            grp = 0 if tt < 2 else 1
            xs = xo[:, tt * 512:(tt + 1) * 512]
            P.op("dve", lambda e, xs=xs, pb=pb, grp=grp: e.scalar_tensor_tensor(
                out=xs, in0=pb[:], scalar=self.modv(l, gwhich, ft, grp), in1=xs, op0=ALU.mult, op1=ALU.add),
                reads=[pk, (xok, tt), "mod"], writes=[(xok, tt)])

    def phase_wout(self, l):
        P, d = self.P, self.din
        with self.phase() as ph:
            mT = ph.sb("mTr", [128, 16, NT], BF16)
            mTv = self.mT.rearrange("(k p) t -> p k t", p=128)
            for q in range(4):
                P.dma(mT[:, :, q * 512:(q + 1) * 512], mTv[:, :, q * 512:(q + 1) * 512], reads=[("mT", f) for f in range(16)], writes=[("mTr", q)])
            ws = WStream(self, ph, [(wsrc(d["w_out"][l], cg * 512, 512), 512) for cg in range(4)], 16, "wo")
            xo_ = [ph.sb(f"xo{i}", [128, NT], F32) for i in range(3)]
            xi = 0
            for cg in range(4):
                w, wk = ws.get(cg)
                for j in range(4):
                    ft = cg * 4 + j
                    xo = xo_[xi % 3]; xok = ("xo", xi % 3); xi += 1
                    P.dma(xo[:], self.xT[ft * 128:(ft + 1) * 128, :], reads=[("xT", ft, t) for t in range(4)], writes=[(xok, t) for t in range(4)])
                    banks = []
                    for tt in range(4):
                        pb, pk = self.bank()
                        self.mm_group(pb, pk, lambda kc: w[:, kc, j * 128:(j + 1) * 128], lambda kc: mT[:, kc, tt * 512:(tt + 1) * 512], 16,
                                      lambda kc: [wk, ("mTr", tt)])
                        banks.append((pb, pk))
                    self.resid_update(l, 2, ft, xo, xok, banks, range(4))
                    P.dma(self.xT[ft * 128:(ft + 1) * 128, :], xo[:], reads=[(xok, t) for t in range(4)], writes=[("xT", ft, t) for t in range(4)])

    def phase_up(self, l, ws):
        P, d = self.P, self.din
        hT = self.hT
        with self.phase() as ph:
            ast = [ph.sb(f"ast{i}", [128, NT], BF16) for i in range(3)]
            rl = [ph.sb(f"url{i}", [128, 512], F32) for i in range(4)]
            ai = 0; ri = 0
            for cg in range(16):
                w, wk = ws.get(cg)
                for j in range(4):
                    ft = cg * 4 + j
                    a = ast[ai % 3]; ak = ("ast", ai % 3); ai += 1
                    for tt in range(4):
                        pb, pk = self.bank()
                        self.mm_group(pb, pk, lambda kc: w[:, kc, j * 128:(j + 1) * 128], lambda kc: hT[:, kc, tt * 512:(tt + 1) * 512], 16,
                                      lambda kc: [wk, ("hT", kc, tt)])
                        r = rl[ri % 4]; rk = ("url", ri % 4); ri += 1
                        P.op("dve", lambda e, r=r, pb=pb: e.tensor_scalar(out=r[:], in0=pb[:], scalar1=0.0, scalar2=None, op0=ALU.max), reads=[pk], writes=[rk])
                        P.op("act", lambda e, a=a, r=r, tt=tt: e.activation(out=a[:, tt * 512:(tt + 1) * 512], in_=r[:], func=AF.Square), reads=[rk], writes=[ak])
                    P.dma(self.aT[ft * 128:(ft + 1) * 128, :], a[:], reads=[ak], writes=[("aT", ft)])

    def phase_down(self, l):
        P, d = self.P, self.din
        with self.phase() as ph:
            ws = WStream(self, ph, [(wsrc(d["w_down"][l], g * 512, 512), 512) for g in range(4)], 64, "wd")
            at_ = [ph.sb(f"dat{i}", [128, 1024], BF16) for i in range(6)]
            xo_ = [ph.sb(f"dxo{i}", [128, 1024], F32) for i in range(4)]
            ati = 0; xi = 0
            for g in range(4):
                w, wk = ws.get(g, prefetch=False)
                for pair in range(2):
                    banks = [[self.bank() for t2 in range(2)] for j in range(4)]
                    for kc in range(64):
                        if kc % 8 == 4:
                            ws.fetch(g + 1, 1)
                        at = at_[ati % 6]; atk = ("dat", ati % 6); ati += 1
                        P.dma(at[:], self.aT[kc * 128:(kc + 1) * 128, pair * 1024:(pair + 1) * 1024], reads=[("aT", kc)], writes=[atk])
                        for j in range(4):
                            for t2 in range(2):
                                pb, pk = banks[j][t2]
                                P.op("pe", lambda e, pb=pb, w=w, kc=kc, j=j, at=at, t2=t2: e.matmul(
                                    pb[:], lhsT=w[:, kc, j * 128:(j + 1) * 128], rhs=at[:, t2 * 512:(t2 + 1) * 512], start=(kc == 0), stop=(kc == 63)),
                                    reads=[wk, atk], writes=[pk])
                    for j in range(4):
                        ft = g * 4 + j
                        xo = xo_[xi % 4]; xok = ("dxo", xi % 4); xi += 1
                        tts = [pair * 2, pair * 2 + 1]
                        P.dma(xo[:], self.xT[ft * 128:(ft + 1) * 128, pair * 1024:(pair + 1) * 1024], reads=[("xT", ft, t) for t in tts],
                              writes=[(xok, t) for t in tts])
                        for t2 in range(2):
                            pb, pk = banks[j][t2]
                            tt = tts[t2]
                            grp = 0 if tt < 2 else 1
                            xs = xo[:, t2 * 512:(t2 + 1) * 512]
                            P.op("dve", lambda e, xs=xs, pb=pb, grp=grp, ft=ft: e.scalar_tensor_tensor(
                                out=xs, in0=pb[:], scalar=self.modv(l, 5, ft, grp), in1=xs, op0=ALU.mult, op1=ALU.add),
                                reads=[pk, (xok, tt), "mod"], writes=[(xok, tt)])
                        P.dma(self.xT[ft * 128:(ft + 1) * 128, pair * 1024:(pair + 1) * 1024], xo[:], reads=[(xok, t) for t in tts],
                              writes=[("xT", ft, t) for t in tts])

    def phase_final(self):
        P = self.P
        with self.phase() as ph:
            yst = [ph.sb(f"yst{i}", [128, D], F32) for i in range(2)]
            yT = [ph.sb(f"yT{i}", [128, 16, 512], F32) for i in range(2)]
            cnt = [0]

            def out(tt, fc, grp, t, tk):
                y = yT[tt % 2]
                P.op("act", lambda e: e.copy(out=y[:, fc, :], in_=t[:]), reads=[tk], writes=[("yT", tt % 2, fc)])
                if fc == 15:
                    for t128 in range(4):
                        ys = yst[cnt[0] % 2]; ysk = ("yst", cnt[0] % 2); cnt[0] += 1
                        for q in range(4):
                            pb, pk = self.bank()
                            for f4 in range(4):
                                f = q * 4 + f4
                                P.op("pe", lambda e, pb=pb, y=y, f=f, f4=f4, t128=t128: e.transpose(
                                    pb[:, f4 * 128:(f4 + 1) * 128], y[:, f, t128 * 128:(t128 + 1) * 128], self.ident[:]),
                                    reads=[("yT", tt % 2, f), "ident"], writes=[pk])
                            if q % 2 == 0:
                                P.op("act", lambda e, ys=ys, pb=pb, q=q: e.copy(out=ys[:, q * 512:(q + 1) * 512], in_=pb[:]), reads=[pk], writes=[ysk])
                            else:
                                P.op("dve", lambda e, ys=ys, pb=pb, q=q: e.tensor_copy(out=ys[:, q * 512:(q + 1) * 512], in_=pb[:]), reads=[pk], writes=[ysk])
                        tok = tt * 512 + t128 * 128
                        dst = self.dout["y_c"][tok:tok + 128, :] if tok < 1024 else self.dout["y_l"][tok - 1024:tok - 1024 + 128, :]
                        P.dma(dst, ys[:], reads=[ysk])
            self.norm_tiles(ph, lambda fc, grp: self.gfin[:, fc:fc + 1], None, out)

    def build(self, stages=None):
        def on(name):
            return stages is None or name in stages
        self.declare()
        self.globals_()
        self.P.barrier()
        if on("xT"):
            self.phase_xT()
        if on("adaln"):
            self.phase_adaln()
        if on("mixers"):
            self.phase_cache()
            self.phase_lam()
        for l in range(DEPTH):
            if stages is not None and f"L{l}" not in stages:
                continue
            with self.phase() as ph:
                self.hT = ph.sb("hT", [128, 16, NT], BF16)
                wsi = self.inproj_ws(ph, l)
                if on("norm1"):
                    self.phase_norm_h(ph, l, 0)
                if "hT" in self.dbg and l == 0:
                    self.P.dma(self.outp("hT", [128, 16, NT], BF16), self.hT[:], reads=[("hT", fc, tt) for fc in range(16) for tt in range(4)])
                if on("inproj"):
                    self.phase_inproj(l, wsi)
            if on("mixers"):
                self.phase_mixers(l)
            if on("merge"):
                self.phase_merge(l)
            if on("wout"):
                self.phase_wout(l)
            with self.phase() as ph:
                self.hT = ph.sb("hT", [128, 16, NT], BF16)
                wsu = self.up_ws(ph, l)
                if on("norm2"):
                    self.phase_norm_h(ph, l, 1)
                if on("up"):
                    self.phase_up(l, wsu)
            if on("down"):
                self.phase_down(l)
        if on("final"):
            self.phase_final()
        if "mod" in self.dbg:
            self.P.dma(self.outp("modo", [128, 2 * 96 * 2], F32), self.mod[:].rearrange("p a b c -> p (a b c)"), reads=["mod"])
        self.P.barrier()
        self.P.emit()
        self.es.close()
        return self.nc

    def bank_i(self, i):
        return self.ps[i], ("ps", i)

    def phase_cache(self):
        P, d = self.P, self.din
        with self.phase() as ph:
            ci = 0
            for l in range(2):
                for (kn, vn, w, krow, vcol) in (("cak", "cav", 512, QK_ROW["ak"], V_COL["av"]), ("cbk", "cbv", 512, QK_ROW["bk"], V_COL["bv"]),
                                               ("cck", "ccv", 128, QK_ROW["ck"], V_COL["cv"])):
                    kf = ph.sb("ckf", [128, 2, 512], F32); kb = ph.sb("ckb", [128, 2, 512], BF16)
                    vf = ph.sb("cvf", [128, 2, 512], F32); vb = ph.sb("cvb", [128, 2, 512], BF16)
                    kT = ph.sb("ckT", [128, 4, 256], BF16)
                    u = P.uid; P.uid += 1
                    P.dma(kf[:, :, :w], d[kn][l].rearrange("(c p) f -> p c f", p=128), writes=[("ckf", u)])
                    P.dma(vf[:, :, :w], d[vn][l].rearrange("(c p) f -> p c f", p=128), writes=[("cvf", u)])
                    P.op("dve", lambda e, kb=kb, kf=kf, w=w: e.tensor_copy(out=kb[:, :, :w], in_=kf[:, :, :w]), reads=[("ckf", u)], writes=[("ckb", u)])
                    P.op("pool", lambda e, vb=vb, vf=vf, w=w: e.tensor_copy(out=vb[:, :, :w], in_=vf[:, :, :w]), reads=[("cvf", u)], writes=[("cvb", u)])
                    P.dma(self.vtok[l][NT:NTX, vcol:vcol + w].rearrange("(c p) f -> p c f", p=128), vb[:, :, :w], reads=[("cvb", u)], writes=[("vtokc", l, vn)])
                    for fb in range(w // 128):
                        pb, pk = self.bank()
                        pbb = pb.bitcast(BF16)
                        for c in range(2):
                            P.op("pe", lambda e, pbb=pbb, kb=kb, c=c, fb=fb: e.transpose(pbb[:, c * 128:(c + 1) * 128], kb[:, c, fb * 128:(fb + 1) * 128], self.identb[:]),
                                 reads=[("ckb", u), "identb"], writes=[pk])
                        P.op("act", lambda e, kT=kT, pbb=pbb, fb=fb: e.copy(out=kT[:, fb, :], in_=pbb[:, 0:256]), reads=[pk], writes=[("ckT", u, fb)])
                        P.dma(self.qkT[l][krow + fb * 128:krow + (fb + 1) * 128, NT:NTX], kT[:, fb, :], reads=[("ckT", u, fb)], writes=[("qkTc", l, kn, fb)])

    def phase_lam(self):
        P, d = self.P, self.din
        self.lam = self.gsb("lam", [128, 2], F32)
        self.gsc = self.gsb("gsc", [128, 2], F32)
        self.esink = self.gsb("esink", [128, 2, 8], F32)
        with self.phase() as ph:
            dl = ph.sb("dl", [128, 2, 4, 64], F32)
            pr = ph.sb("dlp", [128, 2, 2, 64], F32)
            sm = ph.sb("dls", [128, 2, 2], F32)
            ex = ph.sb("dle", [128, 2, 2], F32)
            gn = ph.sb("dgn", [128, 2], F32)
            sk = ph.sb("ssk", [128, 2, 8], F32)
            for l in range(2):
                P.dma(dl[:, l], d["diff_lambda"][l].partition_broadcast(128), writes=["dl"])
                P.dma(gn[:, l:l + 1], d["diff_norm_g"][l].rearrange("(p o) -> p o", o=1), writes=["dgn"])
                P.dma(sk[:, l], d["swa_sink"][l].partition_broadcast(128), writes=["ssk"])
            for l in range(2):
                P.op("dve", lambda e, l=l: e.tensor_tensor(out=pr[:, l], in0=dl[:, l, 0:4:2, :], in1=dl[:, l, 1:4:2, :], op=ALU.mult), reads=["dl"], writes=["dlp"])
                P.op("dve", lambda e, l=l: e.tensor_reduce(out=sm[:, l], in_=pr[:, l], axis=mybir.AxisListType.X, op=ALU.add), reads=["dlp"], writes=["dls"], hard=True)
            P.op("act", lambda e: e.activation(out=ex[:], in_=sm[:], func=AF.Exp), reads=["dls"], writes=["dle"])
            P.op("act", lambda e: e.activation(out=self.esink[:], in_=sk[:], func=AF.Exp), reads=["ssk"], writes=["esink"])
            for l in range(2):
                lam_init = 0.8 - 0.6 * math.exp(-0.3 * l)
                P.op("dve", lambda e, l=l, lam_init=lam_init: e.scalar_tensor_tensor(
                    out=self.lam[:, l:l + 1], in0=ex[:, l, 0:1], scalar=lam_init, in1=ex[:, l, 1:2], op0=ALU.add, op1=ALU.subtract),
                    reads=["dle"], writes=["lam"])
                P.op("dve", lambda e, l=l, lam_init=lam_init: e.tensor_scalar(
                    out=self.gsc[:, l:l + 1], in0=gn[:, l:l + 1], scalar1=1.0 - lam_init, scalar2=None, op0=ALU.mult), reads=["dgn"], writes=["gsc"])

    def softmax_accum(self, bufs, acc, ncols, dvp, chunks, look=2):
        P = self.P
        (bo, bok), (bd, bdk) = acc
        nch = len(chunks)

        def stage_s(ci):
            ch = chunks[ci]
            bs, bsk = self.bank_i(4 + self.sbank % 4); self.sbank += 1
            ns = len(ch["s"])
            for mi, (c0, n, la, ra) in enumerate(ch["s"]):
                P.op("pe", lambda e, bs=bs, c0=c0, n=n, la=la, ra=ra, mi=mi, ns=ns: e.matmul(bs[:, c0:c0 + n], lhsT=la, rhs=ra, start=(mi == 0), stop=(mi == ns - 1),
                                                                                 skip_group_check=True), reads=ch["sreads"], writes=[bsk])
            i = bufs["pti"] % len(bufs["pt"]); bufs["pti"] += 1
            pt, ptk = bufs["pt"][i], ("pt", i)
            if ch.get("bias") is not None:
                j = bufs["tmi"] % len(bufs["tm"]); bufs["tmi"] += 1
                tm, tmk = bufs["tm"][j], ("ptm", j)
                P.op("dve", lambda e, tm=tm, bs=bs, b=ch["bias"]: e.scalar_tensor_tensor(out=tm[:, :ncols], in0=bs[:, :ncols], scalar=0.125, in1=b, op0=ALU.mult, op1=ALU.add),
                     reads=[bsk] + ch["breads"], writes=[tmk])
                P.op("act", lambda e, pt=pt, tm=tm: e.activation(out=pt[:, :ncols], in_=tm[:, :ncols], func=AF.Exp), reads=[tmk], writes=[ptk])
            else:
                P.op("act", lambda e, pt=pt, bs=bs: e.activation(out=pt[:, :ncols], in_=bs[:, :ncols], func=AF.Exp, scale=0.125), reads=[bsk], writes=[ptk])
            if ch.get("mask") is not None:
                mk = ch["mask"]
                P.op("pool", lambda e, pt=pt, mk=mk: e.tensor_tensor(out=pt[:, :ncols].rearrange("p (r q) -> p r q", q=128), in0=pt[:, :ncols].rearrange("p (r q) -> p r q", q=128),
                                                                   in1=mk, op=ALU.mult), reads=[ptk, "cmask"], writes=[ptk])
            return pt, ptk

        def stage_pv(ci, pt, ptk):
            ch = chunks[ci]
            for mi, (c0, n, va) in enumerate(ch["pv"]):
                P.op("pe", lambda e, c0=c0, n=n, va=va, pt=pt, ci=ci, mi=mi: e.matmul(
                    bo[0:dvp, c0:c0 + n], lhsT=va, rhs=pt[:, c0:c0 + n], start=(ci == 0 and mi == 0), stop=(ci == nch - 1), skip_group_check=True),
                    reads=[ptk] + ch["vreads"], writes=[bok])
            P.op("pe", lambda e, pt=pt, ci=ci: e.matmul(bd[0:dvp, 0:ncols], lhsT=self.onesb[:, 0:dvp], rhs=pt[:, :ncols], start=(ci == 0), stop=(ci == nch - 1)),
                 reads=[ptk, "onesb"], writes=[bdk])

        pend = {}
        for ci in range(nch):
            pend[ci] = stage_s(ci)
            if ci >= look:
                stage_pv(ci - look, *pend.pop(ci - look))
        for ci in sorted(pend):
            stage_pv(ci, *pend[ci])

    def act_pow(self, out, in_, power, reads, writes, scale=1.0, bias=None, w2=None):
        P = self.P
        if bias is None:
            P.op("act", lambda e: e.activation(out=out, in_=in_, func=AF.Ln, scale=scale), reads=reads, writes=writes)
        else:
            P.op("act", lambda e: e.activation(out=out, in_=in_, func=AF.Ln, scale=scale, bias=bias), reads=reads + ["epst"], writes=writes)
        P.op("act", lambda e: e.activation(out=out, in_=out, func=AF.Exp, scale=power), reads=writes, writes=writes)

    def attn_bufs(self, ph):
        return dict(pt=[ph.sb(f"pt{i}", [128, 512], BF16) for i in range(4)], pti=0,
                    tm=[ph.sb(f"ptm{i}", [128, 512], F32) for i in range(3)], tmi=0)

    def load_qkv(self, ph, l, qrow, nqh, krow, nkh, vcol, vw, tag):
        P = self.P
        Q = ph.sb("Q" + tag, [64, nqh, NT], BF16); Kt = ph.sb("K" + tag, [64, nkh, NTX], BF16); V = ph.sb("V" + tag, [128, 18, vw], BF16)
        qr = [("qkT", (qrow // 128) + i) for i in range((nqh * 64 + 127) // 128)]
        kr = [("qkT", (krow // 128) + i) for i in range((nkh * 64 + 127) // 128)]
        nm = {0: "av", 512: "bv", 1024: "cv"}[vcol]
        for h0 in range(0, nqh, 4):
            P.dma(Q[:, h0:h0 + 4, :], self.qkT[l][qrow + h0 * 64:qrow + (h0 + 4) * 64, 0:NT].rearrange("(h d) t -> d h t", d=64), reads=qr, writes=["Q" + tag])
        for h0 in range(0, nkh, 4):
            h1 = min(nkh, h0 + 4)
            P.dma(Kt[:, h0:h1, :], self.qkT[l][krow + h0 * 64:krow + h1 * 64, :].rearrange("(h d) t -> d h t", d=64), reads=kr, writes=["K" + tag])
        for c0 in range(0, 18, 6):
            P.dma(V[:, c0:c0 + 6, :], self.vtok[l][c0 * 128:(c0 + 6) * 128, vcol:vcol + vw].rearrange("(c p) f -> p c f", p=128),
                  reads=[("vtok", nm, t) for t in range(16)], writes=["V" + tag])
        return Q, Kt, V

    def phase_attn_a(self, l):
        P = self.P
        with self.phase() as ph:
            Q, Kt, V = self.load_qkv(ph, l, QK_ROW["aq"], 8, QK_ROW["ak"], 8, V_COL["av"], 512, "a")
            bufs = self.attn_bufs(ph)
            NB = 2
            rdA = [[ph.sb(f"ard{i}_{b}", [128, 512], F32) for i in range(2)] for b in range(NB)]
            t0A = [ph.sb(f"at0_{b}", [128, 512], F32) for b in range(NB)]; t1A = [ph.sb(f"at1_{b}", [128, 512], F32) for b in range(NB)]
            osqA = [ph.sb(f"aosq{b}", [128, 512], BF16) for b in range(NB)]; rrA = [ph.sb(f"arr{b}", [128, 512], F32) for b in range(NB)]
            ost = [ph.sb(f"aost{i}", [128, 512], BF16) for i in range(2)]
            oi = 0
            groups = [(s * 256, 256, [s * 256, s * 256 + 128]) for s in range(4)]
            groups += [(1024 + i * 512, 512, [1024 + k * 128 for k in range(10)]) for i in range(2)]
            ui = 0
            for (q0, nq, kst) in groups:
                for h in range(4):
                    accs = []
                    for c in range(2):
                        acc = (self.bank_i(2 * c), self.bank_i(2 * c + 1))
                        ch_ = c * 4 + h
                        chunks = [dict(s=[(0, nq, Kt[:, ch_, k0:k0 + 128], Q[:, ch_, q0:q0 + nq])], sreads=["Qa", "Ka"],
                                       pv=[(0, nq, V[:, k0 // 128, h * 128:(h + 1) * 128])], vreads=["Va"]) for k0 in kst]
                        self.softmax_accum(bufs, acc, nq, 128, chunks)
                        accs.append(acc)
                    (bo0, bok0), (bd0, bdk0) = accs[0]
                    (bo1, bok1), (bd1, bdk1) = accs[1]
                    b = ui % NB; ui += 1
                    rd, t0, t1, osq, rr = rdA[b], t0A[b], t1A[b], osqA[b], rrA[b]
                    k = lambda nm: (nm, b)
                    self.act_pow(rd[0][:, :nq], bd0[:, :nq], -1.0, [bdk0], [k("ard0")])
                    self.act_pow(rd[1][:, :nq], bd1[:, :nq], -1.0, [bdk1], [k("ard1")])
                    P.op("dve", lambda e, nq=nq, t0=t0, rd=rd: e.tensor_tensor(out=t0[:, :nq], in0=bo0[:, :nq], in1=rd[0][:, :nq], op=ALU.mult), reads=[bok0, k("ard0")], writes=[k("at0")])
                    P.op("dve", lambda e, nq=nq, t1=t1, rd=rd: e.scalar_tensor_tensor(out=t1[:, :nq], in0=bo1[:, :nq], scalar=self.lam[:, l:l + 1], in1=rd[1][:, :nq], op0=ALU.mult, op1=ALU.mult),
                         reads=[bok1, k("ard1"), "lam"], writes=[k("at1")])
                    P.op("pool", lambda e, nq=nq, t0=t0, t1=t1: e.tensor_tensor(out=t0[:, :nq], in0=t0[:, :nq], in1=t1[:, :nq], op=ALU.subtract), reads=[k("at0"), k("at1")], writes=[k("at0")])
                    P.op("act", lambda e, nq=nq, osq=osq, t0=t0: e.activation(out=osq[:, :nq], in_=t0[:, :nq], func=AF.Square), reads=[k("at0")], writes=[k("aosq")])
                    bs, bsk = self.bank_i(4 + self.sbank % 4); self.sbank += 1
                    P.op("pe", lambda e, bs=bs, nq=nq, osq=osq: e.matmul(bs[:, :nq], lhsT=self.onesb[:], rhs=osq[:, :nq], start=True, stop=True), reads=[k("aosq"), "onesb"], writes=[bsk])
                    self.act_pow(rr[:, :nq], bs[:, :nq], -0.5, [bsk], [k("arr")], scale=1.0 / 128, bias=self.epst[:, 0:1])
                    o = ost[oi % 2]; ok_ = ("aost", oi % 2); oi += 1
                    P.op("dve", lambda e, o=o, nq=nq, t0=t0, rr=rr: e.scalar_tensor_tensor(out=o[:, :nq], in0=t0[:, :nq], scalar=self.gsc[:, l:l + 1], in1=rr[:, :nq], op0=ALU.mult, op1=ALU.mult),
                         reads=[k("at0"), k("arr"), "gsc"], writes=[ok_])
                    P.dma(self.oT[h * 128:(h + 1) * 128, q0:q0 + nq], o[:, :nq], reads=[ok_], writes=[("oT", 0)])

    def phase_attn_bc_ctx(self, l, which, Q, Kt, V, bufs, ph):
        P = self.P
        rd = ph.sb("brd", [64, 512], F32)
        ost = [ph.sb(f"bost{i}", [64, 512], BF16) for i in range(2)]
        oi = 0
        tag = "b" if which == "b" else "c"
        row0 = 512 if which == "b" else 1024
        for s in range(4):
            q0 = s * 256
            for h in range(8):
                kh = h if which == "b" else h // 4
                acc = (self.bank_i((oi % 2) * 2), self.bank_i((oi % 2) * 2 + 1))
                chunks = [dict(s=[(0, 256, Kt[:, kh, k0:k0 + 128], Q[:, h, q0:q0 + 256])], sreads=["Q" + tag, "K" + tag],
                               pv=[(0, 256, V[:, k0 // 128, kh * 64:(kh + 1) * 64])], vreads=["V" + tag]) for k0 in (q0, q0 + 128)]
                self.softmax_accum(bufs, acc, 256, 64, chunks)
                (bo, bok), (bd, bdk) = acc
                if which == "c":
                    P.op("dve", lambda e, bd=bd, h=h: e.tensor_scalar(out=rd[:, :256], in0=bd[0:64, :256], scalar1=self.esink[0:64, l, h:h + 1], scalar2=None, op0=ALU.add),
                         reads=[bdk, "esink"], writes=["brd"])
                    self.act_pow(rd[:, :256], rd[:, :256], -1.0, ["brd"], ["brd"])
                else:
                    self.act_pow(rd[:, :256], bd[0:64, :256], -1.0, [bdk], ["brd"])
                o = ost[oi % 2]; ok_ = ("bost", oi % 2); oi += 1
                P.op("dve", lambda e, o=o, bo=bo: e.tensor_tensor(out=o[:, :256], in0=bo[0:64, :256], in1=rd[:, :256], op=ALU.mult), reads=[bok, "brd"], writes=[ok_])
                P.dma(self.oT[row0 + h * 64:row0 + (h + 1) * 64, q0:q0 + 256], o[:, :256], reads=[ok_], writes=[("oT", 1 if which == "b" else 2)])

    def phase_attn_b(self, l):
        P, d = self.P, self.din
        with self.phase() as ph:
            Q, Kt, V = self.load_qkv(ph, l, QK_ROW["bq"], 8, QK_ROW["bk"], 8, V_COL["bv"], 512, "b")
            bufs = self.attn_bufs(ph)
            self.phase_attn_bc_ctx(l, "b", Q, Kt, V, bufs, ph)
            braw = ph.sb("braw", [128, 14, 8, 64], F32); BT = ph.sb("BT", [128, 14, 8, 64], F32)
            m01 = ph.sb("m01", [128, 64], F32); mng = ph.sb("mng", [128, 64], F32); zt = ph.sb("bzt", [1, 64], F32)
            P.dma(m01[:], d["c_m01"], writes=["m01"]); P.dma(mng[:], d["c_mng"], writes=["mng"])
            P.op("dve", lambda e: e.memset(zt[:], 0.0), writes=["bzt"])
            nb = self.nbpad[l]
            P.dma(nb[0:1, 0:64], zt[:], reads=["bzt"], writes=["nbp0"])
            P.dma(nb[0:1, 64 + 3720:64 + 3720 + 64], zt[:], reads=["bzt"], writes=["nbp1"])
            P.dma(nb[0:1, 64:64 + 3720], d["na_bias"][l].rearrange("(o h) a b -> o (h a b)", o=1), writes=["nbp2"])
            for jj in range(2):
                for h in range(8):
                    src = bass.AP(nb.tensor, 64 + jj * 31 - 48 + h * 465, [[1, 64], [31, 14], [1, 64]])
                    P.dma(braw[jj * 64:(jj + 1) * 64, :, h, :], src, reads=["nbp0", "nbp1", "nbp2"], writes=[("braw", jj, h)])
            bt3 = BT[:].rearrange("p a h q -> p (a h) q")
            P.op("dve", lambda e: e.tensor_tensor(out=bt3, in0=braw[:].rearrange("p a h q -> p (a h) q")[:, :, ::-1],
                                                  in1=m01[:, None, :].broadcast_to([128, 112, 64]), op=ALU.mult), reads=[("braw", jj, h) for jj in range(2) for h in range(8)] + ["m01"], writes=["BT"])
            P.op("dve", lambda e: e.tensor_tensor(out=bt3, in0=bt3, in1=mng[:, None, :].broadcast_to([128, 112, 64]), op=ALU.add), reads=["BT", "mng"], writes=["BT"])
            OB = ph.sb("OB", [64, 8, 1024], BF16)
            rd = ph.sb("blrd", [64, 512], F32)
            V2 = ph.sb("Vb2", [128, 7, 512], BF16)
            P.dma(V2[:], self.vtok[l][1088:1088 + 7 * 128, 512:1024].rearrange("(c p) f -> p c f", p=128),
                  reads=[("vtok", "bv", t) for t in range(16)], writes=["Vb2"])
            for r in range(16):
                r0 = min(max(r - 4, 0), 8)
                q0 = 1024 + r * 64
                acc = (self.bank_i((r % 2) * 2), self.bank_i((r % 2) * 2 + 1))
                chunks = []
                for m in range(6):
                    if m < 4:
                        k0 = 1024 + (r0 + 2 * m) * 64
                        af = r0 + 2 * m - r + 7
                        bias = BT[:, af].rearrange("p h q -> p (h q)")
                    else:
                        k0 = NT + (m - 4) * 128
                        bias = None
                    Vc_ = V[:, k0 // 128] if k0 % 128 == 0 else V2[:, (k0 - 1088) // 128]
                    chunks.append(dict(s=[(h * 64, 64, Kt[:, h, k0:k0 + 128], Q[:, h, q0:q0 + 64]) for h in range(8)], sreads=["Qb", "Kb"],
                                       pv=[(h * 64, 64, Vc_[:, h * 64:(h + 1) * 64]) for h in range(8)], vreads=["Vb", "Vb2"], bias=bias, breads=["BT"]))
                self.softmax_accum(bufs, acc, 512, 64, chunks)
                (bo, bok), (bd, bdk) = acc
                self.act_pow(rd[:], bd[0:64, :], -1.0, [bdk], ["blrd"])
                P.op("dve", lambda e, bo=bo, r=r: e.tensor_tensor(out=OB[:, :, r * 64:(r + 1) * 64], in0=bo[0:64, :].rearrange("p (h q) -> p h q", q=64),
                                                                  in1=rd[:].rearrange("p (h q) -> p h q", q=64), op=ALU.mult), reads=[bok, "blrd"], writes=["OB"])
            P.dma(self.oT[512:1024, 1024:2048].rearrange("(h d) t -> d h t", d=64), OB[:], reads=["OB"], writes=[("oT", 1)])

    def phase_attn_c(self, l):
        P, d = self.P, self.din
        with self.phase() as ph:
            Q, Kt, V = self.load_qkv(ph, l, QK_ROW["cq"], 8, QK_ROW["ck"], 2, V_COL["cv"], 128, "c")
            bufs = self.attn_bufs(ph)
            self.phase_attn_bc_ctx(l, "c", Q, Kt, V, bufs, ph)
            cm = ph.sb("cmask", [128, 2, 128], BF16)
            P.dma(cm[:], d["c_cmask"], writes=["cmask"])
            OC = ph.sb("OC", [64, 8, 1024], BF16)
            rd = ph.sb("clrd", [64, 512], F32)
            it = 0
            for nbk in range(8):
                q0 = 1024 + nbk * 128
                for g in range(2):
                    acc = (self.bank_i((it % 2) * 2), self.bank_i((it % 2) * 2 + 1)); it += 1
                    chunks = []
                    for (kb, mi) in ((nbk - 1, 0), (nbk, None), (nbk + 1, 1), (8, None), (9, None)):
                        if kb < 0 or (kb > 7 and mi is not None):
                            continue
                        k0 = 1024 + kb * 128
                        mask = None if mi is None else cm[:, mi:mi + 1, :].broadcast_to([128, 4, 128])
                        chunks.append(dict(s=[(rq * 128, 128, Kt[:, g, k0:k0 + 128], Q[:, g * 4 + rq, q0:q0 + 128]) for rq in range(4)], sreads=["Qc", "Kc"],
                                           pv=[(0, 512, V[:, k0 // 128, g * 64:(g + 1) * 64])], vreads=["Vc"], mask=mask))
                    self.softmax_accum(bufs, acc, 512, 64, chunks)
                    (bo, bok), (bd, bdk) = acc
                    P.op("dve", lambda e, bd=bd, g=g: e.tensor_tensor(out=rd[:].rearrange("p (r q) -> p r q", q=128), in0=bd[0:64, :].rearrange("p (r q) -> p r q", q=128),
                                                                 in1=self.esink[0:64, l, g * 4:(g + 1) * 4].unsqueeze(2).broadcast_to([64, 4, 128]), op=ALU.add),
                         reads=[bdk, "esink"], writes=["clrd"])
                    self.act_pow(rd[:], rd[:], -1.0, ["clrd"], ["clrd"])
                    P.op("dve", lambda e, bo=bo, g=g, nbk=nbk: e.tensor_tensor(out=OC[:, g * 4:(g + 1) * 4, nbk * 128:(nbk + 1) * 128], in0=bo[0:64, :].rearrange("p (r q) -> p r q", q=128),
                                                                       in1=rd[:].rearrange("p (r q) -> p r q", q=128), op=ALU.mult), reads=[bok, "clrd"], writes=["OC"])
            P.dma(self.oT[1024:1536, 1024:2048].rearrange("(h d) t -> d h t", d=64), OC[:], reads=["OC"], writes=[("oT", 2)])

    def hy_ffn(self, ph, l, L, h2T, cst):
        P, d = self.P, self.din
        zT = ph.sb("hzT", [33, 1024], F32); h1T = ph.sb("hh1", [64, 1024], F32)
        arg = ph.sb("harg", [64, 512], F32); wr = ph.sb("hwr", [64, 512], F32)
        u = P.uid; P.uid += 1
        P.dma(zT[:, :L], d[f"c_z{L}"], writes=[("hzT", u)])
        for stage in range(2):
            src = zT if stage == 0 else h1T
            dst = h1T if stage == 0 else h2T
            K = 33 if stage == 0 else 64
            wt = cst["w1"] if stage == 0 else cst["w2"]
            fb = cst["fb"][:, stage:stage + 1]
            for n0 in range(0, L, 512):
                n = min(512, L - n0)
                pb, pk = self.bank()
                P.op("pe", lambda e, pb=pb, wt=wt, K=K, src=src, n0=n0, n=n: e.matmul(pb[0:64, :n], lhsT=wt[0:K, :], rhs=src[0:K, n0:n0 + n], start=True, stop=True),
                     reads=["hyc", ("hzT", u), ("hh", u, 0)], writes=[pk])
                P.op("dve", lambda e, pb=pb, n=n, fb=fb: e.tensor_scalar(out=arg[:, :n], in0=pb[0:64, :n], scalar1=cst["freq"][:, 0:1], scalar2=fb, op0=ALU.mult, op1=ALU.add),
                     reads=[pk, "hyc"], writes=["harg"], hard=True)
                P.op("dve", lambda e, n=n: e.tensor_scalar(out=wr[:, :n], in0=arg[:, :n], scalar1=math.pi, scalar2=-2 * math.pi, op0=ALU.is_gt, op1=ALU.mult), reads=["harg"], writes=["hwr"])
                P.op("dve", lambda e, n=n: e.tensor_tensor(out=wr[:, :n], in0=wr[:, :n], in1=arg[:, :n], op=ALU.add), reads=["hwr", "harg"], writes=["hwr"])
                P.op("dve", lambda e, n=n: e.tensor_scalar(out=arg[:, :n], in0=arg[:, :n], scalar1=-math.pi, scalar2=2 * math.pi, op0=ALU.is_lt, op1=ALU.mult), reads=["harg", "hwr"], writes=["harg"])
                P.op("dve", lambda e, n=n: e.tensor_tensor(out=wr[:, :n], in0=wr[:, :n], in1=arg[:, :n], op=ALU.add), reads=["hwr", "harg"], writes=["hwr"])
                P.op("act", lambda e, dst=dst, n0=n0, n=n: e.activation(out=dst[:, n0:n0 + n], in_=wr[:, :n], func=AF.Sin), reads=["hwr"], writes=[("hh", u, stage)])
        return ("hh", u, 1)

    def hy_G(self, ph, l, L, o, h2T, h2k, cst, F, Fk, hs, hd, G, Gk, tmp, hsk="hYre", hdk="hYs"):
        P = self.P
        nch = L // 128
        win, ta, tb = tmp
        ntc = cst["nt"][L]
        for nc_ in range(nch):
            banks = []
            for dr in range(2):
                pb, pk = self.bank()
                col = (o * 2 + dr) * 512
                P.op("pe", lambda e, pb=pb, nc_=nc_, col=col: e.matmul(pb[:, :], lhsT=h2T[0:64, nc_ * 128:(nc_ + 1) * 128], rhs=cst["w3"][0:64, col:col + 512], start=True, stop=True),
                     reads=[h2k, "hyc"], writes=[pk])
                banks.append((pb, pk))
            P.op("act", lambda e, nc_=nc_: e.activation(out=win[:], in_=cst["dabs"][:, o * 1024:(o + 1) * 1024], func=AF.Exp, scale=ntc[:, nc_:nc_ + 1]),
                 reads=["hyc"], writes=["hwin"])
            P.op("dve", lambda e, pb=banks[0][0]: e.tensor_tensor(out=ta[:], in0=pb[:], in1=win[:, 0:512], op=ALU.mult), reads=[banks[0][1], "hwin"], writes=["hta"])
            P.op("dve", lambda e, pb=banks[1][0]: e.tensor_tensor(out=tb[:], in0=pb[:], in1=win[:, 512:1024], op=ALU.mult), reads=[banks[1][1], "hwin"], writes=["htb"])
            P.op("pool", lambda e, nc_=nc_: e.tensor_tensor(out=hs[:, nc_, :], in0=ta[:], in1=tb[:], op=ALU.add), reads=["hta", "htb"], writes=[hsk])
            P.op("pool", lambda e, nc_=nc_: e.tensor_tensor(out=hd[:, nc_, :], in0=tb[:], in1=ta[:], op=ALU.subtract), reads=["hta", "htb"], writes=[hdk])
        for fc in range(nch):
            pr, prk = self.bank(); pi_, pik = self.bank()
            for nc_ in range(nch):
                P.op("pe", lambda e, pr=pr, nc_=nc_, fc=fc: e.matmul(pr[:], lhsT=F[:, nc_, fc * 128:(fc + 1) * 128], rhs=hs[:, nc_, :], start=(nc_ == 0), stop=(nc_ == nch - 1)),
                     reads=[Fk, hsk], writes=[prk])
            for nc_ in range(nch):
                P.op("pe", lambda e, pi_=pi_, nc_=nc_, fc=fc: e.matmul(pi_[:], lhsT=F[:, nc_, L + fc * 128:L + (fc + 1) * 128], rhs=hd[:, nc_, :], start=(nc_ == 0), stop=(nc_ == nch - 1)),
                     reads=[Fk, hdk], writes=[pik])
            P.op("dve", lambda e, pr=pr, fc=fc: e.tensor_tensor(out=G[:, fc, 0, :], in0=pr[:], in1=cst["biasb"][:, o, :], op=ALU.add), reads=[prk, "hyc"], writes=[Gk])
            P.op("act", lambda e, pi_=pi_, fc=fc: e.copy(out=G[:, fc, 1, :], in_=pi_[:]), reads=[pik], writes=[Gk])

    def hy_dft_mult(self, L, F, Fk, xin, xink, G, Gk, Yre, Ys, mt):
        P = self.P
        nch = L // 128
        for fc in range(nch):
            pr, prk = self.bank(); pi_, pik = self.bank()
            for tc in range(nch):
                P.op("pe", lambda e, pr=pr, tc=tc, fc=fc: e.matmul(pr[:], lhsT=F[:, tc, fc * 128:(fc + 1) * 128], rhs=xin[:, tc, :], start=(tc == 0), stop=(tc == nch - 1)),
                     reads=[Fk, xink], writes=[prk])
            for tc in range(nch):
                P.op("pe", lambda e, pi_=pi_, tc=tc, fc=fc: e.matmul(pi_[:], lhsT=F[:, tc, L + fc * 128:L + (fc + 1) * 128], rhs=xin[:, tc, :], start=(tc == 0), stop=(tc == nch - 1)),
                     reads=[Fk, xink], writes=[pik])
            m1, m2, m3, m4 = mt
            P.op("dve", lambda e, pr=pr, fc=fc: e.tensor_tensor(out=m1[:], in0=pr[:], in1=G[:, fc, 0, :], op=ALU.mult), reads=[prk, Gk], writes=["hm1"])
            P.op("dve", lambda e, pi_=pi_, fc=fc: e.tensor_tensor(out=m2[:], in0=pi_[:], in1=G[:, fc, 1, :], op=ALU.mult), reads=[pik, Gk], writes=["hm2"])
            P.op("dve", lambda e, pi_=pi_, fc=fc: e.tensor_tensor(out=m3[:], in0=pi_[:], in1=G[:, fc, 0, :], op=ALU.mult), reads=[pik, Gk], writes=["hm3"])
            P.op("dve", lambda e, pr=pr, fc=fc: e.tensor_tensor(out=m4[:], in0=pr[:], in1=G[:, fc, 1, :], op=ALU.mult), reads=[prk, Gk], writes=["hm4"])
            P.op("pool", lambda e, fc=fc: e.tensor_tensor(out=Yre[:, fc, :], in0=m1[:], in1=m2[:], op=ALU.add), reads=["hm1", "hm2"], writes=["hYre"])
            P.op("pool", lambda e, fc=fc: e.tensor_tensor(out=Ys[:, fc, :], in0=m3[:], in1=m4[:], op=ALU.subtract), reads=["hm3", "hm4"], writes=["hYs"])

    def hy_conv(self, l, tok0, nseq, Ls, cst, bufs):
        P = self.P
        uring, ucf, ucb, x2b, vtm, x1tm, Yre, Ys, mt, ost = bufs
        sw = cst["sw"]
        LT = nseq * Ls
        nchs = Ls // 128
        v3 = lambda ap, a, b: ap[:, 0:LT].rearrange("p (s t) -> p s t", t=Ls)[:, :, a:b]
        ui = 0
        for part in range(3):
            for ct in range(4):
                tile = part * 4 + ct
                u = uring[ui % 2]; uk = ("hu", ui % 2)
                cf = ucf[0]; cfk = ("hcf", 0); ui += 1
                P.dma(u[:, :LT], self.hyT[tile * 128:(tile + 1) * 128, tok0:tok0 + LT], reads=[("hyT", tile)], writes=[uk])
                P.op("dve", lambda e, cf=cf, u=u, tile=tile: e.tensor_scalar(out=cf[:, :LT], in0=u[:, :LT], scalar1=sw[:, 12 + tile:13 + tile], scalar2=None, op0=ALU.mult),
                     reads=[uk, "hyc"], writes=[cfk])
                P.op("dve", lambda e, cf=cf, u=u, tile=tile: e.scalar_tensor_tensor(out=v3(cf, 1, Ls), in0=v3(u, 0, Ls - 1), scalar=sw[:, tile:tile + 1], in1=v3(cf, 1, Ls), op0=ALU.mult, op1=ALU.add),
                     reads=[uk, cfk, "hyc"], writes=[cfk])
                P.op("dve", lambda e, cf=cf, u=u, tile=tile: e.scalar_tensor_tensor(out=v3(cf, 0, Ls - 1), in0=v3(u, 1, Ls), scalar=sw[:, 24 + tile:25 + tile], in1=v3(cf, 0, Ls - 1), op0=ALU.mult, op1=ALU.add),
                     reads=[uk, cfk, "hyc"], writes=[cfk])
                if part < 2:
                    P.op("act", lambda e, cf=cf, ct=ct: e.copy(out=ucb[:, ct, :LT], in_=cf[:, :LT]), reads=[cfk], writes=[("hucb", ct)])
                else:
                    P.op("act", lambda e, cf=cf, ct=ct: e.copy(out=x2b[:, ct, :LT], in_=cf[:, :LT]), reads=[cfk], writes=[("hx2", ct)])
            if part < 2:
                dst = vtm if part == 0 else x1tm
                dk = "hvtm" if part == 0 else "hx1tm"
                for tc in range(LT // 128):
                    pb, pk = self.bank()
                    pbb = pb.bitcast(BF16)
                    for ct in range(4):
                        P.op("pe", lambda e, pbb=pbb, ct=ct, tc=tc: e.transpose(pbb[:, ct * 128:(ct + 1) * 128], ucb[:, ct, tc * 128:(tc + 1) * 128], self.identb[:]),
                             reads=[("hucb", ct), "identb"], writes=[pk])
                    if tc % 2 == 0:
                        P.op("act", lambda e, pbb=pbb, dst=dst, tc=tc: e.copy(out=dst[:, tc, :], in_=pbb[:, 0:512]), reads=[pk], writes=[(dk, tc // nchs)])
                    else:
                        P.op("dve", lambda e, pbb=pbb, dst=dst, tc=tc: e.tensor_copy(out=dst[:, tc, :], in_=pbb[:, 0:512]), reads=[pk], writes=[(dk, tc // nchs)])

    def hy_long(self, l, L, tok0, si, cst, F, Fk, FT, FTk, G0, G1, bufs, gen_G=None):
        P = self.P
        nch = L // 128
        uring, ucf, ucb, x2b_, vtm_, x1tm_, Yre, Ys, mt, ost = bufs
        vtm = vtm_[:, si * nch:(si + 1) * nch]; x1tm = x1tm_[:, si * nch:(si + 1) * nch]
        x2b = x2b_[:, :, si * L:(si + 1) * L]
        vk, x1k = ("hvtm", si), ("hx1tm", si)
        if gen_G is not None:
            gen_G(0)
        self.hy_dft_mult(L, F, Fk, vtm, vk, G0, "hG0", Yre, Ys, mt)
        for tc in range(nch):
            pb, pk = self.bank()
            for fc in range(nch):
                P.op("pe", lambda e, pb=pb, fc=fc, tc=tc: e.matmul(pb[:], lhsT=FT[:, fc, tc * 128:(tc + 1) * 128], rhs=Yre[:, fc, :], start=(fc == 0), stop=False),
                     reads=[FTk, "hYre"], writes=[pk])
            for fc in range(nch):
                P.op("pe", lambda e, pb=pb, fc=fc, tc=tc: e.matmul(pb[:], lhsT=FT[:, nch + fc, tc * 128:(tc + 1) * 128], rhs=Ys[:, fc, :], start=False, stop=(fc == nch - 1)),
                     reads=[FTk, "hYs"], writes=[pk])
            P.op("dve", lambda e, pb=pb, tc=tc: e.scalar_tensor_tensor(out=vtm[:, tc, :], in0=pb[:], scalar=1.0 / L, in1=x1tm[:, tc, :], op0=ALU.mult, op1=ALU.mult),
                 reads=[pk, x1k], writes=[vk])
        if gen_G is not None:
            gen_G(1)
        self.hy_dft_mult(L, F, Fk, vtm, vk, G1, "hG1", Yre, Ys, mt)
        for ct in range(4):
            for t0 in range(0, L, 512):
                n = min(512, L - t0)
                pb, pk = self.bank()
                for fc in range(nch):
                    P.op("pe", lambda e, pb=pb, fc=fc, ct=ct, t0=t0, n=n: e.matmul(pb[:, :n], lhsT=Yre[:, fc, ct * 128:(ct + 1) * 128], rhs=FT[:, fc, t0:t0 + n], start=(fc == 0), stop=False),
                         reads=[FTk, "hYre"], writes=[pk])
                for fc in range(nch):
                    P.op("pe", lambda e, pb=pb, fc=fc, ct=ct, t0=t0, n=n: e.matmul(pb[:, :n], lhsT=Ys[:, fc, ct * 128:(ct + 1) * 128], rhs=FT[:, nch + fc, t0:t0 + n], start=False, stop=(fc == nch - 1)),
                         reads=[FTk, "hYs"], writes=[pk])
                i = self.hoi % 2; self.hoi += 1
                o = ost[i]; ok_ = ("host", i)
                P.op("dve", lambda e, pb=pb, o=o, ct=ct, t0=t0, n=n: e.scalar_tensor_tensor(out=o[:, :n], in0=pb[:, :n], scalar=1.0 / L, in1=x2b[:, ct, t0:t0 + n], op0=ALU.mult, op1=ALU.mult),
                     reads=[pk, ("hx2", ct)], writes=[ok_])
                P.dma(self.oT[1536 + ct * 128:1536 + (ct + 1) * 128, tok0 + t0:tok0 + t0 + n], o[:, :n], reads=[ok_], writes=[("oT", 3)])

    def phase_hyena(self, l):
        P, d = self.P, self.din
        with self.phase() as ph:
            cst = dict(w1=ph.sb("hw1", [33, 64], F32), w2=ph.sb("hw2", [64, 64], F32), w3=ph.sb("hw3", [64, 2048], F32),
                       freq=ph.sb("hfreq", [64, 1], F32), fb=ph.sb("hfb", [64, 2], F32), dabs=ph.sb("hdabs", [128, 2048], F32),
                       biasb=ph.sb("hbiasb", [128, 2, 512], F32), sw=ph.sb("hsw", [128, 36], F32),
                       nt={256: ph.sb("hnt256", [128, 2], F32), 1024: ph.sb("hnt1024", [128, 8], F32)})
            P.dma(cst["w1"][:], d["hy_w1"][l], writes=["hyc"]); P.dma(cst["w2"][:], d["hy_w2"][l], writes=["hyc2"]); P.dma(cst["w3"][:], d["hy_w3"][l], writes=["hyc3"])
            P.dma(cst["freq"][:], d["hy_freq"][l].rearrange("(p o) -> p o", o=1), writes=["hyc4"])
            P.dma(cst["fb"][:, 0:1], d["hy_b1"][l].rearrange("(p o) -> p o", o=1), writes=["hyc5"])
            P.dma(cst["fb"][:, 1:2], d["hy_b2"][l].rearrange("(p o) -> p o", o=1), writes=["hyc6"])
            P.dma(cst["dabs"][:], d["hy_decay"][l].partition_broadcast(128), writes=["hyc7"])
            P.dma(cst["biasb"][:], d["hy_bias"][l].partition_broadcast(128), writes=["hyc8"])
            P.dma(cst["nt"][256][:], d["c_nt256"], writes=["hyc9"]); P.dma(cst["nt"][1024][:], d["c_nt1024"], writes=["hyc10"])
            self.load_T(ph, cst["sw"][:], ["hycsw"], d["hy_short"][l].rearrange("k (t p) -> (k t) p", p=128), 36)
            allk = ["hyc", "hyc2", "hyc3", "hyc4", "hyc5", "hyc6", "hyc7", "hyc8", "hyc9", "hyc10", "hycsw"]
            P.op("dve", lambda e: e.tensor_scalar(out=cst["fb"][:], in0=cst["fb"][:], scalar1=cst["freq"][:, 0:1], scalar2=None, op0=ALU.mult), reads=allk, writes=["hycA"], hard=True)
            P.op("act", lambda e: e.activation(out=cst["dabs"][:], in_=cst["dabs"][:], func=AF.Abs), reads=["hycA"], writes=["hycB"])
            P.op("dve", lambda e: e.tensor_copy(out=cst["nt"][256][:], in_=cst["nt"][256][:]), reads=["hycB"] + allk, writes=["hyc"])
            win = ph.sb("hwin", [128, 1024], F32); ta = ph.sb("hta", [128, 512], F32); tb = ph.sb("htb", [128, 512], F32)
            mt = [ph.sb(f"hm{i}", [128, 512], F32) for i in range(4)]
            ost = [ph.sb(f"host{i}", [128, 512], BF16) for i in range(2)]
            uring = [ph.sb(f"hur{i}", [128, 1024], F32) for i in range(2)]
            ucf = [ph.sb("hcf0", [128, 1024], F32)] * 2
            ucb = ph.sb("hucb", [128, 4, 1024], BF16); x2b = ph.sb("hx2b", [128, 4, 1024], BF16)
            vtm = ph.sb("hvtm", [128, 8, 512], BF16); x1tm = ph.sb("hx1tm", [128, 8, 512], BF16)
            Yre = ph.sb("hYre", [128, 8, 512], BF16); Ys = ph.sb("hYs", [128, 8, 512], BF16)
            hs, hd = Yre, Ys
            h2T = ph.sb("hh2", [64, 256], F32); h2Tb = ph.sb("hh2b", [64, 1024], F32)
            self.hoi = 0
            Ga = ph.sb("hGa", [128, 8, 2, 512], F32)
            bufs = (uring, ucf, ucb, x2b, vtm, x1tm, Yre, Ys, mt, ost)
            with self.phase() as p3:
                h2k = self.hy_ffn(p3, l, 256, h2T, cst)
            with self.phase() as p2:
                F = p2.sb("hF256", [128, 2, 512], BF16); FT = p2.sb("hFT256", [128, 4, 256], BF16)
                P.dma(F[:], d["c_F256"].rearrange("(c p) f -> p c f", p=128), writes=["hF"])
                P.dma(FT[:], d["c_FT256"].rearrange("(c p) f -> p c f", p=128), writes=["hFT"])
                G0 = Ga[:, 0:2]; G1 = Ga[:, 2:4]
                self.hy_G(p2, l, 256, 0, h2T, h2k, cst, F, "hF", hs, hd, G0, "hG0", (win, ta, tb))
                self.hy_G(p2, l, 256, 1, h2T, h2k, cst, F, "hF", hs, hd, G1, "hG1", (win, ta, tb))
                self.hy_conv(l, 0, 4, 256, cst, bufs)
                for s in range(4):
                    self.hy_long(l, 256, s * 256, s, cst, F, "hF", FT, "hFT", G0, G1, bufs)
                    if s == 1:
                        h2k = self.hy_ffn(p2, l, 1024, h2Tb, cst)
            with self.phase() as p2:
                F = p2.sb("hF1024", [128, 8, 2048], BF16); FT = p2.sb("hFT1024", [128, 16, 1024], BF16)
                for q in range(4):
                    P.dma(F[:, q * 2:(q + 1) * 2, :], d["c_F1024"][q * 256:(q + 1) * 256, :].rearrange("(c p) f -> p c f", p=128), writes=[("hF", q)])
                    P.dma(FT[:, q * 4:(q + 1) * 4, :], d["c_FT1024"][q * 512:(q + 1) * 512, :].rearrange("(c p) f -> p c f", p=128), writes=[("hFT", q)])
                P.op("pool", lambda e: e.tensor_copy(out=F[:, 0, 0:2], in_=F[:, 0, 0:2]), reads=[("hF", q) for q in range(4)], writes=["hF"])
                P.op("pool", lambda e: e.tensor_copy(out=FT[:, 0, 0:2], in_=FT[:, 0, 0:2]), reads=[("hFT", q) for q in range(4)], writes=["hFT"])

                def gen_G(o):
                    self.hy_G(p2, l, 1024, o, h2Tb, h2k, cst, F, "hF", hs, hd, Ga, "hG%d" % o, (win, ta, tb))
                self.hy_conv(l, 1024, 1, 1024, cst, bufs)
                self.hy_long(l, 1024, 1024, 0, cst, F, "hF", FT, "hFT", Ga, Ga, bufs, gen_G=gen_G)

    def phase_mixers(self, l):
        self.sbank = 0
        sel = self.dbg.get("mix", "abcd") if isinstance(self.dbg, dict) else "abcd"
        if "a" in sel:
            self.phase_attn_a(l)
        if "b" in sel:
            self.phase_attn_b(l)
        if "c" in sel:
            self.phase_attn_c(l)
        if "d" in sel:
            self.phase_hyena(l)

def _consts():
    c = {}
    c["c_ident"] = np.eye(128, dtype=np.float32)
    c["c_identb"] = np.eye(128).astype(ml_dtypes.bfloat16)
    c["c_onesb"] = np.ones((128, 128)).astype(ml_dtypes.bfloat16)
    f = np.arange(128)
    partner = (f // 32) * 32 + ((f % 32) + 16) % 32
    pm = np.zeros((128, 128), np.float32)
    pm[partner, f] = 1.0
    c["c_rperm"] = pm
    t = np.arange(1024)
    fh = f % 64
    q = fh // 16
    j = fh % 16
    inv = (10000.0 ** (-(j.astype(np.float32)) / 16.0)).astype(np.float32)
    pos = np.where((q < 2)[:, None], (t // 64)[None, :], (t % 64)[None, :]).astype(np.float32)
    ang = (pos * inv[:, None]).astype(np.float32)
    c["c_rcos"] = np.cos(ang).astype(np.float32)
    sgn = np.where(q % 2 == 0, -1.0, 1.0).astype(np.float32)
    c["c_rsin"] = (np.sin(ang) * sgn[:, None]).astype(np.float32)
    kc = (np.arange(128) % 64)[:, None]
    qc = np.arange(64)[None, :]
    col0 = np.clip(qc - 8, 0, 48)
    valid = (kc >= col0) & (kc < col0 + 16)
    c["c_m01"] = valid.astype(np.float32)
    c["c_mng"] = np.where(valid, 0.0, -30000.0).astype(np.float32)
    for L in (256, 1024):
        n = np.arange(L, dtype=np.float32)[:, None]
        t = n / np.float32(max(L - 1, 1))
        w = (np.float32(2.0 * math.pi) * n / np.float32(L)).astype(np.float32)
        bands = np.linspace(1e-4, 15, 16, dtype=np.float32)[None, :]
        z = np.concatenate([t, np.cos(bands * w), -np.sin(bands * w)], axis=-1).astype(np.float32)
        c[f"c_z{L}"] = np.ascontiguousarray(z.T)
        c[f"c_nt{L}"] = np.ascontiguousarray((-t[:, 0]).reshape(L // 128, 128).T.astype(np.float32))
        nn = np.arange(L, dtype=np.float64)[:, None]
        om = math.pi * (2 * np.arange(L, dtype=np.float64)[None, :] + 1) / (2 * L)
        Fm = np.concatenate([np.cos(nn * om), np.sin(nn * om)], axis=1)
        c[f"c_F{L}"] = Fm.astype(np.float32).astype(ml_dtypes.bfloat16)
        c[f"c_FT{L}"] = np.ascontiguousarray(Fm.T).astype(np.float32).astype(ml_dtypes.bfloat16)
    kk = np.arange(128)[:, None]; qq = np.arange(128)[None, :]
    c["c_cmask"] = np.stack([(kk >= qq), (kk <= qq)], axis=1).astype(np.float32).astype(ml_dtypes.bfloat16)
    return c


_CACHE = {}


def kernel(**inputs):
    inputs = {k: np.asarray(v) for k, v in inputs.items()}
    if "nc" not in _CACHE:
        _CACHE["nc"] = Builder().build()
    nc = _CACHE["nc"]
    consts = _consts()
    shared = {k: np.ascontiguousarray(inputs[k]) for k in (
        "w_ada", "b_ada", "g_mix", "w_in", "diff_lambda", "diff_norm_g", "na_bias", "swa_sink", "hy_short", "hy_w1", "hy_b1",
        "hy_w2", "hy_b2", "hy_w3", "hy_freq", "hy_decay", "hy_bias", "w_branch", "w_out", "g_mlp", "w_up", "w_down", "g_final")}
    in_maps = []
    for i in range(8):
        m = dict(shared)
        m.update(consts)
        m["xc"] = np.ascontiguousarray(inputs["x_prompt"][4 * i:4 * i + 4].reshape(1024, D))
        m["xl"] = np.ascontiguousarray(inputs["x_sample"][i])
        m["cak"] = np.ascontiguousarray(inputs["cache_a_k"][i].reshape(2, 256, 512))
        m["cav"] = np.ascontiguousarray(inputs["cache_a_v"][i].reshape(2, 256, 512))
        m["cbk"] = np.ascontiguousarray(inputs["cache_b_k"][i].reshape(2, 256, 512))
        m["cbv"] = np.ascontiguousarray(inputs["cache_b_v"][i].reshape(2, 256, 512))
        m["cck"] = np.ascontiguousarray(inputs["cache_c_k"][i].reshape(2, 256, 128))
        m["ccv"] = np.ascontiguousarray(inputs["cache_c_v"][i].reshape(2, 256, 128))
        m["cvec"] = np.ascontiguousarray(np.stack([inputs["c_ctx"], inputs["c"][i]], axis=0))
        in_maps.append(m)
    res = run_bass_kernel_spmd(nc, in_maps, core_ids=list(range(8)))
    r = res.results
    y_prompt = np.concatenate([r[i]["y_c"].reshape(4, 256, D) for i in range(8)], axis=0)
    y_sample = np.stack([r[i]["y_l"] for i in range(8)], axis=0)
    def cat(name, shp):
        return np.concatenate([r[i][name] for i in range(8)], axis=0).reshape(shp)
    return (y_prompt.astype(np.float32), y_sample.astype(np.float32),
            cat("nak", (32, 2, 256, 2, 4, 64)), cat("nav", (32, 2, 256, 4, 128)),
            cat("nbk", (32, 2, 256, 8, 64)), cat("nbv", (32, 2, 256, 8, 64)),
            cat("nck", (32, 2, 256, 2, 64)), cat("ncv", (32, 2, 256, 2, 64)))
```

```python
import math
from contextlib import ExitStack, contextmanager
import numpy as np
import ml_dtypes
import concourse.bass as bass
import concourse.mybir as mybir
from concourse.bass_utils import run_bass_kernel_spmd

F32 = mybir.dt.float32
BF16 = mybir.dt.bfloat16
AF = mybir.ActivationFunctionType
ALU = mybir.AluOpType

SAME_ENG_SYNC = True


class Prog:
    ENGS = ("pe", "act", "dve", "pool", "sp")
    KROT = 8

    def __init__(self, nc):
        self.nc = nc
        self.ops = {e: [] for e in self.ENGS}
        self.res = {}
        self.ndma = {e: 0 for e in self.ENGS}
        self.dma_refs = {e: [] for e in self.ENGS}
        self._ps_i = 0
        self.uid = 0

    def name(self, base):
        self.uid += 1
        return f"{base}_{self.uid}"

    def _record(self, eng, fn, reads, writes, is_dma, hard=False):
        idx = len(self.ops[eng])
        ref = (eng, idx)
        deps = set()
        for k in reads:
            r = self.res.get(k)
            if r is not None and r[0] is not None:
                deps.add(r[0])
        for k in writes:
            r = self.res.get(k)
            if r is not None:
                if r[0] is not None:
                    deps.add(r[0])
                deps.update(r[1].values())
                deps.update(r[2])
        if not is_dma:
            keep = set()
            for d in deps:
                if d[0] == eng and not self.ops[eng][d[1]]["dma"]:
                    if (SAME_ENG_SYNC or hard) and eng != "pe":
                        keep.add(d)
                else:
                    keep.add(d)
            deps = keep
        op = dict(fn=fn, deps=deps, dma=is_dma, sig=False)
        if is_dma:
            op["dj"] = self.ndma[eng]
            self.ndma[eng] += 1
            self.dma_refs[eng].append(ref)
        self.ops[eng].append(op)
        for k in reads:
            r = self.res.setdefault(k, [None, {}, []])
            if is_dma:
                r[2].append(ref)
            else:
                r[1][eng] = ref
        for k in writes:
            self.res[k] = [ref, {}, []]
        return ref

    def op(self, eng, fn, reads=(), writes=(), hard=False):
        return self._record(eng, fn, tuple(reads), tuple(writes), False, hard)

    def dma(self, out, in_, reads=(), writes=(), q="sp", **kw):
        return self._record(q, lambda e: e.dma_start(out=out, in_=in_, **kw), tuple(reads), tuple(writes), True)

    def barrier(self):
        lasts = set()
        for e in self.ENGS:
            for i in range(len(self.ops[e]) - 1, -1, -1):
                o = self.ops[e][i]
                if o["fn"] is not None and not o["dma"]:
                    lasts.add((e, i))
                    break
            for ref in self.dma_refs[e][-self.KROT:]:
                lasts.add(ref)
        for e in self.ENGS:
            deps = set(d for d in lasts if not (d[0] == e and not self.ops[e][d[1]]["dma"]))
            self.ops[e].append(dict(fn=None, deps=deps, dma=False, sig=False))
        self.res = {}

    def emit(self):
        nc = self.nc
        with ExitStack() as es:
            esem = {e: es.enter_context(nc.semaphore(f"sem_{e}")) for e in self.ENGS}
            rsem = {e: [es.enter_context(nc.semaphore(f"rot_{e}_{j}")) for j in range(self.KROT)]
                    for e in self.ENGS if self.ndma[e] > 0}
            for e in self.ENGS:
                for o in self.ops[e]:
                    for d in o["deps"]:
                        self.ops[d[0]][d[1]]["sig"] = True
            for e in self.ENGS:
                c = 0
                for o in self.ops[e]:
                    if o["dma"]:
                        j = o["dj"]
                        o["done"] = ((e, "r", j % self.KROT), 16 * (j // self.KROT + 1))
                    elif o["sig"]:
                        c += 1
                        o["done"] = ((e, "c"), c)

            def semh(key):
                return esem[key[0]] if key[1] == "c" else rsem[key[0]][key[2]]

            def emit_eng(engobj, e):
                obs = {}
                for o in self.ops[e]:
                    waits = {}
                    for d in o["deps"]:
                        k, v = self.ops[d[0]][d[1]]["done"]
                        if obs.get(k, 0) < v:
                            waits[k] = max(waits.get(k, 0), v)
                    if o["dma"] and o["dj"] >= self.KROT:
                        j = o["dj"]
                        k, v = (e, "r", j % self.KROT), 16 * (j // self.KROT)
                        if obs.get(k, 0) < v:
                            waits[k] = max(waits.get(k, 0), v)
                    for k, v in waits.items():
                        engobj.wait_ge(semh(k), v)
                        obs[k] = v
                    if o["fn"] is None:
                        continue
                    ins = o["fn"](engobj)
                    if o["dma"]:
                        ins.then_inc(semh(o["done"][0]), 16)
                    elif o["sig"]:
                        ins.then_inc(esem[e], 1)

            with nc.Block() as block:
                block.tensor(lambda x: emit_eng(x, "pe"))
                block.scalar(lambda x: emit_eng(x, "act"))
                block.vector(lambda x: emit_eng(x, "dve"))
                block.gpsimd(lambda x: emit_eng(x, "pool"))
                block.sync(lambda x: emit_eng(x, "sp"))


D = 2048
NT = 2048
NTX = NT + 256
KC = 16
DEPTH = 2
IN_COLS = 13568
D_FF = 8192
OFF = dict(aq=0, ak=512, av=1024, bq=1536, bk=2048, bv=2560, cq=3072, ck=3584, cv=3712, hy=3840, gate=5376)
QK_ROW = dict(aq=0, ak=512, bq=1024, bk=1536, cq=2048, ck=2560)
NQK = 2688
V_COL = dict(av=0, bv=512, cv=1024)
NV = 1152
EPS = 1e-6


class Phase:
    def __init__(self, bld):
        self.b = bld
        self.es = ExitStack()

    def sb(self, name, shape, dt):
        return self.es.enter_context(self.b.nc.sbuf_tensor(self.b.P.name(name), list(shape), dt))

    def __enter__(self):
        return self

    def __exit__(self, *a):
        self.b.P.barrier()
        self.es.close()
        return False


class WStream:
    def __init__(self, bld, ph, units, nk, tag, nslots=2, nstage=3, kstage=4):
        self.b, self.units, self.nk, self.tag = bld, units, nk, tag
        self.nslots, self.nstage, self.kstage = nslots, nstage, kstage
        self.w = [ph.sb(f"w{tag}{s}", [128, nk, 512], BF16) for s in range(nslots)]
        self.st = [ph.sb(f"ws{tag}{s}", [128, kstage, 512], F32) for s in range(nstage)]
        self.sti = 0
        self.pieces = {}

    def fetch(self, i, npieces=None):
        if i >= len(self.units):
            return
        P = self.b.P
        src, ncols = self.units[i]
        slot = i % self.nslots
        total = self.nk // self.kstage
        done = self.pieces.get(i, 0)
        todo = total - done if npieces is None else min(npieces, total - done)
        for pc in range(done, done + todo):
            k0 = pc * self.kstage
            s = self.sti % self.nstage
            self.sti += 1
            st, w = self.st[s], self.w[slot]
            P.dma(st[:, :, :ncols], src(k0, self.kstage), writes=[("wst", self.tag, s)])
            P.op("pool", lambda e, st=st, w=w, k0=k0, ncols=ncols: e.tensor_copy(
                out=w[:, k0:k0 + self.kstage, :ncols], in_=st[:, :, :ncols]),
                reads=[("wst", self.tag, s)], writes=[("wbf", self.tag, slot, pc)])
        self.pieces[i] = done + todo

    def get(self, i, prefetch=True):
        self.fetch(i)
        if prefetch:
            self.fetch(i + 1)
        slot = i % self.nslots
        return self.w[slot], (lambda kc: ("wbf", self.tag, slot, kc // self.kstage))


def wsrc(ap2d, c0, ncols):
    def f(k0, nk):
        return ap2d[k0 * 128:(k0 + nk) * 128, c0:c0 + ncols].rearrange("(k p) c -> p k c", p=128)
    return f


class Builder:
    def __init__(self, dbg=None):
        self.nc = nc = bass.Bass("TRN2", target_bir_lowering=False)
        self.P = Prog(nc)
        self.dbg = dbg or {}
        self.es = ExitStack()
        self.din = {}
        self.dout = {}
        self.psi = 0

    def inp(self, name, shape, dt=F32):
        self.din[name] = self.nc.dram_tensor(name, list(shape), dt, kind="ExternalInput").ap()
        return self.din[name]

    def outp(self, name, shape, dt=F32):
        self.dout[name] = self.nc.dram_tensor(name, list(shape), dt, kind="ExternalOutput").ap()
        return self.dout[name]

    def scr(self, name, shape, dt):
        if name in self.dbg:
            return self.outp(name, shape, dt)
        return self.nc.dram_tensor(name, list(shape), dt).ap()

    def gsb(self, name, shape, dt):
        return self.es.enter_context(self.nc.sbuf_tensor(name, list(shape), dt))

    def bank(self):
        i = self.psi % 8
        self.psi += 1
        return self.ps[i], ("ps", i)

    def phase(self):
        return Phase(self)

    def declare(self):
        inp, outp, scr = self.inp, self.outp, self.scr
        inp("xc", [1024, D]); inp("xl", [1024, D])
        inp("cak", [2, 256, 512]); inp("cav", [2, 256, 512]); inp("cbk", [2, 256, 512]); inp("cbv", [2, 256, 512])
        inp("cck", [2, 256, 128]); inp("ccv", [2, 256, 128])
        inp("cvec", [2, D])
        inp("w_ada", [2, D, 6 * D]); inp("b_ada", [2, 6 * D]); inp("g_mix", [2, D]); inp("w_in", [2, D, IN_COLS])
        inp("diff_lambda", [2, 4, 64]); inp("diff_norm_g", [2, 128]); inp("na_bias", [2, 8, 15, 31]); inp("swa_sink", [2, 8])
        inp("hy_short", [2, 3, 1536]); inp("hy_w1", [2, 33, 64]); inp("hy_b1", [2, 64]); inp("hy_w2", [2, 64, 64])
        inp("hy_b2", [2, 64]); inp("hy_w3", [2, 64, 2048]); inp("hy_freq", [2, 64]); inp("hy_decay", [2, 2048])
        inp("hy_bias", [2, 2, 512]); inp("w_branch", [2, 4, 512, D]); inp("w_out", [2, D, D]); inp("g_mlp", [2, D])
        inp("w_up", [2, D, D_FF]); inp("w_down", [2, D_FF, D]); inp("g_final", [D])
        inp("c_ident", [128, 128]); inp("c_identb", [128, 128], BF16); inp("c_onesb", [128, 128], BF16)
        inp("c_rperm", [128, 128]); inp("c_rcos", [128, 1024]); inp("c_rsin", [128, 1024])
        inp("c_m01", [128, 64]); inp("c_mng", [128, 64]); inp("c_cmask", [128, 2, 128], BF16)
        for L in (256, 1024):
            inp(f"c_z{L}", [33, L]); inp(f"c_nt{L}", [128, L // 128]); inp(f"c_F{L}", [L, 2 * L], BF16); inp(f"c_FT{L}", [2 * L, L], BF16)
        outp("y_c", [1024, D]); outp("y_l", [1024, D])
        outp("nak", [4, 2, 256, 512]); outp("nav", [4, 2, 256, 512]); outp("nbk", [4, 2, 256, 512]); outp("nbv", [4, 2, 256, 512])
        outp("nck", [4, 2, 256, 128]); outp("ncv", [4, 2, 256, 128])
        self.xT = scr("xT", [D, NT], F32)
        self.qkT = [scr(f"qkT{l}", [NQK, NTX], BF16) for l in range(2)]
        self.vtok = [scr(f"vtok{l}", [NTX, NV], BF16) for l in range(2)]
        self.hyT = scr("hyT", [1536, NT], F32)
        self.sgT = scr("sgT", [4 * D, NT], BF16)
        self.oT = scr("oT", [D, NT], BF16)
        self.mT = scr("mT", [D, NT], BF16)
        self.aT = scr("aT", [D_FF, NT], BF16)
        self.nbpad = [scr(f"nbpad{l}", [1, 64 + 3720 + 64], F32) for l in range(2)]

    def globals_(self):
        nc, P, g = self.nc, self.P, self.gsb
        self.ps = [self.es.enter_context(nc.psum_tensor(f"psb{i}", [128, 512], F32)) for i in range(8)]
        self.ident = g("ident", [128, 128], F32); self.identb = g("identb", [128, 128], BF16)
        self.onesb = g("onesb", [128, 128], BF16); self.rperm = g("rperm", [128, 128], F32)
        self.epst = g("epst", [128, 1], F32)
        self.mod = g("mod", [128, 2, 96, 2], F32)
        self.gm = g("gm", [128, 2, 2, 16, 2], F32)
        self.gfin = g("gfin", [128, 16], F32)
        self.zero16 = g("zero16", [128, 16], F32)
        d = self.din
        P.dma(self.ident[:], d["c_ident"], writes=["ident"])
        P.dma(self.identb[:], d["c_identb"], writes=["identb"])
        P.dma(self.onesb[:], d["c_onesb"], writes=["onesb"])
        P.dma(self.rperm[:], d["c_rperm"], writes=["rperm"])
        P.op("dve", lambda e: e.memset(self.epst[:], EPS), writes=["epst"])
        P.op("dve", lambda e: e.memset(self.zero16[:], 0.0), writes=["zero16"])

    def load_T(self, ph, dst, dst_keys, src2d, n):
        P = self.P
        tmp = ph.sb("ltT", [128, 128], F32)
        k = ("ltT", P.uid)
        P.dma(tmp[:n, :], src2d, writes=[k])
        pb, pk = self.bank()
        P.op("pe", lambda e: e.transpose(pb[:, :n], tmp[:n, :], self.ident[:n, :n]), reads=[k, "ident"], writes=[pk])
        P.op("dve", lambda e: e.tensor_copy(out=dst, in_=pb[:, :n]), reads=[pk], writes=dst_keys)

    def phase_xT(self):
        P, d = self.P, self.din
        with self.phase() as ph:
            xin = [ph.sb(f"xin{i}", [128, 4, D], F32) for i in range(2)]
            stg = [ph.sb(f"xst{i}", [128, 512], F32) for i in range(4)]
            si = 0
            for g4 in range(4):
                src = d["xc"] if g4 < 2 else d["xl"]
                r0 = (g4 % 2) * 512
                xt = xin[g4 % 2]
                xk = ("xin", g4 % 2)
                P.dma(xt[:], src[r0:r0 + 512, :].rearrange("(t p) f -> p t f", p=128), writes=[xk])
                for fc in range(KC):
                    pb, pk = self.bank()
                    for t in range(4):
                        P.op("pe", lambda e, pb=pb, xt=xt, t=t, fc=fc: e.transpose(
                            pb[:, t * 128:(t + 1) * 128], xt[:, t, fc * 128:(fc + 1) * 128], self.ident[:]),
                            reads=[xk, "ident"], writes=[pk])
                    s = stg[si % 4]; sk = ("xst", si % 4); si += 1
                    eng = "act" if fc % 2 == 0 else "dve"
                    if eng == "act":
                        P.op("act", lambda e, s=s, pb=pb: e.copy(out=s[:], in_=pb[:]), reads=[pk], writes=[sk])
                    else:
                        P.op("dve", lambda e, s=s, pb=pb: e.tensor_copy(out=s[:], in_=pb[:]), reads=[pk], writes=[sk])
                    P.dma(self.xT[fc * 128:(fc + 1) * 128, g4 * 512:(g4 + 1) * 512], s[:], reads=[sk], writes=[("xT", fc, g4)])

    def phase_adaln(self):
        P, d = self.P, self.din
        with self.phase() as ph:
            cT = ph.sb("cT", [128, 2, 16], F32)
            sT = ph.sb("sT", [128, 16, 2], F32)
            bada = ph.sb("bada", [128, 2, 96], F32)
            gT = ph.sb("gT", [128, 2, 2, 16], F32)
            for v in range(2):
                self.load_T(ph, cT[:, v, :], ["cT"], d["cvec"][v].rearrange("(k p) -> k p", p=128), 16)
            for v in range(2):
                P.op("act", lambda e, v=v: e.activation(out=sT[:, :, v], in_=cT[:, v, :], func=AF.Silu), reads=["cT"], writes=["sT"])
            for l in range(2):
                self.load_T(ph, bada[:, l, :], ["bada"], d["b_ada"][l].rearrange("(k p) -> k p", p=128), 96)
                self.load_T(ph, gT[:, l, 0, :], ["gT"], d["g_mix"][l].rearrange("(k p) -> k p", p=128), 16)
                self.load_T(ph, gT[:, l, 1, :], ["gT"], d["g_mlp"][l].rearrange("(k p) -> k p", p=128), 16)
            self.load_T(ph, self.gfin[:], ["gfin"], d["g_final"].rearrange("(k p) -> k p", p=128), 16)
            st = [ph.sb(f"adst{i}", [128, 4, 512], F32) for i in range(3)]
            m2 = [ph.sb(f"adm{i}", [2, 512], F32) for i in range(2)]
            si = 0
            for l in range(2):
                for cg in range(24):
                    pb, pk = self.bank()
                    for ks in range(4):
                        s = st[si % 3]; sk = ("adst", si % 3); si += 1
                        P.dma(s[:], d["w_ada"][l, ks * 512:(ks + 1) * 512, cg * 512:(cg + 1) * 512].rearrange("(k p) c -> p k c", p=128), writes=[sk])
                        for kk in range(4):
                            kc = ks * 4 + kk
                            P.op("pe", lambda e, pb=pb, s=s, kk=kk, kc=kc: e.matmul(
                                pb[0:2, :], lhsT=sT[:, kc, :], rhs=s[:, kk, :], start=(kc == 0), stop=(kc == 15)), reads=[sk, "sT"], writes=[pk])
                    m = m2[cg % 2]; mk = ("adm", cg % 2)
                    P.op("act", lambda e, m=m, pb=pb: e.copy(out=m[:], in_=pb[0:2, :]), reads=[pk], writes=[mk])
                    pb2, pk2 = self.bank()
                    for j in range(4):
                        P.op("pe", lambda e, pb2=pb2, m=m, j=j: e.transpose(pb2[:, 2 * j:2 * j + 2], m[0:2, j * 128:(j + 1) * 128], self.ident[0:2, 0:2]),
                             reads=[mk, "ident"], writes=[pk2])
                    for v in range(2):
                        P.op("dve", lambda e, pb2=pb2, l=l, cg=cg, v=v: e.tensor_tensor(
                            out=self.mod[:, l, cg * 4:(cg + 1) * 4, v], in0=pb2[:, v:8:2], in1=bada[:, l, cg * 4:(cg + 1) * 4], op=ALU.add),
                            reads=[pk2, "bada"], writes=["mod"])
            for l in range(2):
                for w in range(2):
                    sc0 = 16 if w == 0 else 64
                    for v in range(2):
                        P.op("dve", lambda e, l=l, w=w, v=v, sc0=sc0: e.scalar_tensor_tensor(
                            out=self.gm[:, l, w, :, v], in0=self.mod[:, l, sc0:sc0 + 16, v], scalar=1.0, in1=gT[:, l, w, :],
                            op0=ALU.add, op1=ALU.mult), reads=["mod", "gT"], writes=["gm"], hard=True)

    def modv(self, l, which, ft, grp):
        return self.mod[:, l, which * 16 + ft, grp:grp + 1]

    def norm_tiles(self, ph, scale_ap, shift_ap, emit_out):
        P = self.P
        xs = [ph.sb(f"nx{i}", [128, 16, 512], F32) for i in range(2)]
        sq = [ph.sb(f"nsq{i}", [128, 512], BF16) for i in range(4)]
        rs = [ph.sb(f"nrs{i}", [128, 512], F32) for i in range(2)]
        tmp = [ph.sb(f"ntmp{i}", [128, 512], F32) for i in range(4)]
        xTv = self.xT.rearrange("(k p) t -> p k t", p=128)
        ti = 0
        for tt in range(4):
            grp = 0 if tt < 2 else 1
            x = xs[tt % 2]; xk = ("nx", tt % 2)
            for h in range(2):
                P.dma(x[:, h * 8:(h + 1) * 8, :], xTv[:, h * 8:(h + 1) * 8, tt * 512:(tt + 1) * 512],
                      reads=[("xT", fc, tt) for fc in range(h * 8, h * 8 + 8)], writes=[(xk, h)])
            pb, pk = self.bank()
            for fc in range(KC):
                s = sq[fc % 4]; sk = ("nsq", fc % 4)
                P.op("act", lambda e, s=s, x=x, fc=fc: e.activation(out=s[:], in_=x[:, fc, :], func=AF.Square), reads=[(xk, fc // 8)], writes=[sk])
                P.op("pe", lambda e, pb=pb, s=s, fc=fc: e.matmul(pb[:], lhsT=self.onesb[:], rhs=s[:], start=(fc == 0), stop=(fc == 15)),
                     reads=[sk, "onesb"], writes=[pk])
            r = rs[tt % 2]; rk = ("nrs", tt % 2)
            self.act_pow(r[:], pb[:], -0.5, [pk], [rk], scale=1.0 / D, bias=self.epst[:, 0:1])
            for fc in range(KC):
                t = tmp[ti % 4]; tk = ("ntmp", ti % 4); ti += 1
                P.op("dve", lambda e, t=t, x=x, fc=fc, r=r, grp=grp: e.scalar_tensor_tensor(
                    out=t[:], in0=x[:, fc, :], scalar=scale_ap(fc, grp), in1=r[:], op0=ALU.mult, op1=ALU.mult),
                    reads=[(xk, fc // 8), rk, "gm", "gfin"], writes=[tk])
                emit_out(tt, fc, grp, t, tk)

    def phase_norm_h(self, ph, l, which):
        P = self.P
        hT = self.hT

        def out(tt, fc, grp, t, tk):
            P.op("act", lambda e: e.activation(out=hT[:, fc, tt * 512:(tt + 1) * 512], in_=t[:], func=AF.Identity,
                                                bias=self.modv(l, 0 if which == 0 else 3, fc, grp), scale=1.0),
                 reads=[tk, "mod"], writes=[("hT", fc, tt)])
        with self.phase() as p2:
            self.norm_tiles(p2, lambda fc, grp: self.gm[:, l, which, fc, grp:grp + 1], None, out)

    def mm_group(self, pb, pk, lhs_fn, rhs_fn, nk, reads):
        P = self.P
        for kc in range(nk):
            la, ra = lhs_fn(kc), rhs_fn(kc)
            P.op("pe", lambda e, kc=kc, la=la, ra=ra, pb=pb: e.matmul(pb[:, 0:ra.shape[-1]], lhsT=la, rhs=ra, start=(kc == 0), stop=(kc == nk - 1)),
                 reads=reads(kc), writes=[pk])

    def inproj_units(self):
        units = []
        for nm in ("aq", "ak", "av", "bq", "bk", "bv", "cq"):
            units.append((OFF[nm], 512, nm))
        units.append((OFF["ck"], 256, "ckv"))
        for i in range(3):
            units.append((OFF["hy"] + i * 512, 512, ("hy", i)))
        for i in range(16):
            units.append((OFF["gate"] + i * 512, 512, ("gate", i)))
        return units

    def inproj_ws(self, ph, l):
        win = self.din["w_in"][l]
        ws = WStream(self, ph, [(wsrc(win, c0, nc_), nc_) for (c0, nc_, _) in self.inproj_units()], 16, "in")
        ws.fetch(0)
        return ws

    def up_ws(self, ph, l):
        ws = WStream(self, ph, [(wsrc(self.din["w_up"][l], cg * 512, 512), 512) for cg in range(16)], 16, "wu")
        ws.fetch(0)
        return ws

    def phase_inproj(self, l, ws):
        P, d = self.P, self.din
        hT = self.hT
        with self.phase() as ph:
            units = self.inproj_units()
            stb = [ph.sb(f"ipb{i}", [128, NT], BF16) for i in range(3)]
            stf = [ph.sb(f"ipf{i}", [128, NT], F32) for i in range(2)]
            sta = [ph.sb(f"ipa{i}", [128, 512], F32) for i in range(3)]
            stab = [ph.sb(f"ipab{i}", [128, 512], BF16) for i in range(3)]
            r32 = [ph.sb(f"ipr{i}", [128, 512], F32) for i in range(2)]
            t1 = [ph.sb(f"ipt{i}", [128, 512], F32) for i in range(2)]
            t2 = [ph.sb(f"ipu{i}", [128, 512], F32) for i in range(2)]
            rcos = ph.sb("rcos", [128, 1024], F32); rsin = ph.sb("rsin", [128, 1024], F32)
            P.dma(rcos[:], d["c_rcos"], writes=["rcos"]); P.dma(rsin[:], d["c_rsin"], writes=["rsin"])
            cnt = dict(b=0, f=0, a=0, r=0)
            ev = [0]
            for ui, (c0, ncols, kind) in enumerate(units):
                w, wk = ws.get(ui)
                kname = kind if isinstance(kind, str) else kind[0]
                fm_tiles = []
                if kname in ("aq", "ak", "bq", "bk", "cq"):
                    fm_tiles = [(j, "qk", QK_ROW[kname] + j * 128) for j in range(4)]
                elif kname == "ckv":
                    fm_tiles = [(0, "qk", QK_ROW["ck"])]
                elif kname == "hy":
                    fm_tiles = [(j, "hy", kind[1] * 512 + j * 128) for j in range(4)]
                elif kname == "gate":
                    fm_tiles = [(j, "gate", kind[1] * 512 + j * 128) for j in range(4)]
                roped = kname in ("aq", "ak", "cq", "ckv")
                for (j, okind, row0) in fm_tiles:
                    if okind == "hy":
                        st = stf[cnt["f"] % 2]; stk = ("ipf", cnt["f"] % 2); cnt["f"] += 1
                    else:
                        st = stb[cnt["b"] % 3]; stk = ("ipb", cnt["b"] % 3); cnt["b"] += 1
                    for tt in range(4):
                        pb, pk = self.bank()
                        self.mm_group(pb, pk, lambda kc: w[:, kc, j * 128:(j + 1) * 128], lambda kc: hT[:, kc, tt * 512:(tt + 1) * 512], 16,
                                      lambda kc: [wk(kc), ("hT", kc, tt)])
                        o = st[:, tt * 512:(tt + 1) * 512]
                        if okind == "gate":
                            P.op("act", lambda e, o=o, pb=pb: e.activation(out=o, in_=pb[:], func=AF.Sigmoid), reads=[pk], writes=[stk])
                        elif okind == "qk" and roped and tt >= 2:
                            i = cnt["r"] % 2; cnt["r"] += 1
                            r, rk = r32[i], ("ipr", i)
                            a1, a1k = t1[i], ("ipt", i)
                            a2, a2k = t2[i], ("ipu", i)
                            tp = (tt - 2) * 512
                            P.op("act", lambda e, r=r, pb=pb: e.copy(out=r[:], in_=pb[:]), reads=[pk], writes=[rk])
                            pb2, pk2 = self.bank()
                            P.op("pe", lambda e, pb2=pb2, r=r: e.matmul(pb2[:], lhsT=self.rperm[:], rhs=r[:], start=True, stop=True),
                                 reads=[rk, "rperm"], writes=[pk2])
                            P.op("dve", lambda e, a1=a1, r=r, tp=tp: e.tensor_tensor(out=a1[:], in0=r[:], in1=rcos[:, tp:tp + 512], op=ALU.mult),
                                 reads=[rk, "rcos"], writes=[a1k])
                            P.op("dve", lambda e, a2=a2, pb2=pb2, tp=tp: e.tensor_tensor(out=a2[:], in0=pb2[:], in1=rsin[:, tp:tp + 512], op=ALU.mult),
                                 reads=[pk2, "rsin"], writes=[a2k])
                            P.op("pool", lambda e, o=o, a1=a1, a2=a2: e.tensor_tensor(out=o, in0=a1[:], in1=a2[:], op=ALU.add),
                                 reads=[a1k, a2k], writes=[stk])
                        else:
                            ev[0] += 1
                            if ev[0] % 2 == 0:
                                P.op("act", lambda e, o=o, pb=pb: e.copy(out=o, in_=pb[:]), reads=[pk], writes=[stk])
                            else:
                                P.op("dve", lambda e, o=o, pb=pb: e.tensor_copy(out=o, in_=pb[:]), reads=[pk], writes=[stk])
                    if okind == "qk":
                        dst = self.qkT[l][row0:row0 + 128, 0:NT]; dk = ("qkT", row0 // 128)
                    elif okind == "hy":
                        dst = self.hyT[row0:row0 + 128, :]; dk = ("hyT", row0 // 128)
                    else:
                        dst = self.sgT[row0:row0 + 128, :]; dk = ("sgT", row0 // 128)
                    P.dma(dst, st[:], reads=[stk], writes=[dk])
                if kname in ("ak", "av", "bk", "bv", "ckv"):
                    isv = kname in ("av", "bv", "ckv")
                    ntile = 16 if isv else 8
                    for t128 in range(ntile):
                        pb, pk = self.bank()
                        self.mm_group(pb, pk, lambda kc: hT[:, kc, t128 * 128:(t128 + 1) * 128], lambda kc: w[:, kc, 0:ncols], 16,
                                      lambda kc: [wk(kc), ("hT", kc, t128 // 4)])
                        pbv = pb[:, 0:ncols]
                        if t128 < 8:
                            i = cnt["a"] % 3; cnt["a"] += 1
                            s, sk = sta[i], ("ipa", i)
                            P.op("act", lambda e, s=s, pbv=pbv, ncols=ncols: e.copy(out=s[:, 0:ncols], in_=pbv), reads=[pk], writes=[sk])
                            sq_, pos0 = t128 // 2, (t128 % 2) * 128
                            if kname == "ckv":
                                P.dma(self.dout["nck"][sq_, l, pos0:pos0 + 128, :], s[:, 0:128], reads=[sk])
                                P.dma(self.dout["ncv"][sq_, l, pos0:pos0 + 128, :], s[:, 128:256], reads=[sk])
                            else:
                                P.dma(self.dout["n" + kname][sq_, l, pos0:pos0 + 128, :], s[:, 0:512], reads=[sk])
                        if isv:
                            i = cnt["a"] % 3; cnt["a"] += 1
                            s2, s2k = stab[i], ("ipab", i)
                            if kname == "ckv":
                                P.op("act", lambda e, s2=s2, pb=pb: e.copy(out=s2[:, 0:128], in_=pb[:, 128:256]), reads=[pk], writes=[s2k])
                                P.dma(self.vtok[l][t128 * 128:(t128 + 1) * 128, V_COL["cv"]:V_COL["cv"] + 128], s2[:, 0:128], reads=[s2k],
                                      writes=[("vtok", "cv", t128)])
                            else:
                                P.op("act", lambda e, s2=s2, pb=pb: e.copy(out=s2[:], in_=pb[:]), reads=[pk], writes=[s2k])
                                P.dma(self.vtok[l][t128 * 128:(t128 + 1) * 128, V_COL[kname]:V_COL[kname] + 512], s2[:], reads=[s2k],
                                      writes=[("vtok", kname, t128)])

    def phase_merge(self, l):
        P, d = self.P, self.din
        with self.phase() as ph:
            oT = ph.sb("oTr", [128, 16, NT], BF16)
            oTv = self.oT.rearrange("(k p) t -> p k t", p=128)
            for n in range(4):
                P.dma(oT[:, n * 4:(n + 1) * 4, :], oTv[:, n * 4:(n + 1) * 4, :], reads=[("oT", n)], writes=[("oTr", n)])
            acc = ph.sb("macc", [128, 4, NT], F32)
            units = []
            for fg in range(4):
                for n in range(4):
                    units.append((wsrc(d["w_branch"][l, n], fg * 512, 512), 512))
            ws = WStream(self, ph, units, 4, "wb")
            sg = [ph.sb(f"msg{i}", [128, NT], BF16) for i in range(3)]
            tmp = [ph.sb(f"mtmp{i}", [128, 512], F32) for i in range(4)]
            mst = [ph.sb(f"mst{i}", [128, NT], BF16) for i in range(2)]
            sgi = 0; ti = 0; mi = 0
            for fg in range(4):
                for n in range(4):
                    w, wk = ws.get(fg * 4 + n)
                    for j in range(4):
                        ft = fg * 4 + j
                        s = sg[sgi % 3]; sk = ("msg", sgi % 3); sgi += 1
                        P.dma(s[:], self.sgT[n * D + ft * 128:n * D + (ft + 1) * 128, :], reads=[("sgT", (n * D + ft * 128) // 128)], writes=[sk])
                        if n == 3:
                            ms = mst[mi % 2]; msk = ("mst", mi % 2); mi += 1
                        for tt in range(4):
                            pb, pk = self.bank()
                            self.mm_group(pb, pk, lambda kc: w[:, kc, j * 128:(j + 1) * 128], lambda kc: oT[:, n * 4 + kc, tt * 512:(tt + 1) * 512], 4,
                                          lambda kc: [wk(kc), ("oTr", n)])
                            a = acc[:, j, tt * 512:(tt + 1) * 512]; ak = ("macc", j, tt)
                            ss = s[:, tt * 512:(tt + 1) * 512]
                            if n == 0:
                                P.op("dve", lambda e, a=a, pb=pb, ss=ss: e.tensor_tensor(out=a, in0=pb[:], in1=ss, op=ALU.mult), reads=[pk, sk], writes=[ak])
                            else:
                                t = tmp[ti % 4]; tk = ("mtmp", ti % 4); ti += 1
                                P.op("dve", lambda e, t=t, pb=pb, ss=ss: e.tensor_tensor(out=t[:], in0=pb[:], in1=ss, op=ALU.mult), reads=[pk, sk], writes=[tk])
                                if n < 3:
                                    P.op("pool", lambda e, a=a, t=t: e.tensor_tensor(out=a, in0=a, in1=t[:], op=ALU.add), reads=[ak, tk], writes=[ak])
                                else:
                                    mo = ms[:, tt * 512:(tt + 1) * 512]
                                    P.op("pool", lambda e, mo=mo, a=a, t=t: e.tensor_tensor(out=mo, in0=a, in1=t[:], op=ALU.add), reads=[ak, tk], writes=[msk])
                        if n == 3:
                            P.dma(self.mT[ft * 128:(ft + 1) * 128, :], ms[:], reads=[msk], writes=[("mT", ft)])

    def resid_update(self, l, gwhich, ft, xo, xok, banks, tts):
        P = self.P
        for (pb, pk), tt in zip(banks, tts):
            grp = 0 if tt < 2 else 1
            xs = xo[:, tt * 512:(tt + 1) * 512]
            P.op("dve", lambda e, xs=xs, pb=pb, grp=grp: e.scalar_tensor_tensor(
                out=xs, in0=pb[:], scalar=self.modv(l, gwhich, ft, grp), in1=xs, op0=ALU.mult, op1=ALU.add),
                reads=[pk, (xok, tt), "mod"], writes=[(xok, tt)])

    def phase_wout(self, l):
        P, d = self.P, self.din
        with self.phase() as ph:
            mT = ph.sb("mTr", [128, 16, NT], BF16)
            mTv = self.mT.rearrange("(k p) t -> p k t", p=128)
            for q in range(4):
                P.dma(mT[:, :, q * 512:(q + 1) * 512], mTv[:, :, q * 512:(q + 1) * 512], reads=[("mT", f) for f in range(16)], writes=[("mTr", q)])
            ws = WStream(self, ph, [(wsrc(d["w_out"][l], cg * 512, 512), 512) for cg in range(4)], 16, "wo")
            xo_ = [ph.sb(f"xo{i}", [128, NT], F32) for i in range(3)]
            xi = 0
            for cg in range(4):
                w, wk = ws.get(cg)
                for j in range(4):
                    ft = cg * 4 + j
                    xo = xo_[xi % 3]; xok = ("xo", xi % 3); xi += 1
                    P.dma(xo[:], self.xT[ft * 128:(ft + 1) * 128, :], reads=[("xT", ft, t) for t in range(4)], writes=[(xok, t) for t in range(4)])
                    banks = []
                    for tt in range(4):
                        pb, pk = self.bank()
                        self.mm_group(pb, pk, lambda kc: w[:, kc, j * 128:(j + 1) * 128], lambda kc: mT[:, kc, tt * 512:(tt + 1) * 512], 16,
                                      lambda kc: [wk(kc), ("mTr", tt)])
                        banks.append((pb, pk))
                    self.resid_update(l, 2, ft, xo, xok, banks, range(4))
                    P.dma(self.xT[ft * 128:(ft + 1) * 128, :], xo[:], reads=[(xok, t) for t in range(4)], writes=[("xT", ft, t) for t in range(4)])

    def phase_up(self, l, ws):
        P, d = self.P, self.din
        hT = self.hT
        with self.phase() as ph:
            ast = [ph.sb(f"ast{i}", [128, NT], BF16) for i in range(3)]
            rl = [ph.sb(f"url{i}", [128, 512], F32) for i in range(4)]
            ai = 0; ri = 0
            for cg in range(16):
                w, wk = ws.get(cg)
                for j in range(4):
                    ft = cg * 4 + j
                    a = ast[ai % 3]; ak = ("ast", ai % 3); ai += 1
                    for tt in range(4):
                        pb, pk = self.bank()
                        self.mm_group(pb, pk, lambda kc: w[:, kc, j * 128:(j + 1) * 128], lambda kc: hT[:, kc, tt * 512:(tt + 1) * 512], 16,
                                      lambda kc: [wk(kc), ("hT", kc, tt)])
                        r = rl[ri % 4]; rk = ("url", ri % 4); ri += 1
                        P.op("dve", lambda e, r=r, pb=pb: e.tensor_scalar(out=r[:], in0=pb[:], scalar1=0.0, scalar2=None, op0=ALU.max), reads=[pk], writes=[rk])
                        P.op("act", lambda e, a=a, r=r, tt=tt: e.activation(out=a[:, tt * 512:(tt + 1) * 512], in_=r[:], func=AF.Square), reads=[rk], writes=[ak])
                    P.dma(self.aT[ft * 128:(ft + 1) * 128, :], a[:], reads=[ak], writes=[("aT", ft)])

    def phase_down(self, l):
        P, d = self.P, self.din
        with self.phase() as ph:
            ws = WStream(self, ph, [(wsrc(d["w_down"][l], g * 512, 512), 512) for g in range(4)], 64, "wd")
            at_ = [ph.sb(f"dat{i}", [128, 1024], BF16) for i in range(6)]
            xo_ = [ph.sb(f"dxo{i}", [128, 1024], F32) for i in range(4)]
            ati = 0; xi = 0
            for g in range(4):
                w, wk = ws.get(g, prefetch=False)
                for pair in range(2):
                    banks = [[self.bank() for t2 in range(2)] for j in range(4)]
                    for kc in range(64):
                        if kc % 8 == 4:
                            ws.fetch(g + 1, 1)
                        at = at_[ati % 6]; atk = ("dat", ati % 6); ati += 1
                        P.dma(at[:], self.aT[kc * 128:(kc + 1) * 128, pair * 1024:(pair + 1) * 1024], reads=[("aT", kc)], writes=[atk])
                        for j in range(4):
                            for t2 in range(2):
                                pb, pk = banks[j][t2]
                                P.op("pe", lambda e, pb=pb, w=w, kc=kc, j=j, at=at, t2=t2: e.matmul(
                                    pb[:], lhsT=w[:, kc, j * 128:(j + 1) * 128], rhs=at[:, t2 * 512:(t2 + 1) * 512], start=(kc == 0), stop=(kc == 63)),
                                    reads=[wk(kc), atk], writes=[pk])
                    for j in range(4):
                        ft = g * 4 + j
                        xo = xo_[xi % 4]; xok = ("dxo", xi % 4); xi += 1
                        tts = [pair * 2, pair * 2 + 1]
                        P.dma(xo[:], self.xT[ft * 128:(ft + 1) * 128, pair * 1024:(pair + 1) * 1024], reads=[("xT", ft, t) for t in tts],
                              writes=[(xok, t) for t in tts])
                        for t2 in range(2):
                            pb, pk = banks[j][t2]
                            tt = tts[t2]
                            grp = 0 if tt < 2 else 1
                            xs = xo[:, t2 * 512:(t2 + 1) * 512]
                            P.op("dve", lambda e, xs=xs, pb=pb, grp=grp, ft=ft: e.scalar_tensor_tensor(
                                out=xs, in0=pb[:], scalar=self.modv(l, 5, ft, grp), in1=xs, op0=ALU.mult, op1=ALU.add),
                                reads=[pk, (xok, tt), "mod"], writes=[(xok, tt)])
                        P.dma(self.xT[ft * 128:(ft + 1) * 128, pair * 1024:(pair + 1) * 1024], xo[:], reads=[(xok, t) for t in tts],
                              writes=[("xT", ft, t) for t in tts])

    def phase_final(self):
        P = self.P
        with self.phase() as ph:
            yst = [ph.sb(f"yst{i}", [128, D], F32) for i in range(2)]
            yT = [ph.sb(f"yT{i}", [128, 16, 512], F32) for i in range(2)]
            cnt = [0]

            def out(tt, fc, grp, t, tk):
                y = yT[tt % 2]
                P.op("act", lambda e: e.copy(out=y[:, fc, :], in_=t[:]), reads=[tk], writes=[("yT", tt % 2, fc)])
                if fc == 15:
                    for t128 in range(4):
                        ys = yst[cnt[0] % 2]; ysk = ("yst", cnt[0] % 2); cnt[0] += 1
                        for q in range(4):
                            pb, pk = self.bank()
                            for f4 in range(4):
                                f = q * 4 + f4
                                P.op("pe", lambda e, pb=pb, y=y, f=f, f4=f4, t128=t128: e.transpose(
                                    pb[:, f4 * 128:(f4 + 1) * 128], y[:, f, t128 * 128:(t128 + 1) * 128], self.ident[:]),
                                    reads=[("yT", tt % 2, f), "ident"], writes=[pk])
                            if q % 2 == 0:
                                P.op("act", lambda e, ys=ys, pb=pb, q=q: e.copy(out=ys[:, q * 512:(q + 1) * 512], in_=pb[:]), reads=[pk], writes=[ysk])
                            else:
                                P.op("dve", lambda e, ys=ys, pb=pb, q=q: e.tensor_copy(out=ys[:, q * 512:(q + 1) * 512], in_=pb[:]), reads=[pk], writes=[ysk])
                        tok = tt * 512 + t128 * 128
                        dst = self.dout["y_c"][tok:tok + 128, :] if tok < 1024 else self.dout["y_l"][tok - 1024:tok - 1024 + 128, :]
                        P.dma(dst, ys[:], reads=[ysk])
            self.norm_tiles(ph, lambda fc, grp: self.gfin[:, fc:fc + 1], None, out)

    def build(self, stages=None):
        def on(name):
            return stages is None or name in stages
        self.declare()
        self.globals_()
        self.P.barrier()
        if on("xT"):
            self.phase_xT()
        if on("adaln"):
            self.phase_adaln()
        if on("mixers"):
            self.phase_cache()
            self.phase_lam()
        for l in range(DEPTH):
            if stages is not None and f"L{l}" not in stages:
                continue
            with self.phase() as ph:
                self.hT = ph.sb("hT", [128, 16, NT], BF16)
                wsi = self.inproj_ws(ph, l)
                if on("norm1"):
                    self.phase_norm_h(ph, l, 0)
                if "hT" in self.dbg and l == 0:
                    self.P.dma(self.outp("hT", [128, 16, NT], BF16), self.hT[:], reads=[("hT", fc, tt) for fc in range(16) for tt in range(4)])
                if on("inproj"):
                    self.phase_inproj(l, wsi)
            if on("mixers"):
                self.phase_mixers(l)
            if on("merge"):
                self.phase_merge(l)
            if on("wout"):
                self.phase_wout(l)
            with self.phase() as ph:
                self.hT = ph.sb("hT", [128, 16, NT], BF16)
                wsu = self.up_ws(ph, l)
                if on("norm2"):
                    self.phase_norm_h(ph, l, 1)
                if on("up"):
                    self.phase_up(l, wsu)
            if on("down"):
                self.phase_down(l)
        if on("final"):
            self.phase_final()
        if "mod" in self.dbg:
            self.P.dma(self.outp("modo", [128, 2 * 96 * 2], F32), self.mod[:].rearrange("p a b c -> p (a b c)"), reads=["mod"])
        self.P.barrier()
        self.P.emit()
        self.es.close()
        return self.nc

    def bank_i(self, i):
        return self.ps[i], ("ps", i)

    def phase_cache(self):
        P, d = self.P, self.din
        with self.phase() as ph:
            ci = 0
            for l in range(2):
                for (kn, vn, w, krow, vcol) in (("cak", "cav", 512, QK_ROW["ak"], V_COL["av"]), ("cbk", "cbv", 512, QK_ROW["bk"], V_COL["bv"]),
                                               ("cck", "ccv", 128, QK_ROW["ck"], V_COL["cv"])):
                    kf = ph.sb("ckf", [128, 2, 512], F32); kb = ph.sb("ckb", [128, 2, 512], BF16)
                    vf = ph.sb("cvf", [128, 2, 512], F32); vb = ph.sb("cvb", [128, 2, 512], BF16)
                    kT = ph.sb("ckT", [128, 4, 256], BF16)
                    u = P.uid; P.uid += 1
                    P.dma(kf[:, :, :w], d[kn][l].rearrange("(c p) f -> p c f", p=128), writes=[("ckf", u)])
                    P.dma(vf[:, :, :w], d[vn][l].rearrange("(c p) f -> p c f", p=128), writes=[("cvf", u)])
                    P.op("dve", lambda e, kb=kb, kf=kf, w=w: e.tensor_copy(out=kb[:, :, :w], in_=kf[:, :, :w]), reads=[("ckf", u)], writes=[("ckb", u)])
                    P.op("pool", lambda e, vb=vb, vf=vf, w=w: e.tensor_copy(out=vb[:, :, :w], in_=vf[:, :, :w]), reads=[("cvf", u)], writes=[("cvb", u)])
                    P.dma(self.vtok[l][NT:NTX, vcol:vcol + w].rearrange("(c p) f -> p c f", p=128), vb[:, :, :w], reads=[("cvb", u)], writes=[("vtokc", l, vn)])
                    for fb in range(w // 128):
                        pb, pk = self.bank()
                        pbb = pb.bitcast(BF16)
                        for c in range(2):
                            P.op("pe", lambda e, pbb=pbb, kb=kb, c=c, fb=fb: e.transpose(pbb[:, c * 128:(c + 1) * 128], kb[:, c, fb * 128:(fb + 1) * 128], self.identb[:]),
                                 reads=[("ckb", u), "identb"], writes=[pk])
                        P.op("act", lambda e, kT=kT, pbb=pbb, fb=fb: e.copy(out=kT[:, fb, :], in_=pbb[:, 0:256]), reads=[pk], writes=[("ckT", u, fb)])
                        P.dma(self.qkT[l][krow + fb * 128:krow + (fb + 1) * 128, NT:NTX], kT[:, fb, :], reads=[("ckT", u, fb)], writes=[("qkTc", l)])

    def phase_lam(self):
        P, d = self.P, self.din
        self.lam = self.gsb("lam", [128, 2], F32)
        self.gsc = self.gsb("gsc", [128, 2], F32)
        self.esink = self.gsb("esink", [128, 2, 8], F32)
        with self.phase() as ph:
            dl = ph.sb("dl", [128, 2, 4, 64], F32)
            pr = ph.sb("dlp", [128, 2, 2, 64], F32)
            sm = ph.sb("dls", [128, 2, 2], F32)
            ex = ph.sb("dle", [128, 2, 2], F32)
            gn = ph.sb("dgn", [128, 2], F32)
            sk = ph.sb("ssk", [128, 2, 8], F32)
            for l in range(2):
                P.dma(dl[:, l], d["diff_lambda"][l].partition_broadcast(128), writes=["dl"])
                P.dma(gn[:, l:l + 1], d["diff_norm_g"][l].rearrange("(p o) -> p o", o=1), writes=["dgn"])
                P.dma(sk[:, l], d["swa_sink"][l].partition_broadcast(128), writes=["ssk"])
            for l in range(2):
                P.op("dve", lambda e, l=l: e.tensor_tensor(out=pr[:, l], in0=dl[:, l, 0:4:2, :], in1=dl[:, l, 1:4:2, :], op=ALU.mult), reads=["dl"], writes=["dlp"])
                P.op("dve", lambda e, l=l: e.tensor_reduce(out=sm[:, l], in_=pr[:, l], axis=mybir.AxisListType.X, op=ALU.add), reads=["dlp"], writes=["dls"], hard=True)
            P.op("act", lambda e: e.activation(out=ex[:], in_=sm[:], func=AF.Exp), reads=["dls"], writes=["dle"])
            P.op("act", lambda e: e.activation(out=self.esink[:], in_=sk[:], func=AF.Exp), reads=["ssk"], writes=["esink"])
            for l in range(2):
                lam_init = 0.8 - 0.6 * math.exp(-0.3 * l)
                P.op("dve", lambda e, l=l, lam_init=lam_init: e.scalar_tensor_tensor(
                    out=self.lam[:, l:l + 1], in0=ex[:, l, 0:1], scalar=lam_init, in1=ex[:, l, 1:2], op0=ALU.add, op1=ALU.subtract),
                    reads=["dle"], writes=["lam"])
                P.op("dve", lambda e, l=l, lam_init=lam_init: e.tensor_scalar(
                    out=self.gsc[:, l:l + 1], in0=gn[:, l:l + 1], scalar1=1.0 - lam_init, scalar2=None, op0=ALU.mult), reads=["dgn"], writes=["gsc"])

    def softmax_accum(self, bufs, acc, ncols, dvp, chunks, look=2):
        P = self.P
        (bo, bok), (bd, bdk) = acc
        nch = len(chunks)

        def stage_s(ci):
            ch = chunks[ci]
            bs, bsk = self.bank_i(4 + self.sbank % 4); self.sbank += 1
            ns = len(ch["s"])
            for mi, (c0, n, la, ra) in enumerate(ch["s"]):
                P.op("pe", lambda e, bs=bs, c0=c0, n=n, la=la, ra=ra, mi=mi, ns=ns: e.matmul(bs[:, c0:c0 + n], lhsT=la, rhs=ra, start=(mi == 0), stop=(mi == ns - 1),
                                                                                 skip_group_check=True), reads=ch["sreads"], writes=[bsk])
            i = bufs["pti"] % len(bufs["pt"]); bufs["pti"] += 1
            pt, ptk = bufs["pt"][i], ("pt", i)
            if ch.get("bias") is not None:
                j = bufs["tmi"] % len(bufs["tm"]); bufs["tmi"] += 1
                tm, tmk = bufs["tm"][j], ("ptm", j)
                P.op("dve", lambda e, tm=tm, bs=bs, b=ch["bias"]: e.scalar_tensor_tensor(out=tm[:, :ncols], in0=bs[:, :ncols], scalar=0.125, in1=b, op0=ALU.mult, op1=ALU.add),
                     reads=[bsk] + ch["breads"], writes=[tmk])
                P.op("act", lambda e, pt=pt, tm=tm: e.activation(out=pt[:, :ncols], in_=tm[:, :ncols], func=AF.Exp), reads=[tmk], writes=[ptk])
            else:
                P.op("act", lambda e, pt=pt, bs=bs: e.activation(out=pt[:, :ncols], in_=bs[:, :ncols], func=AF.Exp, scale=0.125), reads=[bsk], writes=[ptk])
            if ch.get("mask") is not None:
                mk = ch["mask"]
                P.op("pool", lambda e, pt=pt, mk=mk: e.tensor_tensor(out=pt[:, :ncols].rearrange("p (r q) -> p r q", q=128), in0=pt[:, :ncols].rearrange("p (r q) -> p r q", q=128),
                                                                   in1=mk, op=ALU.mult), reads=[ptk, "cmask"], writes=[ptk])
            return pt, ptk

        def stage_pv(ci, pt, ptk):
            ch = chunks[ci]
            for mi, (c0, n, va) in enumerate(ch["pv"]):
                P.op("pe", lambda e, c0=c0, n=n, va=va, pt=pt, ci=ci, mi=mi: e.matmul(
                    bo[0:dvp, c0:c0 + n], lhsT=va, rhs=pt[:, c0:c0 + n], start=(ci == 0 and mi == 0), stop=(ci == nch - 1), skip_group_check=True),
                    reads=[ptk] + ch["vreads"], writes=[bok])
            P.op("pe", lambda e, pt=pt, ci=ci: e.matmul(bd[0:dvp, 0:ncols], lhsT=self.onesb[:, 0:dvp], rhs=pt[:, :ncols], start=(ci == 0), stop=(ci == nch - 1)),
                 reads=[ptk, "onesb"], writes=[bdk])

        pend = {}
        for ci in range(nch):
            pend[ci] = stage_s(ci)
            if ci >= look:
                stage_pv(ci - look, *pend.pop(ci - look))
        for ci in sorted(pend):
            stage_pv(ci, *pend[ci])

    def act_pow(self, out, in_, power, reads, writes, scale=1.0, bias=None, w2=None):
        P = self.P
        if bias is None:
            P.op("act", lambda e: e.activation(out=out, in_=in_, func=AF.Ln, scale=scale), reads=reads, writes=writes)
        else:
            P.op("act", lambda e: e.activation(out=out, in_=in_, func=AF.Ln, scale=scale, bias=bias), reads=reads + ["epst"], writes=writes)
        P.op("act", lambda e: e.activation(out=out, in_=out, func=AF.Exp, scale=power), reads=writes, writes=writes)

    def attn_bufs(self, ph):
        return dict(pt=[ph.sb(f"pt{i}", [128, 512], BF16) for i in range(4)], pti=0,
                    tm=[ph.sb(f"ptm{i}", [128, 512], F32) for i in range(3)], tmi=0)

    def load_qkv(self, ph, l, qrow, nqh, krow, nkh, vcol, vw, tag):
        P = self.P
        Q = ph.sb("Q" + tag, [64, nqh, NT], BF16); Kt = ph.sb("K" + tag, [64, nkh, NTX], BF16); V = ph.sb("V" + tag, [128, 18, vw], BF16)
        qr = [("qkT", (qrow // 128) + i) for i in range((nqh * 64 + 127) // 128)]
        kr = [("qkT", (krow // 128) + i) for i in range((nkh * 64 + 127) // 128)]
        nm = {0: "av", 512: "bv", 1024: "cv"}[vcol]
        for part, (t0, t1, k1, c0, c1) in (("c", (0, 1024, 1024, 0, 8)), ("l", (1024, NT, NTX, 8, 18))):
            for h0 in range(0, nqh, 4):
                P.dma(Q[:, h0:h0 + 4, t0:t1], self.qkT[l][qrow + h0 * 64:qrow + (h0 + 4) * 64, t0:t1].rearrange("(h d) t -> d h t", d=64), reads=qr, writes=[("Q" + tag, part, h0)])
            for h0 in range(0, nkh, 4):
                h1 = min(nkh, h0 + 4)
                P.dma(Kt[:, h0:h1, t0:k1], self.qkT[l][krow + h0 * 64:krow + h1 * 64, t0:k1].rearrange("(h d) t -> d h t", d=64), reads=kr + [("qkTc", l)], writes=[("K" + tag, part, h0)])
            P.dma(V[:, c0:c1, :], self.vtok[l][c0 * 128:c1 * 128, vcol:vcol + vw].rearrange("(c p) f -> p c f", p=128),
                  reads=[("vtok", nm, t) for t in range(16)], writes=[("V" + tag, part)])
        return Q, Kt, V

    def phase_attn_a(self, l):
        P = self.P
        with self.phase() as ph:
            Q, Kt, V = self.load_qkv(ph, l, QK_ROW["aq"], 8, QK_ROW["ak"], 8, V_COL["av"], 512, "a")
            bufs = self.attn_bufs(ph)
            NB = 2
            rdA = [[ph.sb(f"ard{i}_{b}", [128, 512], F32) for i in range(2)] for b in range(NB)]
            t0A = [ph.sb(f"at0_{b}", [128, 512], F32) for b in range(NB)]; t1A = [ph.sb(f"at1_{b}", [128, 512], F32) for b in range(NB)]
            osqA = [ph.sb(f"aosq{b}", [128, 512], BF16) for b in range(NB)]; rrA = [ph.sb(f"arr{b}", [128, 512], F32) for b in range(NB)]
            ost = [ph.sb(f"aost{i}", [128, 512], BF16) for i in range(2)]
            oi = 0
            groups = [(s * 256, 256, [s * 256, s * 256 + 128]) for s in range(4)]
            groups += [(1024 + i * 512, 512, [1024 + k * 128 for k in range(10)]) for i in range(2)]
            ui = 0
            for (q0, nq, kst) in groups:
                for h in range(4):
                    accs = []
                    for c in range(2):
                        acc = (self.bank_i(2 * c), self.bank_i(2 * c + 1))
                        ch_ = c * 4 + h
                        pt_ = "c" if q0 < 1024 else "l"
                        chunks = [dict(s=[(0, nq, Kt[:, ch_, k0:k0 + 128], Q[:, ch_, q0:q0 + nq])], sreads=[("Qa", pt_, (ch_ // 4) * 4), ("Ka", pt_, (ch_ // 4) * 4)],
                                       pv=[(0, nq, V[:, k0 // 128, h * 128:(h + 1) * 128])], vreads=[("Va", pt_)]) for k0 in kst]
                        self.softmax_accum(bufs, acc, nq, 128, chunks)
                        accs.append(acc)
                    (bo0, bok0), (bd0, bdk0) = accs[0]
                    (bo1, bok1), (bd1, bdk1) = accs[1]
                    b = ui % NB; ui += 1
                    rd, t0, t1, osq, rr = rdA[b], t0A[b], t1A[b], osqA[b], rrA[b]
                    k = lambda nm: (nm, b)
                    self.act_pow(rd[0][:, :nq], bd0[:, :nq], -1.0, [bdk0], [k("ard0")])
                    self.act_pow(rd[1][:, :nq], bd1[:, :nq], -1.0, [bdk1], [k("ard1")])
                    P.op("dve", lambda e, nq=nq, t0=t0, rd=rd: e.tensor_tensor(out=t0[:, :nq], in0=bo0[:, :nq], in1=rd[0][:, :nq], op=ALU.mult), reads=[bok0, k("ard0")], writes=[k("at0")])
                    P.op("dve", lambda e, nq=nq, t1=t1, rd=rd: e.scalar_tensor_tensor(out=t1[:, :nq], in0=bo1[:, :nq], scalar=self.lam[:, l:l + 1], in1=rd[1][:, :nq], op0=ALU.mult, op1=ALU.mult),
                         reads=[bok1, k("ard1"), "lam"], writes=[k("at1")])
                    P.op("pool", lambda e, nq=nq, t0=t0, t1=t1: e.tensor_tensor(out=t0[:, :nq], in0=t0[:, :nq], in1=t1[:, :nq], op=ALU.subtract), reads=[k("at0"), k("at1")], writes=[k("at0")])
                    P.op("act", lambda e, nq=nq, osq=osq, t0=t0: e.activation(out=osq[:, :nq], in_=t0[:, :nq], func=AF.Square), reads=[k("at0")], writes=[k("aosq")])
                    bs, bsk = self.bank_i(4 + self.sbank % 4); self.sbank += 1
                    P.op("pe", lambda e, bs=bs, nq=nq, osq=osq: e.matmul(bs[:, :nq], lhsT=self.onesb[:], rhs=osq[:, :nq], start=True, stop=True), reads=[k("aosq"), "onesb"], writes=[bsk])
                    self.act_pow(rr[:, :nq], bs[:, :nq], -0.5, [bsk], [k("arr")], scale=1.0 / 128, bias=self.epst[:, 0:1])
                    o = ost[oi % 2]; ok_ = ("aost", oi % 2); oi += 1
                    P.op("dve", lambda e, o=o, nq=nq, t0=t0, rr=rr: e.scalar_tensor_tensor(out=o[:, :nq], in0=t0[:, :nq], scalar=self.gsc[:, l:l + 1], in1=rr[:, :nq], op0=ALU.mult, op1=ALU.mult),
                         reads=[k("at0"), k("arr"), "gsc"], writes=[ok_])
                    P.dma(self.oT[h * 128:(h + 1) * 128, q0:q0 + nq], o[:, :nq], reads=[ok_], writes=[("oT", 0)])

    def phase_attn_bc_ctx(self, l, which, Q, Kt, V, bufs, ph):
        P = self.P
        rd = ph.sb("brd", [64, 512], F32)
        ost = [ph.sb(f"bost{i}", [64, 512], BF16) for i in range(2)]
        oi = 0
        tag = "b" if which == "b" else "c"
        row0 = 512 if which == "b" else 1024
        for s in range(4):
            q0 = s * 256
            for h in range(8):
                kh = h if which == "b" else h // 4
                acc = (self.bank_i((oi % 2) * 2), self.bank_i((oi % 2) * 2 + 1))
                chunks = [dict(s=[(0, 256, Kt[:, kh, k0:k0 + 128], Q[:, h, q0:q0 + 256])], sreads=[("Q" + tag, "c", (h // 4) * 4), ("K" + tag, "c", (kh // 4) * 4)],
                               pv=[(0, 256, V[:, k0 // 128, kh * 64:(kh + 1) * 64])], vreads=[("V" + tag, "c")]) for k0 in (q0, q0 + 128)]
                self.softmax_accum(bufs, acc, 256, 64, chunks)
                (bo, bok), (bd, bdk) = acc
                if which == "c":
                    P.op("dve", lambda e, bd=bd, h=h: e.tensor_scalar(out=rd[:, :256], in0=bd[0:64, :256], scalar1=self.esink[0:64, l, h:h + 1], scalar2=None, op0=ALU.add),
                         reads=[bdk, "esink"], writes=["brd"])
                    self.act_pow(rd[:, :256], rd[:, :256], -1.0, ["brd"], ["brd"])
                else:
                    self.act_pow(rd[:, :256], bd[0:64, :256], -1.0, [bdk], ["brd"])
                o = ost[oi % 2]; ok_ = ("bost", oi % 2); oi += 1
                P.op("dve", lambda e, o=o, bo=bo: e.tensor_tensor(out=o[:, :256], in0=bo[0:64, :256], in1=rd[:, :256], op=ALU.mult), reads=[bok, "brd"], writes=[ok_])
                P.dma(self.oT[row0 + h * 64:row0 + (h + 1) * 64, q0:q0 + 256], o[:, :256], reads=[ok_], writes=[("oT", 1 if which == "b" else 2)])

    def phase_attn_b(self, l):
        P, d = self.P, self.din
        with self.phase() as ph:
            Q, Kt, V = self.load_qkv(ph, l, QK_ROW["bq"], 8, QK_ROW["bk"], 8, V_COL["bv"], 512, "b")
            bufs = self.attn_bufs(ph)
            self.phase_attn_bc_ctx(l, "b", Q, Kt, V, bufs, ph)
            braw = ph.sb("braw", [128, 14, 8, 64], F32); BT = ph.sb("BT", [128, 14, 8, 64], F32)
            m01 = ph.sb("m01", [128, 64], F32); mng = ph.sb("mng", [128, 64], F32); zt = ph.sb("bzt", [1, 64], F32)
            P.dma(m01[:], d["c_m01"], writes=["m01"]); P.dma(mng[:], d["c_mng"], writes=["mng"])
            P.op("dve", lambda e: e.memset(zt[:], 0.0), writes=["bzt"])
            nb = self.nbpad[l]
            P.dma(nb[0:1, 0:64], zt[:], reads=["bzt"], writes=["nbp0"])
            P.dma(nb[0:1, 64 + 3720:64 + 3720 + 64], zt[:], reads=["bzt"], writes=["nbp1"])
            P.dma(nb[0:1, 64:64 + 3720], d["na_bias"][l].rearrange("(o h) a b -> o (h a b)", o=1), writes=["nbp2"])
            for jj in range(2):
                for h in range(8):
                    src = bass.AP(nb.tensor, 64 + jj * 31 - 48 + h * 465, [[1, 64], [31, 14], [1, 64]])
                    P.dma(braw[jj * 64:(jj + 1) * 64, :, h, :], src, reads=["nbp0", "nbp1", "nbp2"], writes=[("braw", jj, h)])
            bt3 = BT[:].rearrange("p a h q -> p (a h) q")
            P.op("dve", lambda e: e.tensor_tensor(out=bt3, in0=braw[:].rearrange("p a h q -> p (a h) q")[:, :, ::-1],
                                                  in1=m01[:, None, :].broadcast_to([128, 112, 64]), op=ALU.mult), reads=[("braw", jj, h) for jj in range(2) for h in range(8)] + ["m01"], writes=["BT"])
            P.op("dve", lambda e: e.tensor_tensor(out=bt3, in0=bt3, in1=mng[:, None, :].broadcast_to([128, 112, 64]), op=ALU.add), reads=["BT", "mng"], writes=["BT"])
            OB = ph.sb("OB", [64, 8, 1024], BF16)
            rd = ph.sb("blrd", [64, 512], F32)
            V2 = ph.sb("Vb2", [128, 7, 512], BF16)
            P.dma(V2[:], self.vtok[l][1088:1088 + 7 * 128, 512:1024].rearrange("(c p) f -> p c f", p=128),
                  reads=[("vtok", "bv", t) for t in range(16)], writes=["Vb2"])
            for r in range(16):
                r0 = min(max(r - 4, 0), 8)
                q0 = 1024 + r * 64
                acc = (self.bank_i((r % 2) * 2), self.bank_i((r % 2) * 2 + 1))
                chunks = []
                for m in range(6):
                    if m < 4:
                        k0 = 1024 + (r0 + 2 * m) * 64
                        af = r0 + 2 * m - r + 7
                        bias = BT[:, af].rearrange("p h q -> p (h q)")
                    else:
                        k0 = NT + (m - 4) * 128
                        bias = None
                    Vc_ = V[:, k0 // 128] if k0 % 128 == 0 else V2[:, (k0 - 1088) // 128]
                    chunks.append(dict(s=[(h * 64, 64, Kt[:, h, k0:k0 + 128], Q[:, h, q0:q0 + 64]) for h in range(8)], sreads=[("Qb", "l", 0), ("Qb", "l", 4), ("Kb", "l", 0), ("Kb", "l", 4)],
                                       pv=[(h * 64, 64, Vc_[:, h * 64:(h + 1) * 64]) for h in range(8)], vreads=[("Vb", "l"), "Vb2"], bias=bias, breads=["BT"]))
                self.softmax_accum(bufs, acc, 512, 64, chunks)
                (bo, bok), (bd, bdk) = acc
                self.act_pow(rd[:], bd[0:64, :], -1.0, [bdk], ["blrd"])
                P.op("dve", lambda e, bo=bo, r=r: e.tensor_tensor(out=OB[:, :, r * 64:(r + 1) * 64], in0=bo[0:64, :].rearrange("p (h q) -> p h q", q=64),
                                                                  in1=rd[:].rearrange("p (h q) -> p h q", q=64), op=ALU.mult), reads=[bok, "blrd"], writes=["OB"])
            P.dma(self.oT[512:1024, 1024:2048].rearrange("(h d) t -> d h t", d=64), OB[:], reads=["OB"], writes=[("oT", 1)])

    def phase_attn_c(self, l):
        P, d = self.P, self.din
        with self.phase() as ph:
            Q, Kt, V = self.load_qkv(ph, l, QK_ROW["cq"], 8, QK_ROW["ck"], 2, V_COL["cv"], 128, "c")
            bufs = self.attn_bufs(ph)
            self.phase_attn_bc_ctx(l, "c", Q, Kt, V, bufs, ph)
            cm = ph.sb("cmask", [128, 2, 128], BF16)
            P.dma(cm[:], d["c_cmask"], writes=["cmask"])
            OC = ph.sb("OC", [64, 8, 1024], BF16)
            rd = ph.sb("clrd", [64, 512], F32)
            it = 0
            for nbk in range(8):
                q0 = 1024 + nbk * 128
                for g in range(2):
                    acc = (self.bank_i((it % 2) * 2), self.bank_i((it % 2) * 2 + 1)); it += 1
                    chunks = []
                    for (kb, mi) in ((nbk - 1, 0), (nbk, None), (nbk + 1, 1), (8, None), (9, None)):
                        if kb < 0 or (kb > 7 and mi is not None):
                            continue
                        k0 = 1024 + kb * 128
                        mask = None if mi is None else cm[:, mi:mi + 1, :].broadcast_to([128, 4, 128])
                        chunks.append(dict(s=[(rq * 128, 128, Kt[:, g, k0:k0 + 128], Q[:, g * 4 + rq, q0:q0 + 128]) for rq in range(4)], sreads=[("Qc", "l", g * 4), ("Kc", "l", 0)],
                                           pv=[(0, 512, V[:, k0 // 128, g * 64:(g + 1) * 64])], vreads=[("Vc", "l")], mask=mask))
                    self.softmax_accum(bufs, acc, 512, 64, chunks)
                    (bo, bok), (bd, bdk) = acc
                    P.op("dve", lambda e, bd=bd, g=g: e.tensor_tensor(out=rd[:].rearrange("p (r q) -> p r q", q=128), in0=bd[0:64, :].rearrange("p (r q) -> p r q", q=128),
                                                                 in1=self.esink[0:64, l, g * 4:(g + 1) * 4].unsqueeze(2).broadcast_to([64, 4, 128]), op=ALU.add),
                         reads=[bdk, "esink"], writes=["clrd"])
                    self.act_pow(rd[:], rd[:], -1.0, ["clrd"], ["clrd"])
                    P.op("dve", lambda e, bo=bo, g=g, nbk=nbk: e.tensor_tensor(out=OC[:, g * 4:(g + 1) * 4, nbk * 128:(nbk + 1) * 128], in0=bo[0:64, :].rearrange("p (r q) -> p r q", q=128),
                                                                       in1=rd[:].rearrange("p (r q) -> p r q", q=128), op=ALU.mult), reads=[bok, "clrd"], writes=["OC"])
            P.dma(self.oT[1024:1536, 1024:2048].rearrange("(h d) t -> d h t", d=64), OC[:], reads=["OC"], writes=[("oT", 2)])

    def hy_ffn(self, ph, l, L, h2T, cst):
        P, d = self.P, self.din
        zT = ph.sb("hzT", [33, 1024], F32); h1T = ph.sb("hh1", [64, 1024], F32)
        arg = ph.sb("harg", [64, 512], F32); wr = ph.sb("hwr", [64, 512], F32)
        u = P.uid; P.uid += 1
        P.dma(zT[:, :L], d[f"c_z{L}"], writes=[("hzT", u)])
        for stage in range(2):
            src = zT if stage == 0 else h1T
            dst = h1T if stage == 0 else h2T
            K = 33 if stage == 0 else 64
            wt = cst["w1"] if stage == 0 else cst["w2"]
            fb = cst["fb"][:, stage:stage + 1]
            for n0 in range(0, L, 512):
                n = min(512, L - n0)
                pb, pk = self.bank()
                P.op("pe", lambda e, pb=pb, wt=wt, K=K, src=src, n0=n0, n=n: e.matmul(pb[0:64, :n], lhsT=wt[0:K, :], rhs=src[0:K, n0:n0 + n], start=True, stop=True),
                     reads=["hyc", ("hzT", u), ("hh", u, 0)], writes=[pk])
                P.op("dve", lambda e, pb=pb, n=n, fb=fb: e.tensor_scalar(out=arg[:, :n], in0=pb[0:64, :n], scalar1=cst["freq"][:, 0:1], scalar2=fb, op0=ALU.mult, op1=ALU.add),
                     reads=[pk, "hyc"], writes=["harg"], hard=True)
                P.op("dve", lambda e, n=n: e.tensor_scalar(out=wr[:, :n], in0=arg[:, :n], scalar1=math.pi, scalar2=-2 * math.pi, op0=ALU.is_gt, op1=ALU.mult), reads=["harg"], writes=["hwr"])
                P.op("dve", lambda e, n=n: e.tensor_tensor(out=wr[:, :n], in0=wr[:, :n], in1=arg[:, :n], op=ALU.add), reads=["hwr", "harg"], writes=["hwr"])
                P.op("dve", lambda e, n=n: e.tensor_scalar(out=arg[:, :n], in0=arg[:, :n], scalar1=-math.pi, scalar2=2 * math.pi, op0=ALU.is_lt, op1=ALU.mult), reads=["harg", "hwr"], writes=["harg"])
                P.op("dve", lambda e, n=n: e.tensor_tensor(out=wr[:, :n], in0=wr[:, :n], in1=arg[:, :n], op=ALU.add), reads=["hwr", "harg"], writes=["hwr"])
                P.op("act", lambda e, dst=dst, n0=n0, n=n: e.activation(out=dst[:, n0:n0 + n], in_=wr[:, :n], func=AF.Sin), reads=["hwr"], writes=[("hh", u, stage)])
        return ("hh", u, 1)

    def hy_G(self, ph, l, L, o, h2T, h2k, cst, F, Fk, hs, hd, G, Gk, tmp, hsk="hYre", hdk="hYs"):
        P = self.P
        nch = L // 128
        win, ta, tb = tmp
        ntc = cst["nt"][L]
        for nc_ in range(nch):
            banks = []
            for dr in range(2):
                pb, pk = self.bank()
                col = (o * 2 + dr) * 512
                P.op("pe", lambda e, pb=pb, nc_=nc_, col=col: e.matmul(pb[:, :], lhsT=h2T[0:64, nc_ * 128:(nc_ + 1) * 128], rhs=cst["w3"][0:64, col:col + 512], start=True, stop=True),
                     reads=[h2k, "hyc"], writes=[pk])
                banks.append((pb, pk))
            P.op("act", lambda e, nc_=nc_: e.activation(out=win[:], in_=cst["dabs"][:, o * 1024:(o + 1) * 1024], func=AF.Exp, scale=ntc[:, nc_:nc_ + 1]),
                 reads=["hyc"], writes=["hwin"])
            P.op("dve", lambda e, pb=banks[0][0]: e.tensor_tensor(out=ta[:], in0=pb[:], in1=win[:, 0:512], op=ALU.mult), reads=[banks[0][1], "hwin"], writes=["hta"])
            P.op("dve", lambda e, pb=banks[1][0]: e.tensor_tensor(out=tb[:], in0=pb[:], in1=win[:, 512:1024], op=ALU.mult), reads=[banks[1][1], "hwin"], writes=["htb"])
            P.op("pool", lambda e, nc_=nc_: e.tensor_tensor(out=hs[:, nc_, :], in0=ta[:], in1=tb[:], op=ALU.add), reads=["hta", "htb"], writes=[hsk])
            P.op("pool", lambda e, nc_=nc_: e.tensor_tensor(out=hd[:, nc_, :], in0=tb[:], in1=ta[:], op=ALU.subtract), reads=["hta", "htb"], writes=[hdk])
        for fc in range(nch):
            pr, prk = self.bank(); pi_, pik = self.bank()
            for nc_ in range(nch):
                P.op("pe", lambda e, pr=pr, nc_=nc_, fc=fc: e.matmul(pr[:], lhsT=F[:, nc_, fc * 128:(fc + 1) * 128], rhs=hs[:, nc_, :], start=(nc_ == 0), stop=(nc_ == nch - 1)),
                     reads=[Fk, hsk], writes=[prk])
            for nc_ in range(nch):
                P.op("pe", lambda e, pi_=pi_, nc_=nc_, fc=fc: e.matmul(pi_[:], lhsT=F[:, nc_, L + fc * 128:L + (fc + 1) * 128], rhs=hd[:, nc_, :], start=(nc_ == 0), stop=(nc_ == nch - 1)),
                     reads=[Fk, hdk], writes=[pik])
            P.op("dve", lambda e, pr=pr, fc=fc: e.tensor_tensor(out=G[:, fc, 0, :], in0=pr[:], in1=cst["biasb"][:, o, :], op=ALU.add), reads=[prk, "hyc"], writes=[Gk])
            P.op("act", lambda e, pi_=pi_, fc=fc: e.copy(out=G[:, fc, 1, :], in_=pi_[:]), reads=[pik], writes=[Gk])

    def hy_dft_mult(self, L, F, Fk, xin, xink, G, Gk, Yre, Ys, mt):
        P = self.P
        nch = L // 128
        for fc in range(nch):
            pr, prk = self.bank(); pi_, pik = self.bank()
            for tc in range(nch):
                P.op("pe", lambda e, pr=pr, tc=tc, fc=fc: e.matmul(pr[:], lhsT=F[:, tc, fc * 128:(fc + 1) * 128], rhs=xin[:, tc, :], start=(tc == 0), stop=(tc == nch - 1)),
                     reads=[Fk, xink], writes=[prk])
            for tc in range(nch):
                P.op("pe", lambda e, pi_=pi_, tc=tc, fc=fc: e.matmul(pi_[:], lhsT=F[:, tc, L + fc * 128:L + (fc + 1) * 128], rhs=xin[:, tc, :], start=(tc == 0), stop=(tc == nch - 1)),
                     reads=[Fk, xink], writes=[pik])
            m1, m2, m3, m4 = mt
            P.op("dve", lambda e, pr=pr, fc=fc: e.tensor_tensor(out=m1[:], in0=pr[:], in1=G[:, fc, 0, :], op=ALU.mult), reads=[prk, Gk], writes=["hm1"])
            P.op("dve", lambda e, pi_=pi_, fc=fc: e.tensor_tensor(out=m2[:], in0=pi_[:], in1=G[:, fc, 1, :], op=ALU.mult), reads=[pik, Gk], writes=["hm2"])
            P.op("dve", lambda e, pi_=pi_, fc=fc: e.tensor_tensor(out=m3[:], in0=pi_[:], in1=G[:, fc, 0, :], op=ALU.mult), reads=[pik, Gk], writes=["hm3"])
            P.op("dve", lambda e, pr=pr, fc=fc: e.tensor_tensor(out=m4[:], in0=pr[:], in1=G[:, fc, 1, :], op=ALU.mult), reads=[prk, Gk], writes=["hm4"])
            P.op("pool", lambda e, fc=fc: e.tensor_tensor(out=Yre[:, fc, :], in0=m1[:], in1=m2[:], op=ALU.add), reads=["hm1", "hm2"], writes=["hYre"])
            P.op("pool", lambda e, fc=fc: e.tensor_tensor(out=Ys[:, fc, :], in0=m3[:], in1=m4[:], op=ALU.subtract), reads=["hm3", "hm4"], writes=["hYs"])

    def hy_conv(self, l, tok0, nseq, Ls, cst, bufs):
        P = self.P
        uring, ucf, ucb, x2b, vtm, x1tm, Yre, Ys, mt, ost = bufs
        sw = cst["sw"]
        LT = nseq * Ls
        nchs = Ls // 128
        v3 = lambda ap, a, b: ap[:, 0:LT].rearrange("p (s t) -> p s t", t=Ls)[:, :, a:b]
        ui = 0
        for part in range(3):
            for ct in range(4):
                tile = part * 4 + ct
                u = uring[ui % 2]; uk = ("hu", ui % 2)
                cf = ucf[0]; cfk = ("hcf", 0); ui += 1
                P.dma(u[:, :LT], self.hyT[tile * 128:(tile + 1) * 128, tok0:tok0 + LT], reads=[("hyT", tile)], writes=[uk])
                P.op("dve", lambda e, cf=cf, u=u, tile=tile: e.tensor_scalar(out=cf[:, :LT], in0=u[:, :LT], scalar1=sw[:, 12 + tile:13 + tile], scalar2=None, op0=ALU.mult),
                     reads=[uk, "hyc"], writes=[cfk])
                P.op("dve", lambda e, cf=cf, u=u, tile=tile: e.scalar_tensor_tensor(out=v3(cf, 1, Ls), in0=v3(u, 0, Ls - 1), scalar=sw[:, tile:tile + 1], in1=v3(cf, 1, Ls), op0=ALU.mult, op1=ALU.add),
                     reads=[uk, cfk, "hyc"], writes=[cfk])
                P.op("dve", lambda e, cf=cf, u=u, tile=tile: e.scalar_tensor_tensor(out=v3(cf, 0, Ls - 1), in0=v3(u, 1, Ls), scalar=sw[:, 24 + tile:25 + tile], in1=v3(cf, 0, Ls - 1), op0=ALU.mult, op1=ALU.add),
                     reads=[uk, cfk, "hyc"], writes=[cfk])
                if part < 2:
                    P.op("act", lambda e, cf=cf, ct=ct: e.copy(out=ucb[:, ct, :LT], in_=cf[:, :LT]), reads=[cfk], writes=[("hucb", ct)])
                else:
                    P.op("act", lambda e, cf=cf, ct=ct: e.copy(out=x2b[:, ct, :LT], in_=cf[:, :LT]), reads=[cfk], writes=[("hx2", ct)])
            if part < 2:
                dst = vtm if part == 0 else x1tm
                dk = "hvtm" if part == 0 else "hx1tm"
                for tc in range(LT // 128):
                    pb, pk = self.bank()
                    pbb = pb.bitcast(BF16)
                    for ct in range(4):
                        P.op("pe", lambda e, pbb=pbb, ct=ct, tc=tc: e.transpose(pbb[:, ct * 128:(ct + 1) * 128], ucb[:, ct, tc * 128:(tc + 1) * 128], self.identb[:]),
                             reads=[("hucb", ct), "identb"], writes=[pk])
                    if tc % 2 == 0:
                        P.op("act", lambda e, pbb=pbb, dst=dst, tc=tc: e.copy(out=dst[:, tc, :], in_=pbb[:, 0:512]), reads=[pk], writes=[(dk, tc // nchs)])
                    else:
                        P.op("dve", lambda e, pbb=pbb, dst=dst, tc=tc: e.tensor_copy(out=dst[:, tc, :], in_=pbb[:, 0:512]), reads=[pk], writes=[(dk, tc // nchs)])

    def hy_long(self, l, L, tok0, si, cst, F, Fk, FT, FTk, G0, G1, bufs, gen_G=None):
        P = self.P
        nch = L // 128
        uring, ucf, ucb, x2b_, vtm_, x1tm_, Yre, Ys, mt, ost = bufs
        vtm = vtm_[:, si * nch:(si + 1) * nch]; x1tm = x1tm_[:, si * nch:(si + 1) * nch]
        x2b = x2b_[:, :, si * L:(si + 1) * L]
        vk, x1k = ("hvtm", si), ("hx1tm", si)
        if gen_G is not None:
            gen_G(0)
        self.hy_dft_mult(L, F, Fk, vtm, vk, G0, "hG0", Yre, Ys, mt)
        for tc in range(nch):
            pb, pk = self.bank()
            for fc in range(nch):
                P.op("pe", lambda e, pb=pb, fc=fc, tc=tc: e.matmul(pb[:], lhsT=FT[:, fc, tc * 128:(tc + 1) * 128], rhs=Yre[:, fc, :], start=(fc == 0), stop=False),
                     reads=[FTk, "hYre"], writes=[pk])
            for fc in range(nch):
                P.op("pe", lambda e, pb=pb, fc=fc, tc=tc: e.matmul(pb[:], lhsT=FT[:, nch + fc, tc * 128:(tc + 1) * 128], rhs=Ys[:, fc, :], start=False, stop=(fc == nch - 1)),
                     reads=[FTk, "hYs"], writes=[pk])
            P.op("dve", lambda e, pb=pb, tc=tc: e.scalar_tensor_tensor(out=vtm[:, tc, :], in0=pb[:], scalar=1.0 / L, in1=x1tm[:, tc, :], op0=ALU.mult, op1=ALU.mult),
                 reads=[pk, x1k], writes=[vk])
        if gen_G is not None:
            gen_G(1)
        self.hy_dft_mult(L, F, Fk, vtm, vk, G1, "hG1", Yre, Ys, mt)
        for ct in range(4):
            for t0 in range(0, L, 512):
                n = min(512, L - t0)
                pb, pk = self.bank()
                for fc in range(nch):
                    P.op("pe", lambda e, pb=pb, fc=fc, ct=ct, t0=t0, n=n: e.matmul(pb[:, :n], lhsT=Yre[:, fc, ct * 128:(ct + 1) * 128], rhs=FT[:, fc, t0:t0 + n], start=(fc == 0), stop=False),
                         reads=[FTk, "hYre"], writes=[pk])
                for fc in range(nch):
                    P.op("pe", lambda e, pb=pb, fc=fc, ct=ct, t0=t0, n=n: e.matmul(pb[:, :n], lhsT=Ys[:, fc, ct * 128:(ct + 1) * 128], rhs=FT[:, nch + fc, t0:t0 + n], start=False, stop=(fc == nch - 1)),
                         reads=[FTk, "hYs"], writes=[pk])
                i = self.hoi % 2; self.hoi += 1
                o = ost[i]; ok_ = ("host", i)
                P.op("dve", lambda e, pb=pb, o=o, ct=ct, t0=t0, n=n: e.scalar_tensor_tensor(out=o[:, :n], in0=pb[:, :n], scalar=1.0 / L, in1=x2b[:, ct, t0:t0 + n], op0=ALU.mult, op1=ALU.mult),
                     reads=[pk, ("hx2", ct)], writes=[ok_])
                P.dma(self.oT[1536 + ct * 128:1536 + (ct + 1) * 128, tok0 + t0:tok0 + t0 + n], o[:, :n], reads=[ok_], writes=[("oT", 3)])

    def phase_hyena(self, l):
        P, d = self.P, self.din
        with self.phase() as ph:
            cst = dict(w1=ph.sb("hw1", [33, 64], F32), w2=ph.sb("hw2", [64, 64], F32), w3=ph.sb("hw3", [64, 2048], F32),
                       freq=ph.sb("hfreq", [64, 1], F32), fb=ph.sb("hfb", [64, 2], F32), dabs=ph.sb("hdabs", [128, 2048], F32),
                       biasb=ph.sb("hbiasb", [128, 2, 512], F32), sw=ph.sb("hsw", [128, 36], F32),
                       nt={256: ph.sb("hnt256", [128, 2], F32), 1024: ph.sb("hnt1024", [128, 8], F32)})
            P.dma(cst["w1"][:], d["hy_w1"][l], writes=["hyc"]); P.dma(cst["w2"][:], d["hy_w2"][l], writes=["hyc2"]); P.dma(cst["w3"][:], d["hy_w3"][l], writes=["hyc3"])
            P.dma(cst["freq"][:], d["hy_freq"][l].rearrange("(p o) -> p o", o=1), writes=["hyc4"])
            P.dma(cst["fb"][:, 0:1], d["hy_b1"][l].rearrange("(p o) -> p o", o=1), writes=["hyc5"])
            P.dma(cst["fb"][:, 1:2], d["hy_b2"][l].rearrange("(p o) -> p o", o=1), writes=["hyc6"])
            P.dma(cst["dabs"][:], d["hy_decay"][l].partition_broadcast(128), writes=["hyc7"])
            P.dma(cst["biasb"][:], d["hy_bias"][l].partition_broadcast(128), writes=["hyc8"])
            P.dma(cst["nt"][256][:], d["c_nt256"], writes=["hyc9"]); P.dma(cst["nt"][1024][:], d["c_nt1024"], writes=["hyc10"])
            self.load_T(ph, cst["sw"][:], ["hycsw"], d["hy_short"][l].rearrange("k (t p) -> (k t) p", p=128), 36)
            allk = ["hyc", "hyc2", "hyc3", "hyc4", "hyc5", "hyc6", "hyc7", "hyc8", "hyc9", "hyc10", "hycsw"]
            P.op("dve", lambda e: e.tensor_scalar(out=cst["fb"][:], in0=cst["fb"][:], scalar1=cst["freq"][:, 0:1], scalar2=None, op0=ALU.mult), reads=allk, writes=["hycA"], hard=True)
            P.op("act", lambda e: e.activation(out=cst["dabs"][:], in_=cst["dabs"][:], func=AF.Abs), reads=["hycA"], writes=["hycB"])
            P.op("dve", lambda e: e.tensor_copy(out=cst["nt"][256][:], in_=cst["nt"][256][:]), reads=["hycB"] + allk, writes=["hyc"])
            win = ph.sb("hwin", [128, 1024], F32); ta = ph.sb("hta", [128, 512], F32); tb = ph.sb("htb", [128, 512], F32)
            mt = [ph.sb(f"hm{i}", [128, 512], F32) for i in range(4)]
            ost = [ph.sb(f"host{i}", [128, 512], BF16) for i in range(2)]
            uring = [ph.sb(f"hur{i}", [128, 1024], F32) for i in range(2)]
            ucf = [ph.sb("hcf0", [128, 1024], F32)] * 2
            ucb = ph.sb("hucb", [128, 4, 1024], BF16); x2b = ph.sb("hx2b", [128, 4, 1024], BF16)
            vtm = ph.sb("hvtm", [128, 8, 512], BF16); x1tm = ph.sb("hx1tm", [128, 8, 512], BF16)
            Yre = ph.sb("hYre", [128, 8, 512], BF16); Ys = ph.sb("hYs", [128, 8, 512], BF16)
            hs, hd = Yre, Ys
            h2T = ph.sb("hh2", [64, 256], F32); h2Tb = ph.sb("hh2b", [64, 1024], F32)
            self.hoi = 0
            Ga = ph.sb("hGa", [128, 8, 2, 512], F32)
            bufs = (uring, ucf, ucb, x2b, vtm, x1tm, Yre, Ys, mt, ost)
            with self.phase() as p3:
                h2k = self.hy_ffn(p3, l, 256, h2T, cst)
            with self.phase() as p2:
                F = p2.sb("hF256", [128, 2, 512], BF16); FT = p2.sb("hFT256", [128, 4, 256], BF16)
                P.dma(F[:], d["c_F256"].rearrange("(c p) f -> p c f", p=128), writes=["hF"])
                P.dma(FT[:], d["c_FT256"].rearrange("(c p) f -> p c f", p=128), writes=["hFT"])
                G0 = Ga[:, 0:2]; G1 = Ga[:, 2:4]
                self.hy_G(p2, l, 256, 0, h2T, h2k, cst, F, "hF", hs, hd, G0, "hG0", (win, ta, tb))
                self.hy_G(p2, l, 256, 1, h2T, h2k, cst, F, "hF", hs, hd, G1, "hG1", (win, ta, tb))
                self.hy_conv(l, 0, 4, 256, cst, bufs)
                for s in range(4):
                    self.hy_long(l, 256, s * 256, s, cst, F, "hF", FT, "hFT", G0, G1, bufs)
                    if s == 1:
                        h2k = self.hy_ffn(p2, l, 1024, h2Tb, cst)
            with self.phase() as p2:
                F = p2.sb("hF1024", [128, 8, 2048], BF16); FT = p2.sb("hFT1024", [128, 16, 1024], BF16)
                for q in range(4):
                    P.dma(F[:, q * 2:(q + 1) * 2, :], d["c_F1024"][q * 256:(q + 1) * 256, :].rearrange("(c p) f -> p c f", p=128), writes=[("hF", q)])
                    P.dma(FT[:, q * 4:(q + 1) * 4, :], d["c_FT1024"][q * 512:(q + 1) * 512, :].rearrange("(c p) f -> p c f", p=128), writes=[("hFT", q)])
                P.op("pool", lambda e: e.tensor_copy(out=F[:, 0, 0:2], in_=F[:, 0, 0:2]), reads=[("hF", q) for q in range(4)], writes=["hF"])
                P.op("pool", lambda e: e.tensor_copy(out=FT[:, 0, 0:2], in_=FT[:, 0, 0:2]), reads=[("hFT", q) for q in range(4)], writes=["hFT"])

                def gen_G(o):
                    self.hy_G(p2, l, 1024, o, h2Tb, h2k, cst, F, "hF", hs, hd, Ga, "hG%d" % o, (win, ta, tb))
                self.hy_conv(l, 1024, 1, 1024, cst, bufs)
                self.hy_long(l, 1024, 1024, 0, cst, F, "hF", FT, "hFT", Ga, Ga, bufs, gen_G=gen_G)

    def phase_mixers(self, l):
        self.sbank = 0
        sel = self.dbg.get("mix", "abcd") if isinstance(self.dbg, dict) else "abcd"
        if "a" in sel:
            self.phase_attn_a(l)
        if "b" in sel:
            self.phase_attn_b(l)
        if "c" in sel:
            self.phase_attn_c(l)
        if "d" in sel:
            self.phase_hyena(l)

def _consts():
    c = {}
    c["c_ident"] = np.eye(128, dtype=np.float32)
    c["c_identb"] = np.eye(128).astype(ml_dtypes.bfloat16)
    c["c_onesb"] = np.ones((128, 128)).astype(ml_dtypes.bfloat16)
    f = np.arange(128)
    partner = (f // 32) * 32 + ((f % 32) + 16) % 32
    pm = np.zeros((128, 128), np.float32)
    pm[partner, f] = 1.0
    c["c_rperm"] = pm
    t = np.arange(1024)
    fh = f % 64
    q = fh // 16
    j = fh % 16
    inv = (10000.0 ** (-(j.astype(np.float32)) / 16.0)).astype(np.float32)
    pos = np.where((q < 2)[:, None], (t // 64)[None, :], (t % 64)[None, :]).astype(np.float32)
    ang = (pos * inv[:, None]).astype(np.float32)
    c["c_rcos"] = np.cos(ang).astype(np.float32)
    sgn = np.where(q % 2 == 0, -1.0, 1.0).astype(np.float32)
    c["c_rsin"] = (np.sin(ang) * sgn[:, None]).astype(np.float32)
    kc = (np.arange(128) % 64)[:, None]
    qc = np.arange(64)[None, :]
    col0 = np.clip(qc - 8, 0, 48)
    valid = (kc >= col0) & (kc < col0 + 16)
    c["c_m01"] = valid.astype(np.float32)
    c["c_mng"] = np.where(valid, 0.0, -30000.0).astype(np.float32)
    for L in (256, 1024):
        n = np.arange(L, dtype=np.float32)[:, None]
        t = n / np.float32(max(L - 1, 1))
        w = (np.float32(2.0 * math.pi) * n / np.float32(L)).astype(np.float32)
        bands = np.linspace(1e-4, 15, 16, dtype=np.float32)[None, :]
        z = np.concatenate([t, np.cos(bands * w), -np.sin(bands * w)], axis=-1).astype(np.float32)
        c[f"c_z{L}"] = np.ascontiguousarray(z.T)
        c[f"c_nt{L}"] = np.ascontiguousarray((-t[:, 0]).reshape(L // 128, 128).T.astype(np.float32))
        nn = np.arange(L, dtype=np.float64)[:, None]
        om = math.pi * (2 * np.arange(L, dtype=np.float64)[None, :] + 1) / (2 * L)
        Fm = np.concatenate([np.cos(nn * om), np.sin(nn * om)], axis=1)
        c[f"c_F{L}"] = Fm.astype(np.float32).astype(ml_dtypes.bfloat16)
        c[f"c_FT{L}"] = np.ascontiguousarray(Fm.T).astype(np.float32).astype(ml_dtypes.bfloat16)
    kk = np.arange(128)[:, None]; qq = np.arange(128)[None, :]
    c["c_cmask"] = np.stack([(kk >= qq), (kk <= qq)], axis=1).astype(np.float32).astype(ml_dtypes.bfloat16)
    return c


_CACHE = {}


def kernel(**inputs):
    inputs = {k: np.asarray(v) for k, v in inputs.items()}
    if "nc" not in _CACHE:
        _CACHE["nc"] = Builder().build()
    nc = _CACHE["nc"]
    consts = _consts()
    shared = {k: np.ascontiguousarray(inputs[k]) for k in (
        "w_ada", "b_ada", "g_mix", "w_in", "diff_lambda", "diff_norm_g", "na_bias", "swa_sink", "hy_short", "hy_w1", "hy_b1",
        "hy_w2", "hy_b2", "hy_w3", "hy_freq", "hy_decay", "hy_bias", "w_branch", "w_out", "g_mlp", "w_up", "w_down", "g_final")}
    in_maps = []
    for i in range(8):
        m = dict(shared)
        m.update(consts)
        m["xc"] = np.ascontiguousarray(inputs["x_prompt"][4 * i:4 * i + 4].reshape(1024, D))
        m["xl"] = np.ascontiguousarray(inputs["x_sample"][i])
        m["cak"] = np.ascontiguousarray(inputs["cache_a_k"][i].reshape(2, 256, 512))
        m["cav"] = np.ascontiguousarray(inputs["cache_a_v"][i].reshape(2, 256, 512))
        m["cbk"] = np.ascontiguousarray(inputs["cache_b_k"][i].reshape(2, 256, 512))
        m["cbv"] = np.ascontiguousarray(inputs["cache_b_v"][i].reshape(2, 256, 512))
        m["cck"] = np.ascontiguousarray(inputs["cache_c_k"][i].reshape(2, 256, 128))
        m["ccv"] = np.ascontiguousarray(inputs["cache_c_v"][i].reshape(2, 256, 128))
        m["cvec"] = np.ascontiguousarray(np.stack([inputs["c_ctx"], inputs["c"][i]], axis=0))
        in_maps.append(m)
    res = run_bass_kernel_spmd(nc, in_maps, core_ids=list(range(8)))
    r = res.results
    y_prompt = np.concatenate([r[i]["y_c"].reshape(4, 256, D) for i in range(8)], axis=0)
    y_sample = np.stack([r[i]["y_l"] for i in range(8)], axis=0)
    def cat(name, shp):
        return np.concatenate([r[i][name] for i in range(8)], axis=0).reshape(shp)
    return (y_prompt.astype(np.float32), y_sample.astype(np.float32),
            cat("nak", (32, 2, 256, 2, 4, 64)), cat("nav", (32, 2, 256, 4, 128)),
            cat("nbk", (32, 2, 256, 8, 64)), cat("nbv", (32, 2, 256, 8, 64)),
            cat("nck", (32, 2, 256, 2, 64)), cat("ncv", (32, 2, 256, 2, 64)))
```

```python
import math
from contextlib import ExitStack, contextmanager
import numpy as np
import ml_dtypes
import concourse.bass as bass
import concourse.mybir as mybir
from concourse.bass_utils import run_bass_kernel_spmd

F32 = mybir.dt.float32
BF16 = mybir.dt.bfloat16
AF = mybir.ActivationFunctionType
ALU = mybir.AluOpType

SAME_ENG_SYNC = True


class Prog:
    ENGS = ("pe", "act", "dve", "pool", "sp")
    KROT = 8

    def __init__(self, nc):
        self.nc = nc
        self.ops = {e: [] for e in self.ENGS}
        self.res = {}
        self.ndma = {e: 0 for e in self.ENGS}
        self.dma_refs = {e: [] for e in self.ENGS}
        self._ps_i = 0
        self.uid = 0

    def name(self, base):
        self.uid += 1
        return f"{base}_{self.uid}"

    def _record(self, eng, fn, reads, writes, is_dma, hard=False):
        idx = len(self.ops[eng])
        ref = (eng, idx)
        deps = set()
        raw = set()
        for k in reads:
            r = self.res.get(k)
            if r is not None and r[0] is not None:
                deps.add(r[0])
                raw.add(r[0])
        for k in writes:
            r = self.res.get(k)
            if r is not None:
                if r[0] is not None:
                    deps.add(r[0])
                deps.update(r[1].values())
                deps.update(r[2])
        if not is_dma:
            keep = set()
            for d in deps:
                if d[0] == eng and not self.ops[eng][d[1]]["dma"]:
                    if (SAME_ENG_SYNC or hard) and eng != "pe" and d in raw:
                        keep.add(d)
                else:
                    keep.add(d)
            deps = keep
        op = dict(fn=fn, deps=deps, dma=is_dma, sig=False)
        if is_dma:
            op["dj"] = self.ndma[eng]
            self.ndma[eng] += 1
            self.dma_refs[eng].append(ref)
        self.ops[eng].append(op)
        for k in reads:
            r = self.res.setdefault(k, [None, {}, []])
            if is_dma:
                r[2].append(ref)
            else:
                r[1][eng] = ref
        for k in writes:
            self.res[k] = [ref, {}, []]
        return ref

    def op(self, eng, fn, reads=(), writes=(), hard=False):
        return self._record(eng, fn, tuple(reads), tuple(writes), False, hard)

    def dma(self, out, in_, reads=(), writes=(), q="sp", **kw):
        return self._record(q, lambda e: e.dma_start(out=out, in_=in_, **kw), tuple(reads), tuple(writes), True)

    def barrier(self):
        lasts = set()
        for e in self.ENGS:
            for i in range(len(self.ops[e]) - 1, -1, -1):
                o = self.ops[e][i]
                if o["fn"] is not None and not o["dma"]:
                    lasts.add((e, i))
                    break
            for ref in self.dma_refs[e][-self.KROT:]:
                lasts.add(ref)
        for e in self.ENGS:
            deps = set(d for d in lasts if not (d[0] == e and not self.ops[e][d[1]]["dma"]))
            self.ops[e].append(dict(fn=None, deps=deps, dma=False, sig=False))
        self.res = {}

    def emit(self):
        nc = self.nc
        with ExitStack() as es:
            esem = {e: es.enter_context(nc.semaphore(f"sem_{e}")) for e in self.ENGS}
            rsem = {e: [es.enter_context(nc.semaphore(f"rot_{e}_{j}")) for j in range(self.KROT)]
                    for e in self.ENGS if self.ndma[e] > 0}
            for e in self.ENGS:
                for o in self.ops[e]:
                    for d in o["deps"]:
                        self.ops[d[0]][d[1]]["sig"] = True
            for e in self.ENGS:
                c = 0
                for o in self.ops[e]:
                    if o["dma"]:
                        j = o["dj"]
                        o["done"] = ((e, "r", j % self.KROT), 16 * (j // self.KROT + 1))
                    elif o["sig"]:
                        c += 1
                        o["done"] = ((e, "c"), c)

            def semh(key):
                return esem[key[0]] if key[1] == "c" else rsem[key[0]][key[2]]

            def emit_eng(engobj, e):
                obs = {}
                for o in self.ops[e]:
                    waits = {}
                    for d in o["deps"]:
                        k, v = self.ops[d[0]][d[1]]["done"]
                        if obs.get(k, 0) < v:
                            waits[k] = max(waits.get(k, 0), v)
                    if o["dma"] and o["dj"] >= self.KROT:
                        j = o["dj"]
                        k, v = (e, "r", j % self.KROT), 16 * (j // self.KROT)
                        if obs.get(k, 0) < v:
                            waits[k] = max(waits.get(k, 0), v)
                    for k, v in waits.items():
                        engobj.wait_ge(semh(k), v)
                        obs[k] = v
                    if o["fn"] is None:
                        continue
                    ins = o["fn"](engobj)
                    if o["dma"]:
                        ins.then_inc(semh(o["done"][0]), 16)
                    elif o["sig"]:
                        ins.then_inc(esem[e], 1)

            with nc.Block() as block:
                block.tensor(lambda x: emit_eng(x, "pe"))
                block.scalar(lambda x: emit_eng(x, "act"))
                block.vector(lambda x: emit_eng(x, "dve"))
                block.gpsimd(lambda x: emit_eng(x, "pool"))
                block.sync(lambda x: emit_eng(x, "sp"))


D = 2048
NT = 2048
NTX = NT + 256
KC = 16
DEPTH = 2
IN_COLS = 13568
D_FF = 8192
OFF = dict(aq=0, ak=512, av=1024, bq=1536, bk=2048, bv=2560, cq=3072, ck=3584, cv=3712, hy=3840, gate=5376)
QK_ROW = dict(aq=0, ak=512, bq=1024, bk=1536, cq=2048, ck=2560)
NQK = 2688
V_COL = dict(av=0, bv=512, cv=1024)
NV = 1152
EPS = 1e-6


class Phase:
    def __init__(self, bld):
        self.b = bld
        self.es = ExitStack()

    def sb(self, name, shape, dt):
        return self.es.enter_context(self.b.nc.sbuf_tensor(self.b.P.name(name), list(shape), dt))

    def __enter__(self):
        return self

    def __exit__(self, *a):
        self.b.P.barrier()
        self.es.close()
        return False


class WStream:
    def __init__(self, bld, ph, units, nk, tag, nslots=2, nstage=3, kstage=4):
        self.b, self.units, self.nk, self.tag = bld, units, nk, tag
        self.nslots, self.nstage, self.kstage = nslots, nstage, kstage
        self.w = [ph.sb(f"w{tag}{s}", [128, nk, 512], BF16) for s in range(nslots)]
        self.st = [ph.sb(f"ws{tag}{s}", [128, kstage, 512], F32) for s in range(nstage)]
        self.sti = 0
        self.pieces = {}

    def fetch(self, i, npieces=None):
        if i >= len(self.units):
            return
        P = self.b.P
        src, ncols = self.units[i]
        slot = i % self.nslots
        total = self.nk // self.kstage
        done = self.pieces.get(i, 0)
        todo = total - done if npieces is None else min(npieces, total - done)
        for pc in range(done, done + todo):
            k0 = pc * self.kstage
            s = self.sti % self.nstage
            self.sti += 1
            st, w = self.st[s], self.w[slot]
            P.dma(st[:, :, :ncols], src(k0, self.kstage), writes=[("wst", self.tag, s)])
            P.op("pool", lambda e, st=st, w=w, k0=k0, ncols=ncols: e.tensor_copy(
                out=w[:, k0:k0 + self.kstage, :ncols], in_=st[:, :, :ncols]),
                reads=[("wst", self.tag, s)], writes=[("wbf", self.tag, slot, pc)])
        self.pieces[i] = done + todo

    def get(self, i, prefetch=True):
        self.fetch(i)
        if prefetch:
            self.fetch(i + 1)
        slot = i % self.nslots
        return self.w[slot], (lambda kc: ("wbf", self.tag, slot, kc // self.kstage))


def wsrc(ap2d, c0, ncols):
    def f(k0, nk):
        return ap2d[k0 * 128:(k0 + nk) * 128, c0:c0 + ncols].rearrange("(k p) c -> p k c", p=128)
    return f


class Builder:
    def __init__(self, dbg=None):
        self.nc = nc = bass.Bass("TRN2", target_bir_lowering=False)
        self.P = Prog(nc)
        self.dbg = dbg or {}
        self.es = ExitStack()
        self.din = {}
        self.dout = {}
        self.psi = 0

    def inp(self, name, shape, dt=F32):
        self.din[name] = self.nc.dram_tensor(name, list(shape), dt, kind="ExternalInput").ap()
        return self.din[name]

    def outp(self, name, shape, dt=F32):
        self.dout[name] = self.nc.dram_tensor(name, list(shape), dt, kind="ExternalOutput").ap()
        return self.dout[name]

    def scr(self, name, shape, dt):
        if name in self.dbg:
            return self.outp(name, shape, dt)
        return self.nc.dram_tensor(name, list(shape), dt).ap()

    def gsb(self, name, shape, dt):
        return self.es.enter_context(self.nc.sbuf_tensor(name, list(shape), dt))

    def bank(self):
        i = self.psi % 8
        self.psi += 1
        return self.ps[i], ("ps", i)

    def phase(self):
        return Phase(self)

    def declare(self):
        inp, outp, scr = self.inp, self.outp, self.scr
        inp("xc", [1024, D]); inp("xl", [1024, D])
        inp("cak", [2, 256, 512]); inp("cav", [2, 256, 512]); inp("cbk", [2, 256, 512]); inp("cbv", [2, 256, 512])
        inp("cck", [2, 256, 128]); inp("ccv", [2, 256, 128])
        inp("cvec", [2, D])
        inp("w_ada", [2, D, 6 * D]); inp("b_ada", [2, 6 * D]); inp("g_mix", [2, D]); inp("w_in", [2, D, IN_COLS])
        inp("diff_lambda", [2, 4, 64]); inp("diff_norm_g", [2, 128]); inp("na_bias", [2, 8, 15, 31]); inp("swa_sink", [2, 8])
        inp("hy_short", [2, 3, 1536]); inp("hy_w1", [2, 33, 64]); inp("hy_b1", [2, 64]); inp("hy_w2", [2, 64, 64])
        inp("hy_b2", [2, 64]); inp("hy_w3", [2, 64, 2048]); inp("hy_freq", [2, 64]); inp("hy_decay", [2, 2048])
        inp("hy_bias", [2, 2, 512]); inp("w_branch", [2, 4, 512, D]); inp("w_out", [2, D, D]); inp("g_mlp", [2, D])
        inp("w_up", [2, D, D_FF]); inp("w_down", [2, D_FF, D]); inp("g_final", [D])
        inp("c_ident", [128, 128]); inp("c_identb", [128, 128], BF16); inp("c_onesb", [128, 128], BF16)
        inp("c_rperm", [128, 128]); inp("c_rcos", [128, 1024]); inp("c_rsin", [128, 1024])
        inp("c_m01", [128, 64]); inp("c_mng", [128, 64]); inp("c_cmask", [128, 2, 128], BF16)
        for L in (256, 1024):
            inp(f"c_z{L}", [33, L]); inp(f"c_nt{L}", [128, L // 128]); inp(f"c_F{L}", [L, 2 * L], BF16); inp(f"c_FT{L}", [2 * L, L], BF16)
        outp("y_c", [1024, D]); outp("y_l", [1024, D])
        outp("nak", [4, 2, 256, 512]); outp("nav", [4, 2, 256, 512]); outp("nbk", [4, 2, 256, 512]); outp("nbv", [4, 2, 256, 512])
        outp("nck", [4, 2, 256, 128]); outp("ncv", [4, 2, 256, 128])
        self.xT = scr("xT", [D, NT], F32)
        self.qkT = [scr(f"qkT{l}", [NQK, NTX], BF16) for l in range(2)]
        self.vtok = [scr(f"vtok{l}", [NTX, NV], BF16) for l in range(2)]
        self.hyT = scr("hyT", [1536, NT], F32)
        self.sgT = scr("sgT", [4 * D, NT], BF16)
        self.oT = scr("oT", [D, NT], BF16)
        self.mT = scr("mT", [D, NT], BF16)
        self.aT = scr("aT", [D_FF, NT], BF16)
        self.nbpad = [scr(f"nbpad{l}", [1, 64 + 3720 + 64], F32) for l in range(2)]

    def globals_(self):
        nc, P, g = self.nc, self.P, self.gsb
        self.ps = [self.es.enter_context(nc.psum_tensor(f"psb{i}", [128, 512], F32)) for i in range(8)]
        self.ident = g("ident", [128, 128], F32); self.identb = g("identb", [128, 128], BF16)
        self.onesb = g("onesb", [128, 128], BF16); self.rperm = g("rperm", [128, 128], F32)
        self.epst = g("epst", [128, 1], F32)
        self.mod = g("mod", [128, 2, 96, 2], F32)
        self.gm = g("gm", [128, 2, 2, 16, 2], F32)
        self.gfin = g("gfin", [128, 16], F32)
        self.zero16 = g("zero16", [128, 16], F32)
        d = self.din
        P.dma(self.ident[:], d["c_ident"], writes=["ident"])
        P.dma(self.identb[:], d["c_identb"], writes=["identb"])
        P.dma(self.onesb[:], d["c_onesb"], writes=["onesb"])
        P.dma(self.rperm[:], d["c_rperm"], writes=["rperm"])
        P.op("dve", lambda e: e.memset(self.epst[:], EPS), writes=["epst"])
        P.op("dve", lambda e: e.memset(self.zero16[:], 0.0), writes=["zero16"])

    def load_T(self, ph, dst, dst_keys, src2d, n):
        P = self.P
        tmp = ph.sb("ltT", [128, 128], F32)
        k = ("ltT", P.uid)
        P.dma(tmp[:n, :], src2d, writes=[k])
        pb, pk = self.bank()
        P.op("pe", lambda e: e.transpose(pb[:, :n], tmp[:n, :], self.ident[:n, :n]), reads=[k, "ident"], writes=[pk])
        P.op("dve", lambda e: e.tensor_copy(out=dst, in_=pb[:, :n]), reads=[pk], writes=dst_keys)

    def phase_xT(self):
        P, d = self.P, self.din
        with self.phase() as ph:
            xin = [ph.sb(f"xin{i}", [128, 4, D], F32) for i in range(2)]
            stg = [ph.sb(f"xst{i}", [128, 512], F32) for i in range(4)]
            si = 0
            for g4 in range(4):
                src = d["xc"] if g4 < 2 else d["xl"]
                r0 = (g4 % 2) * 512
                xt = xin[g4 % 2]
                xk = ("xin", g4 % 2)
                P.dma(xt[:], src[r0:r0 + 512, :].rearrange("(t p) f -> p t f", p=128), writes=[xk])
                for fc in range(KC):
                    pb, pk = self.bank()
                    for t in range(4):
                        P.op("pe", lambda e, pb=pb, xt=xt, t=t, fc=fc: e.transpose(
                            pb[:, t * 128:(t + 1) * 128], xt[:, t, fc * 128:(fc + 1) * 128], self.ident[:]),
                            reads=[xk, "ident"], writes=[pk])
                    s = stg[si % 4]; sk = ("xst", si % 4); si += 1
                    eng = "act" if fc % 2 == 0 else "dve"
                    if eng == "act":
                        P.op("act", lambda e, s=s, pb=pb: e.copy(out=s[:], in_=pb[:]), reads=[pk], writes=[sk])
                    else:
                        P.op("dve", lambda e, s=s, pb=pb: e.tensor_copy(out=s[:], in_=pb[:]), reads=[pk], writes=[sk])
                    P.dma(self.xT[fc * 128:(fc + 1) * 128, g4 * 512:(g4 + 1) * 512], s[:], reads=[sk], writes=[("xT", fc, g4)])

    def phase_adaln(self):
        P, d = self.P, self.din
        with self.phase() as ph:
            cT = ph.sb("cT", [128, 2, 16], F32)
            sT = ph.sb("sT", [128, 16, 2], F32)
            bada = ph.sb("bada", [128, 2, 96], F32)
            gT = ph.sb("gT", [128, 2, 2, 16], F32)
            for v in range(2):
                self.load_T(ph, cT[:, v, :], ["cT"], d["cvec"][v].rearrange("(k p) -> k p", p=128), 16)
            for v in range(2):
                P.op("act", lambda e, v=v: e.activation(out=sT[:, :, v], in_=cT[:, v, :], func=AF.Silu), reads=["cT"], writes=["sT"])
            for l in range(2):
                self.load_T(ph, bada[:, l, :], ["bada"], d["b_ada"][l].rearrange("(k p) -> k p", p=128), 96)
                self.load_T(ph, gT[:, l, 0, :], ["gT"], d["g_mix"][l].rearrange("(k p) -> k p", p=128), 16)
                self.load_T(ph, gT[:, l, 1, :], ["gT"], d["g_mlp"][l].rearrange("(k p) -> k p", p=128), 16)
            self.load_T(ph, self.gfin[:], ["gfin"], d["g_final"].rearrange("(k p) -> k p", p=128), 16)
            st = [ph.sb(f"adst{i}", [128, 4, 512], F32) for i in range(3)]
            m2 = [ph.sb(f"adm{i}", [2, 512], F32) for i in range(2)]
            si = 0
            for l in range(2):
                for cg in range(24):
                    pb, pk = self.bank()
                    for ks in range(4):
                        s = st[si % 3]; sk = ("adst", si % 3); si += 1
                        P.dma(s[:], d["w_ada"][l, ks * 512:(ks + 1) * 512, cg * 512:(cg + 1) * 512].rearrange("(k p) c -> p k c", p=128), writes=[sk])
                        for kk in range(4):
                            kc = ks * 4 + kk
                            P.op("pe", lambda e, pb=pb, s=s, kk=kk, kc=kc: e.matmul(
                                pb[0:2, :], lhsT=sT[:, kc, :], rhs=s[:, kk, :], start=(kc == 0), stop=(kc == 15)), reads=[sk, "sT"], writes=[pk])
                    m = m2[cg % 2]; mk = ("adm", cg % 2)
                    P.op("act", lambda e, m=m, pb=pb: e.copy(out=m[:], in_=pb[0:2, :]), reads=[pk], writes=[mk])
                    pb2, pk2 = self.bank()
                    for j in range(4):
                        P.op("pe", lambda e, pb2=pb2, m=m, j=j: e.transpose(pb2[:, 2 * j:2 * j + 2], m[0:2, j * 128:(j + 1) * 128], self.ident[0:2, 0:2]),
                             reads=[mk, "ident"], writes=[pk2])
                    for v in range(2):
                        P.op("dve", lambda e, pb2=pb2, l=l, cg=cg, v=v: e.tensor_tensor(
                            out=self.mod[:, l, cg * 4:(cg + 1) * 4, v], in0=pb2[:, v:8:2], in1=bada[:, l, cg * 4:(cg + 1) * 4], op=ALU.add),
                            reads=[pk2, "bada"], writes=["mod"])
            for l in range(2):
                for w in range(2):
                    sc0 = 16 if w == 0 else 64
                    for v in range(2):
                        P.op("dve", lambda e, l=l, w=w, v=v, sc0=sc0: e.scalar_tensor_tensor(
                            out=self.gm[:, l, w, :, v], in0=self.mod[:, l, sc0:sc0 + 16, v], scalar=1.0, in1=gT[:, l, w, :],
                            op0=ALU.add, op1=ALU.mult), reads=["mod", "gT"], writes=["gm"], hard=True)

    def modv(self, l, which, ft, grp):
        return self.mod[:, l, which * 16 + ft, grp:grp + 1]

    def norm_tiles(self, ph, scale_ap, shift_ap, emit_out):
        P = self.P
        xs = [ph.sb(f"nx{i}", [128, 16, 512], F32) for i in range(2)]
        sq = [ph.sb(f"nsq{i}", [128, 512], BF16) for i in range(4)]
        rs = [ph.sb(f"nrs{i}", [128, 512], F32) for i in range(2)]
        tmp = [ph.sb(f"ntmp{i}", [128, 512], F32) for i in range(4)]
        xTv = self.xT.rearrange("(k p) t -> p k t", p=128)
        ti = 0
        for tt in range(4):
            grp = 0 if tt < 2 else 1
            x = xs[tt % 2]; xk = ("nx", tt % 2)
            for h in range(2):
                P.dma(x[:, h * 8:(h + 1) * 8, :], xTv[:, h * 8:(h + 1) * 8, tt * 512:(tt + 1) * 512],
                      reads=[("xT", fc, tt) for fc in range(h * 8, h * 8 + 8)], writes=[(xk, h)])
            pb, pk = self.bank()
            for fc in range(KC):
                s = sq[fc % 4]; sk = ("nsq", fc % 4)
                P.op("act", lambda e, s=s, x=x, fc=fc: e.activation(out=s[:], in_=x[:, fc, :], func=AF.Square), reads=[(xk, fc // 8)], writes=[sk])
                P.op("pe", lambda e, pb=pb, s=s, fc=fc: e.matmul(pb[:], lhsT=self.onesb[:], rhs=s[:], start=(fc == 0), stop=(fc == 15)),
                     reads=[sk, "onesb"], writes=[pk])
            r = rs[tt % 2]; rk = ("nrs", tt % 2)
            self.act_pow(r[:], pb[:], -0.5, [pk], [rk], scale=1.0 / D, bias=self.epst[:, 0:1])
            for fc in range(KC):
                t = tmp[ti % 4]; tk = ("ntmp", ti % 4); ti += 1
                P.op("dve", lambda e, t=t, x=x, fc=fc, r=r, grp=grp: e.scalar_tensor_tensor(
                    out=t[:], in0=x[:, fc, :], scalar=scale_ap(fc, grp), in1=r[:], op0=ALU.mult, op1=ALU.mult),
                    reads=[(xk, fc // 8), rk, "gm", "gfin"], writes=[tk])
                emit_out(tt, fc, grp, t, tk)

    def phase_norm_h(self, ph, l, which):
        P = self.P
        hT = self.hT

        def out(tt, fc, grp, t, tk):
            P.op("act", lambda e: e.activation(out=hT[:, fc, tt * 512:(tt + 1) * 512], in_=t[:], func=AF.Identity,
                                                bias=self.modv(l, 0 if which == 0 else 3, fc, grp), scale=1.0),
                 reads=[tk, "mod"], writes=[("hT", fc, tt)])
        with self.phase() as p2:
            self.norm_tiles(p2, lambda fc, grp: self.gm[:, l, which, fc, grp:grp + 1], None, out)

    def mm_group(self, pb, pk, lhs_fn, rhs_fn, nk, reads):
        P = self.P
        for kc in range(nk):
            la, ra = lhs_fn(kc), rhs_fn(kc)
            P.op("pe", lambda e, kc=kc, la=la, ra=ra, pb=pb: e.matmul(pb[:, 0:ra.shape[-1]], lhsT=la, rhs=ra, start=(kc == 0), stop=(kc == nk - 1)),
                 reads=reads(kc), writes=[pk])

    def inproj_units(self):
        units = []
        for nm in ("aq", "ak", "av", "bq", "bk", "bv", "cq"):
            units.append((OFF[nm], 512, nm))
        units.append((OFF["ck"], 256, "ckv"))
        for i in range(3):
            units.append((OFF["hy"] + i * 512, 512, ("hy", i)))
        for i in range(16):
            units.append((OFF["gate"] + i * 512, 512, ("gate", i)))
        return units

    def inproj_ws(self, ph, l):
        win = self.din["w_in"][l]
        ws = WStream(self, ph, [(wsrc(win, c0, nc_), nc_) for (c0, nc_, _) in self.inproj_units()], 16, "in")
        ws.fetch(0)
        return ws

    def up_ws(self, ph, l):
        ws = WStream(self, ph, [(wsrc(self.din["w_up"][l], cg * 512, 512), 512) for cg in range(16)], 16, "wu")
        ws.fetch(0)
        return ws

    def phase_inproj(self, l, ws):
        P, d = self.P, self.din
        hT = self.hT
        with self.phase() as ph:
            units = self.inproj_units()
            stb = [ph.sb(f"ipb{i}", [128, NT], BF16) for i in range(3)]
            stf = [ph.sb(f"ipf{i}", [128, NT], F32) for i in range(2)]
            sta = [ph.sb(f"ipa{i}", [128, 512], F32) for i in range(3)]
            stab = [ph.sb(f"ipab{i}", [128, 512], BF16) for i in range(3)]
            r32 = [ph.sb(f"ipr{i}", [128, 512], F32) for i in range(2)]
            t1 = [ph.sb(f"ipt{i}", [128, 512], F32) for i in range(2)]
            t2 = [ph.sb(f"ipu{i}", [128, 512], F32) for i in range(2)]
            rcos = ph.sb("rcos", [128, 1024], F32); rsin = ph.sb("rsin", [128, 1024], F32)
            P.dma(rcos[:], d["c_rcos"], writes=["rcos"]); P.dma(rsin[:], d["c_rsin"], writes=["rsin"])
            cnt = dict(b=0, f=0, a=0, r=0)
            ev = [0]
            for ui, (c0, ncols, kind) in enumerate(units):
                w, wk = ws.get(ui)
                kname = kind if isinstance(kind, str) else kind[0]
                fm_tiles = []
                if kname in ("aq", "ak", "bq", "bk", "cq"):
                    fm_tiles = [(j, "qk", QK_ROW[kname] + j * 128) for j in range(4)]
                elif kname == "ckv":
                    fm_tiles = [(0, "qk", QK_ROW["ck"])]
                elif kname == "hy":
                    fm_tiles = [(j, "hy", kind[1] * 512 + j * 128) for j in range(4)]
                elif kname == "gate":
                    fm_tiles = [(j, "gate", kind[1] * 512 + j * 128) for j in range(4)]
                roped = kname in ("aq", "ak", "cq", "ckv")
                for (j, okind, row0) in fm_tiles:
                    if okind == "hy":
                        st = stf[cnt["f"] % 2]; stk = ("ipf", cnt["f"] % 2); cnt["f"] += 1
                    else:
                        st = stb[cnt["b"] % 3]; stk = ("ipb", cnt["b"] % 3); cnt["b"] += 1
                    for tt in range(4):
                        pb, pk = self.bank()
                        self.mm_group(pb, pk, lambda kc: w[:, kc, j * 128:(j + 1) * 128], lambda kc: hT[:, kc, tt * 512:(tt + 1) * 512], 16,
                                      lambda kc: [wk(kc), ("hT", kc, tt)])
                        o = st[:, tt * 512:(tt + 1) * 512]
                        if okind == "gate":
                            P.op("act", lambda e, o=o, pb=pb: e.activation(out=o, in_=pb[:], func=AF.Sigmoid), reads=[pk], writes=[stk])
                        elif okind == "qk" and roped and tt >= 2:
                            i = cnt["r"] % 2; cnt["r"] += 1
                            r, rk = r32[i], ("ipr", i)
                            a1, a1k = t1[i], ("ipt", i)
                            a2, a2k = t2[i], ("ipu", i)
                            tp = (tt - 2) * 512
                            P.op("act", lambda e, r=r, pb=pb: e.copy(out=r[:], in_=pb[:]), reads=[pk], writes=[rk])
                            pb2, pk2 = self.bank()
                            P.op("pe", lambda e, pb2=pb2, r=r: e.matmul(pb2[:], lhsT=self.rperm[:], rhs=r[:], start=True, stop=True),
                                 reads=[rk, "rperm"], writes=[pk2])
                            P.op("dve", lambda e, a1=a1, r=r, tp=tp: e.tensor_tensor(out=a1[:], in0=r[:], in1=rcos[:, tp:tp + 512], op=ALU.mult),
                                 reads=[rk, "rcos"], writes=[a1k])
                            P.op("dve", lambda e, a2=a2, pb2=pb2, tp=tp: e.tensor_tensor(out=a2[:], in0=pb2[:], in1=rsin[:, tp:tp + 512], op=ALU.mult),
                                 reads=[pk2, "rsin"], writes=[a2k])
                            P.op("pool", lambda e, o=o, a1=a1, a2=a2: e.tensor_tensor(out=o, in0=a1[:], in1=a2[:], op=ALU.add),
                                 reads=[a1k, a2k], writes=[stk])
                        else:
                            ev[0] += 1
                            if ev[0] % 2 == 0:
                                P.op("act", lambda e, o=o, pb=pb: e.copy(out=o, in_=pb[:]), reads=[pk], writes=[stk])
                            else:
                                P.op("dve", lambda e, o=o, pb=pb: e.tensor_copy(out=o, in_=pb[:]), reads=[pk], writes=[stk])
                    if okind == "qk":
                        dst = self.qkT[l][row0:row0 + 128, 0:NT]; dk = ("qkT", row0 // 128)
                    elif okind == "hy":
                        dst = self.hyT[row0:row0 + 128, :]; dk = ("hyT", row0 // 128)
                    else:
                        dst = self.sgT[row0:row0 + 128, :]; dk = ("sgT", row0 // 128)
                    P.dma(dst, st[:], reads=[stk], writes=[dk])
                if kname in ("ak", "av", "bk", "bv", "ckv"):
                    isv = kname in ("av", "bv", "ckv")
                    ntile = 16 if isv else 8
                    for t128 in range(ntile):
                        pb, pk = self.bank()
                        self.mm_group(pb, pk, lambda kc: hT[:, kc, t128 * 128:(t128 + 1) * 128], lambda kc: w[:, kc, 0:ncols], 16,
                                      lambda kc: [wk(kc), ("hT", kc, t128 // 4)])
                        pbv = pb[:, 0:ncols]
                        if t128 < 8:
                            i = cnt["a"] % 3; cnt["a"] += 1
                            s, sk = sta[i], ("ipa", i)
                            P.op("act", lambda e, s=s, pbv=pbv, ncols=ncols: e.copy(out=s[:, 0:ncols], in_=pbv), reads=[pk], writes=[sk])
                            sq_, pos0 = t128 // 2, (t128 % 2) * 128
                            if kname == "ckv":
                                P.dma(self.dout["nck"][sq_, l, pos0:pos0 + 128, :], s[:, 0:128], reads=[sk])
                                P.dma(self.dout["ncv"][sq_, l, pos0:pos0 + 128, :], s[:, 128:256], reads=[sk])
                            else:
                                P.dma(self.dout["n" + kname][sq_, l, pos0:pos0 + 128, :], s[:, 0:512], reads=[sk])
                        if isv:
                            i = cnt["a"] % 3; cnt["a"] += 1
                            s2, s2k = stab[i], ("ipab", i)
                            if kname == "ckv":
                                P.op("act", lambda e, s2=s2, pb=pb: e.copy(out=s2[:, 0:128], in_=pb[:, 128:256]), reads=[pk], writes=[s2k])
                                P.dma(self.vtok[l][t128 * 128:(t128 + 1) * 128, V_COL["cv"]:V_COL["cv"] + 128], s2[:, 0:128], reads=[s2k],
                                      writes=[("vtok", "cv", t128)])
                            else:
                                P.op("act", lambda e, s2=s2, pb=pb: e.copy(out=s2[:], in_=pb[:]), reads=[pk], writes=[s2k])
                                P.dma(self.vtok[l][t128 * 128:(t128 + 1) * 128, V_COL[kname]:V_COL[kname] + 512], s2[:], reads=[s2k],
                                      writes=[("vtok", kname, t128)])

    def phase_merge(self, l):
        P, d = self.P, self.din
        with self.phase() as ph:
            oT = ph.sb("oTr", [128, 16, NT], BF16)
            oTv = self.oT.rearrange("(k p) t -> p k t", p=128)
            for n in range(4):
                P.dma(oT[:, n * 4:(n + 1) * 4, :], oTv[:, n * 4:(n + 1) * 4, :], reads=[("oT", n)], writes=[("oTr", n)])
            acc = ph.sb("macc", [128, 4, NT], F32)
            units = []
            for fg in range(4):
                for n in range(4):
                    units.append((wsrc(d["w_branch"][l, n], fg * 512, 512), 512))
            ws = WStream(self, ph, units, 4, "wb")
            sg = [ph.sb(f"msg{i}", [128, NT], BF16) for i in range(3)]
            tmp = [ph.sb(f"mtmp{i}", [128, 512], F32) for i in range(4)]
            mst = [ph.sb(f"mst{i}", [128, NT], BF16) for i in range(2)]
            sgi = 0; ti = 0; mi = 0
            for fg in range(4):
                for n in range(4):
                    w, wk = ws.get(fg * 4 + n)
                    for j in range(4):
                        ft = fg * 4 + j
                        s = sg[sgi % 3]; sk = ("msg", sgi % 3); sgi += 1
                        P.dma(s[:], self.sgT[n * D + ft * 128:n * D + (ft + 1) * 128, :], reads=[("sgT", (n * D + ft * 128) // 128)], writes=[sk])
                        if n == 3:
                            ms = mst[mi % 2]; msk = ("mst", mi % 2); mi += 1
                        for tt in range(4):
                            pb, pk = self.bank()
                            self.mm_group(pb, pk, lambda kc: w[:, kc, j * 128:(j + 1) * 128], lambda kc: oT[:, n * 4 + kc, tt * 512:(tt + 1) * 512], 4,
                                          lambda kc: [wk(kc), ("oTr", n)])
                            a = acc[:, j, tt * 512:(tt + 1) * 512]; ak = ("macc", j, tt)
                            ss = s[:, tt * 512:(tt + 1) * 512]
                            if n == 0:
                                P.op("dve", lambda e, a=a, pb=pb, ss=ss: e.tensor_tensor(out=a, in0=pb[:], in1=ss, op=ALU.mult), reads=[pk, sk], writes=[ak])
                            else:
                                t = tmp[ti % 4]; tk = ("mtmp", ti % 4); ti += 1
                                P.op("dve", lambda e, t=t, pb=pb, ss=ss: e.tensor_tensor(out=t[:], in0=pb[:], in1=ss, op=ALU.mult), reads=[pk, sk], writes=[tk])
                                if n < 3:
                                    P.op("pool", lambda e, a=a, t=t: e.tensor_tensor(out=a, in0=a, in1=t[:], op=ALU.add), reads=[ak, tk], writes=[ak])
                                else:
                                    mo = ms[:, tt * 512:(tt + 1) * 512]
                                    P.op("pool", lambda e, mo=mo, a=a, t=t: e.tensor_tensor(out=mo, in0=a, in1=t[:], op=ALU.add), reads=[ak, tk], writes=[msk])
                        if n == 3:
                            P.dma(self.mT[ft * 128:(ft + 1) * 128, :], ms[:], reads=[msk], writes=[("mT", ft)])

    def resid_update(self, l, gwhich, ft, xo, xok, banks, tts):
        P = self.P
        for (pb, pk), tt in zip(banks, tts):
            grp = 0 if tt < 2 else 1
            xs = xo[:, tt * 512:(tt + 1) * 512]
            P.op("dve", lambda e, xs=xs, pb=pb, grp=grp: e.scalar_tensor_tensor(
                out=xs, in0=pb[:], scalar=self.modv(l, gwhich, ft, grp), in1=xs, op0=ALU.mult, op1=ALU.add),
                reads=[pk, (xok, tt), "mod"], writes=[(xok, tt)])

    def phase_wout(self, l):
        P, d = self.P, self.din
        with self.phase() as ph:
            mT = ph.sb("mTr", [128, 16, NT], BF16)
            mTv = self.mT.rearrange("(k p) t -> p k t", p=128)
            for q in range(4):
                P.dma(mT[:, :, q * 512:(q + 1) * 512], mTv[:, :, q * 512:(q + 1) * 512], reads=[("mT", f) for f in range(16)], writes=[("mTr", q)])
            ws = WStream(self, ph, [(wsrc(d["w_out"][l], cg * 512, 512), 512) for cg in range(4)], 16, "wo")
            xo_ = [ph.sb(f"xo{i}", [128, NT], F32) for i in range(3)]
            xi = 0
            for cg in range(4):
                w, wk = ws.get(cg)
                for j in range(4):
                    ft = cg * 4 + j
                    xo = xo_[xi % 3]; xok = ("xo", xi % 3); xi += 1
                    P.dma(xo[:], self.xT[ft * 128:(ft + 1) * 128, :], reads=[("xT", ft, t) for t in range(4)], writes=[(xok, t) for t in range(4)])
                    banks = []
                    for tt in range(4):
                        pb, pk = self.bank()
                        self.mm_group(pb, pk, lambda kc: w[:, kc, j * 128:(j + 1) * 128], lambda kc: mT[:, kc, tt * 512:(tt + 1) * 512], 16,
                                      lambda kc: [wk(kc), ("mTr", tt)])
                        banks.append((pb, pk))
                    self.resid_update(l, 2, ft, xo, xok, banks, range(4))
                    P.dma(self.xT[ft * 128:(ft + 1) * 128, :], xo[:], reads=[(xok, t) for t in range(4)], writes=[("xT", ft, t) for t in range(4)])

    def phase_up(self, l, ws):
        P, d = self.P, self.din
        hT = self.hT
        with self.phase() as ph:
            ast = [ph.sb(f"ast{i}", [128, NT], BF16) for i in range(3)]
            rl = [ph.sb(f"url{i}", [128, 512], F32) for i in range(4)]
            ai = 0; ri = 0
            for cg in range(16):
                w, wk = ws.get(cg)
                for j in range(4):
                    ft = cg * 4 + j
                    a = ast[ai % 3]; ak = ("ast", ai % 3); ai += 1
                    for tt in range(4):
                        pb, pk = self.bank()
                        self.mm_group(pb, pk, lambda kc: w[:, kc, j * 128:(j + 1) * 128], lambda kc: hT[:, kc, tt * 512:(tt + 1) * 512], 16,
                                      lambda kc: [wk(kc), ("hT", kc, tt)])
                        r = rl[ri % 4]; rk = ("url", ri % 4); ri += 1
                        P.op("dve", lambda e, r=r, pb=pb: e.tensor_scalar(out=r[:], in0=pb[:], scalar1=0.0, scalar2=None, op0=ALU.max), reads=[pk], writes=[rk])
                        P.op("act", lambda e, a=a, r=r, tt=tt: e.activation(out=a[:, tt * 512:(tt + 1) * 512], in_=r[:], func=AF.Square), reads=[rk], writes=[ak])
                    P.dma(self.aT[ft * 128:(ft + 1) * 128, :], a[:], reads=[ak], writes=[("aT", ft)])

    def phase_down(self, l):
        P, d = self.P, self.din
        with self.phase() as ph:
            ws = WStream(self, ph, [(wsrc(d["w_down"][l], g * 512, 512), 512) for g in range(4)], 64, "wd")
            at_ = [ph.sb(f"dat{i}", [128, 1024], BF16) for i in range(6)]
            xo_ = [ph.sb(f"dxo{i}", [128, 1024], F32) for i in range(4)]
            ati = 0; xi = 0
            for g in range(4):
                w, wk = ws.get(g, prefetch=False)
                for pair in range(2):
                    banks = [[self.bank() for t2 in range(2)] for j in range(4)]
                    for kc in range(64):
                        if kc % 8 == 4:
                            ws.fetch(g + 1, 1)
                        at = at_[ati % 6]; atk = ("dat", ati % 6); ati += 1
                        P.dma(at[:], self.aT[kc * 128:(kc + 1) * 128, pair * 1024:(pair + 1) * 1024], reads=[("aT", kc)], writes=[atk])
                        for j in range(4):
                            for t2 in range(2):
                                pb, pk = banks[j][t2]
                                P.op("pe", lambda e, pb=pb, w=w, kc=kc, j=j, at=at, t2=t2: e.matmul(
                                    pb[:], lhsT=w[:, kc, j * 128:(j + 1) * 128], rhs=at[:, t2 * 512:(t2 + 1) * 512], start=(kc == 0), stop=(kc == 63)),
                                    reads=[wk(kc), atk], writes=[pk])
                    for j in range(4):
                        ft = g * 4 + j
                        xo = xo_[xi % 4]; xok = ("dxo", xi % 4); xi += 1
                        tts = [pair * 2, pair * 2 + 1]
                        P.dma(xo[:], self.xT[ft * 128:(ft + 1) * 128, pair * 1024:(pair + 1) * 1024], reads=[("xT", ft, t) for t in tts],
                              writes=[(xok, t) for t in tts])
                        for t2 in range(2):
                            pb, pk = banks[j][t2]
                            tt = tts[t2]
                            grp = 0 if tt < 2 else 1
                            xs = xo[:, t2 * 512:(t2 + 1) * 512]
                            P.op("dve", lambda e, xs=xs, pb=pb, grp=grp, ft=ft: e.scalar_tensor_tensor(
                                out=xs, in0=pb[:], scalar=self.modv(l, 5, ft, grp), in1=xs, op0=ALU.mult, op1=ALU.add),
                                reads=[pk, (xok, tt), "mod"], writes=[(xok, tt)])
                        P.dma(self.xT[ft * 128:(ft + 1) * 128, pair * 1024:(pair + 1) * 1024], xo[:], reads=[(xok, t) for t in tts],
                              writes=[("xT", ft, t) for t in tts])

    def phase_final(self):
        P = self.P
        with self.phase() as ph:
            yst = [ph.sb(f"yst{i}", [128, D], F32) for i in range(2)]
            yT = [ph.sb(f"yT{i}", [128, 16, 512], F32) for i in range(2)]
            cnt = [0]

            def out(tt, fc, grp, t, tk):
                y = yT[tt % 2]
                P.op("act", lambda e: e.copy(out=y[:, fc, :], in_=t[:]), reads=[tk], writes=[("yT", tt % 2, fc)])
                if fc == 15:
                    for t128 in range(4):
                        ys = yst[cnt[0] % 2]; ysk = ("yst", cnt[0] % 2); cnt[0] += 1
                        for q in range(4):
                            pb, pk = self.bank()
                            for f4 in range(4):
                                f = q * 4 + f4
                                P.op("pe", lambda e, pb=pb, y=y, f=f, f4=f4, t128=t128: e.transpose(
                                    pb[:, f4 * 128:(f4 + 1) * 128], y[:, f, t128 * 128:(t128 + 1) * 128], self.ident[:]),
                                    reads=[("yT", tt % 2, f), "ident"], writes=[pk])
                            if q % 2 == 0:
                                P.op("act", lambda e, ys=ys, pb=pb, q=q: e.copy(out=ys[:, q * 512:(q + 1) * 512], in_=pb[:]), reads=[pk], writes=[ysk])
                            else:
                                P.op("dve", lambda e, ys=ys, pb=pb, q=q: e.tensor_copy(out=ys[:, q * 512:(q + 1) * 512], in_=pb[:]), reads=[pk], writes=[ysk])
                        tok = tt * 512 + t128 * 128
                        dst = self.dout["y_c"][tok:tok + 128, :] if tok < 1024 else self.dout["y_l"][tok - 1024:tok - 1024 + 128, :]
                        P.dma(dst, ys[:], reads=[ysk])
            self.norm_tiles(ph, lambda fc, grp: self.gfin[:, fc:fc + 1], None, out)

    def build(self, stages=None):
        def on(name):
            return stages is None or name in stages
        self.declare()
        self.globals_()
        self.P.barrier()
        if on("xT"):
            self.phase_xT()
        if on("adaln"):
            self.phase_adaln()
        if on("mixers"):
            self.phase_cache()
            self.phase_lam()
        for l in range(DEPTH):
            if stages is not None and f"L{l}" not in stages:
                continue
            with self.phase() as ph:
                self.hT = ph.sb("hT", [128, 16, NT], BF16)
                wsi = self.inproj_ws(ph, l)
                if on("norm1"):
                    self.phase_norm_h(ph, l, 0)
                if "hT" in self.dbg and l == 0:
                    self.P.dma(self.outp("hT", [128, 16, NT], BF16), self.hT[:], reads=[("hT", fc, tt) for fc in range(16) for tt in range(4)])
                if on("inproj"):
                    self.phase_inproj(l, wsi)
            if on("mixers"):
                self.phase_mixers(l)
            if on("merge"):
                self.phase_merge(l)
            if on("wout"):
                self.phase_wout(l)
            with self.phase() as ph:
                self.hT = ph.sb("hT", [128, 16, NT], BF16)
                wsu = self.up_ws(ph, l)
                if on("norm2"):
                    self.phase_norm_h(ph, l, 1)
                if on("up"):
                    self.phase_up(l, wsu)
            if on("down"):
                self.phase_down(l)
        if on("final"):
            self.phase_final()
        if "mod" in self.dbg:
            self.P.dma(self.outp("modo", [128, 2 * 96 * 2], F32), self.mod[:].rearrange("p a b c -> p (a b c)"), reads=["mod"])
        self.P.barrier()
        self.P.emit()
        self.es.close()
        return self.nc

    def bank_i(self, i):
        return self.ps[i], ("ps", i)

    def phase_cache(self):
        P, d = self.P, self.din
        with self.phase() as ph:
            ci = 0
            for l in range(2):
                for (kn, vn, w, krow, vcol) in (("cak", "cav", 512, QK_ROW["ak"], V_COL["av"]), ("cbk", "cbv", 512, QK_ROW["bk"], V_COL["bv"]),
                                               ("cck", "ccv", 128, QK_ROW["ck"], V_COL["cv"])):
                    kf = ph.sb("ckf", [128, 2, 512], F32); kb = ph.sb("ckb", [128, 2, 512], BF16)
                    vf = ph.sb("cvf", [128, 2, 512], F32); vb = ph.sb("cvb", [128, 2, 512], BF16)
                    kT = ph.sb("ckT", [128, 4, 256], BF16)
                    u = P.uid; P.uid += 1
                    P.dma(kf[:, :, :w], d[kn][l].rearrange("(c p) f -> p c f", p=128), writes=[("ckf", u)])
                    P.dma(vf[:, :, :w], d[vn][l].rearrange("(c p) f -> p c f", p=128), writes=[("cvf", u)])
                    P.op("dve", lambda e, kb=kb, kf=kf, w=w: e.tensor_copy(out=kb[:, :, :w], in_=kf[:, :, :w]), reads=[("ckf", u)], writes=[("ckb", u)])
                    P.op("pool", lambda e, vb=vb, vf=vf, w=w: e.tensor_copy(out=vb[:, :, :w], in_=vf[:, :, :w]), reads=[("cvf", u)], writes=[("cvb", u)])
                    P.dma(self.vtok[l][NT:NTX, vcol:vcol + w].rearrange("(c p) f -> p c f", p=128), vb[:, :, :w], reads=[("cvb", u)], writes=[("vtokc", l, vn)])
                    for fb in range(w // 128):
                        pb, pk = self.bank()
                        pbb = pb.bitcast(BF16)
                        for c in range(2):
                            P.op("pe", lambda e, pbb=pbb, kb=kb, c=c, fb=fb: e.transpose(pbb[:, c * 128:(c + 1) * 128], kb[:, c, fb * 128:(fb + 1) * 128], self.identb[:]),
                                 reads=[("ckb", u), "identb"], writes=[pk])
                        P.op("act", lambda e, kT=kT, pbb=pbb, fb=fb: e.copy(out=kT[:, fb, :], in_=pbb[:, 0:256]), reads=[pk], writes=[("ckT", u, fb)])
                        P.dma(self.qkT[l][krow + fb * 128:krow + (fb + 1) * 128, NT:NTX], kT[:, fb, :], reads=[("ckT", u, fb)], writes=[("qkTc", l)])

    def phase_lam(self):
        P, d = self.P, self.din
        self.lam = self.gsb("lam", [128, 2], F32)
        self.gsc = self.gsb("gsc", [128, 2], F32)
        self.esink = self.gsb("esink", [128, 2, 8], F32)
        with self.phase() as ph:
            dl = ph.sb("dl", [128, 2, 4, 64], F32)
            pr = ph.sb("dlp", [128, 2, 2, 64], F32)
            sm = ph.sb("dls", [128, 2, 2], F32)
            ex = ph.sb("dle", [128, 2, 2], F32)
            gn = ph.sb("dgn", [128, 2], F32)
            sk = ph.sb("ssk", [128, 2, 8], F32)
            for l in range(2):
                P.dma(dl[:, l], d["diff_lambda"][l].partition_broadcast(128), writes=["dl"])
                P.dma(gn[:, l:l + 1], d["diff_norm_g"][l].rearrange("(p o) -> p o", o=1), writes=["dgn"])
                P.dma(sk[:, l], d["swa_sink"][l].partition_broadcast(128), writes=["ssk"])
            for l in range(2):
                P.op("dve", lambda e, l=l: e.tensor_tensor(out=pr[:, l], in0=dl[:, l, 0:4:2, :], in1=dl[:, l, 1:4:2, :], op=ALU.mult), reads=["dl"], writes=["dlp"])
                P.op("dve", lambda e, l=l: e.tensor_reduce(out=sm[:, l], in_=pr[:, l], axis=mybir.AxisListType.X, op=ALU.add), reads=["dlp"], writes=["dls"], hard=True)
            P.op("act", lambda e: e.activation(out=ex[:], in_=sm[:], func=AF.Exp), reads=["dls"], writes=["dle"])
            P.op("act", lambda e: e.activation(out=self.esink[:], in_=sk[:], func=AF.Exp), reads=["ssk"], writes=["esink"])
            for l in range(2):
                lam_init = 0.8 - 0.6 * math.exp(-0.3 * l)
                P.op("dve", lambda e, l=l, lam_init=lam_init: e.scalar_tensor_tensor(
                    out=self.lam[:, l:l + 1], in0=ex[:, l, 0:1], scalar=lam_init, in1=ex[:, l, 1:2], op0=ALU.add, op1=ALU.subtract),
                    reads=["dle"], writes=["lam"])
                P.op("dve", lambda e, l=l, lam_init=lam_init: e.tensor_scalar(
                    out=self.gsc[:, l:l + 1], in0=gn[:, l:l + 1], scalar1=1.0 - lam_init, scalar2=None, op0=ALU.mult), reads=["dgn"], writes=["gsc"])

    def softmax_accum(self, bufs, acc, ncols, dvp, chunks, look=2):
        P = self.P
        (bo, bok), (bd, bdk) = acc
        nch = len(chunks)

        def stage_s(ci):
            ch = chunks[ci]
            bs, bsk = self.bank_i(4 + self.sbank % 4); self.sbank += 1
            ns = len(ch["s"])
            for mi, (c0, n, la, ra) in enumerate(ch["s"]):
                P.op("pe", lambda e, bs=bs, c0=c0, n=n, la=la, ra=ra, mi=mi, ns=ns: e.matmul(bs[:, c0:c0 + n], lhsT=la, rhs=ra, start=(mi == 0), stop=(mi == ns - 1),
                                                                                 skip_group_check=True), reads=ch["sreads"], writes=[bsk])
            i = bufs["pti"] % len(bufs["pt"]); bufs["pti"] += 1
            pt, ptk = bufs["pt"][i], ("pt", i)
            if ch.get("bias") is not None:
                j = bufs["tmi"] % len(bufs["tm"]); bufs["tmi"] += 1
                tm, tmk = bufs["tm"][j], ("ptm", j)
                P.op("dve", lambda e, tm=tm, bs=bs, b=ch["bias"]: e.scalar_tensor_tensor(out=tm[:, :ncols], in0=bs[:, :ncols], scalar=0.125, in1=b, op0=ALU.mult, op1=ALU.add),
                     reads=[bsk] + ch["breads"], writes=[tmk])
                P.op("act", lambda e, pt=pt, tm=tm: e.activation(out=pt[:, :ncols], in_=tm[:, :ncols], func=AF.Exp), reads=[tmk], writes=[ptk])
            else:
                P.op("act", lambda e, pt=pt, bs=bs: e.activation(out=pt[:, :ncols], in_=bs[:, :ncols], func=AF.Exp, scale=0.125), reads=[bsk], writes=[ptk])
            if ch.get("mask") is not None:
                mk = ch["mask"]
                P.op("pool", lambda e, pt=pt, mk=mk: e.tensor_tensor(out=pt[:, :ncols].rearrange("p (r q) -> p r q", q=128), in0=pt[:, :ncols].rearrange("p (r q) -> p r q", q=128),
                                                                   in1=mk, op=ALU.mult), reads=[ptk, "cmask"], writes=[ptk])
            return pt, ptk

        def stage_pv(ci, pt, ptk):
            ch = chunks[ci]
            for mi, (c0, n, va) in enumerate(ch["pv"]):
                P.op("pe", lambda e, c0=c0, n=n, va=va, pt=pt, ci=ci, mi=mi: e.matmul(
                    bo[0:dvp, c0:c0 + n], lhsT=va, rhs=pt[:, c0:c0 + n], start=(ci == 0 and mi == 0), stop=(ci == nch - 1), skip_group_check=True),
                    reads=[ptk] + ch["vreads"], writes=[bok])
            P.op("pe", lambda e, pt=pt, ci=ci: e.matmul(bd[0:dvp, 0:ncols], lhsT=self.onesb[:, 0:dvp], rhs=pt[:, :ncols], start=(ci == 0), stop=(ci == nch - 1)),
                 reads=[ptk, "onesb"], writes=[bdk])

        pend = {}
        for ci in range(nch):
            pend[ci] = stage_s(ci)
            if ci >= look:
                stage_pv(ci - look, *pend.pop(ci - look))
        for ci in sorted(pend):
            stage_pv(ci, *pend[ci])

    def act_pow(self, out, in_, power, reads, writes, scale=1.0, bias=None, w2=None):
        P = self.P
        if bias is None:
            P.op("act", lambda e: e.activation(out=out, in_=in_, func=AF.Ln, scale=scale), reads=reads, writes=writes)
        else:
            P.op("act", lambda e: e.activation(out=out, in_=in_, func=AF.Ln, scale=scale, bias=bias), reads=reads + ["epst"], writes=writes)
        P.op("act", lambda e: e.activation(out=out, in_=out, func=AF.Exp, scale=power), reads=writes, writes=writes)

    def attn_bufs(self, ph):
        return dict(pt=[ph.sb(f"pt{i}", [128, 512], BF16) for i in range(4)], pti=0,
                    tm=[ph.sb(f"ptm{i}", [128, 512], F32) for i in range(3)], tmi=0)

    def load_qkv(self, ph, l, qrow, nqh, krow, nkh, vcol, vw, tag):
        P = self.P
        Q = ph.sb("Q" + tag, [64, nqh, NT], BF16); Kt = ph.sb("K" + tag, [64, nkh, NTX], BF16); V = ph.sb("V" + tag, [128, 18, vw], BF16)
        qr = [("qkT", (qrow // 128) + i) for i in range((nqh * 64 + 127) // 128)]
        kr = [("qkT", (krow // 128) + i) for i in range((nkh * 64 + 127) // 128)]
        nm = {0: "av", 512: "bv", 1024: "cv"}[vcol]
        for part, (t0, t1, k1, c0, c1) in (("c", (0, 1024, 1024, 0, 8)), ("l", (1024, NT, NTX, 8, 18))):
            for h0 in range(0, nqh, 4):
                P.dma(Q[:, h0:h0 + 4, t0:t1], self.qkT[l][qrow + h0 * 64:qrow + (h0 + 4) * 64, t0:t1].rearrange("(h d) t -> d h t", d=64), reads=qr, writes=[("Q" + tag, part, h0)])
            for h0 in range(0, nkh, 4):
                h1 = min(nkh, h0 + 4)
                P.dma(Kt[:, h0:h1, t0:k1], self.qkT[l][krow + h0 * 64:krow + h1 * 64, t0:k1].rearrange("(h d) t -> d h t", d=64), reads=kr + [("qkTc", l)], writes=[("K" + tag, part, h0)])
            P.dma(V[:, c0:c1, :], self.vtok[l][c0 * 128:c1 * 128, vcol:vcol + vw].rearrange("(c p) f -> p c f", p=128),
                  reads=[("vtok", nm, t) for t in range(16)], writes=[("V" + tag, part)])
        return Q, Kt, V

    def phase_attn_a(self, l):
        P = self.P
        with self.phase() as ph:
            Q, Kt, V = self.load_qkv(ph, l, QK_ROW["aq"], 8, QK_ROW["ak"], 8, V_COL["av"], 512, "a")
            bufs = self.attn_bufs(ph)
            NB = 2
            rdA = [[ph.sb(f"ard{i}_{b}", [128, 512], F32) for i in range(2)] for b in range(NB)]
            t0A = [ph.sb(f"at0_{b}", [128, 512], F32) for b in range(NB)]; t1A = [ph.sb(f"at1_{b}", [128, 512], F32) for b in range(NB)]
            osqA = [ph.sb(f"aosq{b}", [128, 512], BF16) for b in range(NB)]; rrA = [ph.sb(f"arr{b}", [128, 512], F32) for b in range(NB)]
            ost = [ph.sb(f"aost{i}", [128, 512], BF16) for i in range(2)]
            oi = 0
            groups = [(s * 256, 256, [s * 256, s * 256 + 128]) for s in range(4)]
            groups += [(1024 + i * 512, 512, [1024 + k * 128 for k in range(10)]) for i in range(2)]
            ui = 0
            for (q0, nq, kst) in groups:
                for h in range(4):
                    accs = []
                    for c in range(2):
                        acc = (self.bank_i(2 * c), self.bank_i(2 * c + 1))
                        ch_ = c * 4 + h
                        pt_ = "c" if q0 < 1024 else "l"
                        chunks = [dict(s=[(0, nq, Kt[:, ch_, k0:k0 + 128], Q[:, ch_, q0:q0 + nq])], sreads=[("Qa", pt_, (ch_ // 4) * 4), ("Ka", pt_, (ch_ // 4) * 4)],
                                       pv=[(0, nq, V[:, k0 // 128, h * 128:(h + 1) * 128])], vreads=[("Va", pt_)]) for k0 in kst]
                        self.softmax_accum(bufs, acc, nq, 128, chunks)
                        accs.append(acc)
                    (bo0, bok0), (bd0, bdk0) = accs[0]
                    (bo1, bok1), (bd1, bdk1) = accs[1]
                    b = ui % NB; ui += 1
                    rd, t0, t1, osq, rr = rdA[b], t0A[b], t1A[b], osqA[b], rrA[b]
                    k = lambda nm: (nm, b)
                    self.act_pow(rd[0][:, :nq], bd0[:, :nq], -1.0, [bdk0], [k("ard0")])
                    self.act_pow(rd[1][:, :nq], bd1[:, :nq], -1.0, [bdk1], [k("ard1")])
                    P.op("dve", lambda e, nq=nq, t0=t0, rd=rd: e.tensor_tensor(out=t0[:, :nq], in0=bo0[:, :nq], in1=rd[0][:, :nq], op=ALU.mult), reads=[bok0, k("ard0")], writes=[k("at0")])
                    P.op("dve", lambda e, nq=nq, t1=t1, rd=rd: e.scalar_tensor_tensor(out=t1[:, :nq], in0=bo1[:, :nq], scalar=self.lam[:, l:l + 1], in1=rd[1][:, :nq], op0=ALU.mult, op1=ALU.mult),
                         reads=[bok1, k("ard1"), "lam"], writes=[k("at1")])
                    P.op("pool", lambda e, nq=nq, t0=t0, t1=t1: e.tensor_tensor(out=t0[:, :nq], in0=t0[:, :nq], in1=t1[:, :nq], op=ALU.subtract), reads=[k("at0"), k("at1")], writes=[k("at0")])
                    P.op("act", lambda e, nq=nq, osq=osq, t0=t0: e.activation(out=osq[:, :nq], in_=t0[:, :nq], func=AF.Square), reads=[k("at0")], writes=[k("aosq")])
                    bs, bsk = self.bank_i(4 + self.sbank % 4); self.sbank += 1
                    P.op("pe", lambda e, bs=bs, nq=nq, osq=osq: e.matmul(bs[:, :nq], lhsT=self.onesb[:], rhs=osq[:, :nq], start=True, stop=True), reads=[k("aosq"), "onesb"], writes=[bsk])
                    self.act_pow(rr[:, :nq], bs[:, :nq], -0.5, [bsk], [k("arr")], scale=1.0 / 128, bias=self.epst[:, 0:1])
                    o = ost[oi % 2]; ok_ = ("aost", oi % 2); oi += 1
                    P.op("dve", lambda e, o=o, nq=nq, t0=t0, rr=rr: e.scalar_tensor_tensor(out=o[:, :nq], in0=t0[:, :nq], scalar=self.gsc[:, l:l + 1], in1=rr[:, :nq], op0=ALU.mult, op1=ALU.mult),
                         reads=[k("at0"), k("arr"), "gsc"], writes=[ok_])
                    P.dma(self.oT[h * 128:(h + 1) * 128, q0:q0 + nq], o[:, :nq], reads=[ok_], writes=[("oT", 0)])

    def phase_attn_bc_ctx(self, l, which, Q, Kt, V, bufs, ph):
        P = self.P
        rd = ph.sb("brd", [64, 512], F32)
        ost = [ph.sb(f"bost{i}", [64, 512], BF16) for i in range(2)]
        oi = 0
        tag = "b" if which == "b" else "c"
        row0 = 512 if which == "b" else 1024
        for s in range(4):
            q0 = s * 256
            for h in range(8):
                kh = h if which == "b" else h // 4
                acc = (self.bank_i((oi % 2) * 2), self.bank_i((oi % 2) * 2 + 1))
                chunks = [dict(s=[(0, 256, Kt[:, kh, k0:k0 + 128], Q[:, h, q0:q0 + 256])], sreads=[("Q" + tag, "c", (h // 4) * 4), ("K" + tag, "c", (kh // 4) * 4)],
                               pv=[(0, 256, V[:, k0 // 128, kh * 64:(kh + 1) * 64])], vreads=[("V" + tag, "c")]) for k0 in (q0, q0 + 128)]
                self.softmax_accum(bufs, acc, 256, 64, chunks)
                (bo, bok), (bd, bdk) = acc
                if which == "c":
                    P.op("dve", lambda e, bd=bd, h=h: e.tensor_scalar(out=rd[:, :256], in0=bd[0:64, :256], scalar1=self.esink[0:64, l, h:h + 1], scalar2=None, op0=ALU.add),
                         reads=[bdk, "esink"], writes=["brd"])
                    self.act_pow(rd[:, :256], rd[:, :256], -1.0, ["brd"], ["brd"])
                else:
                    self.act_pow(rd[:, :256], bd[0:64, :256], -1.0, [bdk], ["brd"])
                o = ost[oi % 2]; ok_ = ("bost", oi % 2); oi += 1
                P.op("dve", lambda e, o=o, bo=bo: e.tensor_tensor(out=o[:, :256], in0=bo[0:64, :256], in1=rd[:, :256], op=ALU.mult), reads=[bok, "brd"], writes=[ok_])
                P.dma(self.oT[row0 + h * 64:row0 + (h + 1) * 64, q0:q0 + 256], o[:, :256], reads=[ok_], writes=[("oT", 1 if which == "b" else 2)])

    def phase_attn_b(self, l):
        P, d = self.P, self.din
        with self.phase() as ph:
            Q, Kt, V = self.load_qkv(ph, l, QK_ROW["bq"], 8, QK_ROW["bk"], 8, V_COL["bv"], 512, "b")
            bufs = self.attn_bufs(ph)
            self.phase_attn_bc_ctx(l, "b", Q, Kt, V, bufs, ph)
            braw = ph.sb("braw", [128, 14, 8, 64], F32); BT = ph.sb("BT", [128, 14, 8, 64], F32)
            m01 = ph.sb("m01", [128, 64], F32); mng = ph.sb("mng", [128, 64], F32); zt = ph.sb("bzt", [1, 64], F32)
            P.dma(m01[:], d["c_m01"], writes=["m01"]); P.dma(mng[:], d["c_mng"], writes=["mng"])
            P.op("dve", lambda e: e.memset(zt[:], 0.0), writes=["bzt"])
            nb = self.nbpad[l]
            P.dma(nb[0:1, 0:64], zt[:], reads=["bzt"], writes=["nbp0"])
            P.dma(nb[0:1, 64 + 3720:64 + 3720 + 64], zt[:], reads=["bzt"], writes=["nbp1"])
            P.dma(nb[0:1, 64:64 + 3720], d["na_bias"][l].rearrange("(o h) a b -> o (h a b)", o=1), writes=["nbp2"])
            for jj in range(2):
                for h in range(8):
                    src = bass.AP(nb.tensor, 64 + jj * 31 - 48 + h * 465, [[1, 64], [31, 14], [1, 64]])
                    P.dma(braw[jj * 64:(jj + 1) * 64, :, h, :], src, reads=["nbp0", "nbp1", "nbp2"], writes=[("braw", jj, h)])
            bt3 = BT[:].rearrange("p a h q -> p (a h) q")
            P.op("dve", lambda e: e.tensor_tensor(out=bt3, in0=braw[:].rearrange("p a h q -> p (a h) q")[:, :, ::-1],
                                                  in1=m01[:, None, :].broadcast_to([128, 112, 64]), op=ALU.mult), reads=[("braw", jj, h) for jj in range(2) for h in range(8)] + ["m01"], writes=["BT"])
            P.op("dve", lambda e: e.tensor_tensor(out=bt3, in0=bt3, in1=mng[:, None, :].broadcast_to([128, 112, 64]), op=ALU.add), reads=["BT", "mng"], writes=["BT"])
            OB = ph.sb("OB", [64, 8, 1024], BF16)
            rd = ph.sb("blrd", [64, 512], F32)
            V2 = ph.sb("Vb2", [128, 7, 512], BF16)
            P.dma(V2[:], self.vtok[l][1088:1088 + 7 * 128, 512:1024].rearrange("(c p) f -> p c f", p=128),
                  reads=[("vtok", "bv", t) for t in range(16)], writes=["Vb2"])
            for r in range(16):
                r0 = min(max(r - 4, 0), 8)
                q0 = 1024 + r * 64
                acc = (self.bank_i((r % 2) * 2), self.bank_i((r % 2) * 2 + 1))
                chunks = []
                for m in range(6):
                    if m < 4:
                        k0 = 1024 + (r0 + 2 * m) * 64
                        af = r0 + 2 * m - r + 7
                        bias = BT[:, af].rearrange("p h q -> p (h q)")
                    else:
                        k0 = NT + (m - 4) * 128
                        bias = None
                    Vc_ = V[:, k0 // 128] if k0 % 128 == 0 else V2[:, (k0 - 1088) // 128]
                    chunks.append(dict(s=[(h * 64, 64, Kt[:, h, k0:k0 + 128], Q[:, h, q0:q0 + 64]) for h in range(8)], sreads=[("Qb", "l", 0), ("Qb", "l", 4), ("Kb", "l", 0), ("Kb", "l", 4)],
                                       pv=[(h * 64, 64, Vc_[:, h * 64:(h + 1) * 64]) for h in range(8)], vreads=[("Vb", "l"), "Vb2"], bias=bias, breads=["BT"]))
                self.softmax_accum(bufs, acc, 512, 64, chunks)
                (bo, bok), (bd, bdk) = acc
                self.act_pow(rd[:], bd[0:64, :], -1.0, [bdk], ["blrd"])
                P.op("dve", lambda e, bo=bo, r=r: e.tensor_tensor(out=OB[:, :, r * 64:(r + 1) * 64], in0=bo[0:64, :].rearrange("p (h q) -> p h q", q=64),
                                                                  in1=rd[:].rearrange("p (h q) -> p h q", q=64), op=ALU.mult), reads=[bok, "blrd"], writes=["OB"])
            P.dma(self.oT[512:1024, 1024:2048].rearrange("(h d) t -> d h t", d=64), OB[:], reads=["OB"], writes=[("oT", 1)])

    def phase_attn_c(self, l):
        P, d = self.P, self.din
        with self.phase() as ph:
            Q, Kt, V = self.load_qkv(ph, l, QK_ROW["cq"], 8, QK_ROW["ck"], 2, V_COL["cv"], 128, "c")
            bufs = self.attn_bufs(ph)
            self.phase_attn_bc_ctx(l, "c", Q, Kt, V, bufs, ph)
            cm = ph.sb("cmask", [128, 2, 128], BF16)
            P.dma(cm[:], d["c_cmask"], writes=["cmask"])
            OC = ph.sb("OC", [64, 8, 1024], BF16)
            rd = ph.sb("clrd", [64, 512], F32)
            it = 0
            for nbk in range(8):
                q0 = 1024 + nbk * 128
                for g in range(2):
                    acc = (self.bank_i((it % 2) * 2), self.bank_i((it % 2) * 2 + 1)); it += 1
                    chunks = []
                    for (kb, mi) in ((nbk - 1, 0), (nbk, None), (nbk + 1, 1), (8, None), (9, None)):
                        if kb < 0 or (kb > 7 and mi is not None):
                            continue
                        k0 = 1024 + kb * 128
                        mask = None if mi is None else cm[:, mi:mi + 1, :].broadcast_to([128, 4, 128])
                        chunks.append(dict(s=[(rq * 128, 128, Kt[:, g, k0:k0 + 128], Q[:, g * 4 + rq, q0:q0 + 128]) for rq in range(4)], sreads=[("Qc", "l", g * 4), ("Kc", "l", 0)],
                                           pv=[(0, 512, V[:, k0 // 128, g * 64:(g + 1) * 64])], vreads=[("Vc", "l")], mask=mask))
                    self.softmax_accum(bufs, acc, 512, 64, chunks)
                    (bo, bok), (bd, bdk) = acc
                    P.op("dve", lambda e, bd=bd, g=g: e.tensor_tensor(out=rd[:].rearrange("p (r q) -> p r q", q=128), in0=bd[0:64, :].rearrange("p (r q) -> p r q", q=128),
                                                                 in1=self.esink[0:64, l, g * 4:(g + 1) * 4].unsqueeze(2).broadcast_to([64, 4, 128]), op=ALU.add),
                         reads=[bdk, "esink"], writes=["clrd"])
                    self.act_pow(rd[:], rd[:], -1.0, ["clrd"], ["clrd"])
                    P.op("dve", lambda e, bo=bo, g=g, nbk=nbk: e.tensor_tensor(out=OC[:, g * 4:(g + 1) * 4, nbk * 128:(nbk + 1) * 128], in0=bo[0:64, :].rearrange("p (r q) -> p r q", q=128),
                                                                       in1=rd[:].rearrange("p (r q) -> p r q", q=128), op=ALU.mult), reads=[bok, "clrd"], writes=["OC"])
            P.dma(self.oT[1024:1536, 1024:2048].rearrange("(h d) t -> d h t", d=64), OC[:], reads=["OC"], writes=[("oT", 2)])

    def hy_ffn(self, ph, l, L, h2T, cst):
        P, d = self.P, self.din
        zT = ph.sb("hzT", [33, 1024], F32); h1T = ph.sb("hh1", [64, 1024], F32)
        arg = ph.sb("harg", [64, 512], F32); wr = ph.sb("hwr", [64, 512], F32)
        u = P.uid; P.uid += 1
        P.dma(zT[:, :L], d[f"c_z{L}"], writes=[("hzT", u)])
        for stage in range(2):
            src = zT if stage == 0 else h1T
            dst = h1T if stage == 0 else h2T
            K = 33 if stage == 0 else 64
            wt = cst["w1"] if stage == 0 else cst["w2"]
            fb = cst["fb"][:, stage:stage + 1]
            for n0 in range(0, L, 512):
                n = min(512, L - n0)
                pb, pk = self.bank()
                P.op("pe", lambda e, pb=pb, wt=wt, K=K, src=src, n0=n0, n=n: e.matmul(pb[0:64, :n], lhsT=wt[0:K, :], rhs=src[0:K, n0:n0 + n], start=True, stop=True),
                     reads=["hyc", ("hzT", u), ("hh", u, 0)], writes=[pk])
                P.op("dve", lambda e, pb=pb, n=n, fb=fb: e.tensor_scalar(out=arg[:, :n], in0=pb[0:64, :n], scalar1=cst["freq"][:, 0:1], scalar2=fb, op0=ALU.mult, op1=ALU.add),
                     reads=[pk, "hyc"], writes=["harg"], hard=True)
                P.op("dve", lambda e, n=n: e.tensor_scalar(out=wr[:, :n], in0=arg[:, :n], scalar1=math.pi, scalar2=-2 * math.pi, op0=ALU.is_gt, op1=ALU.mult), reads=["harg"], writes=["hwr"])
                P.op("dve", lambda e, n=n: e.tensor_tensor(out=wr[:, :n], in0=wr[:, :n], in1=arg[:, :n], op=ALU.add), reads=["hwr", "harg"], writes=["hwr"])
                P.op("dve", lambda e, n=n: e.tensor_scalar(out=arg[:, :n], in0=arg[:, :n], scalar1=-math.pi, scalar2=2 * math.pi, op0=ALU.is_lt, op1=ALU.mult), reads=["harg", "hwr"], writes=["harg"])
                P.op("dve", lambda e, n=n: e.tensor_tensor(out=wr[:, :n], in0=wr[:, :n], in1=arg[:, :n], op=ALU.add), reads=["hwr", "harg"], writes=["hwr"])
                P.op("act", lambda e, dst=dst, n0=n0, n=n: e.activation(out=dst[:, n0:n0 + n], in_=wr[:, :n], func=AF.Sin), reads=["hwr"], writes=[("hh", u, stage)])
        return ("hh", u, 1)

    def hy_G(self, ph, l, L, o, h2T, h2k, cst, F, Fk, hs, hd, G, Gk, tmp, hsk="hYre", hdk="hYs"):
        P = self.P
        nch = L // 128
        win, ta, tb = tmp
        ntc = cst["nt"][L]
        for nc_ in range(nch):
            banks = []
            for dr in range(2):
                pb, pk = self.bank()
                col = (o * 2 + dr) * 512
                P.op("pe", lambda e, pb=pb, nc_=nc_, col=col: e.matmul(pb[:, :], lhsT=h2T[0:64, nc_ * 128:(nc_ + 1) * 128], rhs=cst["w3"][0:64, col:col + 512], start=True, stop=True),
                     reads=[h2k, "hyc"], writes=[pk])
                banks.append((pb, pk))
            P.op("act", lambda e, nc_=nc_: e.activation(out=win[:], in_=cst["dabs"][:, o * 1024:(o + 1) * 1024], func=AF.Exp, scale=ntc[:, nc_:nc_ + 1]),
                 reads=["hyc"], writes=["hwin"])
            P.op("dve", lambda e, pb=banks[0][0]: e.tensor_tensor(out=ta[:], in0=pb[:], in1=win[:, 0:512], op=ALU.mult), reads=[banks[0][1], "hwin"], writes=["hta"])
            P.op("dve", lambda e, pb=banks[1][0]: e.tensor_tensor(out=tb[:], in0=pb[:], in1=win[:, 512:1024], op=ALU.mult), reads=[banks[1][1], "hwin"], writes=["htb"])
            P.op("pool", lambda e, nc_=nc_: e.tensor_tensor(out=hs[:, nc_, :], in0=ta[:], in1=tb[:], op=ALU.add), reads=["hta", "htb"], writes=[hsk])
            P.op("pool", lambda e, nc_=nc_: e.tensor_tensor(out=hd[:, nc_, :], in0=tb[:], in1=ta[:], op=ALU.subtract), reads=["hta", "htb"], writes=[hdk])
        for fc in range(nch):
            pr, prk = self.bank(); pi_, pik = self.bank()
            for nc_ in range(nch):
                P.op("pe", lambda e, pr=pr, nc_=nc_, fc=fc: e.matmul(pr[:], lhsT=F[:, nc_, fc * 128:(fc + 1) * 128], rhs=hs[:, nc_, :], start=(nc_ == 0), stop=(nc_ == nch - 1)),
                     reads=[Fk, hsk], writes=[prk])
            for nc_ in range(nch):
                P.op("pe", lambda e, pi_=pi_, nc_=nc_, fc=fc: e.matmul(pi_[:], lhsT=F[:, nc_, L + fc * 128:L + (fc + 1) * 128], rhs=hd[:, nc_, :], start=(nc_ == 0), stop=(nc_ == nch - 1)),
                     reads=[Fk, hdk], writes=[pik])
            P.op("dve", lambda e, pr=pr, fc=fc: e.tensor_tensor(out=G[:, fc, 0, :], in0=pr[:], in1=cst["biasb"][:, o, :], op=ALU.add), reads=[prk, "hyc"], writes=[Gk])
            P.op("act", lambda e, pi_=pi_, fc=fc: e.copy(out=G[:, fc, 1, :], in_=pi_[:]), reads=[pik], writes=[Gk])

    def hy_dft_mult(self, L, F, Fk, xin, xink, G, Gk, Yre, Ys, mt):
        P = self.P
        nch = L // 128
        for fc in range(nch):
            pr, prk = self.bank(); pi_, pik = self.bank()
            for tc in range(nch):
                P.op("pe", lambda e, pr=pr, tc=tc, fc=fc: e.matmul(pr[:], lhsT=F[:, tc, fc * 128:(fc + 1) * 128], rhs=xin[:, tc, :], start=(tc == 0), stop=(tc == nch - 1)),
                     reads=[Fk, xink], writes=[prk])
            for tc in range(nch):
                P.op("pe", lambda e, pi_=pi_, tc=tc, fc=fc: e.matmul(pi_[:], lhsT=F[:, tc, L + fc * 128:L + (fc + 1) * 128], rhs=xin[:, tc, :], start=(tc == 0), stop=(tc == nch - 1)),
                     reads=[Fk, xink], writes=[pik])
            m1, m2, m3, m4 = mt
            P.op("dve", lambda e, pr=pr, fc=fc: e.tensor_tensor(out=m1[:], in0=pr[:], in1=G[:, fc, 0, :], op=ALU.mult), reads=[prk, Gk], writes=["hm1"])
            P.op("dve", lambda e, pi_=pi_, fc=fc: e.tensor_tensor(out=m2[:], in0=pi_[:], in1=G[:, fc, 1, :], op=ALU.mult), reads=[pik, Gk], writes=["hm2"])
            P.op("dve", lambda e, pi_=pi_, fc=fc: e.tensor_tensor(out=m3[:], in0=pi_[:], in1=G[:, fc, 0, :], op=ALU.mult), reads=[pik, Gk], writes=["hm3"])
            P.op("dve", lambda e, pr=pr, fc=fc: e.tensor_tensor(out=m4[:], in0=pr[:], in1=G[:, fc, 1, :], op=ALU.mult), reads=[prk, Gk], writes=["hm4"])
            P.op("pool", lambda e, fc=fc: e.tensor_tensor(out=Yre[:, fc, :], in0=m1[:], in1=m2[:], op=ALU.add), reads=["hm1", "hm2"], writes=["hYre"])
            P.op("pool", lambda e, fc=fc: e.tensor_tensor(out=Ys[:, fc, :], in0=m3[:], in1=m4[:], op=ALU.subtract), reads=["hm3", "hm4"], writes=["hYs"])

    def hy_conv(self, l, tok0, nseq, Ls, cst, bufs):
        P = self.P
        uring, ucf, ucb, x2b, vtm, x1tm, Yre, Ys, mt, ost = bufs
        sw = cst["sw"]
        LT = nseq * Ls
        nchs = Ls // 128
        v3 = lambda ap, a, b: ap[:, 0:LT].rearrange("p (s t) -> p s t", t=Ls)[:, :, a:b]
        ui = 0
        for part in range(3):
            for ct in range(4):
                tile = part * 4 + ct
                u = uring[ui % 2]; uk = ("hu", ui % 2)
                cf = ucf[0]; cfk = ("hcf", 0); ui += 1
                P.dma(u[:, :LT], self.hyT[tile * 128:(tile + 1) * 128, tok0:tok0 + LT], reads=[("hyT", tile)], writes=[uk])
                P.op("dve", lambda e, cf=cf, u=u, tile=tile: e.tensor_scalar(out=cf[:, :LT], in0=u[:, :LT], scalar1=sw[:, 12 + tile:13 + tile], scalar2=None, op0=ALU.mult),
                     reads=[uk, "hyc"], writes=[cfk])
                P.op("dve", lambda e, cf=cf, u=u, tile=tile: e.scalar_tensor_tensor(out=v3(cf, 1, Ls), in0=v3(u, 0, Ls - 1), scalar=sw[:, tile:tile + 1], in1=v3(cf, 1, Ls), op0=ALU.mult, op1=ALU.add),
                     reads=[uk, cfk, "hyc"], writes=[cfk])
                P.op("dve", lambda e, cf=cf, u=u, tile=tile: e.scalar_tensor_tensor(out=v3(cf, 0, Ls - 1), in0=v3(u, 1, Ls), scalar=sw[:, 24 + tile:25 + tile], in1=v3(cf, 0, Ls - 1), op0=ALU.mult, op1=ALU.add),
                     reads=[uk, cfk, "hyc"], writes=[cfk])
                if part < 2:
                    P.op("act", lambda e, cf=cf, ct=ct: e.copy(out=ucb[:, ct, :LT], in_=cf[:, :LT]), reads=[cfk], writes=[("hucb", ct)])
                else:
                    P.op("act", lambda e, cf=cf, ct=ct: e.copy(out=x2b[:, ct, :LT], in_=cf[:, :LT]), reads=[cfk], writes=[("hx2", ct)])
            if part < 2:
                dst = vtm if part == 0 else x1tm
                dk = "hvtm" if part == 0 else "hx1tm"
                for tc in range(LT // 128):
                    pb, pk = self.bank()
                    pbb = pb.bitcast(BF16)
                    for ct in range(4):
                        P.op("pe", lambda e, pbb=pbb, ct=ct, tc=tc: e.transpose(pbb[:, ct * 128:(ct + 1) * 128], ucb[:, ct, tc * 128:(tc + 1) * 128], self.identb[:]),
                             reads=[("hucb", ct), "identb"], writes=[pk])
                    if tc % 2 == 0:
                        P.op("act", lambda e, pbb=pbb, dst=dst, tc=tc: e.copy(out=dst[:, tc, :], in_=pbb[:, 0:512]), reads=[pk], writes=[(dk, tc // nchs)])
                    else:
                        P.op("dve", lambda e, pbb=pbb, dst=dst, tc=tc: e.tensor_copy(out=dst[:, tc, :], in_=pbb[:, 0:512]), reads=[pk], writes=[(dk, tc // nchs)])

    def hy_long(self, l, L, tok0, si, cst, F, Fk, FT, FTk, G0, G1, bufs, gen_G=None):
        P = self.P
        nch = L // 128
        uring, ucf, ucb, x2b_, vtm_, x1tm_, Yre, Ys, mt, ost = bufs
        vtm = vtm_[:, si * nch:(si + 1) * nch]; x1tm = x1tm_[:, si * nch:(si + 1) * nch]
        x2b = x2b_[:, :, si * L:(si + 1) * L]
        vk, x1k = ("hvtm", si), ("hx1tm", si)
        if gen_G is not None:
            gen_G(0)
        self.hy_dft_mult(L, F, Fk, vtm, vk, G0, "hG0", Yre, Ys, mt)
        for tc in range(nch):
            pb, pk = self.bank()
            for fc in range(nch):
                P.op("pe", lambda e, pb=pb, fc=fc, tc=tc: e.matmul(pb[:], lhsT=FT[:, fc, tc * 128:(tc + 1) * 128], rhs=Yre[:, fc, :], start=(fc == 0), stop=False),
                     reads=[FTk, "hYre"], writes=[pk])
            for fc in range(nch):
                P.op("pe", lambda e, pb=pb, fc=fc, tc=tc: e.matmul(pb[:], lhsT=FT[:, nch + fc, tc * 128:(tc + 1) * 128], rhs=Ys[:, fc, :], start=False, stop=(fc == nch - 1)),
                     reads=[FTk, "hYs"], writes=[pk])
            P.op("dve", lambda e, pb=pb, tc=tc: e.scalar_tensor_tensor(out=vtm[:, tc, :], in0=pb[:], scalar=1.0 / L, in1=x1tm[:, tc, :], op0=ALU.mult, op1=ALU.mult),
                 reads=[pk, x1k], writes=[vk])
        if gen_G is not None:
            gen_G(1)
        self.hy_dft_mult(L, F, Fk, vtm, vk, G1, "hG1", Yre, Ys, mt)
        for ct in range(4):
            for t0 in range(0, L, 512):
                n = min(512, L - t0)
                pb, pk = self.bank()
                for fc in range(nch):
                    P.op("pe", lambda e, pb=pb, fc=fc, ct=ct, t0=t0, n=n: e.matmul(pb[:, :n], lhsT=Yre[:, fc, ct * 128:(ct + 1) * 128], rhs=FT[:, fc, t0:t0 + n], start=(fc == 0), stop=False),
                         reads=[FTk, "hYre"], writes=[pk])
                for fc in range(nch):
                    P.op("pe", lambda e, pb=pb, fc=fc, ct=ct, t0=t0, n=n: e.matmul(pb[:, :n], lhsT=Ys[:, fc, ct * 128:(ct + 1) * 128], rhs=FT[:, nch + fc, t0:t0 + n], start=False, stop=(fc == nch - 1)),
                         reads=[FTk, "hYs"], writes=[pk])
                i = self.hoi % 2; self.hoi += 1
                o = ost[i]; ok_ = ("host", i)
                P.op("dve", lambda e, pb=pb, o=o, ct=ct, t0=t0, n=n: e.scalar_tensor_tensor(out=o[:, :n], in0=pb[:, :n], scalar=1.0 / L, in1=x2b[:, ct, t0:t0 + n], op0=ALU.mult, op1=ALU.mult),
                     reads=[pk, ("hx2", ct)], writes=[ok_])
                P.dma(self.oT[1536 + ct * 128:1536 + (ct + 1) * 128, tok0 + t0:tok0 + t0 + n], o[:, :n], reads=[ok_], writes=[("oT", 3)])

    def phase_hyena(self, l):
        P, d = self.P, self.din
        with self.phase() as ph:
            cst = dict(w1=ph.sb("hw1", [33, 64], F32), w2=ph.sb("hw2", [64, 64], F32), w3=ph.sb("hw3", [64, 2048], F32),
                       freq=ph.sb("hfreq", [64, 1], F32), fb=ph.sb("hfb", [64, 2], F32), dabs=ph.sb("hdabs", [128, 2048], F32),
                       biasb=ph.sb("hbiasb", [128, 2, 512], F32), sw=ph.sb("hsw", [128, 36], F32),
                       nt={256: ph.sb("hnt256", [128, 2], F32), 1024: ph.sb("hnt1024", [128, 8], F32)})
            P.dma(cst["w1"][:], d["hy_w1"][l], writes=["hyc"]); P.dma(cst["w2"][:], d["hy_w2"][l], writes=["hyc2"]); P.dma(cst["w3"][:], d["hy_w3"][l], writes=["hyc3"])
            P.dma(cst["freq"][:], d["hy_freq"][l].rearrange("(p o) -> p o", o=1), writes=["hyc4"])
            P.dma(cst["fb"][:, 0:1], d["hy_b1"][l].rearrange("(p o) -> p o", o=1), writes=["hyc5"])
            P.dma(cst["fb"][:, 1:2], d["hy_b2"][l].rearrange("(p o) -> p o", o=1), writes=["hyc6"])
            P.dma(cst["dabs"][:], d["hy_decay"][l].partition_broadcast(128), writes=["hyc7"])
            P.dma(cst["biasb"][:], d["hy_bias"][l].partition_broadcast(128), writes=["hyc8"])
            P.dma(cst["nt"][256][:], d["c_nt256"], writes=["hyc9"]); P.dma(cst["nt"][1024][:], d["c_nt1024"], writes=["hyc10"])
            self.load_T(ph, cst["sw"][:], ["hycsw"], d["hy_short"][l].rearrange("k (t p) -> (k t) p", p=128), 36)
            allk = ["hyc", "hyc2", "hyc3", "hyc4", "hyc5", "hyc6", "hyc7", "hyc8", "hyc9", "hyc10", "hycsw"]
            P.op("dve", lambda e: e.tensor_scalar(out=cst["fb"][:], in0=cst["fb"][:], scalar1=cst["freq"][:, 0:1], scalar2=None, op0=ALU.mult), reads=allk, writes=["hycA"], hard=True)
            P.op("act", lambda e: e.activation(out=cst["dabs"][:], in_=cst["dabs"][:], func=AF.Abs), reads=["hycA"], writes=["hycB"])
            P.op("dve", lambda e: e.tensor_copy(out=cst["nt"][256][:], in_=cst["nt"][256][:]), reads=["hycB"] + allk, writes=["hyc"])
            win = ph.sb("hwin", [128, 1024], F32); ta = ph.sb("hta", [128, 512], F32); tb = ph.sb("htb", [128, 512], F32)
            mt = [ph.sb(f"hm{i}", [128, 512], F32) for i in range(4)]
            ost = [ph.sb(f"host{i}", [128, 512], BF16) for i in range(2)]
            uring = [ph.sb(f"hur{i}", [128, 1024], F32) for i in range(2)]
            ucf = [ph.sb("hcf0", [128, 1024], F32)] * 2
            ucb = ph.sb("hucb", [128, 4, 1024], BF16); x2b = ph.sb("hx2b", [128, 4, 1024], BF16)
            vtm = ph.sb("hvtm", [128, 8, 512], BF16); x1tm = ph.sb("hx1tm", [128, 8, 512], BF16)
            Yre = ph.sb("hYre", [128, 8, 512], BF16); Ys = ph.sb("hYs", [128, 8, 512], BF16)
            hs, hd = Yre, Ys
            h2T = ph.sb("hh2", [64, 256], F32); h2Tb = ph.sb("hh2b", [64, 1024], F32)
            self.hoi = 0
            Ga = ph.sb("hGa", [128, 8, 2, 512], F32)
            bufs = (uring, ucf, ucb, x2b, vtm, x1tm, Yre, Ys, mt, ost)
            with self.phase() as p3:
                h2k = self.hy_ffn(p3, l, 256, h2T, cst)
            with self.phase() as p2:
                F = p2.sb("hF256", [128, 2, 512], BF16); FT = p2.sb("hFT256", [128, 4, 256], BF16)
                P.dma(F[:], d["c_F256"].rearrange("(c p) f -> p c f", p=128), writes=["hF"])
                P.dma(FT[:], d["c_FT256"].rearrange("(c p) f -> p c f", p=128), writes=["hFT"])
                G0 = Ga[:, 0:2]; G1 = Ga[:, 2:4]
                self.hy_G(p2, l, 256, 0, h2T, h2k, cst, F, "hF", hs, hd, G0, "hG0", (win, ta, tb))
                self.hy_G(p2, l, 256, 1, h2T, h2k, cst, F, "hF", hs, hd, G1, "hG1", (win, ta, tb))
                self.hy_conv(l, 0, 4, 256, cst, bufs)
                for s in range(4):
                    self.hy_long(l, 256, s * 256, s, cst, F, "hF", FT, "hFT", G0, G1, bufs)
                    if s == 1:
                        h2k = self.hy_ffn(p2, l, 1024, h2Tb, cst)
            with self.phase() as p2:
                F = p2.sb("hF1024", [128, 8, 2048], BF16); FT = p2.sb("hFT1024", [128, 16, 1024], BF16)
                for q in range(4):
                    P.dma(F[:, q * 2:(q + 1) * 2, :], d["c_F1024"][q * 256:(q + 1) * 256, :].rearrange("(c p) f -> p c f", p=128), writes=[("hF", q)])
                    P.dma(FT[:, q * 4:(q + 1) * 4, :], d["c_FT1024"][q * 512:(q + 1) * 512, :].rearrange("(c p) f -> p c f", p=128), writes=[("hFT", q)])
                P.op("pool", lambda e: e.tensor_copy(out=F[:, 0, 0:2], in_=F[:, 0, 0:2]), reads=[("hF", q) for q in range(4)], writes=["hF"])
                P.op("pool", lambda e: e.tensor_copy(out=FT[:, 0, 0:2], in_=FT[:, 0, 0:2]), reads=[("hFT", q) for q in range(4)], writes=["hFT"])

                def gen_G(o):
                    self.hy_G(p2, l, 1024, o, h2Tb, h2k, cst, F, "hF", hs, hd, Ga, "hG%d" % o, (win, ta, tb))
                self.hy_conv(l, 1024, 1, 1024, cst, bufs)
                self.hy_long(l, 1024, 1024, 0, cst, F, "hF", FT, "hFT", Ga, Ga, bufs, gen_G=gen_G)

    def phase_mixers(self, l):
        self.sbank = 0
        sel = self.dbg.get("mix", "abcd") if isinstance(self.dbg, dict) else "abcd"
        if "a" in sel:
            self.phase_attn_a(l)
        if "b" in sel:
            self.phase_attn_b(l)
        if "c" in sel:
            self.phase_attn_c(l)
        if "d" in sel:
            self.phase_hyena(l)

def _consts():
    c = {}
    c["c_ident"] = np.eye(128, dtype=np.float32)
    c["c_identb"] = np.eye(128).astype(ml_dtypes.bfloat16)
    c["c_onesb"] = np.ones((128, 128)).astype(ml_dtypes.bfloat16)
    f = np.arange(128)
    partner = (f // 32) * 32 + ((f % 32) + 16) % 32
    pm = np.zeros((128, 128), np.float32)
    pm[partner, f] = 1.0
    c["c_rperm"] = pm
    t = np.arange(1024)
    fh = f % 64
    q = fh // 16
    j = fh % 16
    inv = (10000.0 ** (-(j.astype(np.float32)) / 16.0)).astype(np.float32)
    pos = np.where((q < 2)[:, None], (t // 64)[None, :], (t % 64)[None, :]).astype(np.float32)
    ang = (pos * inv[:, None]).astype(np.float32)
    c["c_rcos"] = np.cos(ang).astype(np.float32)
    sgn = np.where(q % 2 == 0, -1.0, 1.0).astype(np.float32)
    c["c_rsin"] = (np.sin(ang) * sgn[:, None]).astype(np.float32)
    kc = (np.arange(128) % 64)[:, None]
    qc = np.arange(64)[None, :]
    col0 = np.clip(qc - 8, 0, 48)
    valid = (kc >= col0) & (kc < col0 + 16)
    c["c_m01"] = valid.astype(np.float32)
    c["c_mng"] = np.where(valid, 0.0, -30000.0).astype(np.float32)
    for L in (256, 1024):
        n = np.arange(L, dtype=np.float32)[:, None]
        t = n / np.float32(max(L - 1, 1))
        w = (np.float32(2.0 * math.pi) * n / np.float32(L)).astype(np.float32)
        bands = np.linspace(1e-4, 15, 16, dtype=np.float32)[None, :]
        z = np.concatenate([t, np.cos(bands * w), -np.sin(bands * w)], axis=-1).astype(np.float32)
        c[f"c_z{L}"] = np.ascontiguousarray(z.T)
        c[f"c_nt{L}"] = np.ascontiguousarray((-t[:, 0]).reshape(L // 128, 128).T.astype(np.float32))
        nn = np.arange(L, dtype=np.float64)[:, None]
        om = math.pi * (2 * np.arange(L, dtype=np.float64)[None, :] + 1) / (2 * L)
        Fm = np.concatenate([np.cos(nn * om), np.sin(nn * om)], axis=1)
        c[f"c_F{L}"] = Fm.astype(np.float32).astype(ml_dtypes.bfloat16)
        c[f"c_FT{L}"] = np.ascontiguousarray(Fm.T).astype(np.float32).astype(ml_dtypes.bfloat16)
    kk = np.arange(128)[:, None]; qq = np.arange(128)[None, :]
    c["c_cmask"] = np.stack([(kk >= qq), (kk <= qq)], axis=1).astype(np.float32).astype(ml_dtypes.bfloat16)
    return c


_CACHE = {}


def kernel(**inputs):
    inputs = {k: np.asarray(v) for k, v in inputs.items()}
    if "nc" not in _CACHE:
        _CACHE["nc"] = Builder().build()
    nc = _CACHE["nc"]
    consts = _consts()
    shared = {k: np.ascontiguousarray(inputs[k]) for k in (
        "w_ada", "b_ada", "g_mix", "w_in", "diff_lambda", "diff_norm_g", "na_bias", "swa_sink", "hy_short", "hy_w1", "hy_b1",
        "hy_w2", "hy_b2", "hy_w3", "hy_freq", "hy_decay", "hy_bias", "w_branch", "w_out", "g_mlp", "w_up", "w_down", "g_final")}
    in_maps = []
    for i in range(8):
        m = dict(shared)
        m.update(consts)
        m["xc"] = np.ascontiguousarray(inputs["x_prompt"][4 * i:4 * i + 4].reshape(1024, D))
        m["xl"] = np.ascontiguousarray(inputs["x_sample"][i])
        m["cak"] = np.ascontiguousarray(inputs["cache_a_k"][i].reshape(2, 256, 512))
        m["cav"] = np.ascontiguousarray(inputs["cache_a_v"][i].reshape(2, 256, 512))
        m["cbk"] = np.ascontiguousarray(inputs["cache_b_k"][i].reshape(2, 256, 512))
        m["cbv"] = np.ascontiguousarray(inputs["cache_b_v"][i].reshape(2, 256, 512))
        m["cck"] = np.ascontiguousarray(inputs["cache_c_k"][i].reshape(2, 256, 128))
        m["ccv"] = np.ascontiguousarray(inputs["cache_c_v"][i].reshape(2, 256, 128))
        m["cvec"] = np.ascontiguousarray(np.stack([inputs["c_ctx"], inputs["c"][i]], axis=0))
        in_maps.append(m)
    res = run_bass_kernel_spmd(nc, in_maps, core_ids=list(range(8)))
    r = res.results
    y_prompt = np.concatenate([r[i]["y_c"].reshape(4, 256, D) for i in range(8)], axis=0)
    y_sample = np.stack([r[i]["y_l"] for i in range(8)], axis=0)
    def cat(name, shp):
        return np.concatenate([r[i][name] for i in range(8)], axis=0).reshape(shp)
    return (y_prompt.astype(np.float32), y_sample.astype(np.float32),
            cat("nak", (32, 2, 256, 2, 4, 64)), cat("nav", (32, 2, 256, 4, 128)),
            cat("nbk", (32, 2, 256, 8, 64)), cat("nbv", (32, 2, 256, 8, 64)),
            cat("nck", (32, 2, 256, 2, 64)), cat("ncv", (32, 2, 256, 2, 64)))
```
